# Optimizing a Trainium2 kernel written in Bass

```python
import math
import jax, jax.numpy as jnp
from jax import lax
import numpy as np

D_MODEL = 2048
BATCH = 2
SEQ = 8192
DEPTH = 4

N_MIXERS = 3
HEAD_DIM = 128
A_Q_HEADS = D_MODEL // HEAD_DIM
A_KV_HEADS = 4
A_BRANCHES = ((128, 1), (512, 4), (2048, 16))
A_SEQ_MULT = max(w for w, _ in A_BRANCHES)
A_QKV_WIDTH = D_MODEL + len(A_BRANCHES) * 2 * A_KV_HEADS * HEAD_DIM
B_HEADS = D_MODEL // (2 * HEAD_DIM)
B_QBLOCK = 128
C_CONV_WIDTH = 3
D_FF = ((8 * D_MODEL // 3 + 255) // 256) * 256
PLE_DIM = 256
N_A = (DEPTH + 2) // 3
N_B = (DEPTH + 1) // 3
N_C = DEPTH // 3
RMS_EPS = 1e-6
SUBLN_EPS = 1e-5

kernel_name = "hybrid_dilated_diffattn_shortconv_macaron"


def rmsnorm(x, g, eps=RMS_EPS):
    xf = x.astype(jnp.float32)
    y = xf * lax.rsqrt(jnp.mean(xf * xf, axis=-1, keepdims=True) + eps)
    return (y * g.astype(jnp.float32)).astype(x.dtype)


def swiglu(h, w_in, w_out):
    gate, up = jnp.split(h @ w_in, 2, axis=-1)
    return (jax.nn.silu(gate) * up) @ w_out


def _to_residues(t, d):
    b, sp = t.shape[:2]
    rest = t.shape[2:]
    t = t.reshape((b, sp // d, d) + rest)
    t = jnp.moveaxis(t, 2, 1)
    return t.reshape((b * d, sp // d) + rest)


def _from_residues(t, b, d):
    n, l = t.shape[:2]
    rest = t.shape[2:]
    t = t.reshape((b, d, l) + rest)
    t = jnp.moveaxis(t, 1, 2)
    return t.reshape((b, l * d) + rest)


def _banded_causal_attention(q, k, v, block):
    n, length, hq, hd = q.shape
    hkv = k.shape[2]
    g = hq // hkv
    nb = length // block
    qb = q.reshape(n, nb, block, hkv, g, hd)

    def with_prev(t):
        tb = t.reshape(n, nb, block, hkv, hd)
        prev = jnp.pad(tb, ((0, 0), (1, 0), (0, 0), (0, 0), (0, 0)))[:, :nb]
        return jnp.concatenate([prev, tb], axis=2)

    kk = with_prev(k)
    vv = with_prev(v)
    s = jnp.einsum('nbqhgd,nbkhd->nbhgqk', qb, kk).astype(jnp.float32) * (hd ** -0.5)
    qi = jnp.arange(block)[:, None] + block
    kj = jnp.arange(2 * block)[None, :]
    dist = qi - kj
    band = (dist >= 0) & (dist <= block)
    first = jnp.arange(nb)[:, None, None] == 0
    mask = band[None] & ~(first & (kj < block)[None])
    s = jnp.where(mask[None, :, None, None], s, -jnp.inf)
    m = jnp.max(s, axis=-1)
    pr = jnp.exp(s - m[..., None])
    l = jnp.sum(pr, axis=-1)
    acc = jnp.einsum('nbhgqk,nbkhd->nbqhgd', pr.astype(v.dtype), vv).astype(jnp.float32)
    m = m.transpose(0, 1, 4, 2, 3).reshape(n, length, hq)
    l = l.transpose(0, 1, 4, 2, 3).reshape(n, length, hq)
    acc = acc.reshape(n, length, hq, hd)
    return m, l, acc


def dilated_sliding_window_attention(h, w_qkv, w_o):
    b, s, _ = h.shape
    sp = -(-s // A_SEQ_MULT) * A_SEQ_MULT
    qkv = jnp.pad(h @ w_qkv, ((0, 0), (0, sp - s), (0, 0)))
    q = qkv[..., :D_MODEL].reshape(b, sp, A_Q_HEADS, HEAD_DIM)
    kv = qkv[..., D_MODEL:].reshape(b, sp, len(A_BRANCHES), 2, A_KV_HEADS, HEAD_DIM)
    ms, ls, accs = [], [], []
    for gi, (window, dil) in enumerate(A_BRANCHES):
        m, l, acc = _banded_causal_attention(
            _to_residues(q, dil), _to_residues(kv[:, :, gi, 0], dil),
            _to_residues(kv[:, :, gi, 1], dil), window // dil)
        ms.append(_from_residues(m, b, dil)[:, :s])
        ls.append(_from_residues(l, b, dil)[:, :s])
        accs.append(_from_residues(acc, b, dil)[:, :s])
    m = jnp.stack(ms)
    l = jnp.stack(ls)
    acc = jnp.stack(accs)
    wgt = jnp.exp(m - jnp.max(m, axis=0, keepdims=True))
    out = jnp.sum(wgt[..., None] * acc, axis=0) / jnp.sum(wgt * l, axis=0)[..., None]
    return out.reshape(b, s, D_MODEL).astype(h.dtype) @ w_o


def differential_attention(h, w_qkv, w_o, lam_params, subln_g, lambda_init):
    b, s, _ = h.shape
    nb = s // B_QBLOCK
    qkv = h @ w_qkv
    q = qkv[..., :D_MODEL].reshape(b, s, B_HEADS, 2, HEAD_DIM)
    k = qkv[..., D_MODEL:2 * D_MODEL].reshape(b, s, B_HEADS, 2, HEAD_DIM)
    v = qkv[..., 2 * D_MODEL:].reshape(b, s, B_HEADS, 2 * HEAD_DIM)
    lp = lam_params.astype(jnp.float32)
    lam = jnp.exp(jnp.sum(lp[0] * lp[1])) - jnp.exp(jnp.sum(lp[2] * lp[3])) + lambda_init
    qb = q.reshape(b, nb, B_QBLOCK, B_HEADS, 2, HEAD_DIM).transpose(1, 0, 3, 4, 2, 5)
    kt = k.transpose(0, 2, 3, 1, 4)
    vt = v.transpose(0, 2, 1, 3)
    kpos = jnp.arange(s)
    scale = HEAD_DIM ** -0.5

    def attend_block(args):
        qblk, bi = args
        sc = jnp.einsum('bhcqd,bhckd->bhcqk', qblk, kt).astype(jnp.float32) * scale
        qpos = bi * B_QBLOCK + jnp.arange(B_QBLOCK)
        sc = jnp.where(kpos[None, :] <= qpos[:, None], sc, -jnp.inf)
        a = jax.nn.softmax(sc, axis=-1)
        diff = a[:, :, 0] - lam * a[:, :, 1]
        return jnp.einsum('bhqk,bhkd->bhqd', diff.astype(vt.dtype), vt)

    o = lax.map(attend_block, (qb, jnp.arange(nb)))
    o = o.transpose(1, 0, 3, 2, 4).reshape(b, s, B_HEADS, 2 * HEAD_DIM)
    o = rmsnorm(o, subln_g, SUBLN_EPS) * (1.0 - lambda_init)
    return o.reshape(b, s, D_MODEL) @ w_o


def short_gated_conv(h, w_in, conv_w, w_out):
    bg, cg, u = jnp.split(h @ w_in, 3, axis=-1)
    z = cg * u
    conv = lax.conv_general_dilated(
        z, conv_w[:, None, :].astype(z.dtype), window_strides=(1,),
        padding=((C_CONV_WIDTH - 1, 0),), dimension_numbers=('NWC', 'WIO', 'NWC'),
        feature_group_count=D_MODEL)
    return (bg * conv) @ w_out


def setup_inputs(seed: int = 0) -> dict:
    key = jax.random.key(seed)
    ks = jax.random.split(key, 32)

    def nrm(k, shape, scale):
        return jax.random.normal(k, shape, jnp.float32) * scale

    def gain(k, shape):
        return 1.0 + 0.02 * jax.random.normal(k, shape, jnp.float32)

    D, F = D_MODEL, D_FF
    return {
        "x": nrm(ks[0], (BATCH, SEQ, D), 1.0),
        "p": nrm(ks[1], (DEPTH, BATCH, SEQ, PLE_DIM), 1.0),
        "norm_ffn1": gain(ks[2], (DEPTH, D)),
        "w_ffn1_in": nrm(ks[3], (DEPTH, D, 2 * F), D ** -0.5),
        "w_ffn1_out": nrm(ks[4], (DEPTH, F, D), F ** -0.5),
        "norm_mix": gain(ks[5], (DEPTH, D)),
        "a_w_qkv": nrm(ks[6], (N_A, D, A_QKV_WIDTH), D ** -0.5),
        "a_w_o": nrm(ks[7], (N_A, D, D), D ** -0.5),
        "b_w_qkv": nrm(ks[8], (N_B, D, 3 * D), D ** -0.5),
        "b_w_o": nrm(ks[9], (N_B, D, D), D ** -0.5),
        "b_lambda": nrm(ks[10], (N_B, 4, HEAD_DIM), 0.1),
        "b_subln": gain(ks[11], (N_B, 2 * HEAD_DIM)),
        "c_w_in": nrm(ks[12], (N_C, D, 3 * D), D ** -0.5),
        "c_conv_w": nrm(ks[13], (N_C, C_CONV_WIDTH, D), C_CONV_WIDTH ** -0.5),
        "c_w_out": nrm(ks[14], (N_C, D, D), D ** -0.5),
        "norm_ffn2": gain(ks[15], (DEPTH, D)),
        "w_ffn2_in": nrm(ks[16], (DEPTH, D, 2 * F), D ** -0.5),
        "w_ffn2_out": nrm(ks[17], (DEPTH, F, D), F ** -0.5),
        "norm_ple": gain(ks[18], (DEPTH, D)),
        "w_ple_gate": nrm(ks[19], (DEPTH, D, D), D ** -0.5),
        "b_ple_gate": nrm(ks[20], (DEPTH, D), 0.02),
        "w_ple_proj": nrm(ks[21], (DEPTH, PLE_DIM, D), PLE_DIM ** -0.5),
        "norm_f": gain(ks[22], (D,)),
    }


def reference(x, p, norm_ffn1, w_ffn1_in, w_ffn1_out, norm_mix, a_w_qkv, a_w_o,
              b_w_qkv, b_w_o, b_lambda, b_subln, c_w_in, c_conv_w, c_w_out,
              norm_ffn2, w_ffn2_in, w_ffn2_out, norm_ple, w_ple_gate, b_ple_gate,
              w_ple_proj, norm_f):
    h = x
    for i in range(DEPTH):
        h = h + 0.5 * swiglu(rmsnorm(h, norm_ffn1[i]), w_ffn1_in[i], w_ffn1_out[i])
        hn = rmsnorm(h, norm_mix[i])
        kind, j = i % N_MIXERS, i // N_MIXERS
        if kind == 0:
            mix = dilated_sliding_window_attention(hn, a_w_qkv[j], a_w_o[j])
        elif kind == 1:
            lambda_init = 0.8 - 0.6 * math.exp(-0.3 * i)
            mix = differential_attention(hn, b_w_qkv[j], b_w_o[j], b_lambda[j],
                                         b_subln[j], lambda_init)
        else:
            mix = short_gated_conv(hn, c_w_in[j], c_conv_w[j], c_w_out[j])
        h = h + mix
        h = h + 0.5 * swiglu(rmsnorm(h, norm_ffn2[i]), w_ffn2_in[i], w_ffn2_out[i])
        gate = jax.nn.sigmoid(rmsnorm(h, norm_ple[i]) @ w_ple_gate[i] + b_ple_gate[i])
        h = h + gate * (p[i] @ w_ple_proj[i])
    return rmsnorm(h, norm_f)
```

```python
import contextlib
import numpy as np
import ml_dtypes
import concourse.bass as bass
import concourse.mybir as mybir
from concourse.bass_utils import run_bass_kernel_spmd

F32 = mybir.dt.float32
BF16 = mybir.dt.bfloat16
AF = mybir.ActivationFunctionType
ALU = mybir.AluOpType

D = 2048
KC = 16
TOK = 2048
HALF = 1024
TT = 512
DFF = 5632
FS = 44
PLE = 256
NCORES = 8
NEG = -30000.0
RMS_EPS = 1e-6
SUBLN_EPS = 1e-5


class Buf:
    __slots__ = ("name", "w", "r", "sem", "cnt", "t")

    def __init__(self, name, t=None):
        self.name = name
        self.w = {}
        self.r = {}
        self.sem = None
        self.cnt = 0
        self.t = t


class Eng:
    def __init__(self, name):
        self.name = name
        self.ops = []
        self.sem = None
        self.cnt = 0
        self.waited = {}


class Prog:
    def __init__(self):
        self.nc = bass.Bass("TRN2", target_bir_lowering=False)
        self.es = contextlib.ExitStack()
        self.engs = {n: Eng(n) for n in ("tensor", "vector", "scalar", "gpsimd", "sync")}
        self.semh = {}
        self.nsem = 0
        for n in ("tensor", "vector", "scalar"):
            self.engs[n].sem = self.new_sem("e_" + n)
        self.out_bufs = []
        self.in_names = []
        self.nbuf = 0

    def new_sem(self, name=None):
        self.nsem += 1
        h = self.es.enter_context(self.nc.semaphore(name or ("s%d" % self.nsem)))
        self.semh[self.nsem] = h
        return self.nsem

    def sb(self, name, shape, dtype):
        t = self.es.enter_context(self.nc.sbuf_tensor("sb_" + name, list(shape), dtype))
        return Buf(name, t)

    def ps(self, name):
        t = self.es.enter_context(self.nc.psum_tensor("ps_" + name, [128, 512], F32))
        return Buf(name, t)

    def dram_in(self, name, shape, dtype=F32):
        self.in_names.append(name)
        return self.nc.dram_tensor(name, list(shape), dtype, kind="ExternalInput").ap()

    def dram_out(self, name, shape, dtype=F32):
        return self.nc.dram_tensor(name, list(shape), dtype, kind="ExternalOutput").ap()

    def dram_tmp(self, name, shape, dtype=F32):
        return self.nc.dram_tensor(name, list(shape), dtype).ap()

    def vbuf(self, name):
        self.nbuf += 1
        return Buf("%s_%d" % (name, self.nbuf))

    def _collect(self, E, reads, writes, extra=()):
        waits = {}

        def need(sem, val):
            if E.waited.get(sem, 0) >= val:
                return
            if waits.get(sem, 0) < val:
                waits[sem] = val

        for b in reads:
            for sem, val in b.w.items():
                need(sem, val)
        for b in writes:
            for sem, val in b.w.items():
                need(sem, val)
            for sem, val in b.r.items():
                need(sem, val)
        for sem, val in extra:
            need(sem, val)
        return waits

    @staticmethod
    def _mark(ev, reads, writes):
        sem, val = ev
        for b in reads:
            if b.r.get(sem, 0) < val:
                b.r[sem] = val
        for b in writes:
            if b.w.get(sem, 0) < val:
                b.w[sem] = val

    def op(self, eng, fn, reads=(), writes=(), inc=True):
        E = self.engs[eng]
        waits = self._collect(E, reads, writes)
        if eng == "tensor":
            waits.pop(E.sem, None)
        for sem, val in waits.items():
            E.waited[sem] = val
        if inc:
            E.cnt += 1
            ev = (E.sem, E.cnt)
        else:
            ev = (E.sem, E.cnt + 1)
        E.ops.append((sorted(waits.items()), fn, (E.sem, 1) if inc else None))
        self._mark(ev, reads, writes)

    def dma(self, eng, out, in_, owner, reads=(), writes=(), **kw):
        E = self.engs[eng]
        if owner.sem is None:
            owner.sem = self.new_sem("d_" + owner.name)
        waits = self._collect(E, reads, writes, extra=[(owner.sem, owner.cnt)] if owner.cnt else [])
        for sem, val in waits.items():
            E.waited[sem] = val
        owner.cnt += 16
        ev = (owner.sem, owner.cnt)
        E.ops.append((sorted(waits.items()), lambda e, o=out, i=in_: e.dma_start(out=o, in_=i, **kw), (owner.sem, 16)))
        self._mark(ev, reads, writes)

    def barrier_wait(self, eng, bufs):
        E = self.engs[eng]
        waits = self._collect(E, [], bufs)
        for sem, val in waits.items():
            E.waited[sem] = val
        if waits:
            E.ops.append((sorted(waits.items()), None, None))

    def check(self):
        sems = {}
        pos = {n: 0 for n in self.engs}
        progress = True
        while progress:
            progress = False
            for n, E in self.engs.items():
                while pos[n] < len(E.ops):
                    waits, fn, inc = E.ops[pos[n]]
                    if any(sems.get(s, 0) < v for s, v in waits):
                        break
                    if inc is not None:
                        sems[inc[0]] = sems.get(inc[0], 0) + inc[1]
                    pos[n] += 1
                    progress = True
        stuck = {n: (pos[n], len(E.ops)) for n, E in self.engs.items() if pos[n] < len(E.ops)}
        if stuck:
            msg = []
            for n in stuck:
                waits, fn, inc = self.engs[n].ops[pos[n]]
                msg.append("%s@%d waits %s have %s" % (n, pos[n], waits, [(s, sems.get(s, 0)) for s, _ in waits]))
            raise RuntimeError("DEADLOCK: " + "; ".join(msg))
        return {n: len(E.ops) for n, E in self.engs.items()}

    def emit(self):
        self.barrier_wait("sync", self.out_bufs)
        self.check()
        nc = self.nc
        semh = self.semh

        def replay(e, E):
            for waits, fn, inc in E.ops:
                for sem, val in waits:
                    e.wait_ge(semh[sem], val)
                if fn is None:
                    continue
                inst = fn(e)
                if inc is not None:
                    inst.then_inc(semh[inc[0]], inc[1])

        with nc.Block() as block:
            @block.tensor
            def _(e):
                replay(e, self.engs["tensor"])

            @block.vector
            def _(e):
                replay(e, self.engs["vector"])

            @block.scalar
            def _(e):
                replay(e, self.engs["scalar"])

            @block.gpsimd
            def _(e):
                replay(e, self.engs["gpsimd"])

            @block.sync
            def _(e):
                replay(e, self.engs["sync"])
        self.es.close()
        return nc


class Rows:
    def __init__(self, P, vecs_ap, nvec):
        self.P = P
        self.act = P.sb("act", [128, FS * HALF], BF16)
        self.xn = P.sb("xn", [128, KC * HALF], BF16)
        self.win = [P.sb("win%d" % i, [128, KC * 128], BF16) for i in range(4)]
        self.wout = [P.sb("wout%d" % i, [128, FS * 128], BF16) for i in range(2)]
        self.ht = [P.sb("ht%d" % i, [128, TT], F32) for i in range(6)]
        self.tmp = [P.sb("tmp%d" % i, [128, TT], F32) for i in range(3)]
        self.sq = [P.sb("sq%d" % i, [128, TT], BF16) for i in range(2)]
        self.rstd = [P.sb("rstd%d" % i, [128, TT], F32) for i in range(2)]
        self.epsc = P.sb("epsc", [128, 2], F32)
        self.rtmp = P.sb("rtmp", [128, TT], F32)
        self.ob = [P.sb("ob%d" % i, [128, TT], BF16) for i in range(4)]
        self.ones = P.sb("ones", [128, 128], BF16)
        self.vecs = P.sb("vecs", [128, nvec], F32)
        self.pt = P.sb("pt", [128, 2 * HALF], BF16)
        self.wp = [P.sb("wp%d" % i, [128, 2 * 128], BF16) for i in range(2)]
        self.banks = [P.ps("bank%d" % i) for i in range(8)]
        self.cnt = {}
        P.op("vector", lambda e: e.memset(self.ones.t[:], 1.0), writes=[self.ones])
        P.op("vector", lambda e: e.memset(self.epsc.t[:, 0:1], RMS_EPS), writes=[self.epsc])
        P.dma("sync", self.vecs.t[:], vecs_ap, self.vecs, writes=[self.vecs])

    def rr(self, key, lst):
        i = self.cnt.get(key, 0)
        self.cnt[key] = i + 1
        return lst[i % len(lst)]

    def xs(self, k, j):
        return self.xn.t[:, k * HALF + j * TT:k * HALF + (j + 1) * TT]

    def stats(self, hsrc, hsrc_bufs, n0, j, eps, dim=D):
        P = self.P
        hs = hsrc.rearrange("(k p) n -> k p n", p=128)
        bank = self.banks[6 + j]
        for k in range(KC):
            ht = self.rr("ht", self.ht)
            P.dma("gpsimd", ht.t[:], hs[k, :, n0:n0 + TT], ht, reads=[hsrc_bufs[n0 // TT]], writes=[ht])
            sq = self.rr("sq", self.sq)
            P.op("scalar", lambda e, s=sq, t=ht: e.activation(out=s.t[:], in_=t.t[:], func=AF.Square),
                 reads=[ht], writes=[sq])
            P.op("tensor", lambda e, s=sq, k=k: e.matmul(bank.t[:], self.ones.t[:], s.t[:], start=(k == 0), stop=(k == KC - 1)),
                 reads=[sq, self.ones], writes=[bank], inc=True)
        P.op("scalar", lambda e: e.activation(out=self.rtmp.t[:], in_=bank.t[:], func=AF.Sqrt, bias=self.epsc.t[:, 0:1], scale=1.0 / dim),
             reads=[bank, self.epsc], writes=[self.rtmp])
        P.op("vector", lambda e: e.reciprocal(out=self.rstd[j].t[:], in_=self.rtmp.t[:]), reads=[self.rtmp], writes=[self.rstd[j]])

    def norm(self, hsrc, hsrc_bufs, half, gcol, eps=RMS_EPS):
        P = self.P
        hs = hsrc.rearrange("(k p) n -> k p n", p=128)
        for j in range(2):
            self.stats(hsrc, hsrc_bufs, half * HALF + j * TT, j, eps)
        for j in range(2):
            n0 = half * HALF + j * TT
            for k in range(KC):
                ht = self.rr("ht", self.ht)
                P.dma("gpsimd", ht.t[:], hs[k, :, n0:n0 + TT], ht, reads=[hsrc_bufs[n0 // TT]], writes=[ht])
                P.op("vector", lambda e, k=k, j=j, t=ht: e.scalar_tensor_tensor(
                    out=self.xs(k, j), in0=t.t[:], scalar=self.vecs.t[:, gcol + k:gcol + k + 1], in1=self.rstd[j].t[:],
                    op0=ALU.mult, op1=ALU.mult), reads=[ht, self.vecs, self.rstd[j]], writes=[self.xn])

    def load_w(self, slot, w_ap, c0, ncol, kchunks, r0=0):
        src = w_ap[r0:r0 + kchunks * 128, :].rearrange("(k p) c -> p k c", p=128)[:, :, c0:c0 + ncol]
        dst = slot.t[:, 0:kchunks * ncol].rearrange("p (k c) -> p k c", k=kchunks)
        self.P.dma("gpsimd", dst, src, slot, writes=[slot])

    def mm_group(self, bank, lhs_fn, rhs_fn, nk, reads, ncol=TT):
        for k in range(nk):
            self.P.op("tensor", lambda e, k=k: e.matmul(bank.t[:, 0:ncol], lhs_fn(k), rhs_fn(k), start=(k == 0), stop=(k == nk - 1)),
                      reads=reads, writes=[bank], inc=(k == nk - 1))

    def add_store(self, bank, scale, hsrc, hsrc_bufs, hdst, hdst_bufs, oc, n0, mul_by=None):
        P = self.P
        hs = hsrc.rearrange("(k p) n -> k p n", p=128)
        hd = hdst.rearrange("(k p) n -> k p n", p=128)
        ht = self.rr("ht", self.ht)
        P.dma("gpsimd", ht.t[:], hs[oc, :, n0:n0 + TT], ht, reads=[hsrc_bufs[n0 // TT]], writes=[ht])
        if mul_by is None:
            P.op("vector", lambda e, t=ht, b=bank: e.scalar_tensor_tensor(
                out=t.t[:], in0=b.t[:], scalar=scale, in1=t.t[:], op0=ALU.mult, op1=ALU.add), reads=[bank, ht], writes=[ht])
        else:
            P.op("vector", lambda e, m=mul_by, b=bank: e.tensor_tensor(out=m.t[:], in0=m.t[:], in1=b.t[:], op=ALU.mult),
                 reads=[bank, mul_by], writes=[mul_by])
            P.op("vector", lambda e, t=ht, m=mul_by: e.tensor_tensor(out=t.t[:], in0=t.t[:], in1=m.t[:], op=ALU.add),
                 reads=[mul_by, ht], writes=[ht])
        P.dma("sync", hd[oc, :, n0:n0 + TT], ht.t[:], ht, reads=[ht], writes=[hdst_bufs[n0 // TT]])

    def ffn(self, hsrc, hsrc_bufs, hdst, hdst_bufs, half, w_in, w_out):
        P = self.P
        for s in range(FS):
            wg = self.rr("win", self.win)
            wu = self.rr("win", self.win)
            self.load_w(wg, w_in, s * 128, 128, KC)
            self.load_w(wu, w_in, DFF + s * 128, 128, KC)
            for j in range(2):
                ba = self.rr("bA", self.banks[0:2])
                bb = self.rr("bB", self.banks[2:4])
                self.mm_group(ba, lambda k, w=wg: w.t[:, k * 128:(k + 1) * 128], lambda k, j=j: self.xs(k, j), KC, [wg, self.xn])
                self.mm_group(bb, lambda k, w=wu: w.t[:, k * 128:(k + 1) * 128], lambda k, j=j: self.xs(k, j), KC, [wu, self.xn])
                tmp = self.rr("tmp", self.tmp)
                P.op("scalar", lambda e, t=tmp, b=ba: e.activation(out=t.t[:], in_=b.t[:], func=AF.Silu), reads=[ba], writes=[tmp])
                P.op("vector", lambda e, t=tmp, b=bb, s=s, j=j: e.tensor_tensor(
                    out=self.act.t[:, s * HALF + j * TT:s * HALF + (j + 1) * TT], in0=t.t[:], in1=b.t[:], op=ALU.mult),
                    reads=[tmp, bb], writes=[self.act])
        for oc in range(KC):
            wo = self.rr("wout", self.wout)
            self.load_w(wo, w_out, oc * 128, 128, FS)
            for j in range(2):
                n0 = half * HALF + j * TT
                bo = self.rr("bO", self.banks[4:6])
                self.mm_group(bo, lambda k, w=wo: w.t[:, k * 128:(k + 1) * 128],
                              lambda k, j=j: self.act.t[:, k * HALF + j * TT:k * HALF + (j + 1) * TT], FS, [wo, self.act])
                self.add_store(bo, 0.5, hsrc, hsrc_bufs, hdst, hdst_bufs, oc, n0)

    def ple(self, hsrc, hsrc_bufs, hdst, hdst_bufs, half, w_gate, bcol, pT, w_proj):
        P = self.P
        ptv = pT.rearrange("(k p) n -> p k n", p=128)[:, :, half * HALF:(half + 1) * HALF]
        P.dma("gpsimd", self.pt.t[:].rearrange("p (k n) -> p k n", k=2), ptv, self.pt, writes=[self.pt])
        for oc in range(KC):
            wg = self.rr("win", self.win)
            self.load_w(wg, w_gate, oc * 128, 128, KC)
            wp = self.rr("wp", self.wp)
            self.load_w(wp, w_proj, oc * 128, 128, 2)
            for j in range(2):
                n0 = half * HALF + j * TT
                ba = self.rr("bA", self.banks[0:2])
                bb = self.rr("bB", self.banks[2:4])
                self.mm_group(ba, lambda k, w=wg: w.t[:, k * 128:(k + 1) * 128], lambda k, j=j: self.xs(k, j), KC, [wg, self.xn])
                self.mm_group(bb, lambda k, w=wp: w.t[:, k * 128:(k + 1) * 128],
                              lambda k, j=j: self.pt.t[:, k * HALF + j * TT:k * HALF + (j + 1) * TT], 2, [wp, self.pt])
                tmp = self.rr("tmp", self.tmp)
                P.op("scalar", lambda e, t=tmp, b=ba, oc=oc: e.activation(
                    out=t.t[:], in_=b.t[:], func=AF.Sigmoid, bias=self.vecs.t[:, bcol + oc:bcol + oc + 1], scale=1.0),
                    reads=[ba, self.vecs], writes=[tmp])
                self.add_store(bb, 1.0, hsrc, hsrc_bufs, hdst, hdst_bufs, oc, n0, mul_by=tmp)

    def proj_fm(self, half, w_ap, c0, nchunks, dst_fn, dst_bufs, evac="copy"):
        P = self.P
        for c in range(nchunks):
            w = self.rr("win", self.win)
            self.load_w(w, w_ap, c0 + c * 128, 128, KC)
            for j in range(2):
                b = self.rr("bA", self.banks[0:4])
                self.mm_group(b, lambda k, w=w: w.t[:, k * 128:(k + 1) * 128], lambda k, j=j: self.xs(k, j), KC, [w, self.xn])
                ob = self.rr("ob", self.ob)
                eng = self.rr("evac", ["scalar", "vector"])
                if eng == "scalar":
                    P.op("scalar", lambda e, o=ob, b=b: e.activation(out=o.t[:], in_=b.t[:], func=AF.Copy), reads=[b], writes=[ob])
                else:
                    P.op("vector", lambda e, o=ob, b=b: e.tensor_copy(out=o.t[:], in_=b.t[:]), reads=[b], writes=[ob])
                P.dma("sync", dst_fn(c, j), ob.t[:], ob, reads=[ob], writes=dst_bufs)

    def oproj(self, hsrc, hsrc_bufs, hdst, hdst_bufs, half, w_o):
        for oc in range(KC):
            w = self.rr("win", self.win)
            self.load_w(w, w_o, oc * 128, 128, KC)
            for j in range(2):
                n0 = half * HALF + j * TT
                b = self.rr("bO", self.banks[4:6])
                self.mm_group(b, lambda k, w=w: w.t[:, k * 128:(k + 1) * 128], lambda k, j=j: self.xs(k, j), KC, [w, self.xn])
                self.add_store(b, 1.0, hsrc, hsrc_bufs, hdst, hdst_bufs, oc, n0)

    def final_norm(self, hsrc, hsrc_bufs, out_ap, out_bufs, half, gcol):
        P = self.P
        hs = hsrc.rearrange("(k p) n -> k p n", p=128)
        od = out_ap.rearrange("(k p) n -> k p n", p=128)
        for j in range(2):
            self.stats(hsrc, hsrc_bufs, half * HALF + j * TT, j, RMS_EPS)
        for j in range(2):
            n0 = half * HALF + j * TT
            for k in range(KC):
                ht = self.rr("ht", self.ht)
                P.dma("gpsimd", ht.t[:], hs[k, :, n0:n0 + TT], ht, reads=[hsrc_bufs[n0 // TT]], writes=[ht])
                P.op("vector", lambda e, k=k, t=ht, j=j: e.scalar_tensor_tensor(
                    out=t.t[:], in0=t.t[:], scalar=self.vecs.t[:, gcol + k:gcol + k + 1], in1=self.rstd[j].t[:],
                    op0=ALU.mult, op1=ALU.mult), reads=[ht, self.vecs, self.rstd[j]], writes=[ht])
                P.dma("sync", od[k, :, n0:n0 + TT], ht.t[:], ht, reads=[ht], writes=out_bufs)

    def evac_bf16(self, bank, ncol=TT):
        P = self.P
        ob = self.rr("ob", self.ob)
        eng = self.rr("evac", ["scalar", "vector"])
        if eng == "scalar":
            P.op("scalar", lambda e, o=ob, b=bank: e.activation(out=o.t[:, 0:ncol], in_=b.t[:, 0:ncol], func=AF.Copy), reads=[bank], writes=[ob])
        else:
            P.op("vector", lambda e, o=ob, b=bank: e.tensor_copy(out=o.t[:, 0:ncol], in_=b.t[:, 0:ncol]), reads=[bank], writes=[ob])
        return ob

    def proj_tm(self, half, w_ap, c0, ncol, dst_fn, dst_bufs):
        P = self.P
        w = self.rr("wout", self.wout)
        self.load_w(w, w_ap, c0, ncol, KC)
        for tb in range(HALF // 128):
            b = self.rr("bA", self.banks[0:4])
            self.mm_group(b, lambda k, tb=tb: self.xn.t[:, k * HALF + tb * 128:k * HALF + (tb + 1) * 128],
                          lambda k, w=w: w.t[:, k * ncol:(k + 1) * ncol], KC, [w, self.xn], ncol=ncol)
            ob = self.evac_bf16(b, ncol)
            P.dma("sync", dst_fn(tb), ob.t[:, 0:ncol], ob, reads=[ob], writes=dst_bufs)

    def load_xn(self, src, src_bufs, half):
        v = src.rearrange("(k p) n -> p k n", p=128)[:, :, half * HALF:(half + 1) * HALF]
        self.P.dma("sync", self.xn.t[:].rearrange("p (k n) -> p k n", k=KC), v, self.xn, reads=src_bufs, writes=[self.xn])

    def proj_conv_in(self, half, w_ap, zT, bgT, dst_bufs):
        P = self.P
        zd = zT.rearrange("(k p) n -> k p n", p=128)
        bd = bgT.rearrange("(k p) n -> k p n", p=128)
        for c in range(KC):
            wb = self.rr("win", self.win)
            wc = self.rr("win", self.win)
            wu = self.rr("win", self.win)
            self.load_w(wb, w_ap, c * 128, 128, KC)
            self.load_w(wc, w_ap, D + c * 128, 128, KC)
            self.load_w(wu, w_ap, 2 * D + c * 128, 128, KC)
            for j in range(2):
                n0 = half * HALF + j * TT
                b0 = self.rr("bA", self.banks[0:2])
                b1 = self.rr("bB", self.banks[2:4])
                b2 = self.rr("bO", self.banks[4:6])
                for (b, w) in ((b0, wb), (b1, wc), (b2, wu)):
                    self.mm_group(b, lambda k, w=w: w.t[:, k * 128:(k + 1) * 128], lambda k, j=j: self.xs(k, j), KC, [w, self.xn])
                t0 = self.rr("ht", self.ht)
                P.op("scalar", lambda e, t=t0, b=b0: e.activation(out=t.t[:], in_=b.t[:], func=AF.Copy), reads=[b0], writes=[t0])
                P.dma("sync", bd[c, :, n0:n0 + TT], t0.t[:], t0, reads=[t0], writes=dst_bufs)
                t1 = self.rr("ht", self.ht)
                P.op("scalar", lambda e, t=t1, b=b1: e.activation(out=t.t[:], in_=b.t[:], func=AF.Copy), reads=[b1], writes=[t1])
                P.op("vector", lambda e, t=t1, b=b2: e.tensor_tensor(out=t.t[:], in0=t.t[:], in1=b.t[:], op=ALU.mult), reads=[b2, t1], writes=[t1])
                P.dma("sync", zd[c, :, n0:n0 + TT], t1.t[:], t1, reads=[t1], writes=dst_bufs)

    def init_attn(self, masks_ap, nmask, sel_ap, ident_ap):
        P = self.P
        self.masks = P.sb("masks", [128, nmask * 128], BF16)
        self.ident = P.sb("ident", [128, 128], BF16)
        self.sel = P.sb("sel", [128, 16], F32)
        P.dma("gpsimd", self.masks.t[:], masks_ap, self.masks, writes=[self.masks])
        P.dma("gpsimd", self.ident.t[:], ident_ap, self.ident, writes=[self.ident])
        P.dma("sync", self.sel.t[:], sel_ap, self.sel, writes=[self.sel])
        self.pts = [P.vbuf("pt") for _ in range(3)]
        self.PT0 = FS * HALF - 3 * TT

    def pt_ap(self, i):
        return self.act.t[:, self.PT0 + i * TT:self.PT0 + (i + 1) * TT]

    def mask_ap(self, mi):
        return self.masks.t[:, mi * 128:(mi + 1) * 128]

    def attn_run(self, blocks, q_ap, qbuf, o_banks, l_bank, scale):
        P = self.P
        n = len(blocks)
        st = [None] * n

        def stage_s(i):
            b = blocks[i]
            c0 = b.get("c0", 0)
            ms = b.get("masks", ())
            sb = self.rr("bS", self.banks[0:3])
            P.op("tensor", lambda e, b=b, sb=sb, c0=c0, ms=ms: e.matmul(sb.t[:, c0:TT], b["K"], q_ap[:, c0:TT], start=True, stop=(len(ms) == 0)),
                 reads=[qbuf] + b["bufs"], writes=[sb], inc=(len(ms) == 0))
            for mi, (map_, mc0, w) in enumerate(ms):
                last = mi == len(ms) - 1
                P.op("tensor", lambda e, sb=sb, map_=map_, mc0=mc0, w=w, last=last: e.matmul(
                    sb.t[:, mc0:mc0 + w], self.ident.t[:], map_, start=False, stop=last),
                    reads=[self.ident, self.masks], writes=[sb], inc=last)
            pi = self.cnt.get("pti", 0)
            self.cnt["pti"] = pi + 1
            pt = self.pts[pi % 3]
            pap = self.pt_ap(pi % 3)
            bias = b.get("bias")
            if bias is None:
                bias = self.sel.t[:, 10:11]
            P.op("scalar", lambda e, sb=sb, pap=pap, c0=c0, bias=bias: e.activation(
                out=pap[:, c0:TT], in_=sb.t[:, c0:TT], func=AF.Exp, bias=bias, scale=scale),
                reads=[sb, self.sel], writes=[pt])
            st[i] = (pt, pap, c0)

        def stage_pv(i):
            pt, pap, c0 = st[i]
            b = blocks[i]
            first, last = (i == 0), (i == n - 1)
            for oi, ob in enumerate(o_banks):
                P.op("tensor", lambda e, ob=ob, v=b["V"][oi], pap=pap, c0=c0, first=first, last=last: e.matmul(
                    ob.t[:, c0:TT], v, pap[:, c0:TT], start=first, stop=last),
                    reads=[pt] + b["bufs"], writes=[ob], inc=False)
            P.op("tensor", lambda e, pap=pap, c0=c0, first=first, last=last: e.matmul(
                l_bank.t[:, c0:TT], self.ones.t[:], pap[:, c0:TT], start=first, stop=last),
                reads=[pt, self.ones], writes=[l_bank] + list(o_banks), inc=True)

        stage_s(0)
        if n > 1:
            stage_s(1)
        for i in range(n):
            stage_pv(i)
            if i + 2 < n:
                stage_s(i + 2)

    def attn_A(self, qA, kloc, vloc, kall, vall, src_bufs, attn, attn_bufs):
        P = self.P
        scale = 128.0 ** -0.5
        A = self.act.t
        QO, KO, VO, PO = 0, 8192, 14336, 20480
        qb, kb, vb = P.vbuf("qA"), P.vbuf("kA"), P.vbuf("vA")
        kpb = [P.vbuf("kpA") for _ in range(3)]
        vpb = [P.vbuf("vpA") for _ in range(3)]
        allb = [qb, kb, vb] + kpb + vpb + self.pts
        for b in allb:
            b.r.update(self.act.r); b.w.update(self.act.w)
        osets = [(self.banks[3], self.banks[4]), (self.banks[5], self.banks[6])]
        attn_v = attn.rearrange("(h d) n -> d h n", d=128)
        for kvh in range(4):
            qsrc = qA.rearrange("(h d) (nb q) -> d nb h q", d=128, q=128)[:, :, kvh * 4:(kvh + 1) * 4, :]
            qdst = A[:, QO:QO + 8192].rearrange("p (nb h q) -> p nb h q", nb=16, h=4)
            for hh in range(4):
                P.dma("sync", qdst[:, :, hh, :], qsrc[:, :, hh, :], qb, reads=src_bufs, writes=[qb])
            ksrc = kloc.rearrange("(g v d) n -> d g v n", g=3, v=4)[:, :, kvh, :]
            P.dma("sync", A[:, KO:KO + 6144].rearrange("p (g n) -> p g n", g=3), ksrc, kb, reads=src_bufs, writes=[kb])
            vsrc = vloc.rearrange("(b p) (g v d) -> p g b v d", p=128, g=3, v=4)[:, :, :, kvh, :]
            vdst = A[:, VO:VO + 6144].rearrange("p (g b d) -> p g b d", g=3, b=16)
            for g in range(3):
                P.dma("sync", vdst[:, g], vsrc[:, g], vb, reads=src_bufs, writes=[vb])
            for s in range(3):
                kbase = PO + s * 6144
                vbase = kbase + 3072
                vv = vall.rearrange("(r ih s jl p) (g v d) -> s p g v r ih jl d", r=4, ih=2, s=4, jl=2, p=128, g=3, v=4)[s]
                for g in range(3):
                    r0 = (g * 4 + kvh) * 128
                    base = ((r0 // 256) * 4 + s) * 256 + (r0 % 256)
                    ka = kall[base:base + 128, :].rearrange("d (r j) -> d r j", r=4)
                    if g < 2:
                        P.dma("sync", A[:, kbase + g * 512:kbase + (g + 1) * 512].rearrange("p (r j) -> p r j", r=4),
                              ka[:, :, 384:512], kpb[s], reads=src_bufs, writes=[kpb[s]])
                        P.dma("sync", A[:, vbase + g * 512:vbase + (g + 1) * 512].rearrange("p (r d) -> p r d", r=4),
                              vv[:, g, kvh, :, 1, 1, :], vpb[s], reads=src_bufs, writes=[vpb[s]])
                    else:
                        P.dma("sync", A[:, kbase + 1024:kbase + 3072].rearrange("p (r j) -> p r j", r=4),
                              ka, kpb[s], reads=src_bufs, writes=[kpb[s]])
                        for r in range(4):
                            for ih in range(2):
                                o_ = vbase + 1024 + r * 512 + ih * 256
                                P.dma("sync", A[:, o_:o_ + 256].rearrange("p (jl d) -> p jl d", jl=2),
                                      vv[:, g, kvh, r, ih, :, :], vpb[s], reads=src_bufs, writes=[vpb[s]])
            for r4 in range(4):
                for jb in range(4):
                    blocks = []

                    def add(g, rk, jbk, mi):
                        m4 = [(self.mask_ap(mi), hh * 128, 128) for hh in range(4)]
                        if jbk >= 0:
                            blk = rk * 4 + jbk
                            blocks.append(dict(K=A[:, KO + g * 2048 + blk * 128:KO + g * 2048 + (blk + 1) * 128],
                                               V=[A[:, VO + (g * 16 + blk) * 128:VO + (g * 16 + blk + 1) * 128]],
                                               bufs=[kb, vb], masks=m4))
                        else:
                            jp = jbk + 4
                            for s in range(3):
                                kbase = PO + s * 6144
                                vbase = kbase + 3072
                                if g < 2:
                                    o = g * 512 + rk * 128
                                else:
                                    o = 1024 + (rk * 4 + jp) * 128
                                blocks.append(dict(K=A[:, kbase + o:kbase + o + 128], V=[A[:, vbase + o:vbase + o + 128]],
                                                   bufs=[kpb[s], vpb[s]], masks=m4, bias=self.sel.t[:, s:s + 1]))

                    add(1, r4, jb, 0)
                    add(1, r4, jb - 1, 1)
                    for dj in range(5):
                        add(2, r4, jb - dj, (2, 3, 3, 3, 4)[dj])
                    for rk in range(4):
                        for dj in range(2):
                            add(0, rk, jb - dj, 5 + (r4 - rk + 3) * 2 + dj)
                    nb = r4 * 4 + jb
                    ob, lb = self.rr("oset", osets)
                    self.attn_run(blocks, A[:, QO + nb * 512:QO + (nb + 1) * 512], qb, [ob], lb, scale)
                    rl = self.rr("tmp", self.tmp)
                    P.op("vector", lambda e, rl=rl, lb=lb: e.reciprocal(out=rl.t[:], in_=lb.t[:]), reads=[lb], writes=[rl])
                    o = self.rr("ob", self.ob)
                    P.op("vector", lambda e, o=o, ob=ob, rl=rl: e.tensor_tensor(out=o.t[:], in0=ob.t[:], in1=rl.t[:], op=ALU.mult),
                         reads=[ob, rl], writes=[o])
                    P.dma("sync", attn_v[:, kvh * 4:(kvh + 1) * 4, nb * 128:(nb + 1) * 128],
                          o.t[:].rearrange("p (h q) -> p h q", h=4), o, reads=[o], writes=attn_bufs)
        for b in allb:
            for k, v in b.r.items():
                self.act.r[k] = max(self.act.r.get(k, 0), v)
            for k, v in b.w.items():
                self.act.w[k] = max(self.act.w.get(k, 0), v)

    def lam_setup(self, lam_ap, lam_init):
        P = self.P
        self.lamt = P.sb("lamt", [128, 8], F32)
        self.onesf = P.sb("onesf", [128, 128], F32)
        P.op("vector", lambda e: e.memset(self.onesf.t[:], 1.0), writes=[self.onesf])
        P.dma("sync", self.lamt.t[:, 0:4], lam_ap, self.lamt, writes=[self.lamt])
        L = self.lamt
        P.op("vector", lambda e: e.tensor_tensor(out=L.t[:, 4:5], in0=L.t[:, 0:1], in1=L.t[:, 1:2], op=ALU.mult), reads=[L], writes=[L])
        P.op("vector", lambda e: e.tensor_tensor(out=L.t[:, 5:6], in0=L.t[:, 2:3], in1=L.t[:, 3:4], op=ALU.mult), reads=[L], writes=[L])
        bank = self.banks[7]
        P.op("tensor", lambda e: e.matmul(bank.t[:, 0:2], self.onesf.t[:], L.t[:, 4:6], start=True, stop=True),
             reads=[L, self.onesf], writes=[bank])
        P.op("scalar", lambda e: e.activation(out=L.t[:, 6:8], in_=bank.t[:, 0:2], func=AF.Exp), reads=[bank], writes=[L])
        P.op("vector", lambda e: e.scalar_tensor_tensor(out=L.t[:, 4:5], in0=L.t[:, 7:8], scalar=-lam_init, in1=L.t[:, 6:7],
                                                        op0=ALU.add, op1=ALU.subtract), reads=[L], writes=[L])

    def attn_B(self, qB, kloc, vloc, kall, vall, src_bufs, attn, attn_bufs, gcol, lam_init):
        P = self.P
        scale = 128.0 ** -0.5
        A = self.act.t
        X = self.xn.t
        XF = self.xn.t.bitcast(F32)
        QO, KA, KO, VA = 0, 4096, 20480, 24576
        qb, kab, kob, vab, vob = P.vbuf("qB"), P.vbuf("kaB"), P.vbuf("koB"), P.vbuf("vaB"), P.vbuf("voB")
        ocb = [P.vbuf("oc0"), P.vbuf("oc1")]
        dfb = P.vbuf("diff")
        allb = [qb, kab, kob, vab, vob, dfb] + ocb + self.pts
        for b in allb:
            b.r.update(self.act.r); b.w.update(self.act.w)
            b.r.update(self.xn.r); b.w.update(self.xn.w)
        oc_ap = [XF[:, 2048:3072], XF[:, 3072:4096]]
        df_ap = XF[:, 4096:5120]
        o_banks = [self.banks[3], self.banks[4]]
        l_bank = self.banks[5]
        sbank = self.banks[6]
        one_m = 1.0 - lam_init
        P.op("vector", lambda e: e.memset(self.epsc.t[:, 1:2], SUBLN_EPS / (one_m * one_m)), reads=[], writes=[self.epsc])
        attn_v = attn.rearrange("(h c d) n -> h c d n", c=2, d=128)
        for h in range(8):
            qs = qB.rearrange("(h c d) n -> h d c n", c=2, d=128)[h]
            P.dma("sync", A[:, QO:QO + 4096].rearrange("p (c n) -> p c n", c=2), qs, qb, reads=src_bufs, writes=[qb])
            ks = kloc.rearrange("(h c d) n -> h d c n", c=2, d=128)[h]
            P.dma("sync", A[:, KO:KO + 4096].rearrange("p (c n) -> p c n", c=2), ks, kob, reads=src_bufs, writes=[kob])
            for c in range(2):
                ka = kall.rearrange("(h s c d) n -> h c d s n", s=4, c=2, d=128)[h, c]
                P.dma("sync", A[:, KA + c * 8192:KA + (c + 1) * 8192].rearrange("p (s n) -> p s n", s=4), ka, kab, reads=src_bufs, writes=[kab])
            va = vall.rearrange("(i s wl p) (h e) -> h s p i wl e", s=4, wl=2, p=128, e=256)[h]
            for s in range(4):
                vd = A[:, VA + s * 4096:VA + (s + 1) * 4096].rearrange("p (i wl e) -> p i wl e", wl=2, e=256)
                for wl in range(2):
                    P.dma("sync", vd[:, :, wl, :], va[s][:, :, wl, :], vab, reads=src_bufs, writes=[vab])
            vo = vloc.rearrange("(b p) (h e) -> h p b e", p=128, e=256)[h]
            P.dma("sync", X[:, 0:4096].rearrange("p (b e) -> p b e", e=256), vo, vob, reads=src_bufs, writes=[vob])
            for r4 in range(4):
                for c in range(2):
                    blocks = []
                    for s in range(4):
                        for blk in range(16):
                            ko = KA + c * 8192 + s * 2048 + blk * 128
                            vo_ = VA + (s * 16 + blk) * 256
                            blocks.append(dict(K=A[:, ko:ko + 128], V=[A[:, vo_:vo_ + 128], A[:, vo_ + 128:vo_ + 256]],
                                               bufs=[kab, vab], bias=self.sel.t[:, 3 + s:4 + s]))
                    for jbk in (3, 2, 1, 0):
                        for rk in range(4):
                            blk = rk * 4 + jbk
                            ko = KO + c * 2048 + blk * 128
                            vo_ = blk * 256
                            mi = 0 if rk <= r4 else 19
                            blocks.append(dict(K=A[:, ko:ko + 128], V=[X[:, vo_:vo_ + 128], X[:, vo_ + 128:vo_ + 256]],
                                               bufs=[kob, vob], masks=[(self.mask_ap(mi), jbk * 128, 128)], c0=jbk * 128))
                    q_ap = A[:, QO + c * 2048 + r4 * 512:QO + c * 2048 + (r4 + 1) * 512]
                    self.attn_run(blocks, q_ap, qb, o_banks, l_bank, scale)
                    rl = self.rr("tmp", self.tmp)
                    P.op("vector", lambda e, rl=rl: e.reciprocal(out=rl.t[:], in_=l_bank.t[:]), reads=[l_bank], writes=[rl])
                    for e2 in range(2):
                        P.op("vector", lambda e, c=c, e2=e2, rl=rl: e.tensor_tensor(
                            out=oc_ap[c][:, e2 * 512:(e2 + 1) * 512], in0=o_banks[e2].t[:], in1=rl.t[:], op=ALU.mult),
                            reads=[o_banks[e2], rl], writes=[ocb[c]])
                P.op("vector", lambda e: e.scalar_tensor_tensor(out=df_ap, in0=oc_ap[1], scalar=self.lamt.t[:, 4:5], in1=oc_ap[0],
                                                                op0=ALU.mult, op1=ALU.add), reads=ocb + [self.lamt], writes=[dfb])
                for e2 in range(2):
                    sq = self.rr("sq", self.sq)
                    P.op("scalar", lambda e, sq=sq, e2=e2: e.activation(out=sq.t[:], in_=df_ap[:, e2 * 512:(e2 + 1) * 512], func=AF.Square),
                         reads=[dfb], writes=[sq])
                    P.op("tensor", lambda e, sq=sq, e2=e2: e.matmul(sbank.t[:], self.ones.t[:], sq.t[:], start=(e2 == 0), stop=(e2 == 1)),
                         reads=[sq, self.ones], writes=[sbank], inc=True)
                P.op("scalar", lambda e: e.activation(out=self.rtmp.t[:], in_=sbank.t[:], func=AF.Sqrt, bias=self.epsc.t[:, 1:2],
                                                      scale=1.0 / (256.0 * one_m * one_m)), reads=[sbank, self.epsc], writes=[self.rtmp])
                P.op("vector", lambda e: e.reciprocal(out=self.rstd[0].t[:], in_=self.rtmp.t[:]), reads=[self.rtmp], writes=[self.rstd[0]])
                for e2 in range(2):
                    o = self.rr("ob", self.ob)
                    P.op("vector", lambda e, o=o, e2=e2: e.scalar_tensor_tensor(
                        out=o.t[:], in0=df_ap[:, e2 * 512:(e2 + 1) * 512], scalar=self.vecs.t[:, gcol + e2:gcol + e2 + 1],
                        in1=self.rstd[0].t[:], op0=ALU.mult, op1=ALU.mult), reads=[dfb, self.vecs, self.rstd[0]], writes=[o])
                    P.dma("sync", attn_v[h, e2, :, r4 * 512:(r4 + 1) * 512], o.t[:], o, reads=[o], writes=attn_bufs)
        for b in allb:
            for tgt in (self.act, self.xn):
                for k, v in b.r.items():
                    tgt.r[k] = max(tgt.r.get(k, 0), v)
                for k, v in b.w.items():
                    tgt.w[k] = max(tgt.w.get(k, 0), v)

    def conv_C(self, zT, bgT, zh_all, src_bufs, attn, attn_bufs, wcol):
        P = self.P
        AF32 = self.act.t.bitcast(F32)
        X = self.xn.t
        sets = []
        for i in range(2):
            base = i * 8192
            sets.append(dict(z=AF32[:, base:base + 2048], bg=AF32[:, base + 2048:base + 4096], acc=AF32[:, base + 4096:base + 6144],
                             hin=AF32[:, base + 6144:base + 6144 + 24], hal=AF32[:, base + 6200:base + 6202],
                             out=X[:, i * 2048:(i + 1) * 2048],
                             zb=P.vbuf("cz"), bb=P.vbuf("cbg"), ab=P.vbuf("cacc"), hb=P.vbuf("chin"), ob=P.vbuf("cout")))
        allb = [s[k] for s in sets for k in ("zb", "bb", "ab", "hb", "ob")]
        for b in allb:
            b.r.update(self.act.r); b.w.update(self.act.w)
            b.r.update(self.xn.r); b.w.update(self.xn.w)
        zd = zT.rearrange("(k p) n -> k p n", p=128)
        bd = bgT.rearrange("(k p) n -> k p n", p=128)
        ad = attn.rearrange("(k p) n -> k p n", p=128)
        zh = zh_all.rearrange("(i s kl p) e -> i kl p s e", s=4, kl=2, p=128)
        for c in range(KC):
            S = sets[c % 2]
            P.dma("gpsimd", S["z"], zd[c], S["zb"], reads=src_bufs, writes=[S["zb"]])
            P.dma("gpsimd", S["bg"], bd[c], S["bb"], reads=src_bufs, writes=[S["bb"]])
            P.dma("sync", S["hin"].rearrange("p (s e) -> p s e", s=3), zh[c // 2, c % 2][:, 0:3, :], S["hb"], reads=src_bufs, writes=[S["hb"]])
            hin, hal = S["hin"], S["hal"]
            P.op("vector", lambda e, hin=hin, hal=hal: e.tensor_scalar(out=hal, in0=hin[:, 0:2], scalar1=self.sel.t[:, 7:8], scalar2=None, op0=ALU.mult),
                 reads=[S["hb"], self.sel], writes=[S["hb"]])
            for s in (1, 2):
                P.op("vector", lambda e, hin=hin, hal=hal, s=s: e.scalar_tensor_tensor(
                    out=hal, in0=hin[:, 8 * s:8 * s + 2], scalar=self.sel.t[:, 7 + s:8 + s], in1=hal, op0=ALU.mult, op1=ALU.add),
                    reads=[S["hb"], self.sel], writes=[S["hb"]])
            w0 = self.vecs.t[:, wcol + c:wcol + c + 1]
            w1 = self.vecs.t[:, wcol + KC + c:wcol + KC + c + 1]
            w2 = self.vecs.t[:, wcol + 2 * KC + c:wcol + 2 * KC + c + 1]
            z, acc = S["z"], S["acc"]
            rd = [S["zb"], S["hb"], self.vecs]
            P.op("vector", lambda e, z=z, acc=acc, w2=w2: e.tensor_scalar(out=acc, in0=z, scalar1=w2, scalar2=None, op0=ALU.mult),
                 reads=rd, writes=[S["ab"]])

            def fma(dst, src, w):
                P.op("vector", lambda e, dst=dst, src=src, w=w: e.scalar_tensor_tensor(out=dst, in0=src, scalar=w, in1=dst, op0=ALU.mult, op1=ALU.add),
                     reads=rd + [S["ab"]], writes=[S["ab"]])

            for r4 in range(4):
                a = acc[:, r4 * 512:(r4 + 1) * 512]
                if r4 >= 1:
                    fma(a, z[:, (r4 - 1) * 512:r4 * 512], w1)
                else:
                    fma(a[:, 1:512], z[:, 3 * 512:3 * 512 + 511], w1)
                    fma(a[:, 0:1], S["hal"][:, 1:2], w1)
                if r4 >= 2:
                    fma(a, z[:, (r4 - 2) * 512:(r4 - 1) * 512], w0)
                else:
                    fma(a[:, 1:512], z[:, (r4 + 2) * 512:(r4 + 2) * 512 + 511], w0)
                    fma(a[:, 0:1], S["hal"][:, r4:r4 + 1], w0)
            P.op("vector", lambda e, S=S: e.tensor_tensor(out=S["out"], in0=S["acc"], in1=S["bg"], op=ALU.mult),
                 reads=[S["ab"], S["bb"]], writes=[S["ob"]])
            P.dma("sync", ad[c], S["out"], S["ob"], reads=[S["ob"]], writes=attn_bufs)
        for b in allb:
            for tgt in (self.act, self.xn):
                for k, v in b.r.items():
                    tgt.r[k] = max(tgt.r.get(k, 0), v)
                for k, v in b.w.items():
                    tgt.w[k] = max(tgt.w.get(k, 0), v)


NV = 400
VL = 80
V_NORMF = 320
V_SUBLN = 336
V_CONV = 338
NMASK = 20
GROUPS = [[0, 1, 2, 3], [4, 5, 6, 7]]
LAM_INIT = 0.8 - 0.6 * float(np.exp(-0.3 * 1))

W_SHAPES = {}
for _i in range(4):
    W_SHAPES["w1i%d" % _i] = (D, 2 * DFF)
    W_SHAPES["w1o%d" % _i] = (DFF, D)
    W_SHAPES["w2i%d" % _i] = (D, 2 * DFF)
    W_SHAPES["w2o%d" % _i] = (DFF, D)
    W_SHAPES["wpg%d" % _i] = (D, D)
    W_SHAPES["wpp%d" % _i] = (PLE, D)
for _j in range(2):
    W_SHAPES["aqkv%d" % _j] = (D, 5120)
    W_SHAPES["ao%d" % _j] = (D, D)
W_SHAPES.update(bqkv=(D, 3 * D), bo=(D, D), cin=(D, 3 * D), cout=(D, D))


def build_net(stop=None, dbg=None):
    P = Prog()
    xT = P.dram_in("xT", [D, TOK])
    pT = [P.dram_in("pT%d" % i, [PLE, TOK]) for i in range(4)]
    vecs = P.dram_in("vecs", [128, NV])
    masks = P.dram_in("masks", [128, NMASK * 128])
    ident = P.dram_in("ident", [128, 128])
    sel = P.dram_in("sel", [128, 16])
    lamT = P.dram_in("lamT", [128, 4])
    class _LazyW(dict):
        def __missing__(self, k):
            self[k] = P.dram_in(k, list(W_SHAPES[k]))
            return self[k]
    W = _LazyW()
    outT = P.dram_out("outT", [D, TOK])
    h = P.dram_tmp("h", [D, TOK])
    q = P.dram_tmp("q", [D, TOK], BF16)
    klA = P.dram_tmp("klA", [1536, TOK], BF16)
    vlA = P.dram_tmp("vlA", [TOK, 1536], BF16)
    kaA = P.dram_tmp("kaA", [4 * 1536, TOK], BF16)
    vaA = P.dram_tmp("vaA", [4 * TOK, 1536], BF16)
    klB = P.dram_tmp("klB", [D, TOK], BF16)
    vlB = P.dram_tmp("vlB", [TOK, D], BF16)
    kaB = P.dram_tmp("kaB", [4 * D, TOK], BF16)
    vaB = P.dram_tmp("vaB", [4 * TOK, D], BF16)
    zT = P.dram_tmp("zT", [D, TOK])
    bgT = P.dram_tmp("bgT", [D, TOK])
    zhl = P.dram_tmp("zhl", [D, 8])
    zha = P.dram_tmp("zha", [4 * D, 8])
    attn = P.dram_tmp("attn", [D, TOK], BF16)

    R = Rows(P, vecs, NV)
    R.init_attn(masks, NMASK, sel, ident)
    R.lam_setup(lamT, LAM_INIT)

    xb = [P.vbuf("x") for _ in range(4)]
    hb = [P.vbuf("h") for _ in range(4)]
    ob = [P.vbuf("out")]
    P.out_bufs += ob
    qb, klb, vlb, kab, vab, atb = (P.vbuf(n) for n in ("q", "kl", "vl", "ka", "va", "attn"))
    zb, zhlb, zhab = P.vbuf("z"), P.vbuf("zhl"), P.vbuf("zha")
    ccb = P.vbuf("cc")

    def cc(in_ap, out_ap, in_bufs, out_bufs):
        E = P.engs["gpsimd"]
        if ccb.sem is None:
            ccb.sem = P.new_sem("cc")
        waits = P._collect(E, in_bufs, out_bufs, extra=[(ccb.sem, ccb.cnt)] if ccb.cnt else [])
        for s_, v_ in waits.items():
            E.waited[s_] = v_
        ccb.cnt += 1
        E.ops.append((sorted(waits.items()), lambda e, i=in_ap, o=out_ap: e.collective_compute(
            "AllGather", ALU.bypass, replica_groups=GROUPS, ins=[i], outs=[o]), (ccb.sem, 1)))
        P._mark((ccb.sem, ccb.cnt), in_bufs, out_bufs)

    def gather(loc, allt, rows, lb, ab):
        for i in range(rows // 256):
            cc(loc[i * 256:(i + 1) * 256, :], allt[i * 1024:(i + 1) * 1024, :], [lb], [ab])

    state = {"hsrc": xT, "hsb": xb}

    def hs():
        return state["hsrc"], state["hsb"]

    def wrote_h():
        state["hsrc"], state["hsb"] = h, hb

    def chain(fns):
        for f in fns:
            for half in range(2):
                f(half)
            if getattr(f, "writes_h", False):
                wrote_h()

    def mk(fn, writes_h=False):
        fn.writes_h = writes_h
        return fn

    def s_ffn(i, which):
        g = i * VL + (0 if which == 1 else 32)
        wi, wo = W["w%di%d" % (which, i)], W["w%do%d" % (which, i)]

        def f(half):
            src, sb_ = hs()
            R.norm(src, sb_, half, g)
            R.ffn(src, sb_, h, hb, half, wi, wo)
        return mk(f, True)

    def s_ple(i):
        def f(half):
            src, sb_ = hs()
            R.norm(src, sb_, half, i * VL + 48)
            R.ple(src, sb_, h, hb, half, W["wpg%d" % i], i * VL + 64, pT[i], W["wpp%d" % i])
        return mk(f, True)

    def s_oproj(w):
        def f(half):
            src, sb_ = hs()
            R.load_xn(attn, [atb], half)
            R.oproj(src, sb_, h, hb, half, w)
        return mk(f, True)

    def s_projA(i, j):
        w = W["aqkv%d" % j]

        def f(half):
            src, sb_ = hs()
            R.norm(src, sb_, half, i * VL + 16)
            qv = q.rearrange("(c p) n -> c p n", p=128)
            kv = klA.rearrange("(c p) n -> c p n", p=128)
            R.proj_fm(half, w, 0, 16, lambda c, jj: qv[c, :, half * HALF + jj * TT:half * HALF + (jj + 1) * TT], [qb])
            for g in range(3):
                R.proj_fm(half, w, D + g * 1024, 4,
                          lambda c, jj, g=g: kv[g * 4 + c, :, half * HALF + jj * TT:half * HALF + (jj + 1) * TT], [klb])
                for grp in range(2):
                    R.proj_tm(half, w, D + g * 1024 + 512 + grp * 256, 256,
                              lambda tb, g=g, grp=grp: vlA[half * HALF + tb * 128:half * HALF + (tb + 1) * 128,
                                                           g * 512 + grp * 256:g * 512 + (grp + 1) * 256], [vlb])
        return mk(f)

    def s_projB(i):
        w = W["bqkv"]

        def f(half):
            src, sb_ = hs()
            R.norm(src, sb_, half, i * VL + 16)
            qv = q.rearrange("(c p) n -> c p n", p=128)
            kv = klB.rearrange("(c p) n -> c p n", p=128)
            R.proj_fm(half, w, 0, 16, lambda c, jj: qv[c, :, half * HALF + jj * TT:half * HALF + (jj + 1) * TT], [qb])
            R.proj_fm(half, w, D, 16, lambda c, jj: kv[c, :, half * HALF + jj * TT:half * HALF + (jj + 1) * TT], [klb])
            for hh in range(8):
                R.proj_tm(half, w, 2 * D + hh * 256, 256,
                          lambda tb, hh=hh: vlB[half * HALF + tb * 128:half * HALF + (tb + 1) * 128, hh * 256:(hh + 1) * 256], [vlb])
        return mk(f)

    def s_projC(i):
        def f(half):
            src, sb_ = hs()
            R.norm(src, sb_, half, i * VL + 16)
            R.proj_conv_in(half, W["cin"], zT, bgT, [zb])
        return mk(f)

    def s_final():
        def f(half):
            src, sb_ = hs()
            R.final_norm(src, sb_, outT, ob, half, V_NORMF)
        return mk(f)

    def halo():
        P.dma("sync", zhl[:, 0:1], zT[:, 2 * 512 + 511:2 * 512 + 512], zhlb, reads=[zb], writes=[zhlb], allow_slow_non_contiguous=True)
        P.dma("sync", zhl[:, 1:2], zT[:, 3 * 512 + 511:3 * 512 + 512], zhlb, reads=[zb], writes=[zhlb], allow_slow_non_contiguous=True)
        for i in range(8):
            cc(zhl[i * 256:(i + 1) * 256, :], zha[i * 1024:(i + 1) * 1024, :], [zhlb], [zhab])

    def gA():
        gather(klA, kaA, 1536, klb, kab)
        gather(vlA, vaA, TOK, vlb, vab)

    def gB():
        gather(klB, kaB, D, klb, kab)
        gather(vlB, vaB, TOK, vlb, vab)

    mixA = lambda: R.attn_A(q, klA, vlA, kaA, vaA, [qb, klb, vlb, kab, vab], attn, [atb])
    mixB = lambda: R.attn_B(q, klB, vlB, kaB, vaB, [qb, klb, vlb, kab, vab], attn, [atb], V_SUBLN, LAM_INIT)
    mixC = lambda: R.conv_C(zT, bgT, zha, [zb, zhab], attn, [atb], V_CONV)
    C1 = lambda th: (lambda: chain([th()]))
    stages = [
        C1(lambda: s_ffn(0, 1)), C1(lambda: s_projA(0, 0)), gA, mixA, C1(lambda: s_oproj(W["ao0"])), C1(lambda: s_ffn(0, 2)), C1(lambda: s_ple(0)),
        C1(lambda: s_ffn(1, 1)), C1(lambda: s_projB(1)), gB, mixB, C1(lambda: s_oproj(W["bo"])), C1(lambda: s_ffn(1, 2)), C1(lambda: s_ple(1)),
        C1(lambda: s_ffn(2, 1)), C1(lambda: s_projC(2)), halo, mixC, C1(lambda: s_oproj(W["cout"])), C1(lambda: s_ffn(2, 2)), C1(lambda: s_ple(2)),
        C1(lambda: s_ffn(3, 1)), C1(lambda: s_projA(3, 1)), gA, mixA, C1(lambda: s_oproj(W["ao1"])), C1(lambda: s_ffn(3, 2)), C1(lambda: s_ple(3)),
        C1(lambda: s_final()),
    ]
    n = len(stages) if stop is None else stop
    for st in stages[:n]:
        st()
    if stop is not None:
        if dbg is None:
            P.dma("sync", outT, h, ob[0], reads=hb, writes=ob)
        else:
            src = {"attn": attn, "q": q, "klA": klA, "vlA": vlA, "klB": klB, "vlB": vlB}[dbg]
            dbo = P.dram_out("dbg", list(src.shape), BF16)
            P.dma("sync", dbo, src, ob[0], reads=[atb, qb, klb, vlb], writes=ob)
            P.dma("sync", outT, h, ob[0], reads=hb, writes=ob)
    nc = P.emit()
    nc._in_names = list(P.in_names)
    return nc


def _r4perm():
    n = np.arange(TOK)
    return 4 * (n % 512) + n // 512


def _masks():
    jk = np.arange(128)[:, None]
    jq = np.arange(128)[None, :]
    vis = []
    vis.append(jk <= jq)
    vis.append(jk >= jq)
    comb = ((jq - jk) % 4) == 0
    vis.append((jk <= jq) & comb)
    vis.append(comb | (jk < -1))
    vis.append((jk >= jq) & comb)
    for dr in range(-3, 4):
        for djb in range(2):
            dt = 4 * (128 * djb + jq - jk) + dr
            vis.append((dt >= 0) & (dt <= 128))
    vis.append(jk < jq)
    m = np.concatenate([np.where(v, 0.0, NEG) for v in vis], axis=1).astype(np.float32)
    assert m.shape == (128, NMASK * 128)
    return m


def _col16(v):
    return np.ascontiguousarray(np.asarray(v, np.float32).reshape(-1, 128).T)


def make_inputs(inputs):
    perm = _r4perm()
    x = np.asarray(inputs["x"], np.float32)
    p = np.asarray(inputs["p"], np.float32)
    vecs = np.zeros((128, NV), np.float32)
    for i in range(4):
        b = i * VL
        vecs[:, b:b + 16] = _col16(inputs["norm_ffn1"][i])
        vecs[:, b + 16:b + 32] = _col16(inputs["norm_mix"][i])
        vecs[:, b + 32:b + 48] = _col16(inputs["norm_ffn2"][i])
        vecs[:, b + 48:b + 64] = _col16(inputs["norm_ple"][i])
        vecs[:, b + 64:b + 80] = _col16(inputs["b_ple_gate"][i])
    vecs[:, V_NORMF:V_NORMF + 16] = _col16(inputs["norm_f"])
    vecs[:, V_SUBLN:V_SUBLN + 2] = _col16(inputs["b_subln"][0])
    for k in range(3):
        vecs[:, V_CONV + 16 * k:V_CONV + 16 * (k + 1)] = _col16(inputs["c_conv_w"][0][k])
    shared = {"vecs": vecs, "masks": _masks(), "ident": np.eye(128, dtype=np.float32),
              "lamT": np.ascontiguousarray(np.asarray(inputs["b_lambda"][0], np.float32).T)}
    for i in range(4):
        shared["w1i%d" % i] = np.asarray(inputs["w_ffn1_in"][i], np.float32)
        shared["w1o%d" % i] = np.asarray(inputs["w_ffn1_out"][i], np.float32)
        shared["w2i%d" % i] = np.asarray(inputs["w_ffn2_in"][i], np.float32)
        shared["w2o%d" % i] = np.asarray(inputs["w_ffn2_out"][i], np.float32)
        shared["wpg%d" % i] = np.asarray(inputs["w_ple_gate"][i], np.float32)
        shared["wpp%d" % i] = np.asarray(inputs["w_ple_proj"][i], np.float32)
    for j in range(2):
        shared["aqkv%d" % j] = np.asarray(inputs["a_w_qkv"][j], np.float32)
        shared["ao%d" % j] = np.asarray(inputs["a_w_o"][j], np.float32)
    shared["bqkv"] = np.asarray(inputs["b_w_qkv"][0], np.float32)
    shared["bo"] = np.asarray(inputs["b_w_o"][0], np.float32)
    shared["cin"] = np.asarray(inputs["c_w_in"][0], np.float32)
    shared["cout"] = np.asarray(inputs["c_w_out"][0], np.float32)
    maps = []
    for core in range(NCORES):
        b, c = divmod(core, 4)
        sl = slice(c * TOK, (c + 1) * TOK)
        m = dict(shared)
        m["xT"] = np.ascontiguousarray(x[b, sl][perm].T)
        for i in range(4):
            m["pT%d" % i] = np.ascontiguousarray(p[i, b, sl][perm].T)
        s = np.zeros((128, 16), np.float32)
        for k in range(3):
            s[:, k] = 0.0 if k == c - 1 else NEG
            s[:, 7 + k] = 1.0 if k == c - 1 else 0.0
        for k in range(4):
            s[:, 3 + k] = 0.0 if k < c else NEG
        m["sel"] = s
        maps.append(m)
    return maps


def assemble(results, key="outT"):
    perm = _r4perm()
    out = np.empty((2, 4 * TOK, D), np.float32)
    for core in range(NCORES):
        b, c = divmod(core, 4)
        o = np.asarray(results[core][key]).astype(np.float32).T
        blk = np.empty_like(o)
        blk[perm] = o
        out[b, c * TOK:(c + 1) * TOK] = blk
    return out


_NC_CACHE = {}


def kernel(**inputs):
    if "net" not in _NC_CACHE:
        _NC_CACHE["net"] = build_net()
    nc = _NC_CACHE["net"]
    maps = [{k: m[k] for k in nc._in_names} for m in make_inputs(inputs)]
    res = run_bass_kernel_spmd(nc, maps, core_ids=list(range(NCORES)))
    return assemble(res.results)
```

```python
import contextlib
import numpy as np
import ml_dtypes
import concourse.bass as bass
import concourse.mybir as mybir
from concourse.bass_utils import run_bass_kernel_spmd

F32 = mybir.dt.float32
BF16 = mybir.dt.bfloat16
AF = mybir.ActivationFunctionType
ALU = mybir.AluOpType

D = 2048
KC = 16
TOK = 2048
HALF = 1024
TT = 512
DFF = 5632
FS = 44
PLE = 256
NCORES = 8
NEG = -30000.0
RMS_EPS = 1e-6
SUBLN_EPS = 1e-5


class Buf:
    __slots__ = ("name", "w", "r", "sem", "cnt", "t")

    def __init__(self, name, t=None):
        self.name = name
        self.w = {}
        self.r = {}
        self.sem = None
        self.cnt = 0
        self.t = t


class Eng:
    def __init__(self, name):
        self.name = name
        self.ops = []
        self.sem = None
        self.cnt = 0
        self.waited = {}


class Prog:
    def __init__(self):
        self.nc = bass.Bass("TRN2", target_bir_lowering=False)
        self.es = contextlib.ExitStack()
        self.engs = {n: Eng(n) for n in ("tensor", "vector", "scalar", "gpsimd", "sync")}
        self.semh = {}
        self.nsem = 0
        for n in ("tensor", "vector", "scalar"):
            self.engs[n].sem = self.new_sem("e_" + n)
        self.out_bufs = []
        self.in_names = []
        self.nbuf = 0

    def new_sem(self, name=None):
        self.nsem += 1
        h = self.es.enter_context(self.nc.semaphore(name or ("s%d" % self.nsem)))
        self.semh[self.nsem] = h
        return self.nsem

    def sb(self, name, shape, dtype):
        t = self.es.enter_context(self.nc.sbuf_tensor("sb_" + name, list(shape), dtype))
        return Buf(name, t)

    def ps(self, name):
        t = self.es.enter_context(self.nc.psum_tensor("ps_" + name, [128, 512], F32))
        return Buf(name, t)

    def dram_in(self, name, shape, dtype=F32):
        self.in_names.append(name)
        return self.nc.dram_tensor(name, list(shape), dtype, kind="ExternalInput").ap()

    def dram_out(self, name, shape, dtype=F32):
        return self.nc.dram_tensor(name, list(shape), dtype, kind="ExternalOutput").ap()

    def dram_tmp(self, name, shape, dtype=F32):
        return self.nc.dram_tensor(name, list(shape), dtype).ap()

    def vbuf(self, name):
        self.nbuf += 1
        return Buf("%s_%d" % (name, self.nbuf))

    def _collect(self, E, reads, writes, extra=()):
        waits = {}

        def need(sem, val):
            if E.waited.get(sem, 0) >= val:
                return
            if waits.get(sem, 0) < val:
                waits[sem] = val

        for b in reads:
            for sem, val in b.w.items():
                need(sem, val)
        for b in writes:
            for sem, val in b.w.items():
                need(sem, val)
            for sem, val in b.r.items():
                need(sem, val)
        for sem, val in extra:
            need(sem, val)
        return waits

    @staticmethod
    def _mark(ev, reads, writes):
        sem, val = ev
        for b in reads:
            if b.r.get(sem, 0) < val:
                b.r[sem] = val
        for b in writes:
            if b.w.get(sem, 0) < val:
                b.w[sem] = val

    def op(self, eng, fn, reads=(), writes=(), inc=True):
        E = self.engs[eng]
        waits = self._collect(E, reads, writes)
        if eng == "tensor":
            waits.pop(E.sem, None)
        for sem, val in waits.items():
            E.waited[sem] = val
        if inc:
            E.cnt += 1
            ev = (E.sem, E.cnt)
        else:
            ev = (E.sem, E.cnt + 1)
        E.ops.append((sorted(waits.items()), fn, (E.sem, 1) if inc else None))
        self._mark(ev, reads, writes)

    def dma(self, eng, out, in_, owner, reads=(), writes=(), **kw):
        E = self.engs[eng]
        if owner.sem is None:
            owner.sem = self.new_sem("d_" + owner.name)
        waits = self._collect(E, reads, writes, extra=[(owner.sem, owner.cnt)] if owner.cnt else [])
        for sem, val in waits.items():
            E.waited[sem] = val
        owner.cnt += 16
        ev = (owner.sem, owner.cnt)
        E.ops.append((sorted(waits.items()), lambda e, o=out, i=in_: e.dma_start(out=o, in_=i, **kw), (owner.sem, 16)))
        self._mark(ev, reads, writes)

    def barrier_wait(self, eng, bufs):
        E = self.engs[eng]
        waits = self._collect(E, [], bufs)
        for sem, val in waits.items():
            E.waited[sem] = val
        if waits:
            E.ops.append((sorted(waits.items()), None, None))

    def check(self):
        sems = {}
        pos = {n: 0 for n in self.engs}
        progress = True
        while progress:
            progress = False
            for n, E in self.engs.items():
                while pos[n] < len(E.ops):
                    waits, fn, inc = E.ops[pos[n]]
                    if any(sems.get(s, 0) < v for s, v in waits):
                        break
                    if inc is not None:
                        sems[inc[0]] = sems.get(inc[0], 0) + inc[1]
                    pos[n] += 1
                    progress = True
        stuck = {n: (pos[n], len(E.ops)) for n, E in self.engs.items() if pos[n] < len(E.ops)}
        if stuck:
            msg = []
            for n in stuck:
                waits, fn, inc = self.engs[n].ops[pos[n]]
                msg.append("%s@%d waits %s have %s" % (n, pos[n], waits, [(s, sems.get(s, 0)) for s, _ in waits]))
            raise RuntimeError("DEADLOCK: " + "; ".join(msg))
        return {n: len(E.ops) for n, E in self.engs.items()}

    def emit(self):
        self.barrier_wait("sync", self.out_bufs)
        self.check()
        nc = self.nc
        semh = self.semh

        def replay(e, E):
            for waits, fn, inc in E.ops:
                for sem, val in waits:
                    e.wait_ge(semh[sem], val)
                if fn is None:
                    continue
                inst = fn(e)
                if inc is not None:
                    inst.then_inc(semh[inc[0]], inc[1])

        with nc.Block() as block:
            @block.tensor
            def _(e):
                replay(e, self.engs["tensor"])

            @block.vector
            def _(e):
                replay(e, self.engs["vector"])

            @block.scalar
            def _(e):
                replay(e, self.engs["scalar"])

            @block.gpsimd
            def _(e):
                replay(e, self.engs["gpsimd"])

            @block.sync
            def _(e):
                replay(e, self.engs["sync"])
        self.es.close()
        return nc


class Rows:
    def __init__(self, P, vecs_ap, nvec):
        self.P = P
        self.act = P.sb("act", [128, FS * HALF], BF16)
        self.xn = P.sb("xn", [128, KC * HALF], BF16)
        self.win = [P.sb("win%d" % i, [128, KC * 128], BF16) for i in range(4)]
        self.wout = [P.sb("wout%d" % i, [128, FS * 128], BF16) for i in range(2)]
        self.ht = [P.sb("ht%d" % i, [128, TT], F32) for i in range(6)]
        self.tmp = [P.sb("tmp%d" % i, [128, TT], F32) for i in range(3)]
        self.sq = [P.sb("sq%d" % i, [128, TT], BF16) for i in range(2)]
        self.rstd = [P.sb("rstd%d" % i, [128, TT], F32) for i in range(2)]
        self.epsc = P.sb("epsc", [128, 2], F32)
        self.rtmp = P.sb("rtmp", [128, TT], F32)
        self.ob = [P.sb("ob%d" % i, [128, TT], BF16) for i in range(4)]
        self.ones = P.sb("ones", [128, 128], BF16)
        self.vecs = P.sb("vecs", [128, nvec], F32)
        self.pt = P.sb("pt", [128, 2 * HALF], BF16)
        self.wp = [P.sb("wp%d" % i, [128, 2 * 128], BF16) for i in range(2)]
        self.banks = [P.ps("bank%d" % i) for i in range(8)]
        self.cnt = {}
        P.op("vector", lambda e: e.memset(self.ones.t[:], 1.0), writes=[self.ones])
        P.op("vector", lambda e: e.memset(self.epsc.t[:, 0:1], RMS_EPS), writes=[self.epsc])
        P.dma("sync", self.vecs.t[:], vecs_ap, self.vecs, writes=[self.vecs])

    def rr(self, key, lst):
        i = self.cnt.get(key, 0)
        self.cnt[key] = i + 1
        return lst[i % len(lst)]

    def xs(self, k, j):
        return self.xn.t[:, k * HALF + j * TT:k * HALF + (j + 1) * TT]

    def stats(self, hsrc, hsrc_bufs, n0, j, eps, dim=D):
        P = self.P
        hs = hsrc.rearrange("(k p) n -> k p n", p=128)
        bank = self.banks[6 + j]
        for k in range(KC):
            ht = self.rr("ht", self.ht)
            P.dma("gpsimd", ht.t[:], hs[k, :, n0:n0 + TT], ht, reads=[hsrc_bufs[n0 // TT]], writes=[ht])
            sq = self.rr("sq", self.sq)
            P.op("scalar", lambda e, s=sq, t=ht: e.activation(out=s.t[:], in_=t.t[:], func=AF.Square),
                 reads=[ht], writes=[sq])
            P.op("tensor", lambda e, s=sq, k=k: e.matmul(bank.t[:], self.ones.t[:], s.t[:], start=(k == 0), stop=(k == KC - 1)),
                 reads=[sq, self.ones], writes=[bank], inc=True)
        P.op("scalar", lambda e: e.activation(out=self.rtmp.t[:], in_=bank.t[:], func=AF.Sqrt, bias=self.epsc.t[:, 0:1], scale=1.0 / dim),
             reads=[bank, self.epsc], writes=[self.rtmp])
        P.op("vector", lambda e: e.reciprocal(out=self.rstd[j].t[:], in_=self.rtmp.t[:]), reads=[self.rtmp], writes=[self.rstd[j]])

    def norm(self, hsrc, hsrc_bufs, half, gcol, eps=RMS_EPS):
        P = self.P
        hs = hsrc.rearrange("(k p) n -> k p n", p=128)
        for j in range(2):
            self.stats(hsrc, hsrc_bufs, half * HALF + j * TT, j, eps)
        for j in range(2):
            n0 = half * HALF + j * TT
            for k in range(KC):
                ht = self.rr("ht", self.ht)
                P.dma("gpsimd", ht.t[:], hs[k, :, n0:n0 + TT], ht, reads=[hsrc_bufs[n0 // TT]], writes=[ht])
                P.op("vector", lambda e, k=k, j=j, t=ht: e.scalar_tensor_tensor(
                    out=self.xs(k, j), in0=t.t[:], scalar=self.vecs.t[:, gcol + k:gcol + k + 1], in1=self.rstd[j].t[:],
                    op0=ALU.mult, op1=ALU.mult), reads=[ht, self.vecs, self.rstd[j]], writes=[self.xn])

    def load_w(self, slot, w_ap, c0, ncol, kchunks, r0=0):
        src = w_ap[r0:r0 + kchunks * 128, :].rearrange("(k p) c -> p k c", p=128)[:, :, c0:c0 + ncol]
        dst = slot.t[:, 0:kchunks * ncol].rearrange("p (k c) -> p k c", k=kchunks)
        self.P.dma("gpsimd", dst, src, slot, writes=[slot])

    def mm_group(self, bank, lhs_fn, rhs_fn, nk, reads, ncol=TT):
        for k in range(nk):
            self.P.op("tensor", lambda e, k=k: e.matmul(bank.t[:, 0:ncol], lhs_fn(k), rhs_fn(k), start=(k == 0), stop=(k == nk - 1)),
                      reads=reads, writes=[bank], inc=(k == nk - 1))

    def add_store(self, bank, scale, hsrc, hsrc_bufs, hdst, hdst_bufs, oc, n0, mul_by=None):
        P = self.P
        hs = hsrc.rearrange("(k p) n -> k p n", p=128)
        hd = hdst.rearrange("(k p) n -> k p n", p=128)
        ht = self.rr("ht", self.ht)
        P.dma("gpsimd", ht.t[:], hs[oc, :, n0:n0 + TT], ht, reads=[hsrc_bufs[n0 // TT]], writes=[ht])
        if mul_by is None:
            P.op("vector", lambda e, t=ht, b=bank: e.scalar_tensor_tensor(
                out=t.t[:], in0=b.t[:], scalar=scale, in1=t.t[:], op0=ALU.mult, op1=ALU.add), reads=[bank, ht], writes=[ht])
        else:
            P.op("vector", lambda e, m=mul_by, b=bank: e.tensor_tensor(out=m.t[:], in0=m.t[:], in1=b.t[:], op=ALU.mult),
                 reads=[bank, mul_by], writes=[mul_by])
            P.op("vector", lambda e, t=ht, m=mul_by: e.tensor_tensor(out=t.t[:], in0=t.t[:], in1=m.t[:], op=ALU.add),
                 reads=[mul_by, ht], writes=[ht])
        P.dma("sync", hd[oc, :, n0:n0 + TT], ht.t[:], ht, reads=[ht], writes=[hdst_bufs[n0 // TT]])

    def ffn(self, hsrc, hsrc_bufs, hdst, hdst_bufs, half, w_in, w_out):
        P = self.P
        for s in range(FS):
            wg = self.rr("win", self.win)
            wu = self.rr("win", self.win)
            self.load_w(wg, w_in, s * 128, 128, KC)
            self.load_w(wu, w_in, DFF + s * 128, 128, KC)
            for j in range(2):
                ba = self.rr("bA", self.banks[0:2])
                bb = self.rr("bB", self.banks[2:4])
                self.mm_group(ba, lambda k, w=wg: w.t[:, k * 128:(k + 1) * 128], lambda k, j=j: self.xs(k, j), KC, [wg, self.xn])
                self.mm_group(bb, lambda k, w=wu: w.t[:, k * 128:(k + 1) * 128], lambda k, j=j: self.xs(k, j), KC, [wu, self.xn])
                tmp = self.rr("tmp", self.tmp)
                P.op("scalar", lambda e, t=tmp, b=ba: e.activation(out=t.t[:], in_=b.t[:], func=AF.Silu), reads=[ba], writes=[tmp])
                P.op("vector", lambda e, t=tmp, b=bb, s=s, j=j: e.tensor_tensor(
                    out=self.act.t[:, s * HALF + j * TT:s * HALF + (j + 1) * TT], in0=t.t[:], in1=b.t[:], op=ALU.mult),
                    reads=[tmp, bb], writes=[self.act])
        for oc in range(KC):
            wo = self.rr("wout", self.wout)
            self.load_w(wo, w_out, oc * 128, 128, FS)
            for j in range(2):
                n0 = half * HALF + j * TT
                bo = self.rr("bO", self.banks[4:6])
                self.mm_group(bo, lambda k, w=wo: w.t[:, k * 128:(k + 1) * 128],
                              lambda k, j=j: self.act.t[:, k * HALF + j * TT:k * HALF + (j + 1) * TT], FS, [wo, self.act])
                self.add_store(bo, 0.5, hsrc, hsrc_bufs, hdst, hdst_bufs, oc, n0)

    def ple(self, hsrc, hsrc_bufs, hdst, hdst_bufs, half, w_gate, bcol, pT, w_proj):
        P = self.P
        ptv = pT.rearrange("(k p) n -> p k n", p=128)[:, :, half * HALF:(half + 1) * HALF]
        P.dma("gpsimd", self.pt.t[:].rearrange("p (k n) -> p k n", k=2), ptv, self.pt, writes=[self.pt])
        for oc in range(KC):
            wg = self.rr("win", self.win)
            self.load_w(wg, w_gate, oc * 128, 128, KC)
            wp = self.rr("wp", self.wp)
            self.load_w(wp, w_proj, oc * 128, 128, 2)
            for j in range(2):
                n0 = half * HALF + j * TT
                ba = self.rr("bA", self.banks[0:2])
                bb = self.rr("bB", self.banks[2:4])
                self.mm_group(ba, lambda k, w=wg: w.t[:, k * 128:(k + 1) * 128], lambda k, j=j: self.xs(k, j), KC, [wg, self.xn])
                self.mm_group(bb, lambda k, w=wp: w.t[:, k * 128:(k + 1) * 128],
                              lambda k, j=j: self.pt.t[:, k * HALF + j * TT:k * HALF + (j + 1) * TT], 2, [wp, self.pt])
                tmp = self.rr("tmp", self.tmp)
                P.op("scalar", lambda e, t=tmp, b=ba, oc=oc: e.activation(
                    out=t.t[:], in_=b.t[:], func=AF.Sigmoid, bias=self.vecs.t[:, bcol + oc:bcol + oc + 1], scale=1.0),
                    reads=[ba, self.vecs], writes=[tmp])
                self.add_store(bb, 1.0, hsrc, hsrc_bufs, hdst, hdst_bufs, oc, n0, mul_by=tmp)

    def proj_fm(self, half, w_ap, c0, nchunks, dst_fn, dst_bufs, evac="copy"):
        P = self.P
        for c in range(nchunks):
            w = self.rr("win", self.win)
            self.load_w(w, w_ap, c0 + c * 128, 128, KC)
            for j in range(2):
                b = self.rr("bA", self.banks[0:4])
                self.mm_group(b, lambda k, w=w: w.t[:, k * 128:(k + 1) * 128], lambda k, j=j: self.xs(k, j), KC, [w, self.xn])
                ob = self.rr("ob", self.ob)
                eng = self.rr("evac", ["scalar", "vector"])
                if eng == "scalar":
                    P.op("scalar", lambda e, o=ob, b=b: e.activation(out=o.t[:], in_=b.t[:], func=AF.Copy), reads=[b], writes=[ob])
                else:
                    P.op("vector", lambda e, o=ob, b=b: e.tensor_copy(out=o.t[:], in_=b.t[:]), reads=[b], writes=[ob])
                P.dma("sync", dst_fn(c, j), ob.t[:], ob, reads=[ob], writes=dst_bufs)

    def oproj(self, hsrc, hsrc_bufs, hdst, hdst_bufs, half, w_o):
        for oc in range(KC):
            w = self.rr("win", self.win)
            self.load_w(w, w_o, oc * 128, 128, KC)
            for j in range(2):
                n0 = half * HALF + j * TT
                b = self.rr("bO", self.banks[4:6])
                self.mm_group(b, lambda k, w=w: w.t[:, k * 128:(k + 1) * 128], lambda k, j=j: self.xs(k, j), KC, [w, self.xn])
                self.add_store(b, 1.0, hsrc, hsrc_bufs, hdst, hdst_bufs, oc, n0)

    def final_norm(self, hsrc, hsrc_bufs, out_ap, out_bufs, half, gcol):
        P = self.P
        hs = hsrc.rearrange("(k p) n -> k p n", p=128)
        od = out_ap.rearrange("(k p) n -> k p n", p=128)
        for j in range(2):
            self.stats(hsrc, hsrc_bufs, half * HALF + j * TT, j, RMS_EPS)
        for j in range(2):
            n0 = half * HALF + j * TT
            for k in range(KC):
                ht = self.rr("ht", self.ht)
                P.dma("gpsimd", ht.t[:], hs[k, :, n0:n0 + TT], ht, reads=[hsrc_bufs[n0 // TT]], writes=[ht])
                P.op("vector", lambda e, k=k, t=ht, j=j: e.scalar_tensor_tensor(
                    out=t.t[:], in0=t.t[:], scalar=self.vecs.t[:, gcol + k:gcol + k + 1], in1=self.rstd[j].t[:],
                    op0=ALU.mult, op1=ALU.mult), reads=[ht, self.vecs, self.rstd[j]], writes=[ht])
                P.dma("sync", od[k, :, n0:n0 + TT], ht.t[:], ht, reads=[ht], writes=out_bufs)

    def evac_bf16(self, bank, ncol=TT):
        P = self.P
        ob = self.rr("ob", self.ob)
        eng = self.rr("evac", ["scalar", "vector"])
        if eng == "scalar":
            P.op("scalar", lambda e, o=ob, b=bank: e.activation(out=o.t[:, 0:ncol], in_=b.t[:, 0:ncol], func=AF.Copy), reads=[bank], writes=[ob])
        else:
            P.op("vector", lambda e, o=ob, b=bank: e.tensor_copy(out=o.t[:, 0:ncol], in_=b.t[:, 0:ncol]), reads=[bank], writes=[ob])
        return ob

    def proj_tm(self, half, w_ap, c0, ncol, dst_fn, dst_bufs):
        P = self.P
        w = self.rr("wout", self.wout)
        self.load_w(w, w_ap, c0, ncol, KC)
        for tb in range(HALF // 128):
            b = self.rr("bA", self.banks[0:4])
            self.mm_group(b, lambda k, tb=tb: self.xn.t[:, k * HALF + tb * 128:k * HALF + (tb + 1) * 128],
                          lambda k, w=w: w.t[:, k * ncol:(k + 1) * ncol], KC, [w, self.xn], ncol=ncol)
            ob = self.evac_bf16(b, ncol)
            P.dma("sync", dst_fn(tb), ob.t[:, 0:ncol], ob, reads=[ob], writes=dst_bufs)

    def load_xn(self, src, src_bufs, half):
        v = src.rearrange("(k p) n -> p k n", p=128)[:, :, half * HALF:(half + 1) * HALF]
        self.P.dma("sync", self.xn.t[:].rearrange("p (k n) -> p k n", k=KC), v, self.xn, reads=src_bufs, writes=[self.xn])

    def proj_conv_in(self, half, w_ap, zT, bgT, dst_bufs):
        P = self.P
        zd = zT.rearrange("(k p) n -> k p n", p=128)
        bd = bgT.rearrange("(k p) n -> k p n", p=128)
        for c in range(KC):
            wb = self.rr("win", self.win)
            wc = self.rr("win", self.win)
            wu = self.rr("win", self.win)
            self.load_w(wb, w_ap, c * 128, 128, KC)
            self.load_w(wc, w_ap, D + c * 128, 128, KC)
            self.load_w(wu, w_ap, 2 * D + c * 128, 128, KC)
            for j in range(2):
                n0 = half * HALF + j * TT
                b0 = self.rr("bA", self.banks[0:2])
                b1 = self.rr("bB", self.banks[2:4])
                b2 = self.rr("bO", self.banks[4:6])
                for (b, w) in ((b0, wb), (b1, wc), (b2, wu)):
                    self.mm_group(b, lambda k, w=w: w.t[:, k * 128:(k + 1) * 128], lambda k, j=j: self.xs(k, j), KC, [w, self.xn])
                t0 = self.rr("ht", self.ht)
                P.op("scalar", lambda e, t=t0, b=b0: e.activation(out=t.t[:], in_=b.t[:], func=AF.Copy), reads=[b0], writes=[t0])
                P.dma("sync", bd[c, :, n0:n0 + TT], t0.t[:], t0, reads=[t0], writes=dst_bufs)
                t1 = self.rr("ht", self.ht)
                P.op("scalar", lambda e, t=t1, b=b1: e.activation(out=t.t[:], in_=b.t[:], func=AF.Copy), reads=[b1], writes=[t1])
                P.op("vector", lambda e, t=t1, b=b2: e.tensor_tensor(out=t.t[:], in0=t.t[:], in1=b.t[:], op=ALU.mult), reads=[b2, t1], writes=[t1])
                P.dma("sync", zd[c, :, n0:n0 + TT], t1.t[:], t1, reads=[t1], writes=dst_bufs)

    def init_attn(self, masks_ap, nmask, sel_ap, ident_ap):
        P = self.P
        self.masks = P.sb("masks", [128, nmask * 128], BF16)
        self.ident = P.sb("ident", [128, 128], BF16)
        self.sel = P.sb("sel", [128, 16], F32)
        P.dma("gpsimd", self.masks.t[:], masks_ap, self.masks, writes=[self.masks])
        P.dma("gpsimd", self.ident.t[:], ident_ap, self.ident, writes=[self.ident])
        P.dma("sync", self.sel.t[:], sel_ap, self.sel, writes=[self.sel])
        self.pts = [P.vbuf("pt") for _ in range(3)]
        self.PT0 = FS * HALF - 3 * TT

    def pt_ap(self, i):
        return self.act.t[:, self.PT0 + i * TT:self.PT0 + (i + 1) * TT]

    def mask_ap(self, mi):
        return self.masks.t[:, mi * 128:(mi + 1) * 128]

    def attn_run(self, blocks, q_ap, qbuf, o_banks, l_bank, scale):
        P = self.P
        n = len(blocks)
        st = [None] * n

        def stage_s(i):
            b = blocks[i]
            c0 = b.get("c0", 0)
            ms = b.get("masks", ())
            sb = self.rr("bS", self.banks[0:3])
            P.op("tensor", lambda e, b=b, sb=sb, c0=c0, ms=ms: e.matmul(sb.t[:, c0:TT], b["K"], q_ap[:, c0:TT], start=True, stop=(len(ms) == 0)),
                 reads=[qbuf] + b["bufs"], writes=[sb], inc=(len(ms) == 0))
            for mi, (map_, mc0, w) in enumerate(ms):
                last = mi == len(ms) - 1
                P.op("tensor", lambda e, sb=sb, map_=map_, mc0=mc0, w=w, last=last: e.matmul(
                    sb.t[:, mc0:mc0 + w], self.ident.t[:], map_, start=False, stop=last),
                    reads=[self.ident, self.masks], writes=[sb], inc=last)
            pi = self.cnt.get("pti", 0)
            self.cnt["pti"] = pi + 1
            pt = self.pts[pi % 3]
            pap = self.pt_ap(pi % 3)
            bias = b.get("bias")
            if bias is None:
                bias = self.sel.t[:, 10:11]
            P.op("scalar", lambda e, sb=sb, pap=pap, c0=c0, bias=bias: e.activation(
                out=pap[:, c0:TT], in_=sb.t[:, c0:TT], func=AF.Exp, bias=bias, scale=scale),
                reads=[sb, self.sel], writes=[pt])
            st[i] = (pt, pap, c0)

        def stage_pv(i):
            pt, pap, c0 = st[i]
            b = blocks[i]
            first, last = (i == 0), (i == n - 1)
            for oi, ob in enumerate(o_banks):
                P.op("tensor", lambda e, ob=ob, v=b["V"][oi], pap=pap, c0=c0, first=first, last=last: e.matmul(
                    ob.t[:, c0:TT], v, pap[:, c0:TT], start=first, stop=last),
                    reads=[pt] + b["bufs"], writes=[ob], inc=False)
            P.op("tensor", lambda e, pap=pap, c0=c0, first=first, last=last: e.matmul(
                l_bank.t[:, c0:TT], self.ones.t[:], pap[:, c0:TT], start=first, stop=last),
                reads=[pt, self.ones], writes=[l_bank] + list(o_banks), inc=True)

        stage_s(0)
        if n > 1:
            stage_s(1)
        for i in range(n):
            stage_pv(i)
            if i + 2 < n:
                stage_s(i + 2)

    def attn_A(self, qA, kloc, vloc, kall, vall, own_bufs, src_bufs, attn, attn_bufs):
        P = self.P
        scale = 128.0 ** -0.5
        A = self.act.t
        QO, KO, VO, PO = 0, 8192, 14336, 20480
        qb, kb, vb = P.vbuf("qA"), P.vbuf("kA"), P.vbuf("vA")
        kpb = [P.vbuf("kpA") for _ in range(3)]
        vpb = [P.vbuf("vpA") for _ in range(3)]
        allb = [qb, kb, vb] + kpb + vpb + self.pts
        for b in allb:
            b.r.update(self.act.r); b.w.update(self.act.w)
        osets = [(self.banks[3], self.banks[4]), (self.banks[5], self.banks[6])]
        attn_v = attn.rearrange("(h d) n -> d h n", d=128)
        for kvh in range(4):
            qsrc = qA.rearrange("(h d) (nb q) -> d nb h q", d=128, q=128)[:, :, kvh * 4:(kvh + 1) * 4, :]
            qdst = A[:, QO:QO + 8192].rearrange("p (nb h q) -> p nb h q", nb=16, h=4)
            for hh in range(4):
                P.dma("sync", qdst[:, :, hh, :], qsrc[:, :, hh, :], qb, reads=own_bufs, writes=[qb])
            ksrc = kloc.rearrange("(g v d) n -> d g v n", g=3, v=4)[:, :, kvh, :]
            P.dma("sync", A[:, KO:KO + 6144].rearrange("p (g n) -> p g n", g=3), ksrc, kb, reads=own_bufs, writes=[kb])
            vsrc = vloc.rearrange("(b p) (g v d) -> p g b v d", p=128, g=3, v=4)[:, :, :, kvh, :]
            vdst = A[:, VO:VO + 6144].rearrange("p (g b d) -> p g b d", g=3, b=16)
            for g in range(3):
                P.dma("sync", vdst[:, g], vsrc[:, g], vb, reads=own_bufs, writes=[vb])
            for s in range(3):
                kbase = PO + s * 6144
                vbase = kbase + 3072
                vv = vall.rearrange("(r ih s jl p) (g v d) -> s p g v r ih jl d", r=4, ih=2, s=4, jl=2, p=128, g=3, v=4)[s]
                for g in range(3):
                    r0 = (g * 4 + kvh) * 128
                    base = ((r0 // 256) * 4 + s) * 256 + (r0 % 256)
                    ka = kall[base:base + 128, :].rearrange("d (r j) -> d r j", r=4)
                    if g < 2:
                        P.dma("sync", A[:, kbase + g * 512:kbase + (g + 1) * 512].rearrange("p (r j) -> p r j", r=4),
                              ka[:, :, 384:512], kpb[s], reads=src_bufs, writes=[kpb[s]])
                        P.dma("sync", A[:, vbase + g * 512:vbase + (g + 1) * 512].rearrange("p (r d) -> p r d", r=4),
                              vv[:, g, kvh, :, 1, 1, :], vpb[s], reads=src_bufs, writes=[vpb[s]])
                    else:
                        P.dma("sync", A[:, kbase + 1024:kbase + 3072].rearrange("p (r j) -> p r j", r=4),
                              ka, kpb[s], reads=src_bufs, writes=[kpb[s]])
                        for r in range(4):
                            for ih in range(2):
                                o_ = vbase + 1024 + r * 512 + ih * 256
                                P.dma("sync", A[:, o_:o_ + 256].rearrange("p (jl d) -> p jl d", jl=2),
                                      vv[:, g, kvh, r, ih, :, :], vpb[s], reads=src_bufs, writes=[vpb[s]])
            for r4 in range(4):
                for jb in range(4):
                    blocks = []

                    def add(g, rk, jbk, mi):
                        m4 = [(self.mask_ap(mi), hh * 128, 128) for hh in range(4)]
                        if jbk >= 0:
                            blk = rk * 4 + jbk
                            blocks.append(dict(K=A[:, KO + g * 2048 + blk * 128:KO + g * 2048 + (blk + 1) * 128],
                                               V=[A[:, VO + (g * 16 + blk) * 128:VO + (g * 16 + blk + 1) * 128]],
                                               bufs=[kb, vb], masks=m4))
                        else:
                            jp = jbk + 4
                            for s in range(3):
                                kbase = PO + s * 6144
                                vbase = kbase + 3072
                                if g < 2:
                                    o = g * 512 + rk * 128
                                else:
                                    o = 1024 + (rk * 4 + jp) * 128
                                blocks.append(dict(K=A[:, kbase + o:kbase + o + 128], V=[A[:, vbase + o:vbase + o + 128]],
                                                   bufs=[kpb[s], vpb[s]], masks=m4, bias=self.sel.t[:, s:s + 1]))

                    add(1, r4, jb, 0)
                    add(1, r4, jb - 1, 1)
                    for dj in range(5):
                        add(2, r4, jb - dj, (2, 3, 3, 3, 4)[dj])
                    for rk in range(4):
                        for dj in range(2):
                            add(0, rk, jb - dj, 5 + (r4 - rk + 3) * 2 + dj)
                    nb = r4 * 4 + jb
                    ob, lb = self.rr("oset", osets)
                    self.attn_run(blocks, A[:, QO + nb * 512:QO + (nb + 1) * 512], qb, [ob], lb, scale)
                    rl = self.rr("tmp", self.tmp)
                    P.op("vector", lambda e, rl=rl, lb=lb: e.reciprocal(out=rl.t[:], in_=lb.t[:]), reads=[lb], writes=[rl])
                    o = self.rr("ob", self.ob)
                    P.op("vector", lambda e, o=o, ob=ob, rl=rl: e.tensor_tensor(out=o.t[:], in0=ob.t[:], in1=rl.t[:], op=ALU.mult),
                         reads=[ob, rl], writes=[o])
                    P.dma("sync", attn_v[:, kvh * 4:(kvh + 1) * 4, nb * 128:(nb + 1) * 128],
                          o.t[:].rearrange("p (h q) -> p h q", h=4), o, reads=[o], writes=attn_bufs)
        for b in allb:
            for k, v in b.r.items():
                self.act.r[k] = max(self.act.r.get(k, 0), v)
            for k, v in b.w.items():
                self.act.w[k] = max(self.act.w.get(k, 0), v)

    def lam_setup(self, lam_ap, lam_init):
        P = self.P
        self.lamt = P.sb("lamt", [128, 8], F32)
        self.onesf = P.sb("onesf", [128, 128], F32)
        P.op("vector", lambda e: e.memset(self.onesf.t[:], 1.0), writes=[self.onesf])
        P.dma("sync", self.lamt.t[:, 0:4], lam_ap, self.lamt, writes=[self.lamt])
        L = self.lamt
        P.op("vector", lambda e: e.tensor_tensor(out=L.t[:, 4:5], in0=L.t[:, 0:1], in1=L.t[:, 1:2], op=ALU.mult), reads=[L], writes=[L])
        P.op("vector", lambda e: e.tensor_tensor(out=L.t[:, 5:6], in0=L.t[:, 2:3], in1=L.t[:, 3:4], op=ALU.mult), reads=[L], writes=[L])
        bank = self.banks[7]
        P.op("tensor", lambda e: e.matmul(bank.t[:, 0:2], self.onesf.t[:], L.t[:, 4:6], start=True, stop=True),
             reads=[L, self.onesf], writes=[bank])
        P.op("scalar", lambda e: e.activation(out=L.t[:, 6:8], in_=bank.t[:, 0:2], func=AF.Exp), reads=[bank], writes=[L])
        P.op("vector", lambda e: e.scalar_tensor_tensor(out=L.t[:, 4:5], in0=L.t[:, 7:8], scalar=-lam_init, in1=L.t[:, 6:7],
                                                        op0=ALU.add, op1=ALU.subtract), reads=[L], writes=[L])

    def attn_B(self, qB, kloc, vloc, kall, vall, own_bufs, src_bufs, attn, attn_bufs, gcol, lam_init):
        P = self.P
        scale = 128.0 ** -0.5
        A = self.act.t
        X = self.xn.t
        XF = self.xn.t.bitcast(F32)
        QO, KA, KO, VA = 0, 4096, 20480, 24576
        qb, kob, vob = P.vbuf("qB"), P.vbuf("koB"), P.vbuf("voB")
        kab = [[P.vbuf("kaB") for _ in range(3)] for _ in range(2)]
        vab = [P.vbuf("vaB") for _ in range(3)]
        ocb = [P.vbuf("oc0"), P.vbuf("oc1")]
        dfb = P.vbuf("diff")
        allb = [qb, kob, vob, dfb] + kab[0] + kab[1] + vab + ocb + self.pts
        for b in allb:
            b.r.update(self.act.r); b.w.update(self.act.w)
            b.r.update(self.xn.r); b.w.update(self.xn.w)
        oc_ap = [XF[:, 2048:3072], XF[:, 3072:4096]]
        df_ap = XF[:, 4096:5120]
        o_banks = [self.banks[3], self.banks[4]]
        l_bank = self.banks[5]
        sbank = self.banks[6]
        one_m = 1.0 - lam_init
        P.op("vector", lambda e: e.memset(self.epsc.t[:, 1:2], SUBLN_EPS / (one_m * one_m)), reads=[], writes=[self.epsc])
        attn_v = attn.rearrange("(h c d) n -> h c d n", c=2, d=128)
        for h in range(8):
            qs = qB.rearrange("(h c d) n -> h d c n", c=2, d=128)[h]
            P.dma("sync", A[:, QO:QO + 4096].rearrange("p (c n) -> p c n", c=2), qs, qb, reads=own_bufs, writes=[qb])
            va = vall.rearrange("(i s wl p) (h e) -> h s p i wl e", s=4, wl=2, p=128, e=256)[h]
            for s in range(3):
                for c in range(2):
                    ka = kall.rearrange("(h s c d) n -> h c s d n", s=4, c=2, d=128)[h, c, s]
                    P.dma("sync", A[:, KA + c * 8192 + s * 2048:KA + c * 8192 + (s + 1) * 2048], ka, kab[c][s], reads=src_bufs, writes=[kab[c][s]])
                vd = A[:, VA + s * 4096:VA + (s + 1) * 4096].rearrange("p (i wl e) -> p i wl e", wl=2, e=256)
                for wl in range(2):
                    P.dma("sync", vd[:, :, wl, :], va[s][:, :, wl, :], vab[s], reads=src_bufs, writes=[vab[s]])
            ks = kloc.rearrange("(h c d) n -> h d c n", c=2, d=128)[h]
            P.dma("sync", A[:, KO:KO + 4096].rearrange("p (c n) -> p c n", c=2), ks, kob, reads=own_bufs, writes=[kob])
            vo = vloc.rearrange("(b p) (h e) -> h p b e", p=128, e=256)[h]
            P.dma("sync", X[:, 0:4096].rearrange("p (b e) -> p b e", e=256), vo, vob, reads=own_bufs, writes=[vob])
            for r4 in range(4):
                for c in range(2):
                    blocks = []
                    for s in range(3):
                        for blk in range(16):
                            ko = KA + c * 8192 + s * 2048 + blk * 128
                            vo_ = VA + (s * 16 + blk) * 256
                            blocks.append(dict(K=A[:, ko:ko + 128], V=[A[:, vo_:vo_ + 128], A[:, vo_ + 128:vo_ + 256]],
                                               bufs=[kab[c][s], vab[s]], bias=self.sel.t[:, 3 + s:4 + s]))
                    for jbk in (3, 2, 1, 0):
                        for rk in range(4):
                            blk = rk * 4 + jbk
                            ko = KO + c * 2048 + blk * 128
                            vo_ = blk * 256
                            mi = 0 if rk <= r4 else 19
                            blocks.append(dict(K=A[:, ko:ko + 128], V=[X[:, vo_:vo_ + 128], X[:, vo_ + 128:vo_ + 256]],
                                               bufs=[kob, vob], masks=[(self.mask_ap(mi), jbk * 128, 128)], c0=jbk * 128))
                    q_ap = A[:, QO + c * 2048 + r4 * 512:QO + c * 2048 + (r4 + 1) * 512]
                    self.attn_run(blocks, q_ap, qb, o_banks, l_bank, scale)
                    rl = self.rr("tmp", self.tmp)
                    P.op("vector", lambda e, rl=rl: e.reciprocal(out=rl.t[:], in_=l_bank.t[:]), reads=[l_bank], writes=[rl])
                    for e2 in range(2):
                        P.op("vector", lambda e, c=c, e2=e2, rl=rl: e.tensor_tensor(
                            out=oc_ap[c][:, e2 * 512:(e2 + 1) * 512], in0=o_banks[e2].t[:], in1=rl.t[:], op=ALU.mult),
                            reads=[o_banks[e2], rl], writes=[ocb[c]])
                P.op("vector", lambda e: e.scalar_tensor_tensor(out=df_ap, in0=oc_ap[1], scalar=self.lamt.t[:, 4:5], in1=oc_ap[0],
                                                                op0=ALU.mult, op1=ALU.add), reads=ocb + [self.lamt], writes=[dfb])
                for e2 in range(2):
                    sq = self.rr("sq", self.sq)
                    P.op("scalar", lambda e, sq=sq, e2=e2: e.activation(out=sq.t[:], in_=df_ap[:, e2 * 512:(e2 + 1) * 512], func=AF.Square),
                         reads=[dfb], writes=[sq])
                    P.op("tensor", lambda e, sq=sq, e2=e2: e.matmul(sbank.t[:], self.ones.t[:], sq.t[:], start=(e2 == 0), stop=(e2 == 1)),
                         reads=[sq, self.ones], writes=[sbank], inc=True)
                P.op("scalar", lambda e: e.activation(out=self.rtmp.t[:], in_=sbank.t[:], func=AF.Sqrt, bias=self.epsc.t[:, 1:2],
                                                      scale=1.0 / (256.0 * one_m * one_m)), reads=[sbank, self.epsc], writes=[self.rtmp])
                P.op("vector", lambda e: e.reciprocal(out=self.rstd[0].t[:], in_=self.rtmp.t[:]), reads=[self.rtmp], writes=[self.rstd[0]])
                for e2 in range(2):
                    o = self.rr("ob", self.ob)
                    P.op("vector", lambda e, o=o, e2=e2: e.scalar_tensor_tensor(
                        out=o.t[:], in0=df_ap[:, e2 * 512:(e2 + 1) * 512], scalar=self.vecs.t[:, gcol + e2:gcol + e2 + 1],
                        in1=self.rstd[0].t[:], op0=ALU.mult, op1=ALU.mult), reads=[dfb, self.vecs, self.rstd[0]], writes=[o])
                    P.dma("sync", attn_v[h, e2, :, r4 * 512:(r4 + 1) * 512], o.t[:], o, reads=[o], writes=attn_bufs)
        for b in allb:
            for tgt in (self.act, self.xn):
                for k, v in b.r.items():
                    tgt.r[k] = max(tgt.r.get(k, 0), v)
                for k, v in b.w.items():
                    tgt.w[k] = max(tgt.w.get(k, 0), v)

    def conv_C(self, zT, bgT, zh_all, src_bufs, attn, attn_bufs, wcol):
        P = self.P
        AF32 = self.act.t.bitcast(F32)
        X = self.xn.t
        sets = []
        for i in range(2):
            base = i * 8192
            sets.append(dict(z=AF32[:, base:base + 2048], bg=AF32[:, base + 2048:base + 4096], acc=AF32[:, base + 4096:base + 6144],
                             hin=AF32[:, base + 6144:base + 6144 + 24], hal=AF32[:, base + 6200:base + 6202],
                             out=X[:, i * 2048:(i + 1) * 2048],
                             zb=P.vbuf("cz"), bb=P.vbuf("cbg"), ab=P.vbuf("cacc"), hb=P.vbuf("chin"), ob=P.vbuf("cout")))
        allb = [s[k] for s in sets for k in ("zb", "bb", "ab", "hb", "ob")]
        for b in allb:
            b.r.update(self.act.r); b.w.update(self.act.w)
            b.r.update(self.xn.r); b.w.update(self.xn.w)
        zd = zT.rearrange("(k p) n -> k p n", p=128)
        bd = bgT.rearrange("(k p) n -> k p n", p=128)
        ad = attn.rearrange("(k p) n -> k p n", p=128)
        zh = zh_all.rearrange("(s k p) e -> k p s e", s=4, p=128)
        for c in range(KC):
            S = sets[c % 2]
            P.dma("gpsimd", S["z"], zd[c], S["zb"], reads=src_bufs, writes=[S["zb"]])
            P.dma("gpsimd", S["bg"], bd[c], S["bb"], reads=src_bufs, writes=[S["bb"]])
            P.dma("sync", S["hin"].rearrange("p (s e) -> p s e", s=3), zh[c][:, 0:3, :], S["hb"], reads=src_bufs, writes=[S["hb"]])
            hin, hal = S["hin"], S["hal"]
            P.op("vector", lambda e, hin=hin, hal=hal: e.tensor_scalar(out=hal, in0=hin[:, 0:2], scalar1=self.sel.t[:, 7:8], scalar2=None, op0=ALU.mult),
                 reads=[S["hb"], self.sel], writes=[S["hb"]])
            for s in (1, 2):
                P.op("vector", lambda e, hin=hin, hal=hal, s=s: e.scalar_tensor_tensor(
                    out=hal, in0=hin[:, 8 * s:8 * s + 2], scalar=self.sel.t[:, 7 + s:8 + s], in1=hal, op0=ALU.mult, op1=ALU.add),
                    reads=[S["hb"], self.sel], writes=[S["hb"]])
            w0 = self.vecs.t[:, wcol + c:wcol + c + 1]
            w1 = self.vecs.t[:, wcol + KC + c:wcol + KC + c + 1]
            w2 = self.vecs.t[:, wcol + 2 * KC + c:wcol + 2 * KC + c + 1]
            z, acc = S["z"], S["acc"]
            rd = [S["zb"], S["hb"], self.vecs]
            P.op("vector", lambda e, z=z, acc=acc, w2=w2: e.tensor_scalar(out=acc, in0=z, scalar1=w2, scalar2=None, op0=ALU.mult),
                 reads=rd, writes=[S["ab"]])

            def fma(dst, src, w):
                P.op("vector", lambda e, dst=dst, src=src, w=w: e.scalar_tensor_tensor(out=dst, in0=src, scalar=w, in1=dst, op0=ALU.mult, op1=ALU.add),
                     reads=rd + [S["ab"]], writes=[S["ab"]])

            for r4 in range(4):
                a = acc[:, r4 * 512:(r4 + 1) * 512]
                if r4 >= 1:
                    fma(a, z[:, (r4 - 1) * 512:r4 * 512], w1)
                else:
                    fma(a[:, 1:512], z[:, 3 * 512:3 * 512 + 511], w1)
                    fma(a[:, 0:1], S["hal"][:, 1:2], w1)
                if r4 >= 2:
                    fma(a, z[:, (r4 - 2) * 512:(r4 - 1) * 512], w0)
                else:
                    fma(a[:, 1:512], z[:, (r4 + 2) * 512:(r4 + 2) * 512 + 511], w0)
                    fma(a[:, 0:1], S["hal"][:, r4:r4 + 1], w0)
            P.op("vector", lambda e, S=S: e.tensor_tensor(out=S["out"], in0=S["acc"], in1=S["bg"], op=ALU.mult),
                 reads=[S["ab"], S["bb"]], writes=[S["ob"]])
            P.dma("sync", ad[c], S["out"], S["ob"], reads=[S["ob"]], writes=attn_bufs)
        for b in allb:
            for tgt in (self.act, self.xn):
                for k, v in b.r.items():
                    tgt.r[k] = max(tgt.r.get(k, 0), v)
                for k, v in b.w.items():
                    tgt.w[k] = max(tgt.w.get(k, 0), v)


NV = 400
VL = 80
V_NORMF = 320
V_SUBLN = 336
V_CONV = 338
NMASK = 20
GROUPS = [[0, 1, 2, 3], [4, 5, 6, 7]]
LAM_INIT = 0.8 - 0.6 * float(np.exp(-0.3 * 1))

W_SHAPES = {}
for _i in range(4):
    W_SHAPES["w1i%d" % _i] = (D, 2 * DFF)
    W_SHAPES["w1o%d" % _i] = (DFF, D)
    W_SHAPES["w2i%d" % _i] = (D, 2 * DFF)
    W_SHAPES["w2o%d" % _i] = (DFF, D)
    W_SHAPES["wpg%d" % _i] = (D, D)
    W_SHAPES["wpp%d" % _i] = (PLE, D)
for _j in range(2):
    W_SHAPES["aqkv%d" % _j] = (D, 5120)
    W_SHAPES["ao%d" % _j] = (D, D)
W_SHAPES.update(bqkv=(D, 3 * D), bo=(D, D), cin=(D, 3 * D), cout=(D, D))


def build_net(stop=None, dbg=None):
    P = Prog()
    xT = P.dram_in("xT", [D, TOK])
    pT = [P.dram_in("pT%d" % i, [PLE, TOK]) for i in range(4)]
    vecs = P.dram_in("vecs", [128, NV])
    masks = P.dram_in("masks", [128, NMASK * 128])
    ident = P.dram_in("ident", [128, 128])
    sel = P.dram_in("sel", [128, 16])
    lamT = P.dram_in("lamT", [128, 4])
    class _LazyW(dict):
        def __missing__(self, k):
            self[k] = P.dram_in(k, list(W_SHAPES[k]))
            return self[k]
    W = _LazyW()
    outT = P.dram_out("outT", [D, TOK])
    h = P.dram_tmp("h", [D, TOK])
    q = P.dram_tmp("q", [D, TOK], BF16)
    klA = P.dram_tmp("klA", [1536, TOK], BF16)
    vlA = P.dram_tmp("vlA", [TOK, 1536], BF16)
    kaA = P.dram_tmp("kaA", [4 * 1536, TOK], BF16)
    vaA = P.dram_tmp("vaA", [4 * TOK, 1536], BF16)
    klB = P.dram_tmp("klB", [D, TOK], BF16)
    vlB = P.dram_tmp("vlB", [TOK, D], BF16)
    kaB = P.dram_tmp("kaB", [4 * D, TOK], BF16)
    vaB = P.dram_tmp("vaB", [4 * TOK, D], BF16)
    zT = P.dram_tmp("zT", [D, TOK])
    bgT = P.dram_tmp("bgT", [D, TOK])
    zhl = P.dram_tmp("zhl", [D, 8])
    zha = P.dram_tmp("zha", [4 * D, 8])
    attn = P.dram_tmp("attn", [D, TOK], BF16)

    R = Rows(P, vecs, NV)
    R.init_attn(masks, NMASK, sel, ident)
    R.lam_setup(lamT, LAM_INIT)

    xb = [P.vbuf("x") for _ in range(4)]
    hb = [P.vbuf("h") for _ in range(4)]
    ob = [P.vbuf("out")]
    P.out_bufs += ob
    qb, klb, vlb, kab, vab, atb = (P.vbuf(n) for n in ("q", "kl", "vl", "ka", "va", "attn"))
    zb, zhlb, zhab = P.vbuf("z"), P.vbuf("zhl"), P.vbuf("zha")
    ccb = P.vbuf("cc")

    def cc(in_ap, out_ap, in_bufs, out_bufs):
        E = P.engs["gpsimd"]
        if ccb.sem is None:
            ccb.sem = P.new_sem("cc")
        waits = P._collect(E, in_bufs, out_bufs)
        for s_, v_ in waits.items():
            E.waited[s_] = v_
        ccb.cnt += 1
        E.ops.append((sorted(waits.items()), lambda e, i=in_ap, o=out_ap: e.collective_compute(
            "AllGather", ALU.bypass, replica_groups=GROUPS, ins=[i], outs=[o]), (ccb.sem, 1)))
        P._mark((ccb.sem, ccb.cnt), in_bufs, out_bufs)

    def gather(loc, allt, rows, lb, ab):
        for i in range(rows // 256):
            cc(loc[i * 256:(i + 1) * 256, :], allt[i * 1024:(i + 1) * 1024, :], [lb], [ab])

    state = {"hsrc": xT, "hsb": xb}

    def hs():
        return state["hsrc"], state["hsb"]

    def wrote_h():
        state["hsrc"], state["hsb"] = h, hb

    def chain(fns):
        for f in fns:
            for half in range(2):
                f(half)
            if getattr(f, "writes_h", False):
                wrote_h()

    def mk(fn, writes_h=False):
        fn.writes_h = writes_h
        return fn

    def s_ffn(i, which):
        g = i * VL + (0 if which == 1 else 32)
        wi, wo = W["w%di%d" % (which, i)], W["w%do%d" % (which, i)]

        def f(half):
            src, sb_ = hs()
            R.norm(src, sb_, half, g)
            R.ffn(src, sb_, h, hb, half, wi, wo)
        return mk(f, True)

    def s_ple(i):
        def f(half):
            src, sb_ = hs()
            R.norm(src, sb_, half, i * VL + 48)
            R.ple(src, sb_, h, hb, half, W["wpg%d" % i], i * VL + 64, pT[i], W["wpp%d" % i])
        return mk(f, True)

    def s_oproj(w):
        def f(half):
            src, sb_ = hs()
            R.load_xn(attn, [atb], half)
            R.oproj(src, sb_, h, hb, half, w)
        return mk(f, True)

    def s_projA(i, j):
        w = W["aqkv%d" % j]

        def f(half):
            src, sb_ = hs()
            R.norm(src, sb_, half, i * VL + 16)
            qv = q.rearrange("(c p) n -> c p n", p=128)
            kv = klA.rearrange("(c p) n -> c p n", p=128)
            R.proj_fm(half, w, 0, 16, lambda c, jj: qv[c, :, half * HALF + jj * TT:half * HALF + (jj + 1) * TT], [qb])
            for g in range(3):
                R.proj_fm(half, w, D + g * 1024, 4,
                          lambda c, jj, g=g: kv[g * 4 + c, :, half * HALF + jj * TT:half * HALF + (jj + 1) * TT], [klb])
                for grp in range(2):
                    R.proj_tm(half, w, D + g * 1024 + 512 + grp * 256, 256,
                              lambda tb, g=g, grp=grp: vlA[half * HALF + tb * 128:half * HALF + (tb + 1) * 128,
                                                           g * 512 + grp * 256:g * 512 + (grp + 1) * 256], [vlb])
        return mk(f)

    def s_projB(i):
        w = W["bqkv"]

        def f(half):
            src, sb_ = hs()
            R.norm(src, sb_, half, i * VL + 16)
            qv = q.rearrange("(c p) n -> c p n", p=128)
            kv = klB.rearrange("(c p) n -> c p n", p=128)
            R.proj_fm(half, w, 0, 16, lambda c, jj: qv[c, :, half * HALF + jj * TT:half * HALF + (jj + 1) * TT], [qb])
            R.proj_fm(half, w, D, 16, lambda c, jj: kv[c, :, half * HALF + jj * TT:half * HALF + (jj + 1) * TT], [klb])
            for hh in range(8):
                R.proj_tm(half, w, 2 * D + hh * 256, 256,
                          lambda tb, hh=hh: vlB[half * HALF + tb * 128:half * HALF + (tb + 1) * 128, hh * 256:(hh + 1) * 256], [vlb])
        return mk(f)

    def s_projC(i):
        def f(half):
            src, sb_ = hs()
            R.norm(src, sb_, half, i * VL + 16)
            R.proj_conv_in(half, W["cin"], zT, bgT, [zb])
        return mk(f)

    def s_final():
        def f(half):
            src, sb_ = hs()
            R.final_norm(src, sb_, outT, ob, half, V_NORMF)
        return mk(f)

    def halo():
        P.dma("sync", zhl[:, 0:1], zT[:, 2 * 512 + 511:2 * 512 + 512], zhlb, reads=[zb], writes=[zhlb], allow_slow_non_contiguous=True)
        P.dma("sync", zhl[:, 1:2], zT[:, 3 * 512 + 511:3 * 512 + 512], zhlb, reads=[zb], writes=[zhlb], allow_slow_non_contiguous=True)
        cc(zhl, zha, [zhlb], [zhab])

    def gA():
        gather(klA, kaA, 1536, klb, kab)
        gather(vlA, vaA, TOK, vlb, vab)

    def gB():
        gather(klB, kaB, D, klb, kab)
        gather(vlB, vaB, TOK, vlb, vab)

    mixA = lambda: R.attn_A(q, klA, vlA, kaA, vaA, [qb, klb, vlb], [kab, vab], attn, [atb])
    mixB = lambda: R.attn_B(q, klB, vlB, kaB, vaB, [qb, klb, vlb], [kab, vab], attn, [atb], V_SUBLN, LAM_INIT)
    mixC = lambda: R.conv_C(zT, bgT, zha, [zb, zhab], attn, [atb], V_CONV)
    C1 = lambda th: (lambda: chain([th()]))
    stages = [
        C1(lambda: s_ffn(0, 1)), C1(lambda: s_projA(0, 0)), gA, mixA, C1(lambda: s_oproj(W["ao0"])), C1(lambda: s_ffn(0, 2)), C1(lambda: s_ple(0)),
        C1(lambda: s_ffn(1, 1)), C1(lambda: s_projB(1)), gB, mixB, C1(lambda: s_oproj(W["bo"])), C1(lambda: s_ffn(1, 2)), C1(lambda: s_ple(1)),
        C1(lambda: s_ffn(2, 1)), C1(lambda: s_projC(2)), halo, mixC, C1(lambda: s_oproj(W["cout"])), C1(lambda: s_ffn(2, 2)), C1(lambda: s_ple(2)),
        C1(lambda: s_ffn(3, 1)), C1(lambda: s_projA(3, 1)), gA, mixA, C1(lambda: s_oproj(W["ao1"])), C1(lambda: s_ffn(3, 2)), C1(lambda: s_ple(3)),
        C1(lambda: s_final()),
    ]
    n = len(stages) if stop is None else stop
    for st in stages[:n]:
        st()
    if stop is not None:
        if dbg is None:
            P.dma("sync", outT, h, ob[0], reads=hb, writes=ob)
        else:
            src = {"attn": attn, "q": q, "klA": klA, "vlA": vlA, "klB": klB, "vlB": vlB}[dbg]
            dbo = P.dram_out("dbg", list(src.shape), BF16)
            P.dma("sync", dbo, src, ob[0], reads=[atb, qb, klb, vlb], writes=ob)
            P.dma("sync", outT, h, ob[0], reads=hb, writes=ob)
    nc = P.emit()
    nc._in_names = list(P.in_names)
    return nc


def _r4perm():
    n = np.arange(TOK)
    return 4 * (n % 512) + n // 512


def _masks():
    jk = np.arange(128)[:, None]
    jq = np.arange(128)[None, :]
    vis = []
    vis.append(jk <= jq)
    vis.append(jk >= jq)
    comb = ((jq - jk) % 4) == 0
    vis.append((jk <= jq) & comb)
    vis.append(comb | (jk < -1))
    vis.append((jk >= jq) & comb)
    for dr in range(-3, 4):
        for djb in range(2):
            dt = 4 * (128 * djb + jq - jk) + dr
            vis.append((dt >= 0) & (dt <= 128))
    vis.append(jk < jq)
    m = np.concatenate([np.where(v, 0.0, NEG) for v in vis], axis=1).astype(np.float32)
    assert m.shape == (128, NMASK * 128)
    return m


def _col16(v):
    return np.ascontiguousarray(np.asarray(v, np.float32).reshape(-1, 128).T)


def make_inputs(inputs):
    perm = _r4perm()
    x = np.asarray(inputs["x"], np.float32)
    p = np.asarray(inputs["p"], np.float32)
    vecs = np.zeros((128, NV), np.float32)
    for i in range(4):
        b = i * VL
        vecs[:, b:b + 16] = _col16(inputs["norm_ffn1"][i])
        vecs[:, b + 16:b + 32] = _col16(inputs["norm_mix"][i])
        vecs[:, b + 32:b + 48] = _col16(inputs["norm_ffn2"][i])
        vecs[:, b + 48:b + 64] = _col16(inputs["norm_ple"][i])
        vecs[:, b + 64:b + 80] = _col16(inputs["b_ple_gate"][i])
    vecs[:, V_NORMF:V_NORMF + 16] = _col16(inputs["norm_f"])
    vecs[:, V_SUBLN:V_SUBLN + 2] = _col16(inputs["b_subln"][0])
    for k in range(3):
        vecs[:, V_CONV + 16 * k:V_CONV + 16 * (k + 1)] = _col16(inputs["c_conv_w"][0][k])
    shared = {"vecs": vecs, "masks": _masks(), "ident": np.eye(128, dtype=np.float32),
              "lamT": np.ascontiguousarray(np.asarray(inputs["b_lambda"][0], np.float32).T)}
    for i in range(4):
        shared["w1i%d" % i] = np.asarray(inputs["w_ffn1_in"][i], np.float32)
        shared["w1o%d" % i] = np.asarray(inputs["w_ffn1_out"][i], np.float32)
        shared["w2i%d" % i] = np.asarray(inputs["w_ffn2_in"][i], np.float32)
        shared["w2o%d" % i] = np.asarray(inputs["w_ffn2_out"][i], np.float32)
        shared["wpg%d" % i] = np.asarray(inputs["w_ple_gate"][i], np.float32)
        shared["wpp%d" % i] = np.asarray(inputs["w_ple_proj"][i], np.float32)
    for j in range(2):
        shared["aqkv%d" % j] = np.asarray(inputs["a_w_qkv"][j], np.float32)
        shared["ao%d" % j] = np.asarray(inputs["a_w_o"][j], np.float32)
    shared["bqkv"] = np.asarray(inputs["b_w_qkv"][0], np.float32)
    shared["bo"] = np.asarray(inputs["b_w_o"][0], np.float32)
    shared["cin"] = np.asarray(inputs["c_w_in"][0], np.float32)
    shared["cout"] = np.asarray(inputs["c_w_out"][0], np.float32)
    maps = []
    for core in range(NCORES):
        b, c = divmod(core, 4)
        sl = slice(c * TOK, (c + 1) * TOK)
        m = dict(shared)
        m["xT"] = np.ascontiguousarray(x[b, sl][perm].T)
        for i in range(4):
            m["pT%d" % i] = np.ascontiguousarray(p[i, b, sl][perm].T)
        s = np.zeros((128, 16), np.float32)
        for k in range(3):
            s[:, k] = 0.0 if k == c - 1 else NEG
            s[:, 7 + k] = 1.0 if k == c - 1 else 0.0
        for k in range(4):
            s[:, 3 + k] = 0.0 if k < c else NEG
        m["sel"] = s
        maps.append(m)
    return maps


def assemble(results, key="outT"):
    perm = _r4perm()
    out = np.empty((2, 4 * TOK, D), np.float32)
    for core in range(NCORES):
        b, c = divmod(core, 4)
        o = np.asarray(results[core][key]).astype(np.float32).T
        blk = np.empty_like(o)
        blk[perm] = o
        out[b, c * TOK:(c + 1) * TOK] = blk
    return out


_NC_CACHE = {}


def kernel(**inputs):
    if "net" not in _NC_CACHE:
        _NC_CACHE["net"] = build_net()
    nc = _NC_CACHE["net"]
    maps = [{k: m[k] for k in nc._in_names} for m in make_inputs(inputs)]
    res = run_bass_kernel_spmd(nc, maps, core_ids=list(range(NCORES)))
    return assemble(res.results)
```

```python
import contextlib
import numpy as np
import ml_dtypes
import concourse.bass as bass
import concourse.mybir as mybir
from concourse.bass_utils import run_bass_kernel_spmd

F32 = mybir.dt.float32
BF16 = mybir.dt.bfloat16
AF = mybir.ActivationFunctionType
ALU = mybir.AluOpType

D = 2048
KC = 16
TOK = 2048
HALF = 1024
TT = 512
DFF = 5632
FS = 44
PLE = 256
NCORES = 8
NEG = -30000.0
RMS_EPS = 1e-6
SUBLN_EPS = 1e-5


class Buf:
    __slots__ = ("name", "w", "r", "sem", "cnt", "t")

    def __init__(self, name, t=None):
        self.name = name
        self.w = {}
        self.r = {}
        self.sem = None
        self.cnt = 0
        self.t = t


class Eng:
    def __init__(self, name):
        self.name = name
        self.ops = []
        self.sem = None
        self.cnt = 0
        self.waited = {}


class Prog:
    def __init__(self):
        self.nc = bass.Bass("TRN2", target_bir_lowering=False)
        self.es = contextlib.ExitStack()
        self.engs = {n: Eng(n) for n in ("tensor", "vector", "scalar", "gpsimd", "sync")}
        self.semh = {}
        self.nsem = 0
        for n in ("tensor", "vector", "scalar"):
            self.engs[n].sem = self.new_sem("e_" + n)
        self.out_bufs = []
        self.in_names = []
        self.nbuf = 0

    def new_sem(self, name=None):
        self.nsem += 1
        h = self.es.enter_context(self.nc.semaphore(name or ("s%d" % self.nsem)))
        self.semh[self.nsem] = h
        return self.nsem

    def sb(self, name, shape, dtype):
        t = self.es.enter_context(self.nc.sbuf_tensor("sb_" + name, list(shape), dtype))
        return Buf(name, t)

    def ps(self, name):
        t = self.es.enter_context(self.nc.psum_tensor("ps_" + name, [128, 512], F32))
        return Buf(name, t)

    def dram_in(self, name, shape, dtype=F32):
        self.in_names.append(name)
        return self.nc.dram_tensor(name, list(shape), dtype, kind="ExternalInput").ap()

    def dram_out(self, name, shape, dtype=F32):
        return self.nc.dram_tensor(name, list(shape), dtype, kind="ExternalOutput").ap()

    def dram_tmp(self, name, shape, dtype=F32):
        return self.nc.dram_tensor(name, list(shape), dtype).ap()

    def vbuf(self, name):
        self.nbuf += 1
        return Buf("%s_%d" % (name, self.nbuf))

    def _collect(self, E, reads, writes, extra=()):
        waits = {}

        def need(sem, val):
            if E.waited.get(sem, 0) >= val:
                return
            if waits.get(sem, 0) < val:
                waits[sem] = val

        for b in reads:
            for sem, val in b.w.items():
                need(sem, val)
        for b in writes:
            for sem, val in b.w.items():
                need(sem, val)
            for sem, val in b.r.items():
                need(sem, val)
        for sem, val in extra:
            need(sem, val)
        return waits

    @staticmethod
    def _mark(ev, reads, writes):
        sem, val = ev
        for b in reads:
            if b.r.get(sem, 0) < val:
                b.r[sem] = val
        for b in writes:
            if b.w.get(sem, 0) < val:
                b.w[sem] = val

    def op(self, eng, fn, reads=(), writes=(), inc=True):
        E = self.engs[eng]
        waits = self._collect(E, reads, writes)
        if eng == "tensor":
            waits.pop(E.sem, None)
        for sem, val in waits.items():
            E.waited[sem] = val
        if inc:
            E.cnt += 1
            ev = (E.sem, E.cnt)
        else:
            ev = (E.sem, E.cnt + 1)
        E.ops.append((sorted(waits.items()), fn, (E.sem, 1) if inc else None))
        self._mark(ev, reads, writes)

    def dma(self, eng, out, in_, owner, reads=(), writes=(), **kw):
        E = self.engs[eng]
        if owner.sem is None:
            owner.sem = self.new_sem("d_" + owner.name)
        waits = self._collect(E, reads, writes, extra=[(owner.sem, owner.cnt)] if owner.cnt else [])
        for sem, val in waits.items():
            E.waited[sem] = val
        owner.cnt += 16
        ev = (owner.sem, owner.cnt)
        E.ops.append((sorted(waits.items()), lambda e, o=out, i=in_: e.dma_start(out=o, in_=i, **kw), (owner.sem, 16)))
        self._mark(ev, reads, writes)

    def barrier_wait(self, eng, bufs):
        E = self.engs[eng]
        waits = self._collect(E, [], bufs)
        for sem, val in waits.items():
            E.waited[sem] = val
        if waits:
            E.ops.append((sorted(waits.items()), None, None))

    def check(self):
        sems = {}
        pos = {n: 0 for n in self.engs}
        progress = True
        while progress:
            progress = False
            for n, E in self.engs.items():
                while pos[n] < len(E.ops):
                    waits, fn, inc = E.ops[pos[n]]
                    if any(sems.get(s, 0) < v for s, v in waits):
                        break
                    if inc is not None:
                        sems[inc[0]] = sems.get(inc[0], 0) + inc[1]
                    pos[n] += 1
                    progress = True
        stuck = {n: (pos[n], len(E.ops)) for n, E in self.engs.items() if pos[n] < len(E.ops)}
        if stuck:
            msg = []
            for n in stuck:
                waits, fn, inc = self.engs[n].ops[pos[n]]
                msg.append("%s@%d waits %s have %s" % (n, pos[n], waits, [(s, sems.get(s, 0)) for s, _ in waits]))
            raise RuntimeError("DEADLOCK: " + "; ".join(msg))
        return {n: len(E.ops) for n, E in self.engs.items()}

    def emit(self):
        self.barrier_wait("sync", self.out_bufs)
        self.check()
        nc = self.nc
        semh = self.semh

        def replay(e, E):
            for waits, fn, inc in E.ops:
                for sem, val in waits:
                    e.wait_ge(semh[sem], val)
                if fn is None:
                    continue
                inst = fn(e)
                if inc is not None:
                    inst.then_inc(semh[inc[0]], inc[1])

        with nc.Block() as block:
            @block.tensor
            def _(e):
                replay(e, self.engs["tensor"])

            @block.vector
            def _(e):
                replay(e, self.engs["vector"])

            @block.scalar
            def _(e):
                replay(e, self.engs["scalar"])

            @block.gpsimd
            def _(e):
                replay(e, self.engs["gpsimd"])

            @block.sync
            def _(e):
                replay(e, self.engs["sync"])
        self.es.close()
        return nc


class Rows:
    def __init__(self, P, vecs_ap, nvec):
        self.P = P
        self.act = P.sb("act", [128, FS * HALF], BF16)
        self.xn = P.sb("xn", [128, KC * HALF], BF16)
        self.win = [P.sb("win%d" % i, [128, KC * 128], BF16) for i in range(4)]
        self.wout = [P.sb("wout%d" % i, [128, FS * 128], BF16) for i in range(2)]
        self.ht = [P.sb("ht%d" % i, [128, TT], F32) for i in range(6)]
        self.tmp = [P.sb("tmp%d" % i, [128, TT], F32) for i in range(3)]
        self.sq = [P.sb("sq%d" % i, [128, TT], BF16) for i in range(2)]
        self.rstd = [P.sb("rstd%d" % i, [128, TT], F32) for i in range(2)]
        self.epsc = P.sb("epsc", [128, 2], F32)
        self.rtmp = P.sb("rtmp", [128, TT], F32)
        self.ob = [P.sb("ob%d" % i, [128, TT], BF16) for i in range(4)]
        self.ones = P.sb("ones", [128, 128], BF16)
        self.vecs = P.sb("vecs", [128, nvec], F32)
        self.pt = P.sb("pt", [128, 2 * HALF], BF16)
        self.wp = [P.sb("wp%d" % i, [128, 2 * 128], BF16) for i in range(2)]
        self.banks = [P.ps("bank%d" % i) for i in range(8)]
        self.cnt = {}
        P.op("vector", lambda e: e.memset(self.ones.t[:], 1.0), writes=[self.ones])
        P.op("vector", lambda e: e.memset(self.epsc.t[:, 0:1], RMS_EPS), writes=[self.epsc])
        P.dma("sync", self.vecs.t[:], vecs_ap, self.vecs, writes=[self.vecs])

    def rr(self, key, lst):
        i = self.cnt.get(key, 0)
        self.cnt[key] = i + 1
        return lst[i % len(lst)]

    def xs(self, k, j):
        return self.xn.t[:, k * HALF + j * TT:k * HALF + (j + 1) * TT]

    def stats(self, hsrc, hsrc_bufs, n0, j, eps, dim=D):
        P = self.P
        hs = hsrc.rearrange("(k p) n -> k p n", p=128)
        bank = self.banks[6 + j]
        for k in range(KC):
            ht = self.rr("ht", self.ht)
            P.dma("gpsimd", ht.t[:], hs[k, :, n0:n0 + TT], ht, reads=[hsrc_bufs[n0 // TT]], writes=[ht])
            sq = self.rr("sq", self.sq)
            P.op("scalar", lambda e, s=sq, t=ht: e.activation(out=s.t[:], in_=t.t[:], func=AF.Square),
                 reads=[ht], writes=[sq])
            P.op("tensor", lambda e, s=sq, k=k: e.matmul(bank.t[:], self.ones.t[:], s.t[:], start=(k == 0), stop=(k == KC - 1)),
                 reads=[sq, self.ones], writes=[bank], inc=True)
        P.op("scalar", lambda e: e.activation(out=self.rtmp.t[:], in_=bank.t[:], func=AF.Sqrt, bias=self.epsc.t[:, 0:1], scale=1.0 / dim),
             reads=[bank, self.epsc], writes=[self.rtmp])
        P.op("vector", lambda e: e.reciprocal(out=self.rstd[j].t[:], in_=self.rtmp.t[:]), reads=[self.rtmp], writes=[self.rstd[j]])

    def norm(self, hsrc, hsrc_bufs, half, gcol, eps=RMS_EPS):
        P = self.P
        hs = hsrc.rearrange("(k p) n -> k p n", p=128)
        for j in range(2):
            self.stats(hsrc, hsrc_bufs, half * HALF + j * TT, j, eps)
        for j in range(2):
            n0 = half * HALF + j * TT
            for k in range(KC):
                ht = self.rr("ht", self.ht)
                P.dma("gpsimd", ht.t[:], hs[k, :, n0:n0 + TT], ht, reads=[hsrc_bufs[n0 // TT]], writes=[ht])
                P.op("vector", lambda e, k=k, j=j, t=ht: e.scalar_tensor_tensor(
                    out=self.xs(k, j), in0=t.t[:], scalar=self.vecs.t[:, gcol + k:gcol + k + 1], in1=self.rstd[j].t[:],
                    op0=ALU.mult, op1=ALU.mult), reads=[ht, self.vecs, self.rstd[j]], writes=[self.xn])

    def load_w(self, slot, w_ap, c0, ncol, kchunks, r0=0):
        src = w_ap[r0:r0 + kchunks * 128, :].rearrange("(k p) c -> p k c", p=128)[:, :, c0:c0 + ncol]
        dst = slot.t[:, 0:kchunks * ncol].rearrange("p (k c) -> p k c", k=kchunks)
        self.P.dma("gpsimd", dst, src, slot, writes=[slot])

    def mm_group(self, bank, lhs_fn, rhs_fn, nk, reads, ncol=TT):
        for k in range(nk):
            self.P.op("tensor", lambda e, k=k: e.matmul(bank.t[:, 0:ncol], lhs_fn(k), rhs_fn(k), start=(k == 0), stop=(k == nk - 1)),
                      reads=reads, writes=[bank], inc=(k == nk - 1))

    def add_store(self, bank, scale, hsrc, hsrc_bufs, hdst, hdst_bufs, oc, n0, mul_by=None):
        P = self.P
        hs = hsrc.rearrange("(k p) n -> k p n", p=128)
        hd = hdst.rearrange("(k p) n -> k p n", p=128)
        ht = self.rr("ht", self.ht)
        P.dma("gpsimd", ht.t[:], hs[oc, :, n0:n0 + TT], ht, reads=[hsrc_bufs[n0 // TT]], writes=[ht])
        if mul_by is None:
            P.op("vector", lambda e, t=ht, b=bank: e.scalar_tensor_tensor(
                out=t.t[:], in0=b.t[:], scalar=scale, in1=t.t[:], op0=ALU.mult, op1=ALU.add), reads=[bank, ht], writes=[ht])
        else:
            P.op("vector", lambda e, m=mul_by, b=bank: e.tensor_tensor(out=m.t[:], in0=m.t[:], in1=b.t[:], op=ALU.mult),
                 reads=[bank, mul_by], writes=[mul_by])
            P.op("vector", lambda e, t=ht, m=mul_by: e.tensor_tensor(out=t.t[:], in0=t.t[:], in1=m.t[:], op=ALU.add),
                 reads=[mul_by, ht], writes=[ht])
        P.dma("sync", hd[oc, :, n0:n0 + TT], ht.t[:], ht, reads=[ht], writes=[hdst_bufs[n0 // TT]])

    def ffn(self, hsrc, hsrc_bufs, hdst, hdst_bufs, half, w_in, w_out):
        P = self.P
        for s in range(FS):
            wg = self.rr("win", self.win)
            wu = self.rr("win", self.win)
            self.load_w(wg, w_in, s * 128, 128, KC)
            self.load_w(wu, w_in, DFF + s * 128, 128, KC)
            for j in range(2):
                ba = self.rr("bA", self.banks[0:2])
                bb = self.rr("bB", self.banks[2:4])
                self.mm_group(ba, lambda k, w=wg: w.t[:, k * 128:(k + 1) * 128], lambda k, j=j: self.xs(k, j), KC, [wg, self.xn])
                self.mm_group(bb, lambda k, w=wu: w.t[:, k * 128:(k + 1) * 128], lambda k, j=j: self.xs(k, j), KC, [wu, self.xn])
                tmp = self.rr("tmp", self.tmp)
                P.op("scalar", lambda e, t=tmp, b=ba: e.activation(out=t.t[:], in_=b.t[:], func=AF.Silu), reads=[ba], writes=[tmp])
                P.op("vector", lambda e, t=tmp, b=bb, s=s, j=j: e.tensor_tensor(
                    out=self.act.t[:, s * HALF + j * TT:s * HALF + (j + 1) * TT], in0=t.t[:], in1=b.t[:], op=ALU.mult),
                    reads=[tmp, bb], writes=[self.act])
        for oc in range(KC):
            wo = self.rr("wout", self.wout)
            self.load_w(wo, w_out, oc * 128, 128, FS)
            for j in range(2):
                n0 = half * HALF + j * TT
                bo = self.rr("bO", self.banks[4:6])
                self.mm_group(bo, lambda k, w=wo: w.t[:, k * 128:(k + 1) * 128],
                              lambda k, j=j: self.act.t[:, k * HALF + j * TT:k * HALF + (j + 1) * TT], FS, [wo, self.act])
                self.add_store(bo, 0.5, hsrc, hsrc_bufs, hdst, hdst_bufs, oc, n0)

    def ple(self, hsrc, hsrc_bufs, hdst, hdst_bufs, half, w_gate, bcol, pT, w_proj):
        P = self.P
        ptv = pT.rearrange("(k p) n -> p k n", p=128)[:, :, half * HALF:(half + 1) * HALF]
        P.dma("gpsimd", self.pt.t[:].rearrange("p (k n) -> p k n", k=2), ptv, self.pt, writes=[self.pt])
        for oc in range(KC):
            wg = self.rr("win", self.win)
            self.load_w(wg, w_gate, oc * 128, 128, KC)
            wp = self.rr("wp", self.wp)
            self.load_w(wp, w_proj, oc * 128, 128, 2)
            for j in range(2):
                n0 = half * HALF + j * TT
                ba = self.rr("bA", self.banks[0:2])
                bb = self.rr("bB", self.banks[2:4])
                self.mm_group(ba, lambda k, w=wg: w.t[:, k * 128:(k + 1) * 128], lambda k, j=j: self.xs(k, j), KC, [wg, self.xn])
                self.mm_group(bb, lambda k, w=wp: w.t[:, k * 128:(k + 1) * 128],
                              lambda k, j=j: self.pt.t[:, k * HALF + j * TT:k * HALF + (j + 1) * TT], 2, [wp, self.pt])
                tmp = self.rr("tmp", self.tmp)
                P.op("scalar", lambda e, t=tmp, b=ba, oc=oc: e.activation(
                    out=t.t[:], in_=b.t[:], func=AF.Sigmoid, bias=self.vecs.t[:, bcol + oc:bcol + oc + 1], scale=1.0),
                    reads=[ba, self.vecs], writes=[tmp])
                self.add_store(bb, 1.0, hsrc, hsrc_bufs, hdst, hdst_bufs, oc, n0, mul_by=tmp)

    def proj_fm(self, half, w_ap, c0, nchunks, dst_fn, dst_bufs, evac="copy"):
        P = self.P
        for c in range(nchunks):
            w = self.rr("win", self.win)
            self.load_w(w, w_ap, c0 + c * 128, 128, KC)
            for j in range(2):
                b = self.rr("bA", self.banks[0:4])
                self.mm_group(b, lambda k, w=w: w.t[:, k * 128:(k + 1) * 128], lambda k, j=j: self.xs(k, j), KC, [w, self.xn])
                ob = self.rr("ob", self.ob)
                eng = self.rr("evac", ["scalar", "vector"])
                if eng == "scalar":
                    P.op("scalar", lambda e, o=ob, b=b: e.activation(out=o.t[:], in_=b.t[:], func=AF.Copy), reads=[b], writes=[ob])
                else:
                    P.op("vector", lambda e, o=ob, b=b: e.tensor_copy(out=o.t[:], in_=b.t[:]), reads=[b], writes=[ob])
                P.dma("sync", dst_fn(c, j), ob.t[:], ob, reads=[ob], writes=dst_bufs)

    def oproj(self, hsrc, hsrc_bufs, hdst, hdst_bufs, half, w_o):
        for oc in range(KC):
            w = self.rr("win", self.win)
            self.load_w(w, w_o, oc * 128, 128, KC)
            for j in range(2):
                n0 = half * HALF + j * TT
                b = self.rr("bO", self.banks[4:6])
                self.mm_group(b, lambda k, w=w: w.t[:, k * 128:(k + 1) * 128], lambda k, j=j: self.xs(k, j), KC, [w, self.xn])
                self.add_store(b, 1.0, hsrc, hsrc_bufs, hdst, hdst_bufs, oc, n0)

    def final_norm(self, hsrc, hsrc_bufs, out_ap, out_bufs, half, gcol):
        P = self.P
        hs = hsrc.rearrange("(k p) n -> k p n", p=128)
        od = out_ap.rearrange("(k p) n -> k p n", p=128)
        for j in range(2):
            self.stats(hsrc, hsrc_bufs, half * HALF + j * TT, j, RMS_EPS)
        for j in range(2):
            n0 = half * HALF + j * TT
            for k in range(KC):
                ht = self.rr("ht", self.ht)
                P.dma("gpsimd", ht.t[:], hs[k, :, n0:n0 + TT], ht, reads=[hsrc_bufs[n0 // TT]], writes=[ht])
                P.op("vector", lambda e, k=k, t=ht, j=j: e.scalar_tensor_tensor(
                    out=t.t[:], in0=t.t[:], scalar=self.vecs.t[:, gcol + k:gcol + k + 1], in1=self.rstd[j].t[:],
                    op0=ALU.mult, op1=ALU.mult), reads=[ht, self.vecs, self.rstd[j]], writes=[ht])
                P.dma("sync", od[k, :, n0:n0 + TT], ht.t[:], ht, reads=[ht], writes=out_bufs)

    def evac_bf16(self, bank, ncol=TT):
        P = self.P
        ob = self.rr("ob", self.ob)
        eng = self.rr("evac", ["scalar", "vector"])
        if eng == "scalar":
            P.op("scalar", lambda e, o=ob, b=bank: e.activation(out=o.t[:, 0:ncol], in_=b.t[:, 0:ncol], func=AF.Copy), reads=[bank], writes=[ob])
        else:
            P.op("vector", lambda e, o=ob, b=bank: e.tensor_copy(out=o.t[:, 0:ncol], in_=b.t[:, 0:ncol]), reads=[bank], writes=[ob])
        return ob

    def proj_tm(self, half, w_ap, c0, ncol, dst_fn, dst_bufs):
        P = self.P
        w = self.rr("wout", self.wout)
        self.load_w(w, w_ap, c0, ncol, KC)
        for tb in range(HALF // 128):
            b = self.rr("bA", self.banks[0:4])
            self.mm_group(b, lambda k, tb=tb: self.xn.t[:, k * HALF + tb * 128:k * HALF + (tb + 1) * 128],
                          lambda k, w=w: w.t[:, k * ncol:(k + 1) * ncol], KC, [w, self.xn], ncol=ncol)
            ob = self.evac_bf16(b, ncol)
            P.dma("sync", dst_fn(tb), ob.t[:, 0:ncol], ob, reads=[ob], writes=dst_bufs)

    def load_xn(self, src, src_bufs, half):
        v = src.rearrange("(k p) n -> p k n", p=128)[:, :, half * HALF:(half + 1) * HALF]
        self.P.dma("sync", self.xn.t[:].rearrange("p (k n) -> p k n", k=KC), v, self.xn, reads=src_bufs, writes=[self.xn])

    def proj_conv_in(self, half, w_ap, zT, bgT, dst_bufs):
        P = self.P
        zd = zT.rearrange("(k p) n -> k p n", p=128)
        bd = bgT.rearrange("(k p) n -> k p n", p=128)
        for c in range(KC):
            wb = self.rr("win", self.win)
            wc = self.rr("win", self.win)
            wu = self.rr("win", self.win)
            self.load_w(wb, w_ap, c * 128, 128, KC)
            self.load_w(wc, w_ap, D + c * 128, 128, KC)
            self.load_w(wu, w_ap, 2 * D + c * 128, 128, KC)
            for j in range(2):
                n0 = half * HALF + j * TT
                b0 = self.rr("bA", self.banks[0:2])
                b1 = self.rr("bB", self.banks[2:4])
                b2 = self.rr("bO", self.banks[4:6])
                for (b, w) in ((b0, wb), (b1, wc), (b2, wu)):
                    self.mm_group(b, lambda k, w=w: w.t[:, k * 128:(k + 1) * 128], lambda k, j=j: self.xs(k, j), KC, [w, self.xn])
                t0 = self.rr("ht", self.ht)
                P.op("scalar", lambda e, t=t0, b=b0: e.activation(out=t.t[:], in_=b.t[:], func=AF.Copy), reads=[b0], writes=[t0])
                P.dma("sync", bd[c, :, n0:n0 + TT], t0.t[:], t0, reads=[t0], writes=dst_bufs)
                t1 = self.rr("ht", self.ht)
                P.op("scalar", lambda e, t=t1, b=b1: e.activation(out=t.t[:], in_=b.t[:], func=AF.Copy), reads=[b1], writes=[t1])
                P.op("vector", lambda e, t=t1, b=b2: e.tensor_tensor(out=t.t[:], in0=t.t[:], in1=b.t[:], op=ALU.mult), reads=[b2, t1], writes=[t1])
                P.dma("sync", zd[c, :, n0:n0 + TT], t1.t[:], t1, reads=[t1], writes=dst_bufs)

    def init_attn(self, masks_ap, nmask, sel_ap, ident_ap):
        P = self.P
        self.masks = P.sb("masks", [128, nmask * 128], BF16)
        self.ident = P.sb("ident", [128, 128], BF16)
        self.sel = P.sb("sel", [128, 16], F32)
        P.dma("gpsimd", self.masks.t[:], masks_ap, self.masks, writes=[self.masks])
        P.dma("gpsimd", self.ident.t[:], ident_ap, self.ident, writes=[self.ident])
        P.dma("sync", self.sel.t[:], sel_ap, self.sel, writes=[self.sel])
        self.pts = [P.vbuf("pt") for _ in range(3)]
        self.laccs = [P.sb("lacc%d" % i, [128, TT], F32) for i in range(2)]
        self.PT0 = FS * HALF - 3 * TT

    def pt_ap(self, i):
        return self.act.t[:, self.PT0 + i * TT:self.PT0 + (i + 1) * TT]

    def mask_ap(self, mi):
        return self.masks.t[:, mi * 128:(mi + 1) * 128]

    def attn_run(self, blocks, q_ap, qbuf, o_banks, l_bank, scale):
        P = self.P
        n = len(blocks)
        st = [None] * n

        def stage_s(i):
            b = blocks[i]
            c0 = b.get("c0", 0)
            ms = b.get("masks", ())
            sb = self.rr("bS", self.banks[0:3])
            P.op("tensor", lambda e, b=b, sb=sb, c0=c0, ms=ms: e.matmul(sb.t[:, c0:TT], b["K"], q_ap[:, c0:TT], start=True, stop=(len(ms) == 0)),
                 reads=[qbuf] + b["bufs"], writes=[sb], inc=(len(ms) == 0))
            for mi, (map_, mc0, w) in enumerate(ms):
                last = mi == len(ms) - 1
                P.op("tensor", lambda e, sb=sb, map_=map_, mc0=mc0, w=w, last=last: e.matmul(
                    sb.t[:, mc0:mc0 + w], self.ident.t[:], map_, start=False, stop=last),
                    reads=[self.ident, self.masks], writes=[sb], inc=last)
            pi = self.cnt.get("pti", 0)
            self.cnt["pti"] = pi + 1
            pt = self.pts[pi % 3]
            pap = self.pt_ap(pi % 3)
            bias = b.get("bias")
            if bias is None:
                bias = self.sel.t[:, 10:11]
            P.op("scalar", lambda e, sb=sb, pap=pap, c0=c0, bias=bias: e.activation(
                out=pap[:, c0:TT], in_=sb.t[:, c0:TT], func=AF.Exp, bias=bias, scale=scale),
                reads=[sb, self.sel], writes=[pt])
            st[i] = (pt, pap, c0)

        lacc = self.rr("lacc", self.laccs)

        def stage_pv(i):
            pt, pap, c0 = st[i]
            b = blocks[i]
            first, last = (i == 0), (i == n - 1)
            for oi, ob in enumerate(o_banks):
                P.op("tensor", lambda e, ob=ob, v=b["V"][oi], pap=pap, c0=c0, first=first, last=last: e.matmul(
                    ob.t[:, c0:TT], v, pap[:, c0:TT], start=first, stop=last),
                    reads=[pt] + b["bufs"], writes=[ob], inc=(oi == len(o_banks) - 1))
            if first:
                P.op("vector", lambda e, pap=pap: e.tensor_copy(out=lacc.t[:], in_=pap), reads=[pt], writes=[lacc])
            else:
                P.op("vector", lambda e, pap=pap, c0=c0: e.tensor_tensor(out=lacc.t[:, c0:TT], in0=lacc.t[:, c0:TT], in1=pap[:, c0:TT], op=ALU.add),
                     reads=[pt, lacc], writes=[lacc])

        stage_s(0)
        if n > 1:
            stage_s(1)
        for i in range(n):
            stage_pv(i)
            if i + 2 < n:
                stage_s(i + 2)
        P.op("tensor", lambda e: e.matmul(l_bank.t[:], self.onesf.t[:], lacc.t[:], start=True, stop=True),
             reads=[lacc, self.onesf], writes=[l_bank])

    def attn_A(self, qA, kloc, vloc, kall, vall, own_bufs, src_bufs, attn, attn_bufs):
        P = self.P
        scale = 128.0 ** -0.5
        A = self.act.t
        QO, KO, VO, PO = 0, 8192, 14336, 20480
        qb, kb, vb = P.vbuf("qA"), P.vbuf("kA"), P.vbuf("vA")
        kpb = [P.vbuf("kpA") for _ in range(3)]
        vpb = [P.vbuf("vpA") for _ in range(3)]
        allb = [qb, kb, vb] + kpb + vpb + self.pts
        for b in allb:
            b.r.update(self.act.r); b.w.update(self.act.w)
        osets = [(self.banks[3], self.banks[4]), (self.banks[5], self.banks[6])]
        attn_v = attn.rearrange("(h d) n -> d h n", d=128)
        for kvh in range(4):
            qsrc = qA.rearrange("(h d) (nb q) -> d nb h q", d=128, q=128)[:, :, kvh * 4:(kvh + 1) * 4, :]
            qdst = A[:, QO:QO + 8192].rearrange("p (nb h q) -> p nb h q", nb=16, h=4)
            for hh in range(4):
                P.dma("sync", qdst[:, :, hh, :], qsrc[:, :, hh, :], qb, reads=own_bufs, writes=[qb])
            ksrc = kloc.rearrange("(g v d) n -> d g v n", g=3, v=4)[:, :, kvh, :]
            P.dma("sync", A[:, KO:KO + 6144].rearrange("p (g n) -> p g n", g=3), ksrc, kb, reads=own_bufs, writes=[kb])
            vsrc = vloc.rearrange("(b p) (g v d) -> p g b v d", p=128, g=3, v=4)[:, :, :, kvh, :]
            vdst = A[:, VO:VO + 6144].rearrange("p (g b d) -> p g b d", g=3, b=16)
            for g in range(3):
                P.dma("sync", vdst[:, g], vsrc[:, g], vb, reads=own_bufs, writes=[vb])
            for s in range(3):
                kbase = PO + s * 6144
                vbase = kbase + 3072
                vv = vall.rearrange("(r ih s jl p) (g v d) -> s p g v r ih jl d", r=4, ih=2, s=4, jl=2, p=128, g=3, v=4)[s]
                for g in range(3):
                    r0 = (g * 4 + kvh) * 128
                    base = ((r0 // 256) * 4 + s) * 256 + (r0 % 256)
                    ka = kall[base:base + 128, :].rearrange("d (r j) -> d r j", r=4)
                    if g < 2:
                        P.dma("sync", A[:, kbase + g * 512:kbase + (g + 1) * 512].rearrange("p (r j) -> p r j", r=4),
                              ka[:, :, 384:512], kpb[s], reads=src_bufs, writes=[kpb[s]])
                        P.dma("sync", A[:, vbase + g * 512:vbase + (g + 1) * 512].rearrange("p (r d) -> p r d", r=4),
                              vv[:, g, kvh, :, 1, 1, :], vpb[s], reads=src_bufs, writes=[vpb[s]])
                    else:
                        P.dma("sync", A[:, kbase + 1024:kbase + 3072].rearrange("p (r j) -> p r j", r=4),
                              ka, kpb[s], reads=src_bufs, writes=[kpb[s]])
                        for r in range(4):
                            for ih in range(2):
                                o_ = vbase + 1024 + r * 512 + ih * 256
                                P.dma("sync", A[:, o_:o_ + 256].rearrange("p (jl d) -> p jl d", jl=2),
                                      vv[:, g, kvh, r, ih, :, :], vpb[s], reads=src_bufs, writes=[vpb[s]])
            for r4 in range(4):
                for jb in range(4):
                    blocks = []

                    def add(g, rk, jbk, mi):
                        m4 = [(self.mask_ap(mi), hh * 128, 128) for hh in range(4)]
                        if jbk >= 0:
                            blk = rk * 4 + jbk
                            blocks.append(dict(K=A[:, KO + g * 2048 + blk * 128:KO + g * 2048 + (blk + 1) * 128],
                                               V=[A[:, VO + (g * 16 + blk) * 128:VO + (g * 16 + blk + 1) * 128]],
                                               bufs=[kb, vb], masks=m4))
                        else:
                            jp = jbk + 4
                            for s in range(3):
                                kbase = PO + s * 6144
                                vbase = kbase + 3072
                                if g < 2:
                                    o = g * 512 + rk * 128
                                else:
                                    o = 1024 + (rk * 4 + jp) * 128
                                blocks.append(dict(K=A[:, kbase + o:kbase + o + 128], V=[A[:, vbase + o:vbase + o + 128]],
                                                   bufs=[kpb[s], vpb[s]], masks=m4, bias=self.sel.t[:, s:s + 1]))

                    add(1, r4, jb, 0)
                    add(1, r4, jb - 1, 1)
                    for dj in range(5):
                        add(2, r4, jb - dj, (2, 3, 3, 3, 4)[dj])
                    for rk in range(4):
                        for dj in range(2):
                            add(0, rk, jb - dj, 5 + (r4 - rk + 3) * 2 + dj)
                    nb = r4 * 4 + jb
                    ob, lb = self.rr("oset", osets)
                    self.attn_run(blocks, A[:, QO + nb * 512:QO + (nb + 1) * 512], qb, [ob], lb, scale)
                    rl = self.rr("tmp", self.tmp)
                    P.op("vector", lambda e, rl=rl, lb=lb: e.reciprocal(out=rl.t[:], in_=lb.t[:]), reads=[lb], writes=[rl])
                    o = self.rr("ob", self.ob)
                    P.op("vector", lambda e, o=o, ob=ob, rl=rl: e.tensor_tensor(out=o.t[:], in0=ob.t[:], in1=rl.t[:], op=ALU.mult),
                         reads=[ob, rl], writes=[o])
                    P.dma("sync", attn_v[:, kvh * 4:(kvh + 1) * 4, nb * 128:(nb + 1) * 128],
                          o.t[:].rearrange("p (h q) -> p h q", h=4), o, reads=[o], writes=attn_bufs)
        for b in allb:
            for k, v in b.r.items():
                self.act.r[k] = max(self.act.r.get(k, 0), v)
            for k, v in b.w.items():
                self.act.w[k] = max(self.act.w.get(k, 0), v)

    def lam_setup(self, lam_ap, lam_init):
        P = self.P
        self.lamt = P.sb("lamt", [128, 8], F32)
        self.onesf = P.sb("onesf", [128, 128], F32)
        P.op("vector", lambda e: e.memset(self.onesf.t[:], 1.0), writes=[self.onesf])
        P.dma("sync", self.lamt.t[:, 0:4], lam_ap, self.lamt, writes=[self.lamt])
        L = self.lamt
        P.op("vector", lambda e: e.tensor_tensor(out=L.t[:, 4:5], in0=L.t[:, 0:1], in1=L.t[:, 1:2], op=ALU.mult), reads=[L], writes=[L])
        P.op("vector", lambda e: e.tensor_tensor(out=L.t[:, 5:6], in0=L.t[:, 2:3], in1=L.t[:, 3:4], op=ALU.mult), reads=[L], writes=[L])
        bank = self.banks[7]
        P.op("tensor", lambda e: e.matmul(bank.t[:, 0:2], self.onesf.t[:], L.t[:, 4:6], start=True, stop=True),
             reads=[L, self.onesf], writes=[bank])
        P.op("scalar", lambda e: e.activation(out=L.t[:, 6:8], in_=bank.t[:, 0:2], func=AF.Exp), reads=[bank], writes=[L])
        P.op("vector", lambda e: e.scalar_tensor_tensor(out=L.t[:, 4:5], in0=L.t[:, 7:8], scalar=-lam_init, in1=L.t[:, 6:7],
                                                        op0=ALU.add, op1=ALU.subtract), reads=[L], writes=[L])

    def attn_B(self, qB, kloc, vloc, kall, vall, own_bufs, src_bufs, attn, attn_bufs, gcol, lam_init):
        P = self.P
        scale = 128.0 ** -0.5
        A = self.act.t
        X = self.xn.t
        XF = self.xn.t.bitcast(F32)
        QO, KA, KO, VA = 0, 4096, 20480, 24576
        qb, kob, vob = P.vbuf("qB"), P.vbuf("koB"), P.vbuf("voB")
        kab = [[P.vbuf("kaB") for _ in range(3)] for _ in range(2)]
        vab = [P.vbuf("vaB") for _ in range(3)]
        ocb = [P.vbuf("oc0"), P.vbuf("oc1")]
        dfb = P.vbuf("diff")
        allb = [qb, kob, vob, dfb] + kab[0] + kab[1] + vab + ocb + self.pts
        for b in allb:
            b.r.update(self.act.r); b.w.update(self.act.w)
            b.r.update(self.xn.r); b.w.update(self.xn.w)
        oc_ap = [XF[:, 2048:3072], XF[:, 3072:4096]]
        df_ap = XF[:, 4096:5120]
        o_banks = [self.banks[3], self.banks[4]]
        l_bank = self.banks[5]
        sbank = self.banks[6]
        one_m = 1.0 - lam_init
        P.op("vector", lambda e: e.memset(self.epsc.t[:, 1:2], SUBLN_EPS / (one_m * one_m)), reads=[], writes=[self.epsc])
        attn_v = attn.rearrange("(h c d) n -> h c d n", c=2, d=128)
        for h in range(8):
            qs = qB.rearrange("(h c d) n -> h d c n", c=2, d=128)[h]
            P.dma("sync", A[:, QO:QO + 4096].rearrange("p (c n) -> p c n", c=2), qs, qb, reads=own_bufs, writes=[qb])
            va = vall.rearrange("(i s wl p) (h e) -> h s p i wl e", s=4, wl=2, p=128, e=256)[h]
            for s in range(3):
                for c in range(2):
                    ka = kall.rearrange("(h s c d) n -> h c s d n", s=4, c=2, d=128)[h, c, s]
                    P.dma("sync", A[:, KA + c * 8192 + s * 2048:KA + c * 8192 + (s + 1) * 2048], ka, kab[c][s], reads=src_bufs, writes=[kab[c][s]])
                vd = A[:, VA + s * 4096:VA + (s + 1) * 4096].rearrange("p (i wl e) -> p i wl e", wl=2, e=256)
                for wl in range(2):
                    P.dma("sync", vd[:, :, wl, :], va[s][:, :, wl, :], vab[s], reads=src_bufs, writes=[vab[s]])
            ks = kloc.rearrange("(h c d) n -> h d c n", c=2, d=128)[h]
            P.dma("sync", A[:, KO:KO + 4096].rearrange("p (c n) -> p c n", c=2), ks, kob, reads=own_bufs, writes=[kob])
            vo = vloc.rearrange("(b p) (h e) -> h p b e", p=128, e=256)[h]
            P.dma("sync", X[:, 0:4096].rearrange("p (b e) -> p b e", e=256), vo, vob, reads=own_bufs, writes=[vob])
            for r4 in range(4):
                for c in range(2):
                    blocks = []
                    for s in range(3):
                        for blk in range(16):
                            ko = KA + c * 8192 + s * 2048 + blk * 128
                            vo_ = VA + (s * 16 + blk) * 256
                            blocks.append(dict(K=A[:, ko:ko + 128], V=[A[:, vo_:vo_ + 128], A[:, vo_ + 128:vo_ + 256]],
                                               bufs=[kab[c][s], vab[s]], bias=self.sel.t[:, 3 + s:4 + s]))
                    for jbk in (3, 2, 1, 0):
                        for rk in range(4):
                            blk = rk * 4 + jbk
                            ko = KO + c * 2048 + blk * 128
                            vo_ = blk * 256
                            mi = 0 if rk <= r4 else 19
                            blocks.append(dict(K=A[:, ko:ko + 128], V=[X[:, vo_:vo_ + 128], X[:, vo_ + 128:vo_ + 256]],
                                               bufs=[kob, vob], masks=[(self.mask_ap(mi), jbk * 128, 128)], c0=jbk * 128))
                    q_ap = A[:, QO + c * 2048 + r4 * 512:QO + c * 2048 + (r4 + 1) * 512]
                    self.attn_run(blocks, q_ap, qb, o_banks, l_bank, scale)
                    rl = self.rr("tmp", self.tmp)
                    P.op("vector", lambda e, rl=rl: e.reciprocal(out=rl.t[:], in_=l_bank.t[:]), reads=[l_bank], writes=[rl])
                    for e2 in range(2):
                        P.op("vector", lambda e, c=c, e2=e2, rl=rl: e.tensor_tensor(
                            out=oc_ap[c][:, e2 * 512:(e2 + 1) * 512], in0=o_banks[e2].t[:], in1=rl.t[:], op=ALU.mult),
                            reads=[o_banks[e2], rl], writes=[ocb[c]])
                P.op("vector", lambda e: e.scalar_tensor_tensor(out=df_ap, in0=oc_ap[1], scalar=self.lamt.t[:, 4:5], in1=oc_ap[0],
                                                                op0=ALU.mult, op1=ALU.add), reads=ocb + [self.lamt], writes=[dfb])
                for e2 in range(2):
                    sq = self.rr("sq", self.sq)
                    P.op("scalar", lambda e, sq=sq, e2=e2: e.activation(out=sq.t[:], in_=df_ap[:, e2 * 512:(e2 + 1) * 512], func=AF.Square),
                         reads=[dfb], writes=[sq])
                    P.op("tensor", lambda e, sq=sq, e2=e2: e.matmul(sbank.t[:], self.ones.t[:], sq.t[:], start=(e2 == 0), stop=(e2 == 1)),
                         reads=[sq, self.ones], writes=[sbank], inc=True)
                P.op("scalar", lambda e: e.activation(out=self.rtmp.t[:], in_=sbank.t[:], func=AF.Sqrt, bias=self.epsc.t[:, 1:2],
                                                      scale=1.0 / (256.0 * one_m * one_m)), reads=[sbank, self.epsc], writes=[self.rtmp])
                P.op("vector", lambda e: e.reciprocal(out=self.rstd[0].t[:], in_=self.rtmp.t[:]), reads=[self.rtmp], writes=[self.rstd[0]])
                for e2 in range(2):
                    o = self.rr("ob", self.ob)
                    P.op("vector", lambda e, o=o, e2=e2: e.scalar_tensor_tensor(
                        out=o.t[:], in0=df_ap[:, e2 * 512:(e2 + 1) * 512], scalar=self.vecs.t[:, gcol + e2:gcol + e2 + 1],
                        in1=self.rstd[0].t[:], op0=ALU.mult, op1=ALU.mult), reads=[dfb, self.vecs, self.rstd[0]], writes=[o])
                    P.dma("sync", attn_v[h, e2, :, r4 * 512:(r4 + 1) * 512], o.t[:], o, reads=[o], writes=attn_bufs)
        for b in allb:
            for tgt in (self.act, self.xn):
                for k, v in b.r.items():
                    tgt.r[k] = max(tgt.r.get(k, 0), v)
                for k, v in b.w.items():
                    tgt.w[k] = max(tgt.w.get(k, 0), v)

    def conv_C(self, zT, bgT, zh_all, src_bufs, attn, attn_bufs, wcol):
        P = self.P
        AF32 = self.act.t.bitcast(F32)
        X = self.xn.t
        sets = []
        for i in range(2):
            base = i * 8192
            sets.append(dict(z=AF32[:, base:base + 2048], bg=AF32[:, base + 2048:base + 4096], acc=AF32[:, base + 4096:base + 6144],
                             hin=AF32[:, base + 6144:base + 6144 + 24], hal=AF32[:, base + 6200:base + 6202],
                             out=X[:, i * 2048:(i + 1) * 2048],
                             zb=P.vbuf("cz"), bb=P.vbuf("cbg"), ab=P.vbuf("cacc"), hb=P.vbuf("chin"), ob=P.vbuf("cout")))
        allb = [s[k] for s in sets for k in ("zb", "bb", "ab", "hb", "ob")]
        for b in allb:
            b.r.update(self.act.r); b.w.update(self.act.w)
            b.r.update(self.xn.r); b.w.update(self.xn.w)
        zd = zT.rearrange("(k p) n -> k p n", p=128)
        bd = bgT.rearrange("(k p) n -> k p n", p=128)
        ad = attn.rearrange("(k p) n -> k p n", p=128)
        zh = zh_all.rearrange("(s k p) e -> k p s e", s=4, p=128)
        for c in range(KC):
            S = sets[c % 2]
            P.dma("gpsimd", S["z"], zd[c], S["zb"], reads=src_bufs, writes=[S["zb"]])
            P.dma("gpsimd", S["bg"], bd[c], S["bb"], reads=src_bufs, writes=[S["bb"]])
            P.dma("sync", S["hin"].rearrange("p (s e) -> p s e", s=3), zh[c][:, 0:3, :], S["hb"], reads=src_bufs, writes=[S["hb"]])
            hin, hal = S["hin"], S["hal"]
            P.op("vector", lambda e, hin=hin, hal=hal: e.tensor_scalar(out=hal, in0=hin[:, 0:2], scalar1=self.sel.t[:, 7:8], scalar2=None, op0=ALU.mult),
                 reads=[S["hb"], self.sel], writes=[S["hb"]])
            for s in (1, 2):
                P.op("vector", lambda e, hin=hin, hal=hal, s=s: e.scalar_tensor_tensor(
                    out=hal, in0=hin[:, 8 * s:8 * s + 2], scalar=self.sel.t[:, 7 + s:8 + s], in1=hal, op0=ALU.mult, op1=ALU.add),
                    reads=[S["hb"], self.sel], writes=[S["hb"]])
            w0 = self.vecs.t[:, wcol + c:wcol + c + 1]
            w1 = self.vecs.t[:, wcol + KC + c:wcol + KC + c + 1]
            w2 = self.vecs.t[:, wcol + 2 * KC + c:wcol + 2 * KC + c + 1]
            z, acc = S["z"], S["acc"]
            rd = [S["zb"], S["hb"], self.vecs]
            P.op("vector", lambda e, z=z, acc=acc, w2=w2: e.tensor_scalar(out=acc, in0=z, scalar1=w2, scalar2=None, op0=ALU.mult),
                 reads=rd, writes=[S["ab"]])

            def fma(dst, src, w):
                P.op("vector", lambda e, dst=dst, src=src, w=w: e.scalar_tensor_tensor(out=dst, in0=src, scalar=w, in1=dst, op0=ALU.mult, op1=ALU.add),
                     reads=rd + [S["ab"]], writes=[S["ab"]])

            for r4 in range(4):
                a = acc[:, r4 * 512:(r4 + 1) * 512]
                if r4 >= 1:
                    fma(a, z[:, (r4 - 1) * 512:r4 * 512], w1)
                else:
                    fma(a[:, 1:512], z[:, 3 * 512:3 * 512 + 511], w1)
                    fma(a[:, 0:1], S["hal"][:, 1:2], w1)
                if r4 >= 2:
                    fma(a, z[:, (r4 - 2) * 512:(r4 - 1) * 512], w0)
                else:
                    fma(a[:, 1:512], z[:, (r4 + 2) * 512:(r4 + 2) * 512 + 511], w0)
                    fma(a[:, 0:1], S["hal"][:, r4:r4 + 1], w0)
            P.op("vector", lambda e, S=S: e.tensor_tensor(out=S["out"], in0=S["acc"], in1=S["bg"], op=ALU.mult),
                 reads=[S["ab"], S["bb"]], writes=[S["ob"]])
            P.dma("sync", ad[c], S["out"], S["ob"], reads=[S["ob"]], writes=attn_bufs)
        for b in allb:
            for tgt in (self.act, self.xn):
                for k, v in b.r.items():
                    tgt.r[k] = max(tgt.r.get(k, 0), v)
                for k, v in b.w.items():
                    tgt.w[k] = max(tgt.w.get(k, 0), v)


NV = 400
VL = 80
V_NORMF = 320
V_SUBLN = 336
V_CONV = 338
NMASK = 20
GROUPS = [[0, 1, 2, 3], [4, 5, 6, 7]]
LAM_INIT = 0.8 - 0.6 * float(np.exp(-0.3 * 1))

W_SHAPES = {}
for _i in range(4):
    W_SHAPES["w1i%d" % _i] = (D, 2 * DFF)
    W_SHAPES["w1o%d" % _i] = (DFF, D)
    W_SHAPES["w2i%d" % _i] = (D, 2 * DFF)
    W_SHAPES["w2o%d" % _i] = (DFF, D)
    W_SHAPES["wpg%d" % _i] = (D, D)
    W_SHAPES["wpp%d" % _i] = (PLE, D)
for _j in range(2):
    W_SHAPES["aqkv%d" % _j] = (D, 5120)
    W_SHAPES["ao%d" % _j] = (D, D)
W_SHAPES.update(bqkv=(D, 3 * D), bo=(D, D), cin=(D, 3 * D), cout=(D, D))


def build_net(stop=None, dbg=None):
    P = Prog()
    xT = P.dram_in("xT", [D, TOK])
    pT = [P.dram_in("pT%d" % i, [PLE, TOK]) for i in range(4)]
    vecs = P.dram_in("vecs", [128, NV])
    masks = P.dram_in("masks", [128, NMASK * 128])
    ident = P.dram_in("ident", [128, 128])
    sel = P.dram_in("sel", [128, 16])
    lamT = P.dram_in("lamT", [128, 4])
    class _LazyW(dict):
        def __missing__(self, k):
            self[k] = P.dram_in(k, list(W_SHAPES[k]))
            return self[k]
    W = _LazyW()
    outT = P.dram_out("outT", [D, TOK])
    h = P.dram_tmp("h", [D, TOK])
    q = P.dram_tmp("q", [D, TOK], BF16)
    klA = P.dram_tmp("klA", [1536, TOK], BF16)
    vlA = P.dram_tmp("vlA", [TOK, 1536], BF16)
    kaA = P.dram_tmp("kaA", [4 * 1536, TOK], BF16)
    vaA = P.dram_tmp("vaA", [4 * TOK, 1536], BF16)
    klB = P.dram_tmp("klB", [D, TOK], BF16)
    vlB = P.dram_tmp("vlB", [TOK, D], BF16)
    kaB = P.dram_tmp("kaB", [4 * D, TOK], BF16)
    vaB = P.dram_tmp("vaB", [4 * TOK, D], BF16)
    zT = P.dram_tmp("zT", [D, TOK])
    bgT = P.dram_tmp("bgT", [D, TOK])
    zhl = P.dram_tmp("zhl", [D, 8])
    zha = P.dram_tmp("zha", [4 * D, 8])
    attn = P.dram_tmp("attn", [D, TOK], BF16)

    R = Rows(P, vecs, NV)
    R.init_attn(masks, NMASK, sel, ident)
    R.lam_setup(lamT, LAM_INIT)

    xb = [P.vbuf("x") for _ in range(4)]
    hb = [P.vbuf("h") for _ in range(4)]
    ob = [P.vbuf("out")]
    P.out_bufs += ob
    qb, klb, vlb, kab, vab, atb = (P.vbuf(n) for n in ("q", "kl", "vl", "ka", "va", "attn"))
    zb, zhlb, zhab = P.vbuf("z"), P.vbuf("zhl"), P.vbuf("zha")
    ccb = P.vbuf("cc")

    def cc(in_ap, out_ap, in_bufs, out_bufs):
        E = P.engs["gpsimd"]
        if ccb.sem is None:
            ccb.sem = P.new_sem("cc")
        waits = P._collect(E, in_bufs, out_bufs)
        for s_, v_ in waits.items():
            E.waited[s_] = v_
        ccb.cnt += 1
        E.ops.append((sorted(waits.items()), lambda e, i=in_ap, o=out_ap: e.collective_compute(
            "AllGather", ALU.bypass, replica_groups=GROUPS, ins=[i], outs=[o]), (ccb.sem, 1)))
        P._mark((ccb.sem, ccb.cnt), in_bufs, out_bufs)

    def gather(loc, allt, rows, lb, ab):
        for i in range(rows // 256):
            cc(loc[i * 256:(i + 1) * 256, :], allt[i * 1024:(i + 1) * 1024, :], [lb], [ab])

    state = {"hsrc": xT, "hsb": xb}

    def hs():
        return state["hsrc"], state["hsb"]

    def wrote_h():
        state["hsrc"], state["hsb"] = h, hb

    def chain(fns):
        for f in fns:
            for half in range(2):
                f(half)
            if getattr(f, "writes_h", False):
                wrote_h()

    def mk(fn, writes_h=False):
        fn.writes_h = writes_h
        return fn

    def s_ffn(i, which):
        g = i * VL + (0 if which == 1 else 32)
        wi, wo = W["w%di%d" % (which, i)], W["w%do%d" % (which, i)]

        def f(half):
            src, sb_ = hs()
            R.norm(src, sb_, half, g)
            R.ffn(src, sb_, h, hb, half, wi, wo)
        return mk(f, True)

    def s_ple(i):
        def f(half):
            src, sb_ = hs()
            R.norm(src, sb_, half, i * VL + 48)
            R.ple(src, sb_, h, hb, half, W["wpg%d" % i], i * VL + 64, pT[i], W["wpp%d" % i])
        return mk(f, True)

    def s_oproj(w):
        def f(half):
            src, sb_ = hs()
            R.load_xn(attn, [atb], half)
            R.oproj(src, sb_, h, hb, half, w)
        return mk(f, True)

    def s_projA(i, j):
        w = W["aqkv%d" % j]

        def f(half):
            src, sb_ = hs()
            R.norm(src, sb_, half, i * VL + 16)
            kv = klA.rearrange("(c p) n -> c p n", p=128)
            for g in range(3):
                R.proj_fm(half, w, D + g * 1024, 4,
                          lambda c, jj, g=g: kv[g * 4 + c, :, half * HALF + jj * TT:half * HALF + (jj + 1) * TT], [klb])
                for grp in range(2):
                    R.proj_tm(half, w, D + g * 1024 + 512 + grp * 256, 256,
                              lambda tb, g=g, grp=grp: vlA[half * HALF + tb * 128:half * HALF + (tb + 1) * 128,
                                                           g * 512 + grp * 256:g * 512 + (grp + 1) * 256], [vlb])
        return mk(f)

    def s_projB(i):
        w = W["bqkv"]

        def f(half):
            src, sb_ = hs()
            R.norm(src, sb_, half, i * VL + 16)
            kv = klB.rearrange("(c p) n -> c p n", p=128)
            R.proj_fm(half, w, D, 16, lambda c, jj: kv[c, :, half * HALF + jj * TT:half * HALF + (jj + 1) * TT], [klb])
            for hh in range(8):
                R.proj_tm(half, w, 2 * D + hh * 256, 256,
                          lambda tb, hh=hh: vlB[half * HALF + tb * 128:half * HALF + (tb + 1) * 128, hh * 256:(hh + 1) * 256], [vlb])
        return mk(f)

    def q_stage(i, w):
        def run():
            qv = q.rearrange("(c p) n -> c p n", p=128)
            for half in (1, 0):
                src, sb_ = hs()
                if half == 0:
                    R.norm(src, sb_, half, i * VL + 16)
                R.proj_fm(half, w, 0, 16, lambda c, jj, half=half: qv[c, :, half * HALF + jj * TT:half * HALF + (jj + 1) * TT], [qb])
        return run

    def s_projC(i):
        def f(half):
            src, sb_ = hs()
            R.norm(src, sb_, half, i * VL + 16)
            R.proj_conv_in(half, W["cin"], zT, bgT, [zb])
        return mk(f)

    def s_final():
        def f(half):
            src, sb_ = hs()
            R.final_norm(src, sb_, outT, ob, half, V_NORMF)
        return mk(f)

    def halo():
        P.dma("sync", zhl[:, 0:1], zT[:, 2 * 512 + 511:2 * 512 + 512], zhlb, reads=[zb], writes=[zhlb], allow_slow_non_contiguous=True)
        P.dma("sync", zhl[:, 1:2], zT[:, 3 * 512 + 511:3 * 512 + 512], zhlb, reads=[zb], writes=[zhlb], allow_slow_non_contiguous=True)
        cc(zhl, zha, [zhlb], [zhab])

    def gA():
        gather(klA, kaA, 1536, klb, kab)
        gather(vlA, vaA, TOK, vlb, vab)

    def gB():
        gather(klB, kaB, D, klb, kab)
        gather(vlB, vaB, TOK, vlb, vab)

    mixA = lambda: R.attn_A(q, klA, vlA, kaA, vaA, [qb, klb, vlb], [kab, vab], attn, [atb])
    mixB = lambda: R.attn_B(q, klB, vlB, kaB, vaB, [qb, klb, vlb], [kab, vab], attn, [atb], V_SUBLN, LAM_INIT)
    mixC = lambda: R.conv_C(zT, bgT, zha, [zb, zhab], attn, [atb], V_CONV)
    C1 = lambda th: (lambda: chain([th()]))
    stages = [
        C1(lambda: s_ffn(0, 1)), C1(lambda: s_projA(0, 0)), gA, (lambda: q_stage(0, W["aqkv0"])()), mixA, C1(lambda: s_oproj(W["ao0"])), C1(lambda: s_ffn(0, 2)), C1(lambda: s_ple(0)),
        C1(lambda: s_ffn(1, 1)), C1(lambda: s_projB(1)), gB, (lambda: q_stage(1, W["bqkv"])()), mixB, C1(lambda: s_oproj(W["bo"])), C1(lambda: s_ffn(1, 2)), C1(lambda: s_ple(1)),
        C1(lambda: s_ffn(2, 1)), C1(lambda: s_projC(2)), halo, mixC, C1(lambda: s_oproj(W["cout"])), C1(lambda: s_ffn(2, 2)), C1(lambda: s_ple(2)),
        C1(lambda: s_ffn(3, 1)), C1(lambda: s_projA(3, 1)), gA, (lambda: q_stage(3, W["aqkv1"])()), mixA, C1(lambda: s_oproj(W["ao1"])), C1(lambda: s_ffn(3, 2)), C1(lambda: s_ple(3)),
        C1(lambda: s_final()),
    ]
    n = len(stages) if stop is None else stop
    for st in stages[:n]:
        st()
    if stop is not None:
        if dbg is None:
            P.dma("sync", outT, h, ob[0], reads=hb, writes=ob)
        else:
            src = {"attn": attn, "q": q, "klA": klA, "vlA": vlA, "klB": klB, "vlB": vlB}[dbg]
            dbo = P.dram_out("dbg", list(src.shape), BF16)
            P.dma("sync", dbo, src, ob[0], reads=[atb, qb, klb, vlb], writes=ob)
            P.dma("sync", outT, h, ob[0], reads=hb, writes=ob)
    nc = P.emit()
    nc._in_names = list(P.in_names)
    return nc


def _r4perm():
    n = np.arange(TOK)
    return 4 * (n % 512) + n // 512


def _masks():
    jk = np.arange(128)[:, None]
    jq = np.arange(128)[None, :]
    vis = []
    vis.append(jk <= jq)
    vis.append(jk >= jq)
    comb = ((jq - jk) % 4) == 0
    vis.append((jk <= jq) & comb)
    vis.append(comb | (jk < -1))
    vis.append((jk >= jq) & comb)
    for dr in range(-3, 4):
        for djb in range(2):
            dt = 4 * (128 * djb + jq - jk) + dr
            vis.append((dt >= 0) & (dt <= 128))
    vis.append(jk < jq)
    m = np.concatenate([np.where(v, 0.0, NEG) for v in vis], axis=1).astype(np.float32)
    assert m.shape == (128, NMASK * 128)
    return m


def _col16(v):
    return np.ascontiguousarray(np.asarray(v, np.float32).reshape(-1, 128).T)


def make_inputs(inputs):
    perm = _r4perm()
    x = np.asarray(inputs["x"], np.float32)
    p = np.asarray(inputs["p"], np.float32)
    vecs = np.zeros((128, NV), np.float32)
    for i in range(4):
        b = i * VL
        vecs[:, b:b + 16] = _col16(inputs["norm_ffn1"][i])
        vecs[:, b + 16:b + 32] = _col16(inputs["norm_mix"][i])
        vecs[:, b + 32:b + 48] = _col16(inputs["norm_ffn2"][i])
        vecs[:, b + 48:b + 64] = _col16(inputs["norm_ple"][i])
        vecs[:, b + 64:b + 80] = _col16(inputs["b_ple_gate"][i])
    vecs[:, V_NORMF:V_NORMF + 16] = _col16(inputs["norm_f"])
    vecs[:, V_SUBLN:V_SUBLN + 2] = _col16(inputs["b_subln"][0])
    for k in range(3):
        vecs[:, V_CONV + 16 * k:V_CONV + 16 * (k + 1)] = _col16(inputs["c_conv_w"][0][k])
    shared = {"vecs": vecs, "masks": _masks(), "ident": np.eye(128, dtype=np.float32),
              "lamT": np.ascontiguousarray(np.asarray(inputs["b_lambda"][0], np.float32).T)}
    for i in range(4):
        shared["w1i%d" % i] = np.asarray(inputs["w_ffn1_in"][i], np.float32)
        shared["w1o%d" % i] = np.asarray(inputs["w_ffn1_out"][i], np.float32)
        shared["w2i%d" % i] = np.asarray(inputs["w_ffn2_in"][i], np.float32)
        shared["w2o%d" % i] = np.asarray(inputs["w_ffn2_out"][i], np.float32)
        shared["wpg%d" % i] = np.asarray(inputs["w_ple_gate"][i], np.float32)
        shared["wpp%d" % i] = np.asarray(inputs["w_ple_proj"][i], np.float32)
    for j in range(2):
        shared["aqkv%d" % j] = np.asarray(inputs["a_w_qkv"][j], np.float32)
        shared["ao%d" % j] = np.asarray(inputs["a_w_o"][j], np.float32)
    shared["bqkv"] = np.asarray(inputs["b_w_qkv"][0], np.float32)
    shared["bo"] = np.asarray(inputs["b_w_o"][0], np.float32)
    shared["cin"] = np.asarray(inputs["c_w_in"][0], np.float32)
    shared["cout"] = np.asarray(inputs["c_w_out"][0], np.float32)
    maps = []
    for core in range(NCORES):
        b, c = divmod(core, 4)
        sl = slice(c * TOK, (c + 1) * TOK)
        m = dict(shared)
        m["xT"] = np.ascontiguousarray(x[b, sl][perm].T)
        for i in range(4):
            m["pT%d" % i] = np.ascontiguousarray(p[i, b, sl][perm].T)
        s = np.zeros((128, 16), np.float32)
        for k in range(3):
            s[:, k] = 0.0 if k == c - 1 else NEG
            s[:, 7 + k] = 1.0 if k == c - 1 else 0.0
        for k in range(4):
            s[:, 3 + k] = 0.0 if k < c else NEG
        m["sel"] = s
        maps.append(m)
    return maps


def assemble(results, key="outT"):
    perm = _r4perm()
    out = np.empty((2, 4 * TOK, D), np.float32)
    for core in range(NCORES):
        b, c = divmod(core, 4)
        o = np.asarray(results[core][key]).astype(np.float32).T
        blk = np.empty_like(o)
        blk[perm] = o
        out[b, c * TOK:(c + 1) * TOK] = blk
    return out


_NC_CACHE = {}


def kernel(**inputs):
    if "net" not in _NC_CACHE:
        _NC_CACHE["net"] = build_net()
    nc = _NC_CACHE["net"]
    maps = [{k: m[k] for k in nc._in_names} for m in make_inputs(inputs)]
    res = run_bass_kernel_spmd(nc, maps, core_ids=list(range(NCORES)))
    return assemble(res.results)
```

```python
import contextlib
import numpy as np
import ml_dtypes
import concourse.bass as bass
import concourse.mybir as mybir
from concourse.bass_utils import run_bass_kernel_spmd

F32 = mybir.dt.float32
BF16 = mybir.dt.bfloat16
AF = mybir.ActivationFunctionType
ALU = mybir.AluOpType

D = 2048
KC = 16
TOK = 2048
HALF = 1024
TT = 512
DFF = 5632
FS = 44
PLE = 256
NCORES = 8
NEG = -30000.0
RMS_EPS = 1e-6
SUBLN_EPS = 1e-5


class Buf:
    __slots__ = ("name", "w", "r", "sem", "cnt", "t")

    def __init__(self, name, t=None):
        self.name = name
        self.w = {}
        self.r = {}
        self.sem = None
        self.cnt = 0
        self.t = t


class Eng:
    def __init__(self, name):
        self.name = name
        self.ops = []
        self.sem = None
        self.cnt = 0
        self.waited = {}


class Prog:
    def __init__(self):
        self.nc = bass.Bass("TRN2", target_bir_lowering=False)
        self.es = contextlib.ExitStack()
        self.engs = {n: Eng(n) for n in ("tensor", "vector", "scalar", "gpsimd", "sync")}
        self.semh = {}
        self.nsem = 0
        for n in ("tensor", "vector", "scalar"):
            self.engs[n].sem = self.new_sem("e_" + n)
        self.out_bufs = []
        self.in_names = []
        self.nbuf = 0

    def new_sem(self, name=None):
        self.nsem += 1
        h = self.es.enter_context(self.nc.semaphore(name or ("s%d" % self.nsem)))
        self.semh[self.nsem] = h
        return self.nsem

    def sb(self, name, shape, dtype):
        t = self.es.enter_context(self.nc.sbuf_tensor("sb_" + name, list(shape), dtype))
        return Buf(name, t)

    def ps(self, name):
        t = self.es.enter_context(self.nc.psum_tensor("ps_" + name, [128, 512], F32))
        return Buf(name, t)

    def dram_in(self, name, shape, dtype=F32):
        self.in_names.append(name)
        return self.nc.dram_tensor(name, list(shape), dtype, kind="ExternalInput").ap()

    def dram_out(self, name, shape, dtype=F32):
        return self.nc.dram_tensor(name, list(shape), dtype, kind="ExternalOutput").ap()

    def dram_tmp(self, name, shape, dtype=F32):
        return self.nc.dram_tensor(name, list(shape), dtype).ap()

    def vbuf(self, name):
        self.nbuf += 1
        return Buf("%s_%d" % (name, self.nbuf))

    def _collect(self, E, reads, writes, extra=()):
        waits = {}

        def need(sem, val):
            if E.waited.get(sem, 0) >= val:
                return
            if waits.get(sem, 0) < val:
                waits[sem] = val

        for b in reads:
            for sem, val in b.w.items():
                need(sem, val)
        for b in writes:
            for sem, val in b.w.items():
                need(sem, val)
            for sem, val in b.r.items():
                need(sem, val)
        for sem, val in extra:
            need(sem, val)
        return waits

    @staticmethod
    def _mark(ev, reads, writes):
        sem, val = ev
        for b in reads:
            if b.r.get(sem, 0) < val:
                b.r[sem] = val
        for b in writes:
            if b.w.get(sem, 0) < val:
                b.w[sem] = val

    def op(self, eng, fn, reads=(), writes=(), inc=True):
        E = self.engs[eng]
        waits = self._collect(E, reads, writes)
        if eng == "tensor":
            waits.pop(E.sem, None)
        for sem, val in waits.items():
            E.waited[sem] = val
        if inc:
            E.cnt += 1
            ev = (E.sem, E.cnt)
        else:
            ev = (E.sem, E.cnt + 1)
        E.ops.append((sorted(waits.items()), fn, (E.sem, 1) if inc else None))
        self._mark(ev, reads, writes)

    def dma(self, eng, out, in_, owner, reads=(), writes=(), **kw):
        E = self.engs[eng]
        if owner.sem is None:
            owner.sem = self.new_sem("d_" + owner.name)
        waits = self._collect(E, reads, writes, extra=[(owner.sem, owner.cnt)] if owner.cnt else [])
        for sem, val in waits.items():
            E.waited[sem] = val
        owner.cnt += 16
        ev = (owner.sem, owner.cnt)
        E.ops.append((sorted(waits.items()), lambda e, o=out, i=in_: e.dma_start(out=o, in_=i, **kw), (owner.sem, 16)))
        self._mark(ev, reads, writes)

    def barrier_wait(self, eng, bufs):
        E = self.engs[eng]
        waits = self._collect(E, [], bufs)
        for sem, val in waits.items():
            E.waited[sem] = val
        if waits:
            E.ops.append((sorted(waits.items()), None, None))

    def check(self):
        sems = {}
        pos = {n: 0 for n in self.engs}
        progress = True
        while progress:
            progress = False
            for n, E in self.engs.items():
                while pos[n] < len(E.ops):
                    waits, fn, inc = E.ops[pos[n]]
                    if any(sems.get(s, 0) < v for s, v in waits):
                        break
                    if inc is not None:
                        sems[inc[0]] = sems.get(inc[0], 0) + inc[1]
                    pos[n] += 1
                    progress = True
        stuck = {n: (pos[n], len(E.ops)) for n, E in self.engs.items() if pos[n] < len(E.ops)}
        if stuck:
            msg = []
            for n in stuck:
                waits, fn, inc = self.engs[n].ops[pos[n]]
                msg.append("%s@%d waits %s have %s" % (n, pos[n], waits, [(s, sems.get(s, 0)) for s, _ in waits]))
            raise RuntimeError("DEADLOCK: " + "; ".join(msg))
        return {n: len(E.ops) for n, E in self.engs.items()}

    def emit(self):
        self.barrier_wait("sync", self.out_bufs)
        self.check()
        nc = self.nc
        semh = self.semh

        def replay(e, E):
            for waits, fn, inc in E.ops:
                for sem, val in waits:
                    e.wait_ge(semh[sem], val)
                if fn is None:
                    continue
                inst = fn(e)
                if inc is not None:
                    inst.then_inc(semh[inc[0]], inc[1])

        with nc.Block() as block:
            @block.tensor
            def _(e):
                replay(e, self.engs["tensor"])

            @block.vector
            def _(e):
                replay(e, self.engs["vector"])

            @block.scalar
            def _(e):
                replay(e, self.engs["scalar"])

            @block.gpsimd
            def _(e):
                replay(e, self.engs["gpsimd"])

            @block.sync
            def _(e):
                replay(e, self.engs["sync"])
        self.es.close()
        return nc


class Rows:
    def __init__(self, P, vecs_ap, nvec):
        self.P = P
        self.act = P.sb("act", [128, FS * HALF], BF16)
        self.xn = P.sb("xn", [128, KC * HALF], BF16)
        self.win = [P.sb("win%d" % i, [128, KC * 128], BF16) for i in range(4)]
        self.wout = [P.sb("wout%d" % i, [128, FS * 128], BF16) for i in range(2)]
        self.ht = [P.sb("ht%d" % i, [128, TT], F32) for i in range(6)]
        self.tmp = [P.sb("tmp%d" % i, [128, TT], F32) for i in range(3)]
        self.sq = [P.sb("sq%d" % i, [128, TT], BF16) for i in range(2)]
        self.rstd = [P.sb("rstd%d" % i, [128, TT], F32) for i in range(2)]
        self.epsc = P.sb("epsc", [128, 2], F32)
        self.rtmp = P.sb("rtmp", [128, TT], F32)
        self.ob = [P.sb("ob%d" % i, [128, TT], BF16) for i in range(4)]
        self.ones = P.sb("ones", [128, 128], BF16)
        self.vecs = P.sb("vecs", [128, nvec], F32)
        self.pt = P.sb("pt", [128, 2 * HALF], BF16)
        self.wp = [P.sb("wp%d" % i, [128, 2 * 128], BF16) for i in range(2)]
        self.banks = [P.ps("bank%d" % i) for i in range(8)]
        self.cnt = {}
        P.op("vector", lambda e: e.memset(self.ones.t[:], 1.0), writes=[self.ones])
        P.op("vector", lambda e: e.memset(self.epsc.t[:, 0:1], RMS_EPS), writes=[self.epsc])
        P.dma("sync", self.vecs.t[:], vecs_ap, self.vecs, writes=[self.vecs])

    def rr(self, key, lst):
        i = self.cnt.get(key, 0)
        self.cnt[key] = i + 1
        return lst[i % len(lst)]

    def xs(self, k, j):
        return self.xn.t[:, k * HALF + j * TT:k * HALF + (j + 1) * TT]

    def stats(self, hsrc, hsrc_bufs, n0, j, eps, dim=D):
        P = self.P
        hs = hsrc.rearrange("(k p) n -> k p n", p=128)
        bank = self.banks[6 + j]
        for k in range(KC):
            ht = self.rr("ht", self.ht)
            P.dma("gpsimd", ht.t[:], hs[k, :, n0:n0 + TT], ht, reads=[hsrc_bufs[n0 // TT]], writes=[ht])
            sq = self.rr("sq", self.sq)
            P.op("scalar", lambda e, s=sq, t=ht: e.activation(out=s.t[:], in_=t.t[:], func=AF.Square),
                 reads=[ht], writes=[sq])
            P.op("tensor", lambda e, s=sq, k=k: e.matmul(bank.t[:], self.ones.t[:], s.t[:], start=(k == 0), stop=(k == KC - 1)),
                 reads=[sq, self.ones], writes=[bank], inc=True)
        P.op("scalar", lambda e: e.activation(out=self.rtmp.t[:], in_=bank.t[:], func=AF.Sqrt, bias=self.epsc.t[:, 0:1], scale=1.0 / dim),
             reads=[bank, self.epsc], writes=[self.rtmp])
        P.op("vector", lambda e: e.reciprocal(out=self.rstd[j].t[:], in_=self.rtmp.t[:]), reads=[self.rtmp], writes=[self.rstd[j]])

    def norm(self, hsrc, hsrc_bufs, half, gcol, eps=RMS_EPS):
        P = self.P
        hs = hsrc.rearrange("(k p) n -> k p n", p=128)
        for j in range(2):
            self.stats(hsrc, hsrc_bufs, half * HALF + j * TT, j, eps)
        for j in range(2):
            n0 = half * HALF + j * TT
            for k in range(KC):
                ht = self.rr("ht", self.ht)
                P.dma("gpsimd", ht.t[:], hs[k, :, n0:n0 + TT], ht, reads=[hsrc_bufs[n0 // TT]], writes=[ht])
                P.op("vector", lambda e, k=k, j=j, t=ht: e.scalar_tensor_tensor(
                    out=self.xs(k, j), in0=t.t[:], scalar=self.vecs.t[:, gcol + k:gcol + k + 1], in1=self.rstd[j].t[:],
                    op0=ALU.mult, op1=ALU.mult), reads=[ht, self.vecs, self.rstd[j]], writes=[self.xn])

    def load_w(self, slot, w_ap, c0, ncol, kchunks, r0=0):
        src = w_ap[r0:r0 + kchunks * 128, :].rearrange("(k p) c -> p k c", p=128)[:, :, c0:c0 + ncol]
        dst = slot.t[:, 0:kchunks * ncol].rearrange("p (k c) -> p k c", k=kchunks)
        self.P.dma("gpsimd", dst, src, slot, writes=[slot])

    def mm_group(self, bank, lhs_fn, rhs_fn, nk, reads, ncol=TT):
        for k in range(nk):
            self.P.op("tensor", lambda e, k=k: e.matmul(bank.t[:, 0:ncol], lhs_fn(k), rhs_fn(k), start=(k == 0), stop=(k == nk - 1)),
                      reads=reads, writes=[bank], inc=(k == nk - 1))

    def add_store(self, bank, scale, hsrc, hsrc_bufs, hdst, hdst_bufs, oc, n0, mul_by=None):
        P = self.P
        hs = hsrc.rearrange("(k p) n -> k p n", p=128)
        hd = hdst.rearrange("(k p) n -> k p n", p=128)
        ht = self.rr("ht", self.ht)
        P.dma("gpsimd", ht.t[:], hs[oc, :, n0:n0 + TT], ht, reads=[hsrc_bufs[n0 // TT]], writes=[ht])
        if mul_by is None:
            P.op("vector", lambda e, t=ht, b=bank: e.scalar_tensor_tensor(
                out=t.t[:], in0=b.t[:], scalar=scale, in1=t.t[:], op0=ALU.mult, op1=ALU.add), reads=[bank, ht], writes=[ht])
        else:
            P.op("vector", lambda e, m=mul_by, b=bank: e.tensor_tensor(out=m.t[:], in0=m.t[:], in1=b.t[:], op=ALU.mult),
                 reads=[bank, mul_by], writes=[mul_by])
            P.op("vector", lambda e, t=ht, m=mul_by: e.tensor_tensor(out=t.t[:], in0=t.t[:], in1=m.t[:], op=ALU.add),
                 reads=[mul_by, ht], writes=[ht])
        P.dma("sync", hd[oc, :, n0:n0 + TT], ht.t[:], ht, reads=[ht], writes=[hdst_bufs[n0 // TT]])

    def ffn(self, hsrc, hsrc_bufs, hdst, hdst_bufs, half, w_in, w_out):
        P = self.P
        for s in range(FS):
            wg = self.rr("win", self.win)
            wu = self.rr("win", self.win)
            self.load_w(wg, w_in, s * 128, 128, KC)
            self.load_w(wu, w_in, DFF + s * 128, 128, KC)
            for j in range(2):
                ba = self.rr("bA", self.banks[0:2])
                bb = self.rr("bB", self.banks[2:4])
                self.mm_group(ba, lambda k, w=wg: w.t[:, k * 128:(k + 1) * 128], lambda k, j=j: self.xs(k, j), KC, [wg, self.xn])
                self.mm_group(bb, lambda k, w=wu: w.t[:, k * 128:(k + 1) * 128], lambda k, j=j: self.xs(k, j), KC, [wu, self.xn])
                tmp = self.rr("tmp", self.tmp)
                P.op("scalar", lambda e, t=tmp, b=ba: e.activation(out=t.t[:], in_=b.t[:], func=AF.Silu), reads=[ba], writes=[tmp])
                P.op("vector", lambda e, t=tmp, b=bb, s=s, j=j: e.tensor_tensor(
                    out=self.act.t[:, s * HALF + j * TT:s * HALF + (j + 1) * TT], in0=t.t[:], in1=b.t[:], op=ALU.mult),
                    reads=[tmp, bb], writes=[self.act])
        for oc in range(KC):
            wo = self.rr("wout", self.wout)
            self.load_w(wo, w_out, oc * 128, 128, FS)
            for j in range(2):
                n0 = half * HALF + j * TT
                bo = self.rr("bO", self.banks[4:6])
                self.mm_group(bo, lambda k, w=wo: w.t[:, k * 128:(k + 1) * 128],
                              lambda k, j=j: self.act.t[:, k * HALF + j * TT:k * HALF + (j + 1) * TT], FS, [wo, self.act])
                self.add_store(bo, 0.5, hsrc, hsrc_bufs, hdst, hdst_bufs, oc, n0)

    def ple(self, hsrc, hsrc_bufs, hdst, hdst_bufs, half, w_gate, bcol, pT, w_proj):
        P = self.P
        ptv = pT.rearrange("(k p) n -> p k n", p=128)[:, :, half * HALF:(half + 1) * HALF]
        P.dma("gpsimd", self.pt.t[:].rearrange("p (k n) -> p k n", k=2), ptv, self.pt, writes=[self.pt])
        for oc in range(KC):
            wg = self.rr("win", self.win)
            self.load_w(wg, w_gate, oc * 128, 128, KC)
            wp = self.rr("wp", self.wp)
            self.load_w(wp, w_proj, oc * 128, 128, 2)
            for j in range(2):
                n0 = half * HALF + j * TT
                ba = self.rr("bA", self.banks[0:2])
                bb = self.rr("bB", self.banks[2:4])
                self.mm_group(ba, lambda k, w=wg: w.t[:, k * 128:(k + 1) * 128], lambda k, j=j: self.xs(k, j), KC, [wg, self.xn])
                self.mm_group(bb, lambda k, w=wp: w.t[:, k * 128:(k + 1) * 128],
                              lambda k, j=j: self.pt.t[:, k * HALF + j * TT:k * HALF + (j + 1) * TT], 2, [wp, self.pt])
                tmp = self.rr("tmp", self.tmp)
                P.op("scalar", lambda e, t=tmp, b=ba, oc=oc: e.activation(
                    out=t.t[:], in_=b.t[:], func=AF.Sigmoid, bias=self.vecs.t[:, bcol + oc:bcol + oc + 1], scale=1.0),
                    reads=[ba, self.vecs], writes=[tmp])
                self.add_store(bb, 1.0, hsrc, hsrc_bufs, hdst, hdst_bufs, oc, n0, mul_by=tmp)

    def proj_fm(self, half, w_ap, c0, nchunks, dst_fn, dst_bufs, evac="copy"):
        P = self.P
        for c in range(nchunks):
            w = self.rr("win", self.win)
            self.load_w(w, w_ap, c0 + c * 128, 128, KC)
            for j in range(2):
                b = self.rr("bA", self.banks[0:4])
                self.mm_group(b, lambda k, w=w: w.t[:, k * 128:(k + 1) * 128], lambda k, j=j: self.xs(k, j), KC, [w, self.xn])
                ob = self.rr("ob", self.ob)
                eng = self.rr("evac", ["scalar", "vector"])
                if eng == "scalar":
                    P.op("scalar", lambda e, o=ob, b=b: e.activation(out=o.t[:], in_=b.t[:], func=AF.Copy), reads=[b], writes=[ob])
                else:
                    P.op("vector", lambda e, o=ob, b=b: e.tensor_copy(out=o.t[:], in_=b.t[:]), reads=[b], writes=[ob])
                P.dma("sync", dst_fn(c, j), ob.t[:], ob, reads=[ob], writes=dst_bufs)

    def oproj(self, hsrc, hsrc_bufs, hdst, hdst_bufs, half, w_o):
        for oc in range(KC):
            w = self.rr("win", self.win)
            self.load_w(w, w_o, oc * 128, 128, KC)
            for j in range(2):
                n0 = half * HALF + j * TT
                b = self.rr("bO", self.banks[4:6])
                self.mm_group(b, lambda k, w=w: w.t[:, k * 128:(k + 1) * 128], lambda k, j=j: self.xs(k, j), KC, [w, self.xn])
                self.add_store(b, 1.0, hsrc, hsrc_bufs, hdst, hdst_bufs, oc, n0)

    def final_norm(self, hsrc, hsrc_bufs, out_ap, out_bufs, half, gcol):
        P = self.P
        hs = hsrc.rearrange("(k p) n -> k p n", p=128)
        od = out_ap.rearrange("(k p) n -> k p n", p=128)
        for j in range(2):
            self.stats(hsrc, hsrc_bufs, half * HALF + j * TT, j, RMS_EPS)
        for j in range(2):
            n0 = half * HALF + j * TT
            for k in range(KC):
                ht = self.rr("ht", self.ht)
                P.dma("gpsimd", ht.t[:], hs[k, :, n0:n0 + TT], ht, reads=[hsrc_bufs[n0 // TT]], writes=[ht])
                P.op("vector", lambda e, k=k, t=ht, j=j: e.scalar_tensor_tensor(
                    out=t.t[:], in0=t.t[:], scalar=self.vecs.t[:, gcol + k:gcol + k + 1], in1=self.rstd[j].t[:],
                    op0=ALU.mult, op1=ALU.mult), reads=[ht, self.vecs, self.rstd[j]], writes=[ht])
                P.dma("sync", od[k, :, n0:n0 + TT], ht.t[:], ht, reads=[ht], writes=out_bufs)

    def evac_bf16(self, bank, ncol=TT):
        P = self.P
        ob = self.rr("ob", self.ob)
        eng = self.rr("evac", ["scalar", "vector"])
        if eng == "scalar":
            P.op("scalar", lambda e, o=ob, b=bank: e.activation(out=o.t[:, 0:ncol], in_=b.t[:, 0:ncol], func=AF.Copy), reads=[bank], writes=[ob])
        else:
            P.op("vector", lambda e, o=ob, b=bank: e.tensor_copy(out=o.t[:, 0:ncol], in_=b.t[:, 0:ncol]), reads=[bank], writes=[ob])
        return ob

    def proj_tm(self, half, w_ap, c0, ncol, dst_fn, dst_bufs):
        P = self.P
        w = self.rr("wout", self.wout)
        self.load_w(w, w_ap, c0, ncol, KC)
        for tb in range(HALF // 128):
            b = self.rr("bA", self.banks[0:4])
            self.mm_group(b, lambda k, tb=tb: self.xn.t[:, k * HALF + tb * 128:k * HALF + (tb + 1) * 128],
                          lambda k, w=w: w.t[:, k * ncol:(k + 1) * ncol], KC, [w, self.xn], ncol=ncol)
            ob = self.evac_bf16(b, ncol)
            P.dma("sync", dst_fn(tb), ob.t[:, 0:ncol], ob, reads=[ob], writes=dst_bufs)

    def load_xn(self, src, src_bufs, half):
        v = src.rearrange("(k p) n -> p k n", p=128)[:, :, half * HALF:(half + 1) * HALF]
        self.P.dma("sync", self.xn.t[:].rearrange("p (k n) -> p k n", k=KC), v, self.xn, reads=src_bufs, writes=[self.xn])

    def proj_conv_in(self, half, w_ap, zT, bgT, dst_bufs):
        P = self.P
        zd = zT.rearrange("(k p) n -> k p n", p=128)
        bd = bgT.rearrange("(k p) n -> k p n", p=128)
        for c in range(KC):
            wb = self.rr("win", self.win)
            wc = self.rr("win", self.win)
            wu = self.rr("win", self.win)
            self.load_w(wb, w_ap, c * 128, 128, KC)
            self.load_w(wc, w_ap, D + c * 128, 128, KC)
            self.load_w(wu, w_ap, 2 * D + c * 128, 128, KC)
            for j in range(2):
                n0 = half * HALF + j * TT
                b0 = self.rr("bA", self.banks[0:2])
                b1 = self.rr("bB", self.banks[2:4])
                b2 = self.rr("bO", self.banks[4:6])
                for (b, w) in ((b0, wb), (b1, wc), (b2, wu)):
                    self.mm_group(b, lambda k, w=w: w.t[:, k * 128:(k + 1) * 128], lambda k, j=j: self.xs(k, j), KC, [w, self.xn])
                t0 = self.rr("ht", self.ht)
                P.op("scalar", lambda e, t=t0, b=b0: e.activation(out=t.t[:], in_=b.t[:], func=AF.Copy), reads=[b0], writes=[t0])
                P.dma("sync", bd[c, :, n0:n0 + TT], t0.t[:], t0, reads=[t0], writes=dst_bufs)
                t1 = self.rr("ht", self.ht)
                P.op("scalar", lambda e, t=t1, b=b1: e.activation(out=t.t[:], in_=b.t[:], func=AF.Copy), reads=[b1], writes=[t1])
                P.op("vector", lambda e, t=t1, b=b2: e.tensor_tensor(out=t.t[:], in0=t.t[:], in1=b.t[:], op=ALU.mult), reads=[b2, t1], writes=[t1])
                P.dma("sync", zd[c, :, n0:n0 + TT], t1.t[:], t1, reads=[t1], writes=dst_bufs)

    def init_attn(self, masks_ap, nmask, sel_ap, ident_ap):
        P = self.P
        self.masks = P.sb("masks", [128, nmask * 128], BF16)
        self.ident = P.sb("ident", [128, 128], BF16)
        self.sel = P.sb("sel", [128, 16], F32)
        P.dma("gpsimd", self.masks.t[:], masks_ap, self.masks, writes=[self.masks])
        P.dma("gpsimd", self.ident.t[:], ident_ap, self.ident, writes=[self.ident])
        P.dma("sync", self.sel.t[:], sel_ap, self.sel, writes=[self.sel])
        self.NPT = 6
        self.LA = 3
        self.pts = [P.vbuf("pt") for _ in range(self.NPT)]
        self.laccs = [P.sb("lacc%d" % i, [128, TT], F32) for i in range(2)]
        self.PT0 = FS * HALF - self.NPT * TT

    def pt_ap(self, i):
        return self.act.t[:, self.PT0 + i * TT:self.PT0 + (i + 1) * TT]

    def mask_ap(self, mi):
        return self.masks.t[:, mi * 128:(mi + 1) * 128]

    def attn_run(self, blocks, q_ap, qbuf, o_banks, l_bank, scale):
        P = self.P
        n = len(blocks)
        st = [None] * n

        def stage_s(i):
            b = blocks[i]
            c0 = b.get("c0", 0)
            ms = b.get("masks", ())
            sb = self.rr("bS", [self.banks[0], self.banks[1], self.banks[2], self.banks[7]])
            P.op("tensor", lambda e, b=b, sb=sb, c0=c0, ms=ms: e.matmul(sb.t[:, c0:TT], b["K"], q_ap[:, c0:TT], start=True, stop=(len(ms) == 0)),
                 reads=[qbuf] + b["bufs"], writes=[sb], inc=(len(ms) == 0))
            for mi, (map_, mc0, w) in enumerate(ms):
                last = mi == len(ms) - 1
                P.op("tensor", lambda e, sb=sb, map_=map_, mc0=mc0, w=w, last=last: e.matmul(
                    sb.t[:, mc0:mc0 + w], self.ident.t[:], map_, start=False, stop=last),
                    reads=[self.ident, self.masks], writes=[sb], inc=last)
            pi = self.cnt.get("pti", 0)
            self.cnt["pti"] = pi + 1
            pt = self.pts[pi % self.NPT]
            pap = self.pt_ap(pi % self.NPT)
            bias = b.get("bias")
            if bias is None:
                bias = self.sel.t[:, 10:11]
            P.op("scalar", lambda e, sb=sb, pap=pap, c0=c0, bias=bias: e.activation(
                out=pap[:, c0:TT], in_=sb.t[:, c0:TT], func=AF.Exp, bias=bias, scale=scale),
                reads=[sb, self.sel], writes=[pt])
            st[i] = (pt, pap, c0)

        lacc = self.rr("lacc", self.laccs)

        def stage_pv(i):
            pt, pap, c0 = st[i]
            b = blocks[i]
            first, last = (i == 0), (i == n - 1)
            for oi, ob in enumerate(o_banks):
                P.op("tensor", lambda e, ob=ob, v=b["V"][oi], pap=pap, c0=c0, first=first, last=last: e.matmul(
                    ob.t[:, c0:TT], v, pap[:, c0:TT], start=first, stop=last),
                    reads=[pt] + b["bufs"], writes=[ob], inc=(oi == len(o_banks) - 1))
            if first:
                P.op("vector", lambda e, pap=pap: e.tensor_copy(out=lacc.t[:], in_=pap), reads=[pt], writes=[lacc])
            else:
                P.op("vector", lambda e, pap=pap, c0=c0: e.tensor_tensor(out=lacc.t[:, c0:TT], in0=lacc.t[:, c0:TT], in1=pap[:, c0:TT], op=ALU.add),
                     reads=[pt, lacc], writes=[lacc])

        LA = self.LA
        for i in range(min(LA, n)):
            stage_s(i)
        for i in range(n):
            stage_pv(i)
            if i + LA < n:
                stage_s(i + LA)
        P.op("tensor", lambda e: e.matmul(l_bank.t[:], self.onesf.t[:], lacc.t[:], start=True, stop=True),
             reads=[lacc, self.onesf], writes=[l_bank])

    def attn_A(self, qA, kloc, vloc, kall, vall, own_bufs, src_bufs, attn, attn_bufs):
        P = self.P
        scale = 128.0 ** -0.5
        A = self.act.t
        QO, KO, VO, PO = 0, 8192, 14336, 20480
        qb, kb, vb = P.vbuf("qA"), P.vbuf("kA"), P.vbuf("vA")
        kpb = [P.vbuf("kpA") for _ in range(3)]
        vpb = [P.vbuf("vpA") for _ in range(3)]
        allb = [qb, kb, vb] + kpb + vpb + self.pts
        for b in allb:
            b.r.update(self.act.r); b.w.update(self.act.w)
        osets = [(self.banks[3], self.banks[4]), (self.banks[5], self.banks[6])]
        attn_v = attn.rearrange("(h d) n -> d h n", d=128)
        for kvh in range(4):
            qsrc = qA.rearrange("(h d) (nb q) -> d nb h q", d=128, q=128)[:, :, kvh * 4:(kvh + 1) * 4, :]
            qdst = A[:, QO:QO + 8192].rearrange("p (nb h q) -> p nb h q", nb=16, h=4)
            for hh in range(4):
                P.dma("sync", qdst[:, :, hh, :], qsrc[:, :, hh, :], qb, reads=own_bufs, writes=[qb])
            ksrc = kloc.rearrange("(g v d) n -> d g v n", g=3, v=4)[:, :, kvh, :]
            P.dma("sync", A[:, KO:KO + 6144].rearrange("p (g n) -> p g n", g=3), ksrc, kb, reads=own_bufs, writes=[kb])
            vsrc = vloc.rearrange("(b p) (g v d) -> p g b v d", p=128, g=3, v=4)[:, :, :, kvh, :]
            vdst = A[:, VO:VO + 6144].rearrange("p (g b d) -> p g b d", g=3, b=16)
            for g in range(3):
                P.dma("sync", vdst[:, g], vsrc[:, g], vb, reads=own_bufs, writes=[vb])
            for s in range(3):
                kbase = PO + s * 6144
                vbase = kbase + 3072
                vv = vall.rearrange("(r ih s jl p) (g v d) -> s p g v r ih jl d", r=4, ih=2, s=4, jl=2, p=128, g=3, v=4)[s]
                for g in range(3):
                    r0 = (g * 4 + kvh) * 128
                    base = ((r0 // 256) * 4 + s) * 256 + (r0 % 256)
                    ka = kall[base:base + 128, :].rearrange("d (r j) -> d r j", r=4)
                    if g < 2:
                        P.dma("sync", A[:, kbase + g * 512:kbase + (g + 1) * 512].rearrange("p (r j) -> p r j", r=4),
                              ka[:, :, 384:512], kpb[s], reads=src_bufs, writes=[kpb[s]])
                        P.dma("sync", A[:, vbase + g * 512:vbase + (g + 1) * 512].rearrange("p (r d) -> p r d", r=4),
                              vv[:, g, kvh, :, 1, 1, :], vpb[s], reads=src_bufs, writes=[vpb[s]])
                    else:
                        P.dma("sync", A[:, kbase + 1024:kbase + 3072].rearrange("p (r j) -> p r j", r=4),
                              ka, kpb[s], reads=src_bufs, writes=[kpb[s]])
                        for r in range(4):
                            for ih in range(2):
                                o_ = vbase + 1024 + r * 512 + ih * 256
                                P.dma("sync", A[:, o_:o_ + 256].rearrange("p (jl d) -> p jl d", jl=2),
                                      vv[:, g, kvh, r, ih, :, :], vpb[s], reads=src_bufs, writes=[vpb[s]])
            for r4 in range(4):
                for jb in range(4):
                    blocks = []

                    def add(g, rk, jbk, mi):
                        m4 = [(self.mask_ap(mi), hh * 128, 128) for hh in range(4)]
                        if jbk >= 0:
                            blk = rk * 4 + jbk
                            blocks.append(dict(K=A[:, KO + g * 2048 + blk * 128:KO + g * 2048 + (blk + 1) * 128],
                                               V=[A[:, VO + (g * 16 + blk) * 128:VO + (g * 16 + blk + 1) * 128]],
                                               bufs=[kb, vb], masks=m4))
                        else:
                            jp = jbk + 4
                            for s in range(3):
                                kbase = PO + s * 6144
                                vbase = kbase + 3072
                                if g < 2:
                                    o = g * 512 + rk * 128
                                else:
                                    o = 1024 + (rk * 4 + jp) * 128
                                blocks.append(dict(K=A[:, kbase + o:kbase + o + 128], V=[A[:, vbase + o:vbase + o + 128]],
                                                   bufs=[kpb[s], vpb[s]], masks=m4, bias=self.sel.t[:, s:s + 1]))

                    add(1, r4, jb, 0)
                    add(1, r4, jb - 1, 1)
                    for dj in range(5):
                        add(2, r4, jb - dj, (2, 3, 3, 3, 4)[dj])
                    for rk in range(4):
                        for dj in range(2):
                            add(0, rk, jb - dj, 5 + (r4 - rk + 3) * 2 + dj)
                    nb = r4 * 4 + jb
                    ob, lb = self.rr("oset", osets)
                    self.attn_run(blocks, A[:, QO + nb * 512:QO + (nb + 1) * 512], qb, [ob], lb, scale)
                    rl = self.rr("tmp", self.tmp)
                    P.op("vector", lambda e, rl=rl, lb=lb: e.reciprocal(out=rl.t[:], in_=lb.t[:]), reads=[lb], writes=[rl])
                    o = self.rr("ob", self.ob)
                    P.op("vector", lambda e, o=o, ob=ob, rl=rl: e.tensor_tensor(out=o.t[:], in0=ob.t[:], in1=rl.t[:], op=ALU.mult),
                         reads=[ob, rl], writes=[o])
                    P.dma("sync", attn_v[:, kvh * 4:(kvh + 1) * 4, nb * 128:(nb + 1) * 128],
                          o.t[:].rearrange("p (h q) -> p h q", h=4), o, reads=[o], writes=attn_bufs)
        for b in allb:
            for k, v in b.r.items():
                self.act.r[k] = max(self.act.r.get(k, 0), v)
            for k, v in b.w.items():
                self.act.w[k] = max(self.act.w.get(k, 0), v)

    def lam_setup(self, lam_ap, lam_init):
        P = self.P
        self.lamt = P.sb("lamt", [128, 8], F32)
        self.onesf = P.sb("onesf", [128, 128], F32)
        P.op("vector", lambda e: e.memset(self.onesf.t[:], 1.0), writes=[self.onesf])
        P.dma("sync", self.lamt.t[:, 0:4], lam_ap, self.lamt, writes=[self.lamt])
        L = self.lamt
        P.op("vector", lambda e: e.tensor_tensor(out=L.t[:, 4:5], in0=L.t[:, 0:1], in1=L.t[:, 1:2], op=ALU.mult), reads=[L], writes=[L])
        P.op("vector", lambda e: e.tensor_tensor(out=L.t[:, 5:6], in0=L.t[:, 2:3], in1=L.t[:, 3:4], op=ALU.mult), reads=[L], writes=[L])
        bank = self.banks[7]
        P.op("tensor", lambda e: e.matmul(bank.t[:, 0:2], self.onesf.t[:], L.t[:, 4:6], start=True, stop=True),
             reads=[L, self.onesf], writes=[bank])
        P.op("scalar", lambda e: e.activation(out=L.t[:, 6:8], in_=bank.t[:, 0:2], func=AF.Exp), reads=[bank], writes=[L])
        P.op("vector", lambda e: e.scalar_tensor_tensor(out=L.t[:, 4:5], in0=L.t[:, 7:8], scalar=-lam_init, in1=L.t[:, 6:7],
                                                        op0=ALU.add, op1=ALU.subtract), reads=[L], writes=[L])

    def attn_B(self, qB, kloc, vloc, kall, vall, own_bufs, src_bufs, attn, attn_bufs, gcol, lam_init):
        P = self.P
        scale = 128.0 ** -0.5
        A = self.act.t
        X = self.xn.t
        XF = self.xn.t.bitcast(F32)
        QO, KA, KO, VA = 0, 4096, 20480, 24576
        qb, kob, vob = P.vbuf("qB"), P.vbuf("koB"), P.vbuf("voB")
        kab = [[P.vbuf("kaB") for _ in range(3)] for _ in range(2)]
        vab = [P.vbuf("vaB") for _ in range(3)]
        ocb = [P.vbuf("oc0"), P.vbuf("oc1")]
        dfb = P.vbuf("diff")
        allb = [qb, kob, vob, dfb] + kab[0] + kab[1] + vab + ocb + self.pts
        for b in allb:
            b.r.update(self.act.r); b.w.update(self.act.w)
            b.r.update(self.xn.r); b.w.update(self.xn.w)
        oc_ap = [XF[:, 2048:3072], XF[:, 3072:4096]]
        df_ap = XF[:, 4096:5120]
        o_banks = [self.banks[3], self.banks[4]]
        l_bank = self.banks[5]
        sbank = self.banks[6]
        one_m = 1.0 - lam_init
        P.op("vector", lambda e: e.memset(self.epsc.t[:, 1:2], SUBLN_EPS / (one_m * one_m)), reads=[], writes=[self.epsc])
        attn_v = attn.rearrange("(h c d) n -> h c d n", c=2, d=128)
        for h in range(8):
            qs = qB.rearrange("(h c d) n -> h d c n", c=2, d=128)[h]
            P.dma("sync", A[:, QO:QO + 4096].rearrange("p (c n) -> p c n", c=2), qs, qb, reads=own_bufs, writes=[qb])
            va = vall.rearrange("(i s wl p) (h e) -> h s p i wl e", s=4, wl=2, p=128, e=256)[h]
            for s in range(3):
                for c in range(2):
                    ka = kall.rearrange("(h s c d) n -> h c s d n", s=4, c=2, d=128)[h, c, s]
                    P.dma("sync", A[:, KA + c * 8192 + s * 2048:KA + c * 8192 + (s + 1) * 2048], ka, kab[c][s], reads=src_bufs, writes=[kab[c][s]])
                vd = A[:, VA + s * 4096:VA + (s + 1) * 4096].rearrange("p (i wl e) -> p i wl e", wl=2, e=256)
                for wl in range(2):
                    P.dma("sync", vd[:, :, wl, :], va[s][:, :, wl, :], vab[s], reads=src_bufs, writes=[vab[s]])
            ks = kloc.rearrange("(h c d) n -> h d c n", c=2, d=128)[h]
            P.dma("sync", A[:, KO:KO + 4096].rearrange("p (c n) -> p c n", c=2), ks, kob, reads=own_bufs, writes=[kob])
            vo = vloc.rearrange("(b p) (h e) -> h p b e", p=128, e=256)[h]
            P.dma("sync", X[:, 0:4096].rearrange("p (b e) -> p b e", e=256), vo, vob, reads=own_bufs, writes=[vob])
            for r4 in range(4):
                for c in range(2):
                    blocks = []
                    for s in range(3):
                        for blk in range(16):
                            ko = KA + c * 8192 + s * 2048 + blk * 128
                            vo_ = VA + (s * 16 + blk) * 256
                            blocks.append(dict(K=A[:, ko:ko + 128], V=[A[:, vo_:vo_ + 128], A[:, vo_ + 128:vo_ + 256]],
                                               bufs=[kab[c][s], vab[s]], bias=self.sel.t[:, 3 + s:4 + s]))
                    for jbk in (3, 2, 1, 0):
                        for rk in range(4):
                            blk = rk * 4 + jbk
                            ko = KO + c * 2048 + blk * 128
                            vo_ = blk * 256
                            mi = 0 if rk <= r4 else 19
                            blocks.append(dict(K=A[:, ko:ko + 128], V=[X[:, vo_:vo_ + 128], X[:, vo_ + 128:vo_ + 256]],
                                               bufs=[kob, vob], masks=[(self.mask_ap(mi), jbk * 128, 128)], c0=jbk * 128))
                    q_ap = A[:, QO + c * 2048 + r4 * 512:QO + c * 2048 + (r4 + 1) * 512]
                    self.attn_run(blocks, q_ap, qb, o_banks, l_bank, scale)
                    rl = self.rr("tmp", self.tmp)
                    P.op("vector", lambda e, rl=rl: e.reciprocal(out=rl.t[:], in_=l_bank.t[:]), reads=[l_bank], writes=[rl])
                    for e2 in range(2):
                        P.op("vector", lambda e, c=c, e2=e2, rl=rl: e.tensor_tensor(
                            out=oc_ap[c][:, e2 * 512:(e2 + 1) * 512], in0=o_banks[e2].t[:], in1=rl.t[:], op=ALU.mult),
                            reads=[o_banks[e2], rl], writes=[ocb[c]])
                P.op("vector", lambda e: e.scalar_tensor_tensor(out=df_ap, in0=oc_ap[1], scalar=self.lamt.t[:, 4:5], in1=oc_ap[0],
                                                                op0=ALU.mult, op1=ALU.add), reads=ocb + [self.lamt], writes=[dfb])
                for e2 in range(2):
                    sq = self.rr("sq", self.sq)
                    P.op("scalar", lambda e, sq=sq, e2=e2: e.activation(out=sq.t[:], in_=df_ap[:, e2 * 512:(e2 + 1) * 512], func=AF.Square),
                         reads=[dfb], writes=[sq])
                    P.op("tensor", lambda e, sq=sq, e2=e2: e.matmul(sbank.t[:], self.ones.t[:], sq.t[:], start=(e2 == 0), stop=(e2 == 1)),
                         reads=[sq, self.ones], writes=[sbank], inc=True)
                P.op("scalar", lambda e: e.activation(out=self.rtmp.t[:], in_=sbank.t[:], func=AF.Sqrt, bias=self.epsc.t[:, 1:2],
                                                      scale=1.0 / (256.0 * one_m * one_m)), reads=[sbank, self.epsc], writes=[self.rtmp])
                P.op("vector", lambda e: e.reciprocal(out=self.rstd[0].t[:], in_=self.rtmp.t[:]), reads=[self.rtmp], writes=[self.rstd[0]])
                for e2 in range(2):
                    o = self.rr("ob", self.ob)
                    P.op("vector", lambda e, o=o, e2=e2: e.scalar_tensor_tensor(
                        out=o.t[:], in0=df_ap[:, e2 * 512:(e2 + 1) * 512], scalar=self.vecs.t[:, gcol + e2:gcol + e2 + 1],
                        in1=self.rstd[0].t[:], op0=ALU.mult, op1=ALU.mult), reads=[dfb, self.vecs, self.rstd[0]], writes=[o])
                    P.dma("sync", attn_v[h, e2, :, r4 * 512:(r4 + 1) * 512], o.t[:], o, reads=[o], writes=attn_bufs)
        for b in allb:
            for tgt in (self.act, self.xn):
                for k, v in b.r.items():
                    tgt.r[k] = max(tgt.r.get(k, 0), v)
                for k, v in b.w.items():
                    tgt.w[k] = max(tgt.w.get(k, 0), v)

    def conv_C(self, zT, bgT, zh_all, src_bufs, attn, attn_bufs, wcol):
        P = self.P
        AF32 = self.act.t.bitcast(F32)
        X = self.xn.t
        sets = []
        for i in range(2):
            base = i * 8192
            sets.append(dict(z=AF32[:, base:base + 2048], bg=AF32[:, base + 2048:base + 4096], acc=AF32[:, base + 4096:base + 6144],
                             hin=AF32[:, base + 6144:base + 6144 + 24], hal=AF32[:, base + 6200:base + 6202],
                             out=X[:, i * 2048:(i + 1) * 2048],
                             zb=P.vbuf("cz"), bb=P.vbuf("cbg"), ab=P.vbuf("cacc"), hb=P.vbuf("chin"), ob=P.vbuf("cout")))
        allb = [s[k] for s in sets for k in ("zb", "bb", "ab", "hb", "ob")]
        for b in allb:
            b.r.update(self.act.r); b.w.update(self.act.w)
            b.r.update(self.xn.r); b.w.update(self.xn.w)
        zd = zT.rearrange("(k p) n -> k p n", p=128)
        bd = bgT.rearrange("(k p) n -> k p n", p=128)
        ad = attn.rearrange("(k p) n -> k p n", p=128)
        zh = zh_all.rearrange("(s k p) e -> k p s e", s=4, p=128)
        for c in range(KC):
            S = sets[c % 2]
            P.dma("gpsimd", S["z"], zd[c], S["zb"], reads=src_bufs, writes=[S["zb"]])
            P.dma("gpsimd", S["bg"], bd[c], S["bb"], reads=src_bufs, writes=[S["bb"]])
            P.dma("sync", S["hin"].rearrange("p (s e) -> p s e", s=3), zh[c][:, 0:3, :], S["hb"], reads=src_bufs, writes=[S["hb"]])
            hin, hal = S["hin"], S["hal"]
            P.op("vector", lambda e, hin=hin, hal=hal: e.tensor_scalar(out=hal, in0=hin[:, 0:2], scalar1=self.sel.t[:, 7:8], scalar2=None, op0=ALU.mult),
                 reads=[S["hb"], self.sel], writes=[S["hb"]])
            for s in (1, 2):
                P.op("vector", lambda e, hin=hin, hal=hal, s=s: e.scalar_tensor_tensor(
                    out=hal, in0=hin[:, 8 * s:8 * s + 2], scalar=self.sel.t[:, 7 + s:8 + s], in1=hal, op0=ALU.mult, op1=ALU.add),
                    reads=[S["hb"], self.sel], writes=[S["hb"]])
            w0 = self.vecs.t[:, wcol + c:wcol + c + 1]
            w1 = self.vecs.t[:, wcol + KC + c:wcol + KC + c + 1]
            w2 = self.vecs.t[:, wcol + 2 * KC + c:wcol + 2 * KC + c + 1]
            z, acc = S["z"], S["acc"]
            rd = [S["zb"], S["hb"], self.vecs]
            P.op("vector", lambda e, z=z, acc=acc, w2=w2: e.tensor_scalar(out=acc, in0=z, scalar1=w2, scalar2=None, op0=ALU.mult),
                 reads=rd, writes=[S["ab"]])

            def fma(dst, src, w):
                P.op("vector", lambda e, dst=dst, src=src, w=w: e.scalar_tensor_tensor(out=dst, in0=src, scalar=w, in1=dst, op0=ALU.mult, op1=ALU.add),
                     reads=rd + [S["ab"]], writes=[S["ab"]])

            for r4 in range(4):
                a = acc[:, r4 * 512:(r4 + 1) * 512]
                if r4 >= 1:
                    fma(a, z[:, (r4 - 1) * 512:r4 * 512], w1)
                else:
                    fma(a[:, 1:512], z[:, 3 * 512:3 * 512 + 511], w1)
                    fma(a[:, 0:1], S["hal"][:, 1:2], w1)
                if r4 >= 2:
                    fma(a, z[:, (r4 - 2) * 512:(r4 - 1) * 512], w0)
                else:
                    fma(a[:, 1:512], z[:, (r4 + 2) * 512:(r4 + 2) * 512 + 511], w0)
                    fma(a[:, 0:1], S["hal"][:, r4:r4 + 1], w0)
            P.op("vector", lambda e, S=S: e.tensor_tensor(out=S["out"], in0=S["acc"], in1=S["bg"], op=ALU.mult),
                 reads=[S["ab"], S["bb"]], writes=[S["ob"]])
            P.dma("sync", ad[c], S["out"], S["ob"], reads=[S["ob"]], writes=attn_bufs)
        for b in allb:
            for tgt in (self.act, self.xn):
                for k, v in b.r.items():
                    tgt.r[k] = max(tgt.r.get(k, 0), v)
                for k, v in b.w.items():
                    tgt.w[k] = max(tgt.w.get(k, 0), v)


NV = 400
VL = 80
V_NORMF = 320
V_SUBLN = 336
V_CONV = 338
NMASK = 20
GROUPS = [[0, 1, 2, 3], [4, 5, 6, 7]]
LAM_INIT = 0.8 - 0.6 * float(np.exp(-0.3 * 1))

W_SHAPES = {}
for _i in range(4):
    W_SHAPES["w1i%d" % _i] = (D, 2 * DFF)
    W_SHAPES["w1o%d" % _i] = (DFF, D)
    W_SHAPES["w2i%d" % _i] = (D, 2 * DFF)
    W_SHAPES["w2o%d" % _i] = (DFF, D)
    W_SHAPES["wpg%d" % _i] = (D, D)
    W_SHAPES["wpp%d" % _i] = (PLE, D)
for _j in range(2):
    W_SHAPES["aqkv%d" % _j] = (D, 5120)
    W_SHAPES["ao%d" % _j] = (D, D)
W_SHAPES.update(bqkv=(D, 3 * D), bo=(D, D), cin=(D, 3 * D), cout=(D, D))


def build_net(stop=None, dbg=None):
    P = Prog()
    xT = P.dram_in("xT", [D, TOK])
    pT = [P.dram_in("pT%d" % i, [PLE, TOK]) for i in range(4)]
    vecs = P.dram_in("vecs", [128, NV])
    masks = P.dram_in("masks", [128, NMASK * 128])
    ident = P.dram_in("ident", [128, 128])
    sel = P.dram_in("sel", [128, 16])
    lamT = P.dram_in("lamT", [128, 4])
    class _LazyW(dict):
        def __missing__(self, k):
            self[k] = P.dram_in(k, list(W_SHAPES[k]))
            return self[k]
    W = _LazyW()
    outT = P.dram_out("outT", [D, TOK])
    h = P.dram_tmp("h", [D, TOK])
    q = P.dram_tmp("q", [D, TOK], BF16)
    klA = P.dram_tmp("klA", [1536, TOK], BF16)
    vlA = P.dram_tmp("vlA", [TOK, 1536], BF16)
    kaA = P.dram_tmp("kaA", [4 * 1536, TOK], BF16)
    vaA = P.dram_tmp("vaA", [4 * TOK, 1536], BF16)
    klB = P.dram_tmp("klB", [D, TOK], BF16)
    vlB = P.dram_tmp("vlB", [TOK, D], BF16)
    kaB = P.dram_tmp("kaB", [4 * D, TOK], BF16)
    vaB = P.dram_tmp("vaB", [4 * TOK, D], BF16)
    zT = P.dram_tmp("zT", [D, TOK])
    bgT = P.dram_tmp("bgT", [D, TOK])
    zhl = P.dram_tmp("zhl", [D, 8])
    zha = P.dram_tmp("zha", [4 * D, 8])
    attn = P.dram_tmp("attn", [D, TOK], BF16)

    R = Rows(P, vecs, NV)
    R.init_attn(masks, NMASK, sel, ident)
    R.lam_setup(lamT, LAM_INIT)

    xb = [P.vbuf("x") for _ in range(4)]
    hb = [P.vbuf("h") for _ in range(4)]
    ob = [P.vbuf("out")]
    P.out_bufs += ob
    qb, klb, vlb, kab, vab, atb = (P.vbuf(n) for n in ("q", "kl", "vl", "ka", "va", "attn"))
    zb, zhlb, zhab = P.vbuf("z"), P.vbuf("zhl"), P.vbuf("zha")
    ccb = P.vbuf("cc")

    def cc(in_ap, out_ap, in_bufs, out_bufs):
        E = P.engs["gpsimd"]
        if ccb.sem is None:
            ccb.sem = P.new_sem("cc")
        waits = P._collect(E, in_bufs, out_bufs)
        for s_, v_ in waits.items():
            E.waited[s_] = v_
        ccb.cnt += 1
        E.ops.append((sorted(waits.items()), lambda e, i=in_ap, o=out_ap: e.collective_compute(
            "AllGather", ALU.bypass, replica_groups=GROUPS, ins=[i], outs=[o]), (ccb.sem, 1)))
        P._mark((ccb.sem, ccb.cnt), in_bufs, out_bufs)

    def gather(loc, allt, rows, lb, ab):
        for i in range(rows // 256):
            cc(loc[i * 256:(i + 1) * 256, :], allt[i * 1024:(i + 1) * 1024, :], [lb], [ab])

    state = {"hsrc": xT, "hsb": xb}

    def hs():
        return state["hsrc"], state["hsb"]

    def wrote_h():
        state["hsrc"], state["hsb"] = h, hb

    def chain(fns):
        for f in fns:
            for half in range(2):
                f(half)
            if getattr(f, "writes_h", False):
                wrote_h()

    def mk(fn, writes_h=False):
        fn.writes_h = writes_h
        return fn

    def s_ffn(i, which):
        g = i * VL + (0 if which == 1 else 32)
        wi, wo = W["w%di%d" % (which, i)], W["w%do%d" % (which, i)]

        def f(half):
            src, sb_ = hs()
            R.norm(src, sb_, half, g)
            R.ffn(src, sb_, h, hb, half, wi, wo)
        return mk(f, True)

    def s_ple(i):
        def f(half):
            src, sb_ = hs()
            R.norm(src, sb_, half, i * VL + 48)
            R.ple(src, sb_, h, hb, half, W["wpg%d" % i], i * VL + 64, pT[i], W["wpp%d" % i])
        return mk(f, True)

    def s_oproj(w):
        def f(half):
            src, sb_ = hs()
            R.load_xn(attn, [atb], half)
            R.oproj(src, sb_, h, hb, half, w)
        return mk(f, True)

    def s_projA(i, j):
        w = W["aqkv%d" % j]

        def f(half):
            src, sb_ = hs()
            R.norm(src, sb_, half, i * VL + 16)
            kv = klA.rearrange("(c p) n -> c p n", p=128)
            for g in range(3):
                R.proj_fm(half, w, D + g * 1024, 4,
                          lambda c, jj, g=g: kv[g * 4 + c, :, half * HALF + jj * TT:half * HALF + (jj + 1) * TT], [klb])
                for grp in range(2):
                    R.proj_tm(half, w, D + g * 1024 + 512 + grp * 256, 256,
                              lambda tb, g=g, grp=grp: vlA[half * HALF + tb * 128:half * HALF + (tb + 1) * 128,
                                                           g * 512 + grp * 256:g * 512 + (grp + 1) * 256], [vlb])
        return mk(f)

    def s_projB(i):
        w = W["bqkv"]

        def f(half):
            src, sb_ = hs()
            R.norm(src, sb_, half, i * VL + 16)
            kv = klB.rearrange("(c p) n -> c p n", p=128)
            R.proj_fm(half, w, D, 16, lambda c, jj: kv[c, :, half * HALF + jj * TT:half * HALF + (jj + 1) * TT], [klb])
            for hh in range(8):
                R.proj_tm(half, w, 2 * D + hh * 256, 256,
                          lambda tb, hh=hh: vlB[half * HALF + tb * 128:half * HALF + (tb + 1) * 128, hh * 256:(hh + 1) * 256], [vlb])
        return mk(f)

    def q_stage(i, w):
        def run():
            qv = q.rearrange("(c p) n -> c p n", p=128)
            for half in (1, 0):
                src, sb_ = hs()
                if half == 0:
                    R.norm(src, sb_, half, i * VL + 16)
                R.proj_fm(half, w, 0, 16, lambda c, jj, half=half: qv[c, :, half * HALF + jj * TT:half * HALF + (jj + 1) * TT], [qb])
        return run

    def s_projC(i):
        def f(half):
            src, sb_ = hs()
            R.norm(src, sb_, half, i * VL + 16)
            R.proj_conv_in(half, W["cin"], zT, bgT, [zb])
        return mk(f)

    def s_final():
        def f(half):
            src, sb_ = hs()
            R.final_norm(src, sb_, outT, ob, half, V_NORMF)
        return mk(f)

    def halo():
        P.dma("sync", zhl[:, 0:1], zT[:, 2 * 512 + 511:2 * 512 + 512], zhlb, reads=[zb], writes=[zhlb], allow_slow_non_contiguous=True)
        P.dma("sync", zhl[:, 1:2], zT[:, 3 * 512 + 511:3 * 512 + 512], zhlb, reads=[zb], writes=[zhlb], allow_slow_non_contiguous=True)
        cc(zhl, zha, [zhlb], [zhab])

    def gA():
        gather(klA, kaA, 1536, klb, kab)
        gather(vlA, vaA, TOK, vlb, vab)

    def gB():
        gather(klB, kaB, D, klb, kab)
        gather(vlB, vaB, TOK, vlb, vab)

    mixA = lambda: R.attn_A(q, klA, vlA, kaA, vaA, [qb, klb, vlb], [kab, vab], attn, [atb])
    mixB = lambda: R.attn_B(q, klB, vlB, kaB, vaB, [qb, klb, vlb], [kab, vab], attn, [atb], V_SUBLN, LAM_INIT)
    mixC = lambda: R.conv_C(zT, bgT, zha, [zb, zhab], attn, [atb], V_CONV)
    C1 = lambda th: (lambda: chain([th()]))
    stages = [
        C1(lambda: s_ffn(0, 1)), C1(lambda: s_projA(0, 0)), gA, (lambda: q_stage(0, W["aqkv0"])()), mixA, C1(lambda: s_oproj(W["ao0"])), C1(lambda: s_ffn(0, 2)), C1(lambda: s_ple(0)),
        C1(lambda: s_ffn(1, 1)), C1(lambda: s_projB(1)), gB, (lambda: q_stage(1, W["bqkv"])()), mixB, C1(lambda: s_oproj(W["bo"])), C1(lambda: s_ffn(1, 2)), C1(lambda: s_ple(1)),
        C1(lambda: s_ffn(2, 1)), C1(lambda: s_projC(2)), halo, mixC, C1(lambda: s_oproj(W["cout"])), C1(lambda: s_ffn(2, 2)), C1(lambda: s_ple(2)),
        C1(lambda: s_ffn(3, 1)), C1(lambda: s_projA(3, 1)), gA, (lambda: q_stage(3, W["aqkv1"])()), mixA, C1(lambda: s_oproj(W["ao1"])), C1(lambda: s_ffn(3, 2)), C1(lambda: s_ple(3)),
        C1(lambda: s_final()),
    ]
    n = len(stages) if stop is None else stop
    for st in stages[:n]:
        st()
    if stop is not None:
        if dbg is None:
            P.dma("sync", outT, h, ob[0], reads=hb, writes=ob)
        else:
            src = {"attn": attn, "q": q, "klA": klA, "vlA": vlA, "klB": klB, "vlB": vlB}[dbg]
            dbo = P.dram_out("dbg", list(src.shape), BF16)
            P.dma("sync", dbo, src, ob[0], reads=[atb, qb, klb, vlb], writes=ob)
            P.dma("sync", outT, h, ob[0], reads=hb, writes=ob)
    nc = P.emit()
    nc._in_names = list(P.in_names)
    return nc


def _r4perm():
    n = np.arange(TOK)
    return 4 * (n % 512) + n // 512


def _masks():
    jk = np.arange(128)[:, None]
    jq = np.arange(128)[None, :]
    vis = []
    vis.append(jk <= jq)
    vis.append(jk >= jq)
    comb = ((jq - jk) % 4) == 0
    vis.append((jk <= jq) & comb)
    vis.append(comb | (jk < -1))
    vis.append((jk >= jq) & comb)
    for dr in range(-3, 4):
        for djb in range(2):
            dt = 4 * (128 * djb + jq - jk) + dr
            vis.append((dt >= 0) & (dt <= 128))
    vis.append(jk < jq)
    m = np.concatenate([np.where(v, 0.0, NEG) for v in vis], axis=1).astype(np.float32)
    assert m.shape == (128, NMASK * 128)
    return m


def _col16(v):
    return np.ascontiguousarray(np.asarray(v, np.float32).reshape(-1, 128).T)


def make_inputs(inputs):
    perm = _r4perm()
    x = np.asarray(inputs["x"], np.float32)
    p = np.asarray(inputs["p"], np.float32)
    vecs = np.zeros((128, NV), np.float32)
    for i in range(4):
        b = i * VL
        vecs[:, b:b + 16] = _col16(inputs["norm_ffn1"][i])
        vecs[:, b + 16:b + 32] = _col16(inputs["norm_mix"][i])
        vecs[:, b + 32:b + 48] = _col16(inputs["norm_ffn2"][i])
        vecs[:, b + 48:b + 64] = _col16(inputs["norm_ple"][i])
        vecs[:, b + 64:b + 80] = _col16(inputs["b_ple_gate"][i])
    vecs[:, V_NORMF:V_NORMF + 16] = _col16(inputs["norm_f"])
    vecs[:, V_SUBLN:V_SUBLN + 2] = _col16(inputs["b_subln"][0])
    for k in range(3):
        vecs[:, V_CONV + 16 * k:V_CONV + 16 * (k + 1)] = _col16(inputs["c_conv_w"][0][k])
    shared = {"vecs": vecs, "masks": _masks(), "ident": np.eye(128, dtype=np.float32),
              "lamT": np.ascontiguousarray(np.asarray(inputs["b_lambda"][0], np.float32).T)}
    for i in range(4):
        shared["w1i%d" % i] = np.asarray(inputs["w_ffn1_in"][i], np.float32)
        shared["w1o%d" % i] = np.asarray(inputs["w_ffn1_out"][i], np.float32)
        shared["w2i%d" % i] = np.asarray(inputs["w_ffn2_in"][i], np.float32)
        shared["w2o%d" % i] = np.asarray(inputs["w_ffn2_out"][i], np.float32)
        shared["wpg%d" % i] = np.asarray(inputs["w_ple_gate"][i], np.float32)
        shared["wpp%d" % i] = np.asarray(inputs["w_ple_proj"][i], np.float32)
    for j in range(2):
        shared["aqkv%d" % j] = np.asarray(inputs["a_w_qkv"][j], np.float32)
        shared["ao%d" % j] = np.asarray(inputs["a_w_o"][j], np.float32)
    shared["bqkv"] = np.asarray(inputs["b_w_qkv"][0], np.float32)
    shared["bo"] = np.asarray(inputs["b_w_o"][0], np.float32)
    shared["cin"] = np.asarray(inputs["c_w_in"][0], np.float32)
    shared["cout"] = np.asarray(inputs["c_w_out"][0], np.float32)
    maps = []
    for core in range(NCORES):
        b, c = divmod(core, 4)
        sl = slice(c * TOK, (c + 1) * TOK)
        m = dict(shared)
        m["xT"] = np.ascontiguousarray(x[b, sl][perm].T)
        for i in range(4):
            m["pT%d" % i] = np.ascontiguousarray(p[i, b, sl][perm].T)
        s = np.zeros((128, 16), np.float32)
        for k in range(3):
            s[:, k] = 0.0 if k == c - 1 else NEG
            s[:, 7 + k] = 1.0 if k == c - 1 else 0.0
        for k in range(4):
            s[:, 3 + k] = 0.0 if k < c else NEG
        m["sel"] = s
        maps.append(m)
    return maps


def assemble(results, key="outT"):
    perm = _r4perm()
    out = np.empty((2, 4 * TOK, D), np.float32)
    for core in range(NCORES):
        b, c = divmod(core, 4)
        o = np.asarray(results[core][key]).astype(np.float32).T
        blk = np.empty_like(o)
        blk[perm] = o
        out[b, c * TOK:(c + 1) * TOK] = blk
    return out


_NC_CACHE = {}


def kernel(**inputs):
    if "net" not in _NC_CACHE:
        _NC_CACHE["net"] = build_net()
    nc = _NC_CACHE["net"]
    maps = [{k: m[k] for k in nc._in_names} for m in make_inputs(inputs)]
    res = run_bass_kernel_spmd(nc, maps, core_ids=list(range(NCORES)))
    return assemble(res.results)
```

```python
import contextlib
import numpy as np
import ml_dtypes
import concourse.bass as bass
import concourse.mybir as mybir
from concourse.bass_utils import run_bass_kernel_spmd

F32 = mybir.dt.float32
BF16 = mybir.dt.bfloat16
AF = mybir.ActivationFunctionType
ALU = mybir.AluOpType

D = 2048
KC = 16
TOK = 2048
HALF = 1024
TT = 512
DFF = 5632
FS = 44
PLE = 256
NCORES = 8
NEG = -30000.0
RMS_EPS = 1e-6
SUBLN_EPS = 1e-5


class Buf:
    __slots__ = ("name", "w", "r", "sem", "cnt", "t")

    def __init__(self, name, t=None):
        self.name = name
        self.w = {}
        self.r = {}
        self.sem = None
        self.cnt = 0
        self.t = t


class Eng:
    def __init__(self, name):
        self.name = name
        self.ops = []
        self.sem = None
        self.cnt = 0
        self.waited = {}


class Prog:
    def __init__(self):
        self.nc = bass.Bass("TRN2", target_bir_lowering=False)
        self.es = contextlib.ExitStack()
        self.engs = {n: Eng(n) for n in ("tensor", "vector", "scalar", "gpsimd", "sync")}
        self.semh = {}
        self.nsem = 0
        for n in ("tensor", "vector", "scalar"):
            self.engs[n].sem = self.new_sem("e_" + n)
        self.out_bufs = []
        self.in_names = []
        self.nbuf = 0

    def new_sem(self, name=None):
        self.nsem += 1
        h = self.es.enter_context(self.nc.semaphore(name or ("s%d" % self.nsem)))
        self.semh[self.nsem] = h
        return self.nsem

    def sb(self, name, shape, dtype):
        t = self.es.enter_context(self.nc.sbuf_tensor("sb_" + name, list(shape), dtype))
        return Buf(name, t)

    def ps(self, name):
        t = self.es.enter_context(self.nc.psum_tensor("ps_" + name, [128, 512], F32))
        return Buf(name, t)

    def dram_in(self, name, shape, dtype=F32):
        self.in_names.append(name)
        return self.nc.dram_tensor(name, list(shape), dtype, kind="ExternalInput").ap()

    def dram_out(self, name, shape, dtype=F32):
        return self.nc.dram_tensor(name, list(shape), dtype, kind="ExternalOutput").ap()

    def dram_tmp(self, name, shape, dtype=F32):
        return self.nc.dram_tensor(name, list(shape), dtype).ap()

    def vbuf(self, name):
        self.nbuf += 1
        return Buf("%s_%d" % (name, self.nbuf))

    def _collect(self, E, reads, writes, extra=()):
        waits = {}

        def need(sem, val):
            if E.waited.get(sem, 0) >= val:
                return
            if waits.get(sem, 0) < val:
                waits[sem] = val

        for b in reads:
            for sem, val in b.w.items():
                need(sem, val)
        for b in writes:
            for sem, val in b.w.items():
                need(sem, val)
            for sem, val in b.r.items():
                need(sem, val)
        for sem, val in extra:
            need(sem, val)
        return waits

    @staticmethod
    def _mark(ev, reads, writes):
        sem, val = ev
        for b in reads:
            if b.r.get(sem, 0) < val:
                b.r[sem] = val
        for b in writes:
            if b.w.get(sem, 0) < val:
                b.w[sem] = val

    def op(self, eng, fn, reads=(), writes=(), inc=True):
        E = self.engs[eng]
        waits = self._collect(E, reads, writes)
        if eng == "tensor":
            waits.pop(E.sem, None)
        for sem, val in waits.items():
            E.waited[sem] = val
        if inc:
            E.cnt += 1
            ev = (E.sem, E.cnt)
        else:
            ev = (E.sem, E.cnt + 1)
        E.ops.append((sorted(waits.items()), fn, (E.sem, 1) if inc else None))
        self._mark(ev, reads, writes)

    def dma(self, eng, out, in_, owner, reads=(), writes=(), **kw):
        E = self.engs[eng]
        if owner.sem is None:
            owner.sem = self.new_sem("d_" + owner.name)
        waits = self._collect(E, reads, writes, extra=[(owner.sem, owner.cnt)] if owner.cnt else [])
        for sem, val in waits.items():
            E.waited[sem] = val
        owner.cnt += 16
        ev = (owner.sem, owner.cnt)
        E.ops.append((sorted(waits.items()), lambda e, o=out, i=in_: e.dma_start(out=o, in_=i, **kw), (owner.sem, 16)))
        self._mark(ev, reads, writes)

    def barrier_wait(self, eng, bufs):
        E = self.engs[eng]
        waits = self._collect(E, [], bufs)
        for sem, val in waits.items():
            E.waited[sem] = val
        if waits:
            E.ops.append((sorted(waits.items()), None, None))

    def check(self):
        sems = {}
        pos = {n: 0 for n in self.engs}
        progress = True
        while progress:
            progress = False
            for n, E in self.engs.items():
                while pos[n] < len(E.ops):
                    waits, fn, inc = E.ops[pos[n]]
                    if any(sems.get(s, 0) < v for s, v in waits):
                        break
                    if inc is not None:
                        sems[inc[0]] = sems.get(inc[0], 0) + inc[1]
                    pos[n] += 1
                    progress = True
        stuck = {n: (pos[n], len(E.ops)) for n, E in self.engs.items() if pos[n] < len(E.ops)}
        if stuck:
            msg = []
            for n in stuck:
                waits, fn, inc = self.engs[n].ops[pos[n]]
                msg.append("%s@%d waits %s have %s" % (n, pos[n], waits, [(s, sems.get(s, 0)) for s, _ in waits]))
            raise RuntimeError("DEADLOCK: " + "; ".join(msg))
        return {n: len(E.ops) for n, E in self.engs.items()}

    def emit(self):
        self.barrier_wait("sync", self.out_bufs)
        self.check()
        nc = self.nc
        semh = self.semh

        def replay(e, E):
            for waits, fn, inc in E.ops:
                for sem, val in waits:
                    e.wait_ge(semh[sem], val)
                if fn is None:
                    continue
                inst = fn(e)
                if inc is not None:
                    inst.then_inc(semh[inc[0]], inc[1])

        with nc.Block() as block:
            @block.tensor
            def _(e):
                replay(e, self.engs["tensor"])

            @block.vector
            def _(e):
                replay(e, self.engs["vector"])

            @block.scalar
            def _(e):
                replay(e, self.engs["scalar"])

            @block.gpsimd
            def _(e):
                replay(e, self.engs["gpsimd"])

            @block.sync
            def _(e):
                replay(e, self.engs["sync"])
        self.es.close()
        return nc


class Rows:
    def __init__(self, P, vecs_ap, nvec):
        self.P = P
        self.act = P.sb("act", [128, FS * HALF], BF16)
        self.xn = P.sb("xn", [128, KC * HALF], BF16)
        self.win = [P.sb("win%d" % i, [128, KC * 128], BF16) for i in range(4)]
        self.wout = [P.sb("wout%d" % i, [128, FS * 128], BF16) for i in range(2)]
        self.ht = [P.sb("ht%d" % i, [128, TT], F32) for i in range(6)]
        self.tmp = [P.sb("tmp%d" % i, [128, TT], F32) for i in range(3)]
        self.sq = [P.sb("sq%d" % i, [128, TT], BF16) for i in range(2)]
        self.rstd = [P.sb("rstd%d" % i, [128, TT], F32) for i in range(2)]
        self.epsc = P.sb("epsc", [128, 2], F32)
        self.rtmp = P.sb("rtmp", [128, TT], F32)
        self.ob = [P.sb("ob%d" % i, [128, TT], BF16) for i in range(4)]
        self.ones = P.sb("ones", [128, 128], BF16)
        self.vecs = P.sb("vecs", [128, nvec], F32)
        self.pt = P.sb("pt", [128, 2 * HALF], BF16)
        self.wp = [P.sb("wp%d" % i, [128, 2 * 128], BF16) for i in range(2)]
        self.banks = [P.ps("bank%d" % i) for i in range(8)]
        self.cnt = {}
        P.op("vector", lambda e: e.memset(self.ones.t[:], 1.0), writes=[self.ones])
        P.op("vector", lambda e: e.memset(self.epsc.t[:, 0:1], RMS_EPS), writes=[self.epsc])
        P.dma("sync", self.vecs.t[:], vecs_ap, self.vecs, writes=[self.vecs])

    def rr(self, key, lst):
        i = self.cnt.get(key, 0)
        self.cnt[key] = i + 1
        return lst[i % len(lst)]

    def xs(self, k, j):
        return self.xn.t[:, k * HALF + j * TT:k * HALF + (j + 1) * TT]

    def stats(self, hsrc, hsrc_bufs, n0, j, eps, dim=D):
        P = self.P
        hs = hsrc.rearrange("(k p) n -> k p n", p=128)
        bank = self.banks[6 + j]
        for k in range(KC):
            ht = self.rr("ht", self.ht)
            P.dma("gpsimd", ht.t[:], hs[k, :, n0:n0 + TT], ht, reads=[hsrc_bufs[n0 // TT]], writes=[ht])
            sq = self.rr("sq", self.sq)
            P.op("scalar", lambda e, s=sq, t=ht: e.activation(out=s.t[:], in_=t.t[:], func=AF.Square),
                 reads=[ht], writes=[sq])
            P.op("tensor", lambda e, s=sq, k=k: e.matmul(bank.t[:], self.ones.t[:], s.t[:], start=(k == 0), stop=(k == KC - 1)),
                 reads=[sq, self.ones], writes=[bank], inc=True)
        P.op("scalar", lambda e: e.activation(out=self.rtmp.t[:], in_=bank.t[:], func=AF.Sqrt, bias=self.epsc.t[:, 0:1], scale=1.0 / dim),
             reads=[bank, self.epsc], writes=[self.rtmp])
        P.op("vector", lambda e: e.reciprocal(out=self.rstd[j].t[:], in_=self.rtmp.t[:]), reads=[self.rtmp], writes=[self.rstd[j]])

    def norm(self, hsrc, hsrc_bufs, half, gcol, eps=RMS_EPS):
        P = self.P
        hs = hsrc.rearrange("(k p) n -> k p n", p=128)
        for j in range(2):
            n0 = half * HALF + j * TT
            bank = self.banks[6 + j]
            for k in range(KC):
                ht = self.rr("ht", self.ht)
                P.dma("gpsimd", ht.t[:], hs[k, :, n0:n0 + TT], ht, reads=[hsrc_bufs[n0 // TT]], writes=[ht])
                sq = self.rr("sq", self.sq)
                P.op("scalar", lambda e, s=sq, t=ht: e.activation(out=s.t[:], in_=t.t[:], func=AF.Square),
                     reads=[ht], writes=[sq])
                P.op("tensor", lambda e, s=sq, k=k, bank=bank: e.matmul(bank.t[:], self.ones.t[:], s.t[:], start=(k == 0), stop=(k == KC - 1)),
                     reads=[sq, self.ones], writes=[bank], inc=True)
                P.op("vector", lambda e, k=k, j=j, t=ht: e.tensor_scalar(
                    out=self.xs(k, j), in0=t.t[:], scalar1=self.vecs.t[:, gcol + k:gcol + k + 1], scalar2=None, op0=ALU.mult),
                    reads=[ht, self.vecs], writes=[self.xn])
            P.op("scalar", lambda e, bank=bank: e.activation(out=self.rtmp.t[:], in_=bank.t[:], func=AF.Sqrt, bias=self.epsc.t[:, 0:1], scale=1.0 / D),
                 reads=[bank, self.epsc], writes=[self.rtmp])
            P.op("vector", lambda e, j=j: e.reciprocal(out=self.rstd[j].t[:], in_=self.rtmp.t[:]), reads=[self.rtmp], writes=[self.rstd[j]])
        for j in range(2):
            for k in range(KC):
                P.op("vector", lambda e, k=k, j=j: e.tensor_tensor(out=self.xs(k, j), in0=self.xs(k, j), in1=self.rstd[j].t[:], op=ALU.mult),
                     reads=[self.rstd[j], self.xn], writes=[self.xn])

    def load_w(self, slot, w_ap, c0, ncol, kchunks, r0=0):
        src = w_ap[r0:r0 + kchunks * 128, :].rearrange("(k p) c -> p k c", p=128)[:, :, c0:c0 + ncol]
        dst = slot.t[:, 0:kchunks * ncol].rearrange("p (k c) -> p k c", k=kchunks)
        self.P.dma("gpsimd", dst, src, slot, writes=[slot])

    def mm_group(self, bank, lhs_fn, rhs_fn, nk, reads, ncol=TT):
        for k in range(nk):
            self.P.op("tensor", lambda e, k=k: e.matmul(bank.t[:, 0:ncol], lhs_fn(k), rhs_fn(k), start=(k == 0), stop=(k == nk - 1)),
                      reads=reads, writes=[bank], inc=(k == nk - 1))

    def add_store(self, bank, scale, hsrc, hsrc_bufs, hdst, hdst_bufs, oc, n0, mul_by=None):
        P = self.P
        hs = hsrc.rearrange("(k p) n -> k p n", p=128)
        hd = hdst.rearrange("(k p) n -> k p n", p=128)
        ht = self.rr("ht", self.ht)
        P.dma("gpsimd", ht.t[:], hs[oc, :, n0:n0 + TT], ht, reads=[hsrc_bufs[n0 // TT]], writes=[ht])
        if mul_by is None:
            P.op("vector", lambda e, t=ht, b=bank: e.scalar_tensor_tensor(
                out=t.t[:], in0=b.t[:], scalar=scale, in1=t.t[:], op0=ALU.mult, op1=ALU.add), reads=[bank, ht], writes=[ht])
        else:
            P.op("vector", lambda e, m=mul_by, b=bank: e.tensor_tensor(out=m.t[:], in0=m.t[:], in1=b.t[:], op=ALU.mult),
                 reads=[bank, mul_by], writes=[mul_by])
            P.op("vector", lambda e, t=ht, m=mul_by: e.tensor_tensor(out=t.t[:], in0=t.t[:], in1=m.t[:], op=ALU.add),
                 reads=[mul_by, ht], writes=[ht])
        P.dma("sync", hd[oc, :, n0:n0 + TT], ht.t[:], ht, reads=[ht], writes=[hdst_bufs[n0 // TT]])

    def ffn(self, hsrc, hsrc_bufs, hdst, hdst_bufs, half, w_in, w_out):
        P = self.P
        for s in range(FS):
            wg = self.rr("win", self.win)
            wu = self.rr("win", self.win)
            self.load_w(wg, w_in, s * 128, 128, KC)
            self.load_w(wu, w_in, DFF + s * 128, 128, KC)
            for j in range(2):
                ba = self.rr("bA", self.banks[0:2])
                bb = self.rr("bB", self.banks[2:4])
                self.mm_group(ba, lambda k, w=wg: w.t[:, k * 128:(k + 1) * 128], lambda k, j=j: self.xs(k, j), KC, [wg, self.xn])
                self.mm_group(bb, lambda k, w=wu: w.t[:, k * 128:(k + 1) * 128], lambda k, j=j: self.xs(k, j), KC, [wu, self.xn])
                tmp = self.rr("tmp", self.tmp)
                P.op("scalar", lambda e, t=tmp, b=ba: e.activation(out=t.t[:], in_=b.t[:], func=AF.Silu), reads=[ba], writes=[tmp])
                P.op("vector", lambda e, t=tmp, b=bb, s=s, j=j: e.tensor_tensor(
                    out=self.act.t[:, s * HALF + j * TT:s * HALF + (j + 1) * TT], in0=t.t[:], in1=b.t[:], op=ALU.mult),
                    reads=[tmp, bb], writes=[self.act])
        for oc in range(KC):
            wo = self.rr("wout", self.wout)
            self.load_w(wo, w_out, oc * 128, 128, FS)
            for j in range(2):
                n0 = half * HALF + j * TT
                bo = self.rr("bO", self.banks[4:6])
                self.mm_group(bo, lambda k, w=wo: w.t[:, k * 128:(k + 1) * 128],
                              lambda k, j=j: self.act.t[:, k * HALF + j * TT:k * HALF + (j + 1) * TT], FS, [wo, self.act])
                self.add_store(bo, 0.5, hsrc, hsrc_bufs, hdst, hdst_bufs, oc, n0)

    def ple(self, hsrc, hsrc_bufs, hdst, hdst_bufs, half, w_gate, bcol, pT, w_proj):
        P = self.P
        ptv = pT.rearrange("(k p) n -> p k n", p=128)[:, :, half * HALF:(half + 1) * HALF]
        P.dma("gpsimd", self.pt.t[:].rearrange("p (k n) -> p k n", k=2), ptv, self.pt, writes=[self.pt])
        for oc in range(KC):
            wg = self.rr("win", self.win)
            self.load_w(wg, w_gate, oc * 128, 128, KC)
            wp = self.rr("wp", self.wp)
            self.load_w(wp, w_proj, oc * 128, 128, 2)
            for j in range(2):
                n0 = half * HALF + j * TT
                ba = self.rr("bA", self.banks[0:2])
                bb = self.rr("bB", self.banks[2:4])
                self.mm_group(ba, lambda k, w=wg: w.t[:, k * 128:(k + 1) * 128], lambda k, j=j: self.xs(k, j), KC, [wg, self.xn])
                self.mm_group(bb, lambda k, w=wp: w.t[:, k * 128:(k + 1) * 128],
                              lambda k, j=j: self.pt.t[:, k * HALF + j * TT:k * HALF + (j + 1) * TT], 2, [wp, self.pt])
                tmp = self.rr("tmp", self.tmp)
                P.op("scalar", lambda e, t=tmp, b=ba, oc=oc: e.activation(
                    out=t.t[:], in_=b.t[:], func=AF.Sigmoid, bias=self.vecs.t[:, bcol + oc:bcol + oc + 1], scale=1.0),
                    reads=[ba, self.vecs], writes=[tmp])
                self.add_store(bb, 1.0, hsrc, hsrc_bufs, hdst, hdst_bufs, oc, n0, mul_by=tmp)

    def proj_fm(self, half, w_ap, c0, nchunks, dst_fn, dst_bufs, evac="copy"):
        P = self.P
        for c in range(nchunks):
            w = self.rr("win", self.win)
            self.load_w(w, w_ap, c0 + c * 128, 128, KC)
            for j in range(2):
                b = self.rr("bA", self.banks[0:4])
                self.mm_group(b, lambda k, w=w: w.t[:, k * 128:(k + 1) * 128], lambda k, j=j: self.xs(k, j), KC, [w, self.xn])
                ob = self.rr("ob", self.ob)
                eng = self.rr("evac", ["scalar", "vector"])
                if eng == "scalar":
                    P.op("scalar", lambda e, o=ob, b=b: e.activation(out=o.t[:], in_=b.t[:], func=AF.Copy), reads=[b], writes=[ob])
                else:
                    P.op("vector", lambda e, o=ob, b=b: e.tensor_copy(out=o.t[:], in_=b.t[:]), reads=[b], writes=[ob])
                P.dma("sync", dst_fn(c, j), ob.t[:], ob, reads=[ob], writes=dst_bufs)

    def oproj(self, hsrc, hsrc_bufs, hdst, hdst_bufs, half, w_o):
        for oc in range(KC):
            w = self.rr("win", self.win)
            self.load_w(w, w_o, oc * 128, 128, KC)
            for j in range(2):
                n0 = half * HALF + j * TT
                b = self.rr("bO", self.banks[4:6])
                self.mm_group(b, lambda k, w=w: w.t[:, k * 128:(k + 1) * 128], lambda k, j=j: self.xs(k, j), KC, [w, self.xn])
                self.add_store(b, 1.0, hsrc, hsrc_bufs, hdst, hdst_bufs, oc, n0)

    def final_norm(self, hsrc, hsrc_bufs, out_ap, out_bufs, half, gcol):
        P = self.P
        hs = hsrc.rearrange("(k p) n -> k p n", p=128)
        od = out_ap.rearrange("(k p) n -> k p n", p=128)
        for j in range(2):
            self.stats(hsrc, hsrc_bufs, half * HALF + j * TT, j, RMS_EPS)
        for j in range(2):
            n0 = half * HALF + j * TT
            for k in range(KC):
                ht = self.rr("ht", self.ht)
                P.dma("gpsimd", ht.t[:], hs[k, :, n0:n0 + TT], ht, reads=[hsrc_bufs[n0 // TT]], writes=[ht])
                P.op("vector", lambda e, k=k, t=ht, j=j: e.scalar_tensor_tensor(
                    out=t.t[:], in0=t.t[:], scalar=self.vecs.t[:, gcol + k:gcol + k + 1], in1=self.rstd[j].t[:],
                    op0=ALU.mult, op1=ALU.mult), reads=[ht, self.vecs, self.rstd[j]], writes=[ht])
                P.dma("sync", od[k, :, n0:n0 + TT], ht.t[:], ht, reads=[ht], writes=out_bufs)

    def evac_bf16(self, bank, ncol=TT):
        P = self.P
        ob = self.rr("ob", self.ob)
        eng = self.rr("evac", ["scalar", "vector"])
        if eng == "scalar":
            P.op("scalar", lambda e, o=ob, b=bank: e.activation(out=o.t[:, 0:ncol], in_=b.t[:, 0:ncol], func=AF.Copy), reads=[bank], writes=[ob])
        else:
            P.op("vector", lambda e, o=ob, b=bank: e.tensor_copy(out=o.t[:, 0:ncol], in_=b.t[:, 0:ncol]), reads=[bank], writes=[ob])
        return ob

    def proj_tm(self, half, w_ap, c0, ncol, dst_fn, dst_bufs):
        P = self.P
        w = self.rr("wout", self.wout)
        self.load_w(w, w_ap, c0, ncol, KC)
        for tb in range(HALF // 128):
            b = self.rr("bA", self.banks[0:4])
            self.mm_group(b, lambda k, tb=tb: self.xn.t[:, k * HALF + tb * 128:k * HALF + (tb + 1) * 128],
                          lambda k, w=w: w.t[:, k * ncol:(k + 1) * ncol], KC, [w, self.xn], ncol=ncol)
            ob = self.evac_bf16(b, ncol)
            P.dma("sync", dst_fn(tb), ob.t[:, 0:ncol], ob, reads=[ob], writes=dst_bufs)

    def load_xn(self, src, src_bufs, half):
        v = src.rearrange("(k p) n -> p k n", p=128)[:, :, half * HALF:(half + 1) * HALF]
        self.P.dma("sync", self.xn.t[:].rearrange("p (k n) -> p k n", k=KC), v, self.xn, reads=src_bufs, writes=[self.xn])

    def proj_conv_in(self, half, w_ap, zT, bgT, dst_bufs):
        P = self.P
        zd = zT.rearrange("(k p) n -> k p n", p=128)
        bd = bgT.rearrange("(k p) n -> k p n", p=128)
        for c in range(KC):
            wb = self.rr("win", self.win)
            wc = self.rr("win", self.win)
            wu = self.rr("win", self.win)
            self.load_w(wb, w_ap, c * 128, 128, KC)
            self.load_w(wc, w_ap, D + c * 128, 128, KC)
            self.load_w(wu, w_ap, 2 * D + c * 128, 128, KC)
            for j in range(2):
                n0 = half * HALF + j * TT
                b0 = self.rr("bA", self.banks[0:2])
                b1 = self.rr("bB", self.banks[2:4])
                b2 = self.rr("bO", self.banks[4:6])
                for (b, w) in ((b0, wb), (b1, wc), (b2, wu)):
                    self.mm_group(b, lambda k, w=w: w.t[:, k * 128:(k + 1) * 128], lambda k, j=j: self.xs(k, j), KC, [w, self.xn])
                t0 = self.rr("ht", self.ht)
                P.op("scalar", lambda e, t=t0, b=b0: e.activation(out=t.t[:], in_=b.t[:], func=AF.Copy), reads=[b0], writes=[t0])
                P.dma("sync", bd[c, :, n0:n0 + TT], t0.t[:], t0, reads=[t0], writes=dst_bufs)
                t1 = self.rr("ht", self.ht)
                P.op("scalar", lambda e, t=t1, b=b1: e.activation(out=t.t[:], in_=b.t[:], func=AF.Copy), reads=[b1], writes=[t1])
                P.op("vector", lambda e, t=t1, b=b2: e.tensor_tensor(out=t.t[:], in0=t.t[:], in1=b.t[:], op=ALU.mult), reads=[b2, t1], writes=[t1])
                P.dma("sync", zd[c, :, n0:n0 + TT], t1.t[:], t1, reads=[t1], writes=dst_bufs)

    def init_attn(self, masks_ap, nmask, sel_ap, ident_ap):
        P = self.P
        self.masks = P.sb("masks", [128, nmask * 128], BF16)
        self.ident = P.sb("ident", [128, 128], BF16)
        self.sel = P.sb("sel", [128, 16], F32)
        P.dma("gpsimd", self.masks.t[:], masks_ap, self.masks, writes=[self.masks])
        P.dma("gpsimd", self.ident.t[:], ident_ap, self.ident, writes=[self.ident])
        P.dma("sync", self.sel.t[:], sel_ap, self.sel, writes=[self.sel])
        self.NPT = 6
        self.LA = 3
        self.pts = [P.vbuf("pt") for _ in range(self.NPT)]
        self.laccs = [P.sb("lacc%d" % i, [128, TT], F32) for i in range(2)]
        self.PT0 = FS * HALF - self.NPT * TT

    def pt_ap(self, i):
        return self.act.t[:, self.PT0 + i * TT:self.PT0 + (i + 1) * TT]

    def mask_ap(self, mi):
        return self.masks.t[:, mi * 128:(mi + 1) * 128]

    def attn_run(self, blocks, q_ap, qbuf, o_banks, l_bank, scale):
        P = self.P
        n = len(blocks)
        st = [None] * n

        def stage_s(i):
            b = blocks[i]
            c0 = b.get("c0", 0)
            ms = b.get("masks", ())
            sb = self.rr("bS", [self.banks[0], self.banks[1], self.banks[2], self.banks[7]])
            P.op("tensor", lambda e, b=b, sb=sb, c0=c0, ms=ms: e.matmul(sb.t[:, c0:TT], b["K"], q_ap[:, c0:TT], start=True, stop=(len(ms) == 0)),
                 reads=[qbuf] + b["bufs"], writes=[sb], inc=(len(ms) == 0))
            for mi, (map_, mc0, w) in enumerate(ms):
                last = mi == len(ms) - 1
                P.op("tensor", lambda e, sb=sb, map_=map_, mc0=mc0, w=w, last=last: e.matmul(
                    sb.t[:, mc0:mc0 + w], self.ident.t[:], map_, start=False, stop=last),
                    reads=[self.ident, self.masks], writes=[sb], inc=last)
            pi = self.cnt.get("pti", 0)
            self.cnt["pti"] = pi + 1
            pt = self.pts[pi % self.NPT]
            pap = self.pt_ap(pi % self.NPT)
            bias = b.get("bias")
            if bias is None:
                bias = self.sel.t[:, 10:11]
            P.op("scalar", lambda e, sb=sb, pap=pap, c0=c0, bias=bias: e.activation(
                out=pap[:, c0:TT], in_=sb.t[:, c0:TT], func=AF.Exp, bias=bias, scale=scale),
                reads=[sb, self.sel], writes=[pt])
            st[i] = (pt, pap, c0)

        lacc = self.rr("lacc", self.laccs)

        def stage_pv(i):
            pt, pap, c0 = st[i]
            b = blocks[i]
            first, last = (i == 0), (i == n - 1)
            for oi, ob in enumerate(o_banks):
                P.op("tensor", lambda e, ob=ob, v=b["V"][oi], pap=pap, c0=c0, first=first, last=last: e.matmul(
                    ob.t[:, c0:TT], v, pap[:, c0:TT], start=first, stop=last),
                    reads=[pt] + b["bufs"], writes=[ob], inc=(oi == len(o_banks) - 1))
            if first:
                P.op("vector", lambda e, pap=pap: e.tensor_copy(out=lacc.t[:], in_=pap), reads=[pt], writes=[lacc])
            else:
                P.op("vector", lambda e, pap=pap, c0=c0: e.tensor_tensor(out=lacc.t[:, c0:TT], in0=lacc.t[:, c0:TT], in1=pap[:, c0:TT], op=ALU.add),
                     reads=[pt, lacc], writes=[lacc])

        LA = self.LA
        for i in range(min(LA, n)):
            stage_s(i)
        for i in range(n):
            stage_pv(i)
            if i + LA < n:
                stage_s(i + LA)
        P.op("tensor", lambda e: e.matmul(l_bank.t[:], self.onesf.t[:], lacc.t[:], start=True, stop=True),
             reads=[lacc, self.onesf], writes=[l_bank])

    def attn_A(self, qA, kloc, vloc, kall, vall, own_bufs, src_bufs, attn, attn_bufs):
        P = self.P
        scale = 128.0 ** -0.5
        A = self.act.t
        QO, KO, VO, PO = 0, 8192, 14336, 20480
        qb, kb, vb = P.vbuf("qA"), P.vbuf("kA"), P.vbuf("vA")
        kpb = [P.vbuf("kpA") for _ in range(3)]
        vpb = [P.vbuf("vpA") for _ in range(3)]
        allb = [qb, kb, vb] + kpb + vpb + self.pts
        for b in allb:
            b.r.update(self.act.r); b.w.update(self.act.w)
        osets = [(self.banks[3], self.banks[4]), (self.banks[5], self.banks[6])]
        attn_v = attn.rearrange("(h d) n -> d h n", d=128)
        for kvh in range(4):
            qsrc = qA.rearrange("(h d) (nb q) -> d nb h q", d=128, q=128)[:, :, kvh * 4:(kvh + 1) * 4, :]
            qdst = A[:, QO:QO + 8192].rearrange("p (nb h q) -> p nb h q", nb=16, h=4)
            for hh in range(4):
                P.dma("sync", qdst[:, :, hh, :], qsrc[:, :, hh, :], qb, reads=own_bufs, writes=[qb])
            ksrc = kloc.rearrange("(g v d) n -> d g v n", g=3, v=4)[:, :, kvh, :]
            P.dma("sync", A[:, KO:KO + 6144].rearrange("p (g n) -> p g n", g=3), ksrc, kb, reads=own_bufs, writes=[kb])
            vsrc = vloc.rearrange("(b p) (g v d) -> p g b v d", p=128, g=3, v=4)[:, :, :, kvh, :]
            vdst = A[:, VO:VO + 6144].rearrange("p (g b d) -> p g b d", g=3, b=16)
            for g in range(3):
                P.dma("sync", vdst[:, g], vsrc[:, g], vb, reads=own_bufs, writes=[vb])
            for s in range(3):
                kbase = PO + s * 6144
                vbase = kbase + 3072
                vv = vall.rearrange("(r ih s jl p) (g v d) -> s p g v r ih jl d", r=4, ih=2, s=4, jl=2, p=128, g=3, v=4)[s]
                for g in range(3):
                    r0 = (g * 4 + kvh) * 128
                    base = ((r0 // 256) * 4 + s) * 256 + (r0 % 256)
                    ka = kall[base:base + 128, :].rearrange("d (r j) -> d r j", r=4)
                    if g < 2:
                        P.dma("sync", A[:, kbase + g * 512:kbase + (g + 1) * 512].rearrange("p (r j) -> p r j", r=4),
                              ka[:, :, 384:512], kpb[s], reads=src_bufs, writes=[kpb[s]])
                        P.dma("sync", A[:, vbase + g * 512:vbase + (g + 1) * 512].rearrange("p (r d) -> p r d", r=4),
                              vv[:, g, kvh, :, 1, 1, :], vpb[s], reads=src_bufs, writes=[vpb[s]])
                    else:
                        P.dma("sync", A[:, kbase + 1024:kbase + 3072].rearrange("p (r j) -> p r j", r=4),
                              ka, kpb[s], reads=src_bufs, writes=[kpb[s]])
                        for r in range(4):
                            for ih in range(2):
                                o_ = vbase + 1024 + r * 512 + ih * 256
                                P.dma("sync", A[:, o_:o_ + 256].rearrange("p (jl d) -> p jl d", jl=2),
                                      vv[:, g, kvh, r, ih, :, :], vpb[s], reads=src_bufs, writes=[vpb[s]])
            for r4 in range(4):
                for jb in range(4):
                    blocks = []

                    def add(g, rk, jbk, mi):
                        m4 = [(self.mask_ap(mi), hh * 128, 128) for hh in range(4)]
                        if jbk >= 0:
                            blk = rk * 4 + jbk
                            blocks.append(dict(K=A[:, KO + g * 2048 + blk * 128:KO + g * 2048 + (blk + 1) * 128],
                                               V=[A[:, VO + (g * 16 + blk) * 128:VO + (g * 16 + blk + 1) * 128]],
                                               bufs=[kb, vb], masks=m4))
                        else:
                            jp = jbk + 4
                            for s in range(3):
                                kbase = PO + s * 6144
                                vbase = kbase + 3072
                                if g < 2:
                                    o = g * 512 + rk * 128
                                else:
                                    o = 1024 + (rk * 4 + jp) * 128
                                blocks.append(dict(K=A[:, kbase + o:kbase + o + 128], V=[A[:, vbase + o:vbase + o + 128]],
                                                   bufs=[kpb[s], vpb[s]], masks=m4, bias=self.sel.t[:, s:s + 1]))

                    add(1, r4, jb, 0)
                    add(1, r4, jb - 1, 1)
                    for dj in range(5):
                        add(2, r4, jb - dj, (2, 3, 3, 3, 4)[dj])
                    for rk in range(4):
                        for dj in range(2):
                            add(0, rk, jb - dj, 5 + (r4 - rk + 3) * 2 + dj)
                    nb = r4 * 4 + jb
                    ob, lb = self.rr("oset", osets)
                    self.attn_run(blocks, A[:, QO + nb * 512:QO + (nb + 1) * 512], qb, [ob], lb, scale)
                    rl = self.rr("tmp", self.tmp)
                    P.op("vector", lambda e, rl=rl, lb=lb: e.reciprocal(out=rl.t[:], in_=lb.t[:]), reads=[lb], writes=[rl])
                    o = self.rr("ob", self.ob)
                    P.op("vector", lambda e, o=o, ob=ob, rl=rl: e.tensor_tensor(out=o.t[:], in0=ob.t[:], in1=rl.t[:], op=ALU.mult),
                         reads=[ob, rl], writes=[o])
                    P.dma("sync", attn_v[:, kvh * 4:(kvh + 1) * 4, nb * 128:(nb + 1) * 128],
                          o.t[:].rearrange("p (h q) -> p h q", h=4), o, reads=[o], writes=attn_bufs)
        for b in allb:
            for k, v in b.r.items():
                self.act.r[k] = max(self.act.r.get(k, 0), v)
            for k, v in b.w.items():
                self.act.w[k] = max(self.act.w.get(k, 0), v)

    def lam_setup(self, lam_ap, lam_init):
        P = self.P
        self.lamt = P.sb("lamt", [128, 8], F32)
        self.onesf = P.sb("onesf", [128, 128], F32)
        P.op("vector", lambda e: e.memset(self.onesf.t[:], 1.0), writes=[self.onesf])
        P.dma("sync", self.lamt.t[:, 0:4], lam_ap, self.lamt, writes=[self.lamt])
        L = self.lamt
        P.op("vector", lambda e: e.tensor_tensor(out=L.t[:, 4:5], in0=L.t[:, 0:1], in1=L.t[:, 1:2], op=ALU.mult), reads=[L], writes=[L])
        P.op("vector", lambda e: e.tensor_tensor(out=L.t[:, 5:6], in0=L.t[:, 2:3], in1=L.t[:, 3:4], op=ALU.mult), reads=[L], writes=[L])
        bank = self.banks[7]
        P.op("tensor", lambda e: e.matmul(bank.t[:, 0:2], self.onesf.t[:], L.t[:, 4:6], start=True, stop=True),
             reads=[L, self.onesf], writes=[bank])
        P.op("scalar", lambda e: e.activation(out=L.t[:, 6:8], in_=bank.t[:, 0:2], func=AF.Exp), reads=[bank], writes=[L])
        P.op("vector", lambda e: e.scalar_tensor_tensor(out=L.t[:, 4:5], in0=L.t[:, 7:8], scalar=-lam_init, in1=L.t[:, 6:7],
                                                        op0=ALU.add, op1=ALU.subtract), reads=[L], writes=[L])

    def attn_B(self, qB, kloc, vloc, kall, vall, own_bufs, src_bufs, attn, attn_bufs, gcol, lam_init):
        P = self.P
        scale = 128.0 ** -0.5
        A = self.act.t
        X = self.xn.t
        XF = self.xn.t.bitcast(F32)
        QO, KA, KO, VA = 0, 4096, 20480, 24576
        qb, kob, vob = P.vbuf("qB"), P.vbuf("koB"), P.vbuf("voB")
        kab = [[P.vbuf("kaB") for _ in range(3)] for _ in range(2)]
        vab = [P.vbuf("vaB") for _ in range(3)]
        ocb = [P.vbuf("oc0"), P.vbuf("oc1")]
        dfb = P.vbuf("diff")
        allb = [qb, kob, vob, dfb] + kab[0] + kab[1] + vab + ocb + self.pts
        for b in allb:
            b.r.update(self.act.r); b.w.update(self.act.w)
            b.r.update(self.xn.r); b.w.update(self.xn.w)
        oc_ap = [XF[:, 2048:3072], XF[:, 3072:4096]]
        df_ap = XF[:, 4096:5120]
        o_banks = [self.banks[3], self.banks[4]]
        l_bank = self.banks[5]
        sbank = self.banks[6]
        one_m = 1.0 - lam_init
        P.op("vector", lambda e: e.memset(self.epsc.t[:, 1:2], SUBLN_EPS / (one_m * one_m)), reads=[], writes=[self.epsc])
        attn_v = attn.rearrange("(h c d) n -> h c d n", c=2, d=128)
        for h in range(8):
            qs = qB.rearrange("(h c d) n -> h d c n", c=2, d=128)[h]
            P.dma("sync", A[:, QO:QO + 4096].rearrange("p (c n) -> p c n", c=2), qs, qb, reads=own_bufs, writes=[qb])
            va = vall.rearrange("(i s wl p) (h e) -> h s p i wl e", s=4, wl=2, p=128, e=256)[h]
            for s in range(3):
                for c in range(2):
                    ka = kall.rearrange("(h s c d) n -> h c s d n", s=4, c=2, d=128)[h, c, s]
                    P.dma("sync", A[:, KA + c * 8192 + s * 2048:KA + c * 8192 + (s + 1) * 2048], ka, kab[c][s], reads=src_bufs, writes=[kab[c][s]])
                vd = A[:, VA + s * 4096:VA + (s + 1) * 4096].rearrange("p (i wl e) -> p i wl e", wl=2, e=256)
                for wl in range(2):
                    P.dma("sync", vd[:, :, wl, :], va[s][:, :, wl, :], vab[s], reads=src_bufs, writes=[vab[s]])
            ks = kloc.rearrange("(h c d) n -> h d c n", c=2, d=128)[h]
            P.dma("sync", A[:, KO:KO + 4096].rearrange("p (c n) -> p c n", c=2), ks, kob, reads=own_bufs, writes=[kob])
            vo = vloc.rearrange("(b p) (h e) -> h p b e", p=128, e=256)[h]
            P.dma("sync", X[:, 0:4096].rearrange("p (b e) -> p b e", e=256), vo, vob, reads=own_bufs, writes=[vob])
            for r4 in range(4):
                for c in range(2):
                    blocks = []
                    for s in range(3):
                        for blk in range(16):
                            ko = KA + c * 8192 + s * 2048 + blk * 128
                            vo_ = VA + (s * 16 + blk) * 256
                            blocks.append(dict(K=A[:, ko:ko + 128], V=[A[:, vo_:vo_ + 128], A[:, vo_ + 128:vo_ + 256]],
                                               bufs=[kab[c][s], vab[s]], bias=self.sel.t[:, 3 + s:4 + s]))
                    for jbk in (3, 2, 1, 0):
                        for rk in range(4):
                            blk = rk * 4 + jbk
                            ko = KO + c * 2048 + blk * 128
                            vo_ = blk * 256
                            mi = 0 if rk <= r4 else 19
                            blocks.append(dict(K=A[:, ko:ko + 128], V=[X[:, vo_:vo_ + 128], X[:, vo_ + 128:vo_ + 256]],
                                               bufs=[kob, vob], masks=[(self.mask_ap(mi), jbk * 128, 128)], c0=jbk * 128))
                    q_ap = A[:, QO + c * 2048 + r4 * 512:QO + c * 2048 + (r4 + 1) * 512]
                    self.attn_run(blocks, q_ap, qb, o_banks, l_bank, scale)
                    rl = self.rr("tmp", self.tmp)
                    P.op("vector", lambda e, rl=rl: e.reciprocal(out=rl.t[:], in_=l_bank.t[:]), reads=[l_bank], writes=[rl])
                    for e2 in range(2):
                        P.op("vector", lambda e, c=c, e2=e2, rl=rl: e.tensor_tensor(
                            out=oc_ap[c][:, e2 * 512:(e2 + 1) * 512], in0=o_banks[e2].t[:], in1=rl.t[:], op=ALU.mult),
                            reads=[o_banks[e2], rl], writes=[ocb[c]])
                P.op("vector", lambda e: e.scalar_tensor_tensor(out=df_ap, in0=oc_ap[1], scalar=self.lamt.t[:, 4:5], in1=oc_ap[0],
                                                                op0=ALU.mult, op1=ALU.add), reads=ocb + [self.lamt], writes=[dfb])
                for e2 in range(2):
                    sq = self.rr("sq", self.sq)
                    P.op("scalar", lambda e, sq=sq, e2=e2: e.activation(out=sq.t[:], in_=df_ap[:, e2 * 512:(e2 + 1) * 512], func=AF.Square),
                         reads=[dfb], writes=[sq])
                    P.op("tensor", lambda e, sq=sq, e2=e2: e.matmul(sbank.t[:], self.ones.t[:], sq.t[:], start=(e2 == 0), stop=(e2 == 1)),
                         reads=[sq, self.ones], writes=[sbank], inc=True)
                P.op("scalar", lambda e: e.activation(out=self.rtmp.t[:], in_=sbank.t[:], func=AF.Sqrt, bias=self.epsc.t[:, 1:2],
                                                      scale=1.0 / (256.0 * one_m * one_m)), reads=[sbank, self.epsc], writes=[self.rtmp])
                P.op("vector", lambda e: e.reciprocal(out=self.rstd[0].t[:], in_=self.rtmp.t[:]), reads=[self.rtmp], writes=[self.rstd[0]])
                for e2 in range(2):
                    o = self.rr("ob", self.ob)
                    P.op("vector", lambda e, o=o, e2=e2: e.scalar_tensor_tensor(
                        out=o.t[:], in0=df_ap[:, e2 * 512:(e2 + 1) * 512], scalar=self.vecs.t[:, gcol + e2:gcol + e2 + 1],
                        in1=self.rstd[0].t[:], op0=ALU.mult, op1=ALU.mult), reads=[dfb, self.vecs, self.rstd[0]], writes=[o])
                    P.dma("sync", attn_v[h, e2, :, r4 * 512:(r4 + 1) * 512], o.t[:], o, reads=[o], writes=attn_bufs)
        for b in allb:
            for tgt in (self.act, self.xn):
                for k, v in b.r.items():
                    tgt.r[k] = max(tgt.r.get(k, 0), v)
                for k, v in b.w.items():
                    tgt.w[k] = max(tgt.w.get(k, 0), v)

    def conv_C(self, zT, bgT, zh_all, src_bufs, attn, attn_bufs, wcol):
        P = self.P
        AF32 = self.act.t.bitcast(F32)
        X = self.xn.t
        sets = []
        for i in range(2):
            base = i * 8192
            sets.append(dict(z=AF32[:, base:base + 2048], bg=AF32[:, base + 2048:base + 4096], acc=AF32[:, base + 4096:base + 6144],
                             hin=AF32[:, base + 6144:base + 6144 + 24], hal=AF32[:, base + 6200:base + 6202],
                             out=X[:, i * 2048:(i + 1) * 2048],
                             zb=P.vbuf("cz"), bb=P.vbuf("cbg"), ab=P.vbuf("cacc"), hb=P.vbuf("chin"), ob=P.vbuf("cout")))
        allb = [s[k] for s in sets for k in ("zb", "bb", "ab", "hb", "ob")]
        for b in allb:
            b.r.update(self.act.r); b.w.update(self.act.w)
            b.r.update(self.xn.r); b.w.update(self.xn.w)
        zd = zT.rearrange("(k p) n -> k p n", p=128)
        bd = bgT.rearrange("(k p) n -> k p n", p=128)
        ad = attn.rearrange("(k p) n -> k p n", p=128)
        zh = zh_all.rearrange("(s k p) e -> k p s e", s=4, p=128)
        for c in range(KC):
            S = sets[c % 2]
            P.dma("gpsimd", S["z"], zd[c], S["zb"], reads=src_bufs, writes=[S["zb"]])
            P.dma("gpsimd", S["bg"], bd[c], S["bb"], reads=src_bufs, writes=[S["bb"]])
            P.dma("sync", S["hin"].rearrange("p (s e) -> p s e", s=3), zh[c][:, 0:3, :], S["hb"], reads=src_bufs, writes=[S["hb"]])
            hin, hal = S["hin"], S["hal"]
            P.op("vector", lambda e, hin=hin, hal=hal: e.tensor_scalar(out=hal, in0=hin[:, 0:2], scalar1=self.sel.t[:, 7:8], scalar2=None, op0=ALU.mult),
                 reads=[S["hb"], self.sel], writes=[S["hb"]])
            for s in (1, 2):
                P.op("vector", lambda e, hin=hin, hal=hal, s=s: e.scalar_tensor_tensor(
                    out=hal, in0=hin[:, 8 * s:8 * s + 2], scalar=self.sel.t[:, 7 + s:8 + s], in1=hal, op0=ALU.mult, op1=ALU.add),
                    reads=[S["hb"], self.sel], writes=[S["hb"]])
            w0 = self.vecs.t[:, wcol + c:wcol + c + 1]
            w1 = self.vecs.t[:, wcol + KC + c:wcol + KC + c + 1]
            w2 = self.vecs.t[:, wcol + 2 * KC + c:wcol + 2 * KC + c + 1]
            z, acc = S["z"], S["acc"]
            rd = [S["zb"], S["hb"], self.vecs]
            P.op("vector", lambda e, z=z, acc=acc, w2=w2: e.tensor_scalar(out=acc, in0=z, scalar1=w2, scalar2=None, op0=ALU.mult),
                 reads=rd, writes=[S["ab"]])

            def fma(dst, src, w):
                P.op("vector", lambda e, dst=dst, src=src, w=w: e.scalar_tensor_tensor(out=dst, in0=src, scalar=w, in1=dst, op0=ALU.mult, op1=ALU.add),
                     reads=rd + [S["ab"]], writes=[S["ab"]])

            for r4 in range(4):
                a = acc[:, r4 * 512:(r4 + 1) * 512]
                if r4 >= 1:
                    fma(a, z[:, (r4 - 1) * 512:r4 * 512], w1)
                else:
                    fma(a[:, 1:512], z[:, 3 * 512:3 * 512 + 511], w1)
                    fma(a[:, 0:1], S["hal"][:, 1:2], w1)
                if r4 >= 2:
                    fma(a, z[:, (r4 - 2) * 512:(r4 - 1) * 512], w0)
                else:
                    fma(a[:, 1:512], z[:, (r4 + 2) * 512:(r4 + 2) * 512 + 511], w0)
                    fma(a[:, 0:1], S["hal"][:, r4:r4 + 1], w0)
            P.op("vector", lambda e, S=S: e.tensor_tensor(out=S["out"], in0=S["acc"], in1=S["bg"], op=ALU.mult),
                 reads=[S["ab"], S["bb"]], writes=[S["ob"]])
            P.dma("sync", ad[c], S["out"], S["ob"], reads=[S["ob"]], writes=attn_bufs)
        for b in allb:
            for tgt in (self.act, self.xn):
                for k, v in b.r.items():
                    tgt.r[k] = max(tgt.r.get(k, 0), v)
                for k, v in b.w.items():
                    tgt.w[k] = max(tgt.w.get(k, 0), v)


NV = 400
VL = 80
V_NORMF = 320
V_SUBLN = 336
V_CONV = 338
NMASK = 20
GROUPS = [[0, 1, 2, 3], [4, 5, 6, 7]]
LAM_INIT = 0.8 - 0.6 * float(np.exp(-0.3 * 1))

W_SHAPES = {}
for _i in range(4):
    W_SHAPES["w1i%d" % _i] = (D, 2 * DFF)
    W_SHAPES["w1o%d" % _i] = (DFF, D)
    W_SHAPES["w2i%d" % _i] = (D, 2 * DFF)
    W_SHAPES["w2o%d" % _i] = (DFF, D)
    W_SHAPES["wpg%d" % _i] = (D, D)
    W_SHAPES["wpp%d" % _i] = (PLE, D)
for _j in range(2):
    W_SHAPES["aqkv%d" % _j] = (D, 5120)
    W_SHAPES["ao%d" % _j] = (D, D)
W_SHAPES.update(bqkv=(D, 3 * D), bo=(D, D), cin=(D, 3 * D), cout=(D, D))


def build_net(stop=None, dbg=None):
    P = Prog()
    xT = P.dram_in("xT", [D, TOK])
    pT = [P.dram_in("pT%d" % i, [PLE, TOK]) for i in range(4)]
    vecs = P.dram_in("vecs", [128, NV])
    masks = P.dram_in("masks", [128, NMASK * 128])
    ident = P.dram_in("ident", [128, 128])
    sel = P.dram_in("sel", [128, 16])
    lamT = P.dram_in("lamT", [128, 4])
    class _LazyW(dict):
        def __missing__(self, k):
            self[k] = P.dram_in(k, list(W_SHAPES[k]))
            return self[k]
    W = _LazyW()
    outT = P.dram_out("outT", [D, TOK])
    h = P.dram_tmp("h", [D, TOK])
    q = P.dram_tmp("q", [D, TOK], BF16)
    klA = P.dram_tmp("klA", [1536, TOK], BF16)
    vlA = P.dram_tmp("vlA", [TOK, 1536], BF16)
    kaA = P.dram_tmp("kaA", [4 * 1536, TOK], BF16)
    vaA = P.dram_tmp("vaA", [4 * TOK, 1536], BF16)
    klB = P.dram_tmp("klB", [D, TOK], BF16)
    vlB = P.dram_tmp("vlB", [TOK, D], BF16)
    kaB = P.dram_tmp("kaB", [4 * D, TOK], BF16)
    vaB = P.dram_tmp("vaB", [4 * TOK, D], BF16)
    zT = P.dram_tmp("zT", [D, TOK])
    bgT = P.dram_tmp("bgT", [D, TOK])
    zhl = P.dram_tmp("zhl", [D, 8])
    zha = P.dram_tmp("zha", [4 * D, 8])
    attn = P.dram_tmp("attn", [D, TOK], BF16)

    R = Rows(P, vecs, NV)
    R.init_attn(masks, NMASK, sel, ident)
    R.lam_setup(lamT, LAM_INIT)

    xb = [P.vbuf("x") for _ in range(4)]
    hb = [P.vbuf("h") for _ in range(4)]
    ob = [P.vbuf("out")]
    P.out_bufs += ob
    qb, klb, vlb, kab, vab, atb = (P.vbuf(n) for n in ("q", "kl", "vl", "ka", "va", "attn"))
    zb, zhlb, zhab = P.vbuf("z"), P.vbuf("zhl"), P.vbuf("zha")
    ccb = P.vbuf("cc")

    def cc(in_ap, out_ap, in_bufs, out_bufs):
        E = P.engs["gpsimd"]
        if ccb.sem is None:
            ccb.sem = P.new_sem("cc")
        waits = P._collect(E, in_bufs, out_bufs)
        for s_, v_ in waits.items():
            E.waited[s_] = v_
        ccb.cnt += 1
        E.ops.append((sorted(waits.items()), lambda e, i=in_ap, o=out_ap: e.collective_compute(
            "AllGather", ALU.bypass, replica_groups=GROUPS, ins=[i], outs=[o]), (ccb.sem, 1)))
        P._mark((ccb.sem, ccb.cnt), in_bufs, out_bufs)

    def gather(loc, allt, rows, lb, ab):
        for i in range(rows // 256):
            cc(loc[i * 256:(i + 1) * 256, :], allt[i * 1024:(i + 1) * 1024, :], [lb], [ab])

    state = {"hsrc": xT, "hsb": xb}

    def hs():
        return state["hsrc"], state["hsb"]

    def wrote_h():
        state["hsrc"], state["hsb"] = h, hb

    def chain(fns):
        for f in fns:
            for half in range(2):
                f(half)
            if getattr(f, "writes_h", False):
                wrote_h()

    def mk(fn, writes_h=False):
        fn.writes_h = writes_h
        return fn

    def s_ffn(i, which):
        g = i * VL + (0 if which == 1 else 32)
        wi, wo = W["w%di%d" % (which, i)], W["w%do%d" % (which, i)]

        def f(half):
            src, sb_ = hs()
            R.norm(src, sb_, half, g)
            R.ffn(src, sb_, h, hb, half, wi, wo)
        return mk(f, True)

    def s_ple(i):
        def f(half):
            src, sb_ = hs()
            R.norm(src, sb_, half, i * VL + 48)
            R.ple(src, sb_, h, hb, half, W["wpg%d" % i], i * VL + 64, pT[i], W["wpp%d" % i])
        return mk(f, True)

    def s_oproj(w):
        def f(half):
            src, sb_ = hs()
            R.load_xn(attn, [atb], half)
            R.oproj(src, sb_, h, hb, half, w)
        return mk(f, True)

    def s_projA(i, j):
        w = W["aqkv%d" % j]

        def f(half):
            src, sb_ = hs()
            R.norm(src, sb_, half, i * VL + 16)
            qv = q.rearrange("(c p) n -> c p n", p=128)
            R.proj_fm(half, w, 0, 16, lambda c, jj: qv[c, :, half * HALF + jj * TT:half * HALF + (jj + 1) * TT], [qb])
            kv = klA.rearrange("(c p) n -> c p n", p=128)
            for g in range(3):
                R.proj_fm(half, w, D + g * 1024, 4,
                          lambda c, jj, g=g: kv[g * 4 + c, :, half * HALF + jj * TT:half * HALF + (jj + 1) * TT], [klb])
                for grp in range(2):
                    R.proj_tm(half, w, D + g * 1024 + 512 + grp * 256, 256,
                              lambda tb, g=g, grp=grp: vlA[half * HALF + tb * 128:half * HALF + (tb + 1) * 128,
                                                           g * 512 + grp * 256:g * 512 + (grp + 1) * 256], [vlb])
        return mk(f)

    def s_projB(i):
        w = W["bqkv"]

        def f(half):
            src, sb_ = hs()
            R.norm(src, sb_, half, i * VL + 16)
            qv = q.rearrange("(c p) n -> c p n", p=128)
            R.proj_fm(half, w, 0, 16, lambda c, jj: qv[c, :, half * HALF + jj * TT:half * HALF + (jj + 1) * TT], [qb])
            kv = klB.rearrange("(c p) n -> c p n", p=128)
            R.proj_fm(half, w, D, 16, lambda c, jj: kv[c, :, half * HALF + jj * TT:half * HALF + (jj + 1) * TT], [klb])
            for hh in range(8):
                R.proj_tm(half, w, 2 * D + hh * 256, 256,
                          lambda tb, hh=hh: vlB[half * HALF + tb * 128:half * HALF + (tb + 1) * 128, hh * 256:(hh + 1) * 256], [vlb])
        return mk(f)

    def q_stage(i, w):
        def run():
            qv = q.rearrange("(c p) n -> c p n", p=128)
            for half in (1, 0):
                src, sb_ = hs()
                if half == 0:
                    R.norm(src, sb_, half, i * VL + 16)
                R.proj_fm(half, w, 0, 16, lambda c, jj, half=half: qv[c, :, half * HALF + jj * TT:half * HALF + (jj + 1) * TT], [qb])
        return run

    def s_projC(i):
        def f(half):
            src, sb_ = hs()
            R.norm(src, sb_, half, i * VL + 16)
            R.proj_conv_in(half, W["cin"], zT, bgT, [zb])
        return mk(f)

    def s_final():
        def f(half):
            src, sb_ = hs()
            R.final_norm(src, sb_, outT, ob, half, V_NORMF)
        return mk(f)

    def halo():
        P.dma("sync", zhl[:, 0:1], zT[:, 2 * 512 + 511:2 * 512 + 512], zhlb, reads=[zb], writes=[zhlb], allow_slow_non_contiguous=True)
        P.dma("sync", zhl[:, 1:2], zT[:, 3 * 512 + 511:3 * 512 + 512], zhlb, reads=[zb], writes=[zhlb], allow_slow_non_contiguous=True)
        cc(zhl, zha, [zhlb], [zhab])

    def gA():
        gather(klA, kaA, 1536, klb, kab)
        gather(vlA, vaA, TOK, vlb, vab)

    def gB():
        gather(klB, kaB, D, klb, kab)
        gather(vlB, vaB, TOK, vlb, vab)

    mixA = lambda: R.attn_A(q, klA, vlA, kaA, vaA, [qb, klb, vlb], [kab, vab], attn, [atb])
    mixB = lambda: R.attn_B(q, klB, vlB, kaB, vaB, [qb, klb, vlb], [kab, vab], attn, [atb], V_SUBLN, LAM_INIT)
    mixC = lambda: R.conv_C(zT, bgT, zha, [zb, zhab], attn, [atb], V_CONV)
    C1 = lambda th: (lambda: chain([th()]))
    stages = [
        C1(lambda: s_ffn(0, 1)), C1(lambda: s_projA(0, 0)), gA, mixA, C1(lambda: s_oproj(W["ao0"])), C1(lambda: s_ffn(0, 2)), C1(lambda: s_ple(0)),
        C1(lambda: s_ffn(1, 1)), C1(lambda: s_projB(1)), gB, mixB, C1(lambda: s_oproj(W["bo"])), C1(lambda: s_ffn(1, 2)), C1(lambda: s_ple(1)),
        C1(lambda: s_ffn(2, 1)), C1(lambda: s_projC(2)), halo, mixC, C1(lambda: s_oproj(W["cout"])), C1(lambda: s_ffn(2, 2)), C1(lambda: s_ple(2)),
        C1(lambda: s_ffn(3, 1)), C1(lambda: s_projA(3, 1)), gA, mixA, C1(lambda: s_oproj(W["ao1"])), C1(lambda: s_ffn(3, 2)), C1(lambda: s_ple(3)),
        C1(lambda: s_final()),
    ]
    n = len(stages) if stop is None else stop
    for st in stages[:n]:
        st()
    if stop is not None:
        if dbg is None:
            P.dma("sync", outT, h, ob[0], reads=hb, writes=ob)
        else:
            src = {"attn": attn, "q": q, "klA": klA, "vlA": vlA, "klB": klB, "vlB": vlB}[dbg]
            dbo = P.dram_out("dbg", list(src.shape), BF16)
            P.dma("sync", dbo, src, ob[0], reads=[atb, qb, klb, vlb], writes=ob)
            P.dma("sync", outT, h, ob[0], reads=hb, writes=ob)
    nc = P.emit()
    nc._in_names = list(P.in_names)
    return nc


def _r4perm():
    n = np.arange(TOK)
    return 4 * (n % 512) + n // 512


def _masks():
    jk = np.arange(128)[:, None]
    jq = np.arange(128)[None, :]
    vis = []
    vis.append(jk <= jq)
    vis.append(jk >= jq)
    comb = ((jq - jk) % 4) == 0
    vis.append((jk <= jq) & comb)
    vis.append(comb | (jk < -1))
    vis.append((jk >= jq) & comb)
    for dr in range(-3, 4):
        for djb in range(2):
            dt = 4 * (128 * djb + jq - jk) + dr
            vis.append((dt >= 0) & (dt <= 128))
    vis.append(jk < jq)
    m = np.concatenate([np.where(v, 0.0, NEG) for v in vis], axis=1).astype(np.float32)
    assert m.shape == (128, NMASK * 128)
    return m


def _col16(v):
    return np.ascontiguousarray(np.asarray(v, np.float32).reshape(-1, 128).T)


def make_inputs(inputs):
    perm = _r4perm()
    x = np.asarray(inputs["x"], np.float32)
    p = np.asarray(inputs["p"], np.float32)
    vecs = np.zeros((128, NV), np.float32)
    for i in range(4):
        b = i * VL
        vecs[:, b:b + 16] = _col16(inputs["norm_ffn1"][i])
        vecs[:, b + 16:b + 32] = _col16(inputs["norm_mix"][i])
        vecs[:, b + 32:b + 48] = _col16(inputs["norm_ffn2"][i])
        vecs[:, b + 48:b + 64] = _col16(inputs["norm_ple"][i])
        vecs[:, b + 64:b + 80] = _col16(inputs["b_ple_gate"][i])
    vecs[:, V_NORMF:V_NORMF + 16] = _col16(inputs["norm_f"])
    vecs[:, V_SUBLN:V_SUBLN + 2] = _col16(inputs["b_subln"][0])
    for k in range(3):
        vecs[:, V_CONV + 16 * k:V_CONV + 16 * (k + 1)] = _col16(inputs["c_conv_w"][0][k])
    shared = {"vecs": vecs, "masks": _masks(), "ident": np.eye(128, dtype=np.float32),
              "lamT": np.ascontiguousarray(np.asarray(inputs["b_lambda"][0], np.float32).T)}
    for i in range(4):
        shared["w1i%d" % i] = np.asarray(inputs["w_ffn1_in"][i], np.float32)
        shared["w1o%d" % i] = np.asarray(inputs["w_ffn1_out"][i], np.float32)
        shared["w2i%d" % i] = np.asarray(inputs["w_ffn2_in"][i], np.float32)
        shared["w2o%d" % i] = np.asarray(inputs["w_ffn2_out"][i], np.float32)
        shared["wpg%d" % i] = np.asarray(inputs["w_ple_gate"][i], np.float32)
        shared["wpp%d" % i] = np.asarray(inputs["w_ple_proj"][i], np.float32)
    for j in range(2):
        shared["aqkv%d" % j] = np.asarray(inputs["a_w_qkv"][j], np.float32)
        shared["ao%d" % j] = np.asarray(inputs["a_w_o"][j], np.float32)
    shared["bqkv"] = np.asarray(inputs["b_w_qkv"][0], np.float32)
    shared["bo"] = np.asarray(inputs["b_w_o"][0], np.float32)
    shared["cin"] = np.asarray(inputs["c_w_in"][0], np.float32)
    shared["cout"] = np.asarray(inputs["c_w_out"][0], np.float32)
    maps = []
    for core in range(NCORES):
        b, c = divmod(core, 4)
        sl = slice(c * TOK, (c + 1) * TOK)
        m = dict(shared)
        m["xT"] = np.ascontiguousarray(x[b, sl][perm].T)
        for i in range(4):
            m["pT%d" % i] = np.ascontiguousarray(p[i, b, sl][perm].T)
        s = np.zeros((128, 16), np.float32)
        for k in range(3):
            s[:, k] = 0.0 if k == c - 1 else NEG
            s[:, 7 + k] = 1.0 if k == c - 1 else 0.0
        for k in range(4):
            s[:, 3 + k] = 0.0 if k < c else NEG
        m["sel"] = s
        maps.append(m)
    return maps


def assemble(results, key="outT"):
    perm = _r4perm()
    out = np.empty((2, 4 * TOK, D), np.float32)
    for core in range(NCORES):
        b, c = divmod(core, 4)
        o = np.asarray(results[core][key]).astype(np.float32).T
        blk = np.empty_like(o)
        blk[perm] = o
        out[b, c * TOK:(c + 1) * TOK] = blk
    return out


_NC_CACHE = {}


def kernel(**inputs):
    if "net" not in _NC_CACHE:
        _NC_CACHE["net"] = build_net()
    nc = _NC_CACHE["net"]
    maps = [{k: m[k] for k in nc._in_names} for m in make_inputs(inputs)]
    res = run_bass_kernel_spmd(nc, maps, core_ids=list(range(NCORES)))
    return assemble(res.results)
```

```python
import contextlib
import numpy as np
import ml_dtypes
import concourse.bass as bass
import concourse.mybir as mybir
from concourse.bass_utils import run_bass_kernel_spmd

F32 = mybir.dt.float32
BF16 = mybir.dt.bfloat16
AF = mybir.ActivationFunctionType
ALU = mybir.AluOpType

D = 2048
KC = 16
TOK = 2048
HALF = 1024
TT = 512
DFF = 5632
FS = 44
PLE = 256
NCORES = 8
NEG = -30000.0
RMS_EPS = 1e-6
SUBLN_EPS = 1e-5


class Buf:
    __slots__ = ("name", "w", "r", "sem", "cnt", "t")

    def __init__(self, name, t=None):
        self.name = name
        self.w = {}
        self.r = {}
        self.sem = None
        self.cnt = 0
        self.t = t


class Eng:
    def __init__(self, name):
        self.name = name
        self.ops = []
        self.sem = None
        self.cnt = 0
        self.waited = {}


class Prog:
    def __init__(self):
        self.nc = bass.Bass("TRN2", target_bir_lowering=False)
        self.es = contextlib.ExitStack()
        self.engs = {n: Eng(n) for n in ("tensor", "vector", "scalar", "gpsimd", "sync")}
        self.semh = {}
        self.nsem = 0
        for n in ("tensor", "vector", "scalar"):
            self.engs[n].sem = self.new_sem("e_" + n)
        self.out_bufs = []
        self.in_names = []
        self.nbuf = 0

    def new_sem(self, name=None):
        self.nsem += 1
        h = self.es.enter_context(self.nc.semaphore(name or ("s%d" % self.nsem)))
        self.semh[self.nsem] = h
        return self.nsem

    def sb(self, name, shape, dtype):
        t = self.es.enter_context(self.nc.sbuf_tensor("sb_" + name, list(shape), dtype))
        return Buf(name, t)

    def ps(self, name):
        t = self.es.enter_context(self.nc.psum_tensor("ps_" + name, [128, 512], F32))
        return Buf(name, t)

    def dram_in(self, name, shape, dtype=F32):
        self.in_names.append(name)
        return self.nc.dram_tensor(name, list(shape), dtype, kind="ExternalInput").ap()

    def dram_out(self, name, shape, dtype=F32):
        return self.nc.dram_tensor(name, list(shape), dtype, kind="ExternalOutput").ap()

    def dram_tmp(self, name, shape, dtype=F32):
        return self.nc.dram_tensor(name, list(shape), dtype).ap()

    def vbuf(self, name):
        self.nbuf += 1
        return Buf("%s_%d" % (name, self.nbuf))

    def _collect(self, E, reads, writes, extra=()):
        waits = {}

        def need(sem, val):
            if E.waited.get(sem, 0) >= val:
                return
            if waits.get(sem, 0) < val:
                waits[sem] = val

        for b in reads:
            for sem, val in b.w.items():
                need(sem, val)
        for b in writes:
            for sem, val in b.w.items():
                need(sem, val)
            for sem, val in b.r.items():
                need(sem, val)
        for sem, val in extra:
            need(sem, val)
        return waits

    @staticmethod
    def _mark(ev, reads, writes):
        sem, val = ev
        for b in reads:
            if b.r.get(sem, 0) < val:
                b.r[sem] = val
        for b in writes:
            if b.w.get(sem, 0) < val:
                b.w[sem] = val

    def op(self, eng, fn, reads=(), writes=(), inc=True):
        E = self.engs[eng]
        waits = self._collect(E, reads, writes)
        if eng == "tensor":
            waits.pop(E.sem, None)
        for sem, val in waits.items():
            E.waited[sem] = val
        if inc:
            E.cnt += 1
            ev = (E.sem, E.cnt)
        else:
            ev = (E.sem, E.cnt + 1)
        E.ops.append((sorted(waits.items()), fn, (E.sem, 1) if inc else None))
        self._mark(ev, reads, writes)

    def dma(self, eng, out, in_, owner, reads=(), writes=(), **kw):
        E = self.engs[eng]
        if owner.sem is None:
            owner.sem = self.new_sem("d_" + owner.name)
        waits = self._collect(E, reads, writes, extra=[(owner.sem, owner.cnt)] if owner.cnt else [])
        for sem, val in waits.items():
            E.waited[sem] = val
        owner.cnt += 16
        ev = (owner.sem, owner.cnt)
        E.ops.append((sorted(waits.items()), lambda e, o=out, i=in_: e.dma_start(out=o, in_=i, **kw), (owner.sem, 16)))
        self._mark(ev, reads, writes)

    def barrier_wait(self, eng, bufs):
        E = self.engs[eng]
        waits = self._collect(E, [], bufs)
        for sem, val in waits.items():
            E.waited[sem] = val
        if waits:
            E.ops.append((sorted(waits.items()), None, None))

    def check(self):
        sems = {}
        pos = {n: 0 for n in self.engs}
        progress = True
        while progress:
            progress = False
            for n, E in self.engs.items():
                while pos[n] < len(E.ops):
                    waits, fn, inc = E.ops[pos[n]]
                    if any(sems.get(s, 0) < v for s, v in waits):
                        break
                    if inc is not None:
                        sems[inc[0]] = sems.get(inc[0], 0) + inc[1]
                    pos[n] += 1
                    progress = True
        stuck = {n: (pos[n], len(E.ops)) for n, E in self.engs.items() if pos[n] < len(E.ops)}
        if stuck:
            msg = []
            for n in stuck:
                waits, fn, inc = self.engs[n].ops[pos[n]]
                msg.append("%s@%d waits %s have %s" % (n, pos[n], waits, [(s, sems.get(s, 0)) for s, _ in waits]))
            raise RuntimeError("DEADLOCK: " + "; ".join(msg))
        return {n: len(E.ops) for n, E in self.engs.items()}

    def emit(self):
        self.barrier_wait("sync", self.out_bufs)
        self.check()
        nc = self.nc
        semh = self.semh

        def replay(e, E):
            for waits, fn, inc in E.ops:
                for sem, val in waits:
                    e.wait_ge(semh[sem], val)
                if fn is None:
                    continue
                inst = fn(e)
                if inc is not None:
                    inst.then_inc(semh[inc[0]], inc[1])

        with nc.Block() as block:
            @block.tensor
            def _(e):
                replay(e, self.engs["tensor"])

            @block.vector
            def _(e):
                replay(e, self.engs["vector"])

            @block.scalar
            def _(e):
                replay(e, self.engs["scalar"])

            @block.gpsimd
            def _(e):
                replay(e, self.engs["gpsimd"])

            @block.sync
            def _(e):
                replay(e, self.engs["sync"])
        self.es.close()
        return nc


class Rows:
    def __init__(self, P, vecs_ap, nvec):
        self.P = P
        self.act = P.sb("act", [128, FS * HALF], BF16)
        self.xn = P.sb("xn", [128, KC * HALF], BF16)
        self.win = [P.sb("win%d" % i, [128, KC * 128], BF16) for i in range(4)]
        self.wout = [P.sb("wout%d" % i, [128, FS * 128], BF16) for i in range(2)]
        self.ht = [P.sb("ht%d" % i, [128, TT], F32) for i in range(6)]
        self.tmp = [P.sb("tmp%d" % i, [128, TT], F32) for i in range(3)]
        self.sq = [P.sb("sq%d" % i, [128, TT], BF16) for i in range(2)]
        self.rstd = [P.sb("rstd%d" % i, [128, TT], F32) for i in range(2)]
        self.epsc = P.sb("epsc", [128, 2], F32)
        self.rtmp = P.sb("rtmp", [128, TT], F32)
        self.ob = [P.sb("ob%d" % i, [128, TT], BF16) for i in range(4)]
        self.ones = P.sb("ones", [128, 128], BF16)
        self.vecs = P.sb("vecs", [128, nvec], F32)
        self.pt = P.sb("pt", [128, 2 * HALF], BF16)
        self.wp = [P.sb("wp%d" % i, [128, 2 * 128], BF16) for i in range(2)]
        self.banks = [P.ps("bank%d" % i) for i in range(8)]
        self.cnt = {}
        P.op("vector", lambda e: e.memset(self.ones.t[:], 1.0), writes=[self.ones])
        P.op("vector", lambda e: e.memset(self.epsc.t[:, 0:1], RMS_EPS), writes=[self.epsc])
        P.dma("sync", self.vecs.t[:], vecs_ap, self.vecs, writes=[self.vecs])

    def rr(self, key, lst):
        i = self.cnt.get(key, 0)
        self.cnt[key] = i + 1
        return lst[i % len(lst)]

    def xs(self, k, j):
        return self.xn.t[:, k * HALF + j * TT:k * HALF + (j + 1) * TT]

    def stats(self, hsrc, hsrc_bufs, n0, j, eps, dim=D):
        P = self.P
        hs = hsrc.rearrange("(k p) n -> k p n", p=128)
        bank = self.banks[6 + j]
        for k in range(KC):
            ht = self.rr("ht", self.ht)
            P.dma("gpsimd", ht.t[:], hs[k, :, n0:n0 + TT], ht, reads=[hsrc_bufs[n0 // TT]], writes=[ht])
            sq = self.rr("sq", self.sq)
            P.op("scalar", lambda e, s=sq, t=ht: e.activation(out=s.t[:], in_=t.t[:], func=AF.Square),
                 reads=[ht], writes=[sq])
            P.op("tensor", lambda e, s=sq, k=k: e.matmul(bank.t[:], self.ones.t[:], s.t[:], start=(k == 0), stop=(k == KC - 1)),
                 reads=[sq, self.ones], writes=[bank], inc=True)
        P.op("scalar", lambda e: e.activation(out=self.rtmp.t[:], in_=bank.t[:], func=AF.Sqrt, bias=self.epsc.t[:, 0:1], scale=1.0 / dim),
             reads=[bank, self.epsc], writes=[self.rtmp])
        P.op("vector", lambda e: e.reciprocal(out=self.rstd[j].t[:], in_=self.rtmp.t[:]), reads=[self.rtmp], writes=[self.rstd[j]])

    def norm(self, hsrc, hsrc_bufs, half, gcol, eps=RMS_EPS):
        P = self.P
        hs = hsrc.rearrange("(k p) n -> k p n", p=128)
        for j in range(2):
            n0 = half * HALF + j * TT
            bank = self.banks[6 + j]
            for k in range(KC):
                ht = self.rr("ht", self.ht)
                P.dma("gpsimd", ht.t[:], hs[k, :, n0:n0 + TT], ht, reads=[hsrc_bufs[n0 // TT]], writes=[ht])
                sq = self.rr("sq", self.sq)
                P.op("scalar", lambda e, s=sq, t=ht: e.activation(out=s.t[:], in_=t.t[:], func=AF.Square),
                     reads=[ht], writes=[sq])
                P.op("tensor", lambda e, s=sq, k=k, bank=bank: e.matmul(bank.t[:], self.ones.t[:], s.t[:], start=(k == 0), stop=(k == KC - 1)),
                     reads=[sq, self.ones], writes=[bank], inc=True)
                P.op("vector", lambda e, k=k, j=j, t=ht: e.tensor_scalar(
                    out=self.xs(k, j), in0=t.t[:], scalar1=self.vecs.t[:, gcol + k:gcol + k + 1], scalar2=None, op0=ALU.mult),
                    reads=[ht, self.vecs], writes=[self.xn])
            P.op("scalar", lambda e, bank=bank: e.activation(out=self.rtmp.t[:], in_=bank.t[:], func=AF.Sqrt, bias=self.epsc.t[:, 0:1], scale=1.0 / D),
                 reads=[bank, self.epsc], writes=[self.rtmp])
            P.op("vector", lambda e, j=j: e.reciprocal(out=self.rstd[j].t[:], in_=self.rtmp.t[:]), reads=[self.rtmp], writes=[self.rstd[j]])
        for j in range(2):
            for k in range(KC):
                P.op("vector", lambda e, k=k, j=j: e.tensor_tensor(out=self.xs(k, j), in0=self.xs(k, j), in1=self.rstd[j].t[:], op=ALU.mult),
                     reads=[self.rstd[j], self.xn], writes=[self.xn])

    def load_w(self, slot, w_ap, c0, ncol, kchunks, r0=0):
        src = w_ap[r0:r0 + kchunks * 128, :].rearrange("(k p) c -> p k c", p=128)[:, :, c0:c0 + ncol]
        dst = slot.t[:, 0:kchunks * ncol].rearrange("p (k c) -> p k c", k=kchunks)
        self.P.dma("gpsimd", dst, src, slot, writes=[slot])

    def mm_group(self, bank, lhs_fn, rhs_fn, nk, reads, ncol=TT):
        for k in range(nk):
            self.P.op("tensor", lambda e, k=k: e.matmul(bank.t[:, 0:ncol], lhs_fn(k), rhs_fn(k), start=(k == 0), stop=(k == nk - 1)),
                      reads=reads, writes=[bank], inc=(k == nk - 1))

    def add_store(self, bank, scale, hsrc, hsrc_bufs, hdst, hdst_bufs, oc, n0, mul_by=None):
        P = self.P
        hs = hsrc.rearrange("(k p) n -> k p n", p=128)
        hd = hdst.rearrange("(k p) n -> k p n", p=128)
        ht = self.rr("ht", self.ht)
        P.dma("gpsimd", ht.t[:], hs[oc, :, n0:n0 + TT], ht, reads=[hsrc_bufs[n0 // TT]], writes=[ht])
        if mul_by is None:
            P.op("vector", lambda e, t=ht, b=bank: e.scalar_tensor_tensor(
                out=t.t[:], in0=b.t[:], scalar=scale, in1=t.t[:], op0=ALU.mult, op1=ALU.add), reads=[bank, ht], writes=[ht])
        else:
            P.op("vector", lambda e, m=mul_by, b=bank: e.tensor_tensor(out=m.t[:], in0=m.t[:], in1=b.t[:], op=ALU.mult),
                 reads=[bank, mul_by], writes=[mul_by])
            P.op("vector", lambda e, t=ht, m=mul_by: e.tensor_tensor(out=t.t[:], in0=t.t[:], in1=m.t[:], op=ALU.add),
                 reads=[mul_by, ht], writes=[ht])
        P.dma("sync", hd[oc, :, n0:n0 + TT], ht.t[:], ht, reads=[ht], writes=[hdst_bufs[n0 // TT]])

    def ffn(self, hsrc, hsrc_bufs, hdst, hdst_bufs, half, w_in, w_out):
        P = self.P
        for s in range(FS):
            wg = self.rr("win", self.win)
            wu = self.rr("win", self.win)
            self.load_w(wg, w_in, s * 128, 128, KC)
            self.load_w(wu, w_in, DFF + s * 128, 128, KC)
            for j in range(2):
                ba = self.rr("bA", self.banks[0:2])
                bb = self.rr("bB", self.banks[2:4])
                self.mm_group(ba, lambda k, w=wg: w.t[:, k * 128:(k + 1) * 128], lambda k, j=j: self.xs(k, j), KC, [wg, self.xn])
                self.mm_group(bb, lambda k, w=wu: w.t[:, k * 128:(k + 1) * 128], lambda k, j=j: self.xs(k, j), KC, [wu, self.xn])
                tmp = self.rr("tmp", self.tmp)
                P.op("scalar", lambda e, t=tmp, b=ba: e.activation(out=t.t[:], in_=b.t[:], func=AF.Silu), reads=[ba], writes=[tmp])
                P.op("vector", lambda e, t=tmp, b=bb, s=s, j=j: e.tensor_tensor(
                    out=self.act.t[:, s * HALF + j * TT:s * HALF + (j + 1) * TT], in0=t.t[:], in1=b.t[:], op=ALU.mult),
                    reads=[tmp, bb], writes=[self.act])
        for oc in range(KC):
            wo = self.rr("wout", self.wout)
            self.load_w(wo, w_out, oc * 128, 128, FS)
            for j in range(2):
                n0 = half * HALF + j * TT
                bo = self.rr("bO", self.banks[4:6])
                self.mm_group(bo, lambda k, w=wo: w.t[:, k * 128:(k + 1) * 128],
                              lambda k, j=j: self.act.t[:, k * HALF + j * TT:k * HALF + (j + 1) * TT], FS, [wo, self.act])
                self.add_store(bo, 0.5, hsrc, hsrc_bufs, hdst, hdst_bufs, oc, n0)

    def ple(self, hsrc, hsrc_bufs, hdst, hdst_bufs, half, w_gate, bcol, pT, w_proj):
        P = self.P
        ptv = pT.rearrange("(k p) n -> p k n", p=128)[:, :, half * HALF:(half + 1) * HALF]
        P.dma("gpsimd", self.pt.t[:].rearrange("p (k n) -> p k n", k=2), ptv, self.pt, writes=[self.pt])
        for oc in range(KC):
            wg = self.rr("win", self.win)
            self.load_w(wg, w_gate, oc * 128, 128, KC)
            wp = self.rr("wp", self.wp)
            self.load_w(wp, w_proj, oc * 128, 128, 2)
            for j in range(2):
                n0 = half * HALF + j * TT
                ba = self.rr("bA", self.banks[0:2])
                bb = self.rr("bB", self.banks[2:4])
                self.mm_group(ba, lambda k, w=wg: w.t[:, k * 128:(k + 1) * 128], lambda k, j=j: self.xs(k, j), KC, [wg, self.xn])
                self.mm_group(bb, lambda k, w=wp: w.t[:, k * 128:(k + 1) * 128],
                              lambda k, j=j: self.pt.t[:, k * HALF + j * TT:k * HALF + (j + 1) * TT], 2, [wp, self.pt])
                tmp = self.rr("tmp", self.tmp)
                P.op("scalar", lambda e, t=tmp, b=ba, oc=oc: e.activation(
                    out=t.t[:], in_=b.t[:], func=AF.Sigmoid, bias=self.vecs.t[:, bcol + oc:bcol + oc + 1], scale=1.0),
                    reads=[ba, self.vecs], writes=[tmp])
                self.add_store(bb, 1.0, hsrc, hsrc_bufs, hdst, hdst_bufs, oc, n0, mul_by=tmp)

    def proj_fm(self, half, w_ap, c0, nchunks, dst_fn, dst_bufs, evac="copy"):
        P = self.P
        for c in range(nchunks):
            w = self.rr("win", self.win)
            self.load_w(w, w_ap, c0 + c * 128, 128, KC)
            for j in range(2):
                b = self.rr("bA", self.banks[0:4])
                self.mm_group(b, lambda k, w=w: w.t[:, k * 128:(k + 1) * 128], lambda k, j=j: self.xs(k, j), KC, [w, self.xn])
                ob = self.rr("ob", self.ob)
                eng = self.rr("evac", ["scalar", "vector"])
                if eng == "scalar":
                    P.op("scalar", lambda e, o=ob, b=b: e.activation(out=o.t[:], in_=b.t[:], func=AF.Copy), reads=[b], writes=[ob])
                else:
                    P.op("vector", lambda e, o=ob, b=b: e.tensor_copy(out=o.t[:], in_=b.t[:]), reads=[b], writes=[ob])
                P.dma("sync", dst_fn(c, j), ob.t[:], ob, reads=[ob], writes=dst_bufs)

    def oproj(self, hsrc, hsrc_bufs, hdst, hdst_bufs, half, w_o):
        for oc in range(KC):
            w = self.rr("win", self.win)
            self.load_w(w, w_o, oc * 128, 128, KC)
            for j in range(2):
                n0 = half * HALF + j * TT
                b = self.rr("bO", self.banks[4:6])
                self.mm_group(b, lambda k, w=w: w.t[:, k * 128:(k + 1) * 128], lambda k, j=j: self.xs(k, j), KC, [w, self.xn])
                self.add_store(b, 1.0, hsrc, hsrc_bufs, hdst, hdst_bufs, oc, n0)

    def final_norm(self, hsrc, hsrc_bufs, out_ap, out_bufs, half, gcol):
        P = self.P
        hs = hsrc.rearrange("(k p) n -> k p n", p=128)
        od = out_ap.rearrange("(k p) n -> k p n", p=128)
        for j in range(2):
            self.stats(hsrc, hsrc_bufs, half * HALF + j * TT, j, RMS_EPS)
        for j in range(2):
            n0 = half * HALF + j * TT
            for k in range(KC):
                ht = self.rr("ht", self.ht)
                P.dma("gpsimd", ht.t[:], hs[k, :, n0:n0 + TT], ht, reads=[hsrc_bufs[n0 // TT]], writes=[ht])
                P.op("vector", lambda e, k=k, t=ht, j=j: e.scalar_tensor_tensor(
                    out=t.t[:], in0=t.t[:], scalar=self.vecs.t[:, gcol + k:gcol + k + 1], in1=self.rstd[j].t[:],
                    op0=ALU.mult, op1=ALU.mult), reads=[ht, self.vecs, self.rstd[j]], writes=[ht])
                P.dma("sync", od[k, :, n0:n0 + TT], ht.t[:], ht, reads=[ht], writes=out_bufs)

    def evac_bf16(self, bank, ncol=TT):
        P = self.P
        ob = self.rr("ob", self.ob)
        eng = self.rr("evac", ["scalar", "vector"])
        if eng == "scalar":
            P.op("scalar", lambda e, o=ob, b=bank: e.activation(out=o.t[:, 0:ncol], in_=b.t[:, 0:ncol], func=AF.Copy), reads=[bank], writes=[ob])
        else:
            P.op("vector", lambda e, o=ob, b=bank: e.tensor_copy(out=o.t[:, 0:ncol], in_=b.t[:, 0:ncol]), reads=[bank], writes=[ob])
        return ob

    def proj_tm(self, half, w_ap, c0, ncol, dst_fn, dst_bufs):
        P = self.P
        w = self.rr("wout", self.wout)
        self.load_w(w, w_ap, c0, ncol, KC)
        for tb in range(HALF // 128):
            b = self.rr("bA", self.banks[0:4])
            self.mm_group(b, lambda k, tb=tb: self.xn.t[:, k * HALF + tb * 128:k * HALF + (tb + 1) * 128],
                          lambda k, w=w: w.t[:, k * ncol:(k + 1) * ncol], KC, [w, self.xn], ncol=ncol)
            ob = self.evac_bf16(b, ncol)
            P.dma("sync", dst_fn(tb), ob.t[:, 0:ncol], ob, reads=[ob], writes=dst_bufs)

    def load_xn(self, src, src_bufs, half):
        v = src.rearrange("(k p) n -> p k n", p=128)[:, :, half * HALF:(half + 1) * HALF]
        self.P.dma("sync", self.xn.t[:].rearrange("p (k n) -> p k n", k=KC), v, self.xn, reads=src_bufs, writes=[self.xn])

    def proj_conv_in(self, half, w_ap, zT, bgT, dst_bufs):
        P = self.P
        zd = zT.rearrange("(k p) n -> k p n", p=128)
        bd = bgT.rearrange("(k p) n -> k p n", p=128)
        for c in range(KC):
            wb = self.rr("win", self.win)
            wc = self.rr("win", self.win)
            wu = self.rr("win", self.win)
            self.load_w(wb, w_ap, c * 128, 128, KC)
            self.load_w(wc, w_ap, D + c * 128, 128, KC)
            self.load_w(wu, w_ap, 2 * D + c * 128, 128, KC)
            for j in range(2):
                n0 = half * HALF + j * TT
                b0 = self.rr("bA", self.banks[0:2])
                b1 = self.rr("bB", self.banks[2:4])
                b2 = self.rr("bO", self.banks[4:6])
                for (b, w) in ((b0, wb), (b1, wc), (b2, wu)):
                    self.mm_group(b, lambda k, w=w: w.t[:, k * 128:(k + 1) * 128], lambda k, j=j: self.xs(k, j), KC, [w, self.xn])
                t0 = self.rr("ht", self.ht)
                P.op("scalar", lambda e, t=t0, b=b0: e.activation(out=t.t[:], in_=b.t[:], func=AF.Copy), reads=[b0], writes=[t0])
                P.dma("sync", bd[c, :, n0:n0 + TT], t0.t[:], t0, reads=[t0], writes=dst_bufs)
                t1 = self.rr("ht", self.ht)
                P.op("scalar", lambda e, t=t1, b=b1: e.activation(out=t.t[:], in_=b.t[:], func=AF.Copy), reads=[b1], writes=[t1])
                P.op("vector", lambda e, t=t1, b=b2: e.tensor_tensor(out=t.t[:], in0=t.t[:], in1=b.t[:], op=ALU.mult), reads=[b2, t1], writes=[t1])
                P.dma("sync", zd[c, :, n0:n0 + TT], t1.t[:], t1, reads=[t1], writes=dst_bufs)

    def init_attn(self, masks_ap, nmask, sel_ap, ident_ap):
        P = self.P
        self.masks = P.sb("masks", [128, nmask * 128], BF16)
        self.ident = P.sb("ident", [128, 128], BF16)
        self.sel = P.sb("sel", [128, 16], F32)
        P.dma("gpsimd", self.masks.t[:], masks_ap, self.masks, writes=[self.masks])
        P.dma("gpsimd", self.ident.t[:], ident_ap, self.ident, writes=[self.ident])
        P.dma("sync", self.sel.t[:], sel_ap, self.sel, writes=[self.sel])
        self.NPT = 6
        self.LA = 3
        self.pts = [P.vbuf("pt") for _ in range(self.NPT)]
        self.laccs = [P.sb("lacc%d" % i, [128, TT], F32) for i in range(2)]
        self.PT0 = FS * HALF - self.NPT * TT

    def pt_ap(self, i):
        return self.act.t[:, self.PT0 + i * TT:self.PT0 + (i + 1) * TT]

    def mask_ap(self, mi):
        return self.masks.t[:, mi * 128:(mi + 1) * 128]

    def attn_run(self, blocks, q_ap, qbuf, o_banks, l_bank, scale):
        P = self.P
        n = len(blocks)
        st = [None] * n

        def stage_s(i):
            b = blocks[i]
            c0 = b.get("c0", 0)
            ms = b.get("masks", ())
            sb = self.rr("bS", [self.banks[0], self.banks[1], self.banks[2], self.banks[7]])
            P.op("tensor", lambda e, b=b, sb=sb, c0=c0, ms=ms: e.matmul(sb.t[:, c0:TT], b["K"], q_ap[:, c0:TT], start=True, stop=(len(ms) == 0)),
                 reads=[qbuf] + b["bufs"], writes=[sb], inc=(len(ms) == 0))
            for mi, (map_, mc0, w) in enumerate(ms):
                last = mi == len(ms) - 1
                P.op("tensor", lambda e, sb=sb, map_=map_, mc0=mc0, w=w, last=last: e.matmul(
                    sb.t[:, mc0:mc0 + w], self.ident.t[:], map_, start=False, stop=last),
                    reads=[self.ident, self.masks], writes=[sb], inc=last)
            pi = self.cnt.get("pti", 0)
            self.cnt["pti"] = pi + 1
            pt = self.pts[pi % self.NPT]
            pap = self.pt_ap(pi % self.NPT)
            bias = b.get("bias")
            if bias is None:
                bias = self.sel.t[:, 10:11]
            P.op("scalar", lambda e, sb=sb, pap=pap, c0=c0, bias=bias: e.activation(
                out=pap[:, c0:TT], in_=sb.t[:, c0:TT], func=AF.Exp, bias=bias, scale=scale),
                reads=[sb, self.sel], writes=[pt])
            st[i] = (pt, pap, c0)

        lacc = self.rr("lacc", self.laccs)

        def stage_pv(i):
            pt, pap, c0 = st[i]
            b = blocks[i]
            first, last = (i == 0), (i == n - 1)
            for oi, ob in enumerate(o_banks):
                P.op("tensor", lambda e, ob=ob, v=b["V"][oi], pap=pap, c0=c0, first=first, last=last: e.matmul(
                    ob.t[:, c0:TT], v, pap[:, c0:TT], start=first, stop=last),
                    reads=[pt] + b["bufs"], writes=[ob], inc=(oi == len(o_banks) - 1))
            if first:
                P.op("vector", lambda e, pap=pap: e.tensor_copy(out=lacc.t[:], in_=pap), reads=[pt], writes=[lacc])
            else:
                P.op("vector", lambda e, pap=pap, c0=c0: e.tensor_tensor(out=lacc.t[:, c0:TT], in0=lacc.t[:, c0:TT], in1=pap[:, c0:TT], op=ALU.add),
                     reads=[pt, lacc], writes=[lacc])

        LA = self.LA
        for i in range(min(LA, n)):
            stage_s(i)
        for i in range(n):
            stage_pv(i)
            if i + LA < n:
                stage_s(i + LA)
        P.op("tensor", lambda e: e.matmul(l_bank.t[:], self.onesf.t[:], lacc.t[:], start=True, stop=True),
             reads=[lacc, self.onesf], writes=[l_bank])

    def attn_A(self, qA, kloc, vloc, kall, vall, own_bufs, src_bufs, attn, attn_bufs):
        P = self.P
        scale = 128.0 ** -0.5
        A = self.act.t
        QO, KO, VO, PO = 0, 8192, 14336, 20480
        qb, kb, vb = P.vbuf("qA"), P.vbuf("kA"), P.vbuf("vA")
        kpb = [P.vbuf("kpA") for _ in range(3)]
        vpb = [P.vbuf("vpA") for _ in range(3)]
        allb = [qb, kb, vb] + kpb + vpb + self.pts
        for b in allb:
            b.r.update(self.act.r); b.w.update(self.act.w)
        osets = [(self.banks[3], self.banks[4]), (self.banks[5], self.banks[6])]
        attn_v = attn.rearrange("(h d) n -> d h n", d=128)
        for kvh in range(4):
            qsrc = qA.rearrange("(h d) (nb q) -> d nb h q", d=128, q=128)[:, :, kvh * 4:(kvh + 1) * 4, :]
            qdst = A[:, QO:QO + 8192].rearrange("p (nb h q) -> p nb h q", nb=16, h=4)
            for hh in range(4):
                P.dma("sync", qdst[:, :, hh, :], qsrc[:, :, hh, :], qb, reads=own_bufs, writes=[qb])
            ksrc = kloc.rearrange("(g v d) n -> d g v n", g=3, v=4)[:, :, kvh, :]
            P.dma("sync", A[:, KO:KO + 6144].rearrange("p (g n) -> p g n", g=3), ksrc, kb, reads=own_bufs, writes=[kb])
            vsrc = vloc.rearrange("(b p) (g v d) -> p g b v d", p=128, g=3, v=4)[:, :, :, kvh, :]
            vdst = A[:, VO:VO + 6144].rearrange("p (g b d) -> p g b d", g=3, b=16)
            for g in range(3):
                P.dma("sync", vdst[:, g], vsrc[:, g], vb, reads=own_bufs, writes=[vb])
            for s in range(3):
                kbase = PO + s * 6144
                vbase = kbase + 3072
                vv = vall.rearrange("(r ih s jl p) (g v d) -> s p g v r ih jl d", r=4, ih=2, s=4, jl=2, p=128, g=3, v=4)[s]
                for g in range(3):
                    r0 = (g * 4 + kvh) * 128
                    base = ((r0 // 256) * 4 + s) * 256 + (r0 % 256)
                    ka = kall[base:base + 128, :].rearrange("d (r j) -> d r j", r=4)
                    if g < 2:
                        P.dma("sync", A[:, kbase + g * 512:kbase + (g + 1) * 512].rearrange("p (r j) -> p r j", r=4),
                              ka[:, :, 384:512], kpb[s], reads=src_bufs, writes=[kpb[s]])
                        P.dma("sync", A[:, vbase + g * 512:vbase + (g + 1) * 512].rearrange("p (r d) -> p r d", r=4),
                              vv[:, g, kvh, :, 1, 1, :], vpb[s], reads=src_bufs, writes=[vpb[s]])
                    else:
                        P.dma("sync", A[:, kbase + 1024:kbase + 3072].rearrange("p (r j) -> p r j", r=4),
                              ka, kpb[s], reads=src_bufs, writes=[kpb[s]])
                        for r in range(4):
                            for ih in range(2):
                                o_ = vbase + 1024 + r * 512 + ih * 256
                                P.dma("sync", A[:, o_:o_ + 256].rearrange("p (jl d) -> p jl d", jl=2),
                                      vv[:, g, kvh, r, ih, :, :], vpb[s], reads=src_bufs, writes=[vpb[s]])
            for r4 in range(4):
                for jb in range(4):
                    blocks = []

                    def add(g, rk, jbk, mi):
                        m4 = [(self.mask_ap(mi), hh * 128, 128) for hh in range(4)]
                        if jbk >= 0:
                            blk = rk * 4 + jbk
                            blocks.append(dict(K=A[:, KO + g * 2048 + blk * 128:KO + g * 2048 + (blk + 1) * 128],
                                               V=[A[:, VO + (g * 16 + blk) * 128:VO + (g * 16 + blk + 1) * 128]],
                                               bufs=[kb, vb], masks=m4))
                        else:
                            jp = jbk + 4
                            for s in range(3):
                                kbase = PO + s * 6144
                                vbase = kbase + 3072
                                if g < 2:
                                    o = g * 512 + rk * 128
                                else:
                                    o = 1024 + (rk * 4 + jp) * 128
                                blocks.append(dict(K=A[:, kbase + o:kbase + o + 128], V=[A[:, vbase + o:vbase + o + 128]],
                                                   bufs=[kpb[s], vpb[s]], masks=m4, bias=self.sel.t[:, s:s + 1]))

                    add(1, r4, jb, 0)
                    add(1, r4, jb - 1, 1)
                    for dj in range(5):
                        add(2, r4, jb - dj, (2, 3, 3, 3, 4)[dj])
                    for rk in range(4):
                        for dj in range(2):
                            add(0, rk, jb - dj, 5 + (r4 - rk + 3) * 2 + dj)
                    nb = r4 * 4 + jb
                    ob, lb = self.rr("oset", osets)
                    self.attn_run(blocks, A[:, QO + nb * 512:QO + (nb + 1) * 512], qb, [ob], lb, scale)
                    rl = self.rr("tmp", self.tmp)
                    P.op("vector", lambda e, rl=rl, lb=lb: e.reciprocal(out=rl.t[:], in_=lb.t[:]), reads=[lb], writes=[rl])
                    o = self.rr("ob", self.ob)
                    P.op("vector", lambda e, o=o, ob=ob, rl=rl: e.tensor_tensor(out=o.t[:], in0=ob.t[:], in1=rl.t[:], op=ALU.mult),
                         reads=[ob, rl], writes=[o])
                    P.dma("sync", attn_v[:, kvh * 4:(kvh + 1) * 4, nb * 128:(nb + 1) * 128],
                          o.t[:].rearrange("p (h q) -> p h q", h=4), o, reads=[o], writes=attn_bufs)
        for b in allb:
            for k, v in b.r.items():
                self.act.r[k] = max(self.act.r.get(k, 0), v)
            for k, v in b.w.items():
                self.act.w[k] = max(self.act.w.get(k, 0), v)

    def lam_setup(self, lam_ap, lam_init):
        P = self.P
        self.lamt = P.sb("lamt", [128, 8], F32)
        self.onesf = P.sb("onesf", [128, 128], F32)
        P.op("vector", lambda e: e.memset(self.onesf.t[:], 1.0), writes=[self.onesf])
        P.dma("sync", self.lamt.t[:, 0:4], lam_ap, self.lamt, writes=[self.lamt])
        L = self.lamt
        P.op("vector", lambda e: e.tensor_tensor(out=L.t[:, 4:5], in0=L.t[:, 0:1], in1=L.t[:, 1:2], op=ALU.mult), reads=[L], writes=[L])
        P.op("vector", lambda e: e.tensor_tensor(out=L.t[:, 5:6], in0=L.t[:, 2:3], in1=L.t[:, 3:4], op=ALU.mult), reads=[L], writes=[L])
        bank = self.banks[7]
        P.op("tensor", lambda e: e.matmul(bank.t[:, 0:2], self.onesf.t[:], L.t[:, 4:6], start=True, stop=True),
             reads=[L, self.onesf], writes=[bank])
        P.op("scalar", lambda e: e.activation(out=L.t[:, 6:8], in_=bank.t[:, 0:2], func=AF.Exp), reads=[bank], writes=[L])
        P.op("vector", lambda e: e.scalar_tensor_tensor(out=L.t[:, 4:5], in0=L.t[:, 7:8], scalar=-lam_init, in1=L.t[:, 6:7],
                                                        op0=ALU.add, op1=ALU.subtract), reads=[L], writes=[L])

    def attn_B(self, qB, kloc, vloc, kall, vall, own_bufs, src_bufs, attn, attn_bufs, gcol, lam_init):
        P = self.P
        scale = 128.0 ** -0.5
        A = self.act.t
        X = self.xn.t
        XF = self.xn.t.bitcast(F32)
        QO, KA, KO, VA = 0, 4096, 20480, 24576
        qb, kob, vob = P.vbuf("qB"), P.vbuf("koB"), P.vbuf("voB")
        kab = [[P.vbuf("kaB") for _ in range(3)] for _ in range(2)]
        vab = [P.vbuf("vaB") for _ in range(3)]
        ocb = [P.vbuf("oc0"), P.vbuf("oc1")]
        dfb = P.vbuf("diff")
        allb = [qb, kob, vob, dfb] + kab[0] + kab[1] + vab + ocb + self.pts
        for b in allb:
            b.r.update(self.act.r); b.w.update(self.act.w)
            b.r.update(self.xn.r); b.w.update(self.xn.w)
        oc_ap = [XF[:, 2048:3072], XF[:, 3072:4096]]
        df_ap = XF[:, 4096:5120]
        o_banks = [self.banks[3], self.banks[4]]
        l_bank = self.banks[5]
        sbank = self.banks[6]
        one_m = 1.0 - lam_init
        P.op("vector", lambda e: e.memset(self.epsc.t[:, 1:2], SUBLN_EPS / (one_m * one_m)), reads=[], writes=[self.epsc])
        attn_v = attn.rearrange("(h c d) n -> h c d n", c=2, d=128)
        for h in range(8):
            qs = qB.rearrange("(h c d) n -> h d c n", c=2, d=128)[h]
            P.dma("sync", A[:, QO:QO + 4096].rearrange("p (c n) -> p c n", c=2), qs, qb, reads=own_bufs, writes=[qb])
            va = vall.rearrange("(i s wl p) (h e) -> h s p i wl e", s=4, wl=2, p=128, e=256)[h]
            for s in range(3):
                for c in range(2):
                    ka = kall.rearrange("(h s c d) n -> h c s d n", s=4, c=2, d=128)[h, c, s]
                    P.dma("sync", A[:, KA + c * 8192 + s * 2048:KA + c * 8192 + (s + 1) * 2048], ka, kab[c][s], reads=src_bufs, writes=[kab[c][s]])
                vd = A[:, VA + s * 4096:VA + (s + 1) * 4096].rearrange("p (i wl e) -> p i wl e", wl=2, e=256)
                for wl in range(2):
                    P.dma("sync", vd[:, :, wl, :], va[s][:, :, wl, :], vab[s], reads=src_bufs, writes=[vab[s]])
            ks = kloc.rearrange("(h c d) n -> h d c n", c=2, d=128)[h]
            P.dma("sync", A[:, KO:KO + 4096].rearrange("p (c n) -> p c n", c=2), ks, kob, reads=own_bufs, writes=[kob])
            vo = vloc.rearrange("(b p) (h e) -> h p b e", p=128, e=256)[h]
            P.dma("sync", X[:, 0:4096].rearrange("p (b e) -> p b e", e=256), vo, vob, reads=own_bufs, writes=[vob])
            for r4 in range(4):
                for c in range(2):
                    blocks = []
                    for s in range(3):
                        for blk in range(16):
                            ko = KA + c * 8192 + s * 2048 + blk * 128
                            vo_ = VA + (s * 16 + blk) * 256
                            blocks.append(dict(K=A[:, ko:ko + 128], V=[A[:, vo_:vo_ + 128], A[:, vo_ + 128:vo_ + 256]],
                                               bufs=[kab[c][s], vab[s]], bias=self.sel.t[:, 3 + s:4 + s]))
                    for jbk in (3, 2, 1, 0):
                        for rk in range(4):
                            blk = rk * 4 + jbk
                            ko = KO + c * 2048 + blk * 128
                            vo_ = blk * 256
                            mi = 0 if rk <= r4 else 19
                            blocks.append(dict(K=A[:, ko:ko + 128], V=[X[:, vo_:vo_ + 128], X[:, vo_ + 128:vo_ + 256]],
                                               bufs=[kob, vob], masks=[(self.mask_ap(mi), jbk * 128, 128)], c0=jbk * 128))
                    q_ap = A[:, QO + c * 2048 + r4 * 512:QO + c * 2048 + (r4 + 1) * 512]
                    self.attn_run(blocks, q_ap, qb, o_banks, l_bank, scale)
                    for e2 in range(2):
                        P.op("vector", lambda e, c=c, e2=e2: e.tensor_copy(out=oc_ap[c][:, e2 * 512:(e2 + 1) * 512], in_=o_banks[e2].t[:]),
                             reads=[o_banks[e2]], writes=[ocb[c]])
                    rl = self.rr("tmp", self.tmp)
                    P.op("vector", lambda e, rl=rl: e.reciprocal(out=rl.t[:], in_=l_bank.t[:]), reads=[l_bank], writes=[rl])
                    for e2 in range(2):
                        P.op("vector", lambda e, c=c, e2=e2, rl=rl: e.tensor_tensor(
                            out=oc_ap[c][:, e2 * 512:(e2 + 1) * 512], in0=oc_ap[c][:, e2 * 512:(e2 + 1) * 512], in1=rl.t[:], op=ALU.mult),
                            reads=[ocb[c], rl], writes=[ocb[c]])
                P.op("vector", lambda e: e.scalar_tensor_tensor(out=df_ap, in0=oc_ap[1], scalar=self.lamt.t[:, 4:5], in1=oc_ap[0],
                                                                op0=ALU.mult, op1=ALU.add), reads=ocb + [self.lamt], writes=[dfb])
                for e2 in range(2):
                    sq = self.rr("sq", self.sq)
                    P.op("scalar", lambda e, sq=sq, e2=e2: e.activation(out=sq.t[:], in_=df_ap[:, e2 * 512:(e2 + 1) * 512], func=AF.Square),
                         reads=[dfb], writes=[sq])
                    P.op("tensor", lambda e, sq=sq, e2=e2: e.matmul(sbank.t[:], self.ones.t[:], sq.t[:], start=(e2 == 0), stop=(e2 == 1)),
                         reads=[sq, self.ones], writes=[sbank], inc=True)
                P.op("scalar", lambda e: e.activation(out=self.rtmp.t[:], in_=sbank.t[:], func=AF.Sqrt, bias=self.epsc.t[:, 1:2],
                                                      scale=1.0 / (256.0 * one_m * one_m)), reads=[sbank, self.epsc], writes=[self.rtmp])
                P.op("vector", lambda e: e.reciprocal(out=self.rstd[0].t[:], in_=self.rtmp.t[:]), reads=[self.rtmp], writes=[self.rstd[0]])
                for e2 in range(2):
                    o = self.rr("ob", self.ob)
                    P.op("vector", lambda e, o=o, e2=e2: e.scalar_tensor_tensor(
                        out=o.t[:], in0=df_ap[:, e2 * 512:(e2 + 1) * 512], scalar=self.vecs.t[:, gcol + e2:gcol + e2 + 1],
                        in1=self.rstd[0].t[:], op0=ALU.mult, op1=ALU.mult), reads=[dfb, self.vecs, self.rstd[0]], writes=[o])
                    P.dma("sync", attn_v[h, e2, :, r4 * 512:(r4 + 1) * 512], o.t[:], o, reads=[o], writes=attn_bufs)
        for b in allb:
            for tgt in (self.act, self.xn):
                for k, v in b.r.items():
                    tgt.r[k] = max(tgt.r.get(k, 0), v)
                for k, v in b.w.items():
                    tgt.w[k] = max(tgt.w.get(k, 0), v)

    def conv_C(self, zT, bgT, zh_all, src_bufs, attn, attn_bufs, wcol):
        P = self.P
        AF32 = self.act.t.bitcast(F32)
        X = self.xn.t
        sets = []
        for i in range(2):
            base = i * 8192
            sets.append(dict(z=AF32[:, base:base + 2048], bg=AF32[:, base + 2048:base + 4096], acc=AF32[:, base + 4096:base + 6144],
                             hin=AF32[:, base + 6144:base + 6144 + 24], hal=AF32[:, base + 6200:base + 6202],
                             out=X[:, i * 2048:(i + 1) * 2048],
                             zb=P.vbuf("cz"), bb=P.vbuf("cbg"), ab=P.vbuf("cacc"), hb=P.vbuf("chin"), ob=P.vbuf("cout")))
        allb = [s[k] for s in sets for k in ("zb", "bb", "ab", "hb", "ob")]
        for b in allb:
            b.r.update(self.act.r); b.w.update(self.act.w)
            b.r.update(self.xn.r); b.w.update(self.xn.w)
        zd = zT.rearrange("(k p) n -> k p n", p=128)
        bd = bgT.rearrange("(k p) n -> k p n", p=128)
        ad = attn.rearrange("(k p) n -> k p n", p=128)
        zh = zh_all.rearrange("(s k p) e -> k p s e", s=4, p=128)
        for c in range(KC):
            S = sets[c % 2]
            P.dma("gpsimd", S["z"], zd[c], S["zb"], reads=src_bufs, writes=[S["zb"]])
            P.dma("gpsimd", S["bg"], bd[c], S["bb"], reads=src_bufs, writes=[S["bb"]])
            P.dma("sync", S["hin"].rearrange("p (s e) -> p s e", s=3), zh[c][:, 0:3, :], S["hb"], reads=src_bufs, writes=[S["hb"]])
            hin, hal = S["hin"], S["hal"]
            P.op("vector", lambda e, hin=hin, hal=hal: e.tensor_scalar(out=hal, in0=hin[:, 0:2], scalar1=self.sel.t[:, 7:8], scalar2=None, op0=ALU.mult),
                 reads=[S["hb"], self.sel], writes=[S["hb"]])
            for s in (1, 2):
                P.op("vector", lambda e, hin=hin, hal=hal, s=s: e.scalar_tensor_tensor(
                    out=hal, in0=hin[:, 8 * s:8 * s + 2], scalar=self.sel.t[:, 7 + s:8 + s], in1=hal, op0=ALU.mult, op1=ALU.add),
                    reads=[S["hb"], self.sel], writes=[S["hb"]])
            w0 = self.vecs.t[:, wcol + c:wcol + c + 1]
            w1 = self.vecs.t[:, wcol + KC + c:wcol + KC + c + 1]
            w2 = self.vecs.t[:, wcol + 2 * KC + c:wcol + 2 * KC + c + 1]
            z, acc = S["z"], S["acc"]
            rd = [S["zb"], S["hb"], self.vecs]
            P.op("vector", lambda e, z=z, acc=acc, w2=w2: e.tensor_scalar(out=acc, in0=z, scalar1=w2, scalar2=None, op0=ALU.mult),
                 reads=rd, writes=[S["ab"]])

            def fma(dst, src, w):
                P.op("vector", lambda e, dst=dst, src=src, w=w: e.scalar_tensor_tensor(out=dst, in0=src, scalar=w, in1=dst, op0=ALU.mult, op1=ALU.add),
                     reads=rd + [S["ab"]], writes=[S["ab"]])

            for r4 in range(4):
                a = acc[:, r4 * 512:(r4 + 1) * 512]
                if r4 >= 1:
                    fma(a, z[:, (r4 - 1) * 512:r4 * 512], w1)
                else:
                    fma(a[:, 1:512], z[:, 3 * 512:3 * 512 + 511], w1)
                    fma(a[:, 0:1], S["hal"][:, 1:2], w1)
                if r4 >= 2:
                    fma(a, z[:, (r4 - 2) * 512:(r4 - 1) * 512], w0)
                else:
                    fma(a[:, 1:512], z[:, (r4 + 2) * 512:(r4 + 2) * 512 + 511], w0)
                    fma(a[:, 0:1], S["hal"][:, r4:r4 + 1], w0)
            P.op("vector", lambda e, S=S: e.tensor_tensor(out=S["out"], in0=S["acc"], in1=S["bg"], op=ALU.mult),
                 reads=[S["ab"], S["bb"]], writes=[S["ob"]])
            P.dma("sync", ad[c], S["out"], S["ob"], reads=[S["ob"]], writes=attn_bufs)
        for b in allb:
            for tgt in (self.act, self.xn):
                for k, v in b.r.items():
                    tgt.r[k] = max(tgt.r.get(k, 0), v)
                for k, v in b.w.items():
                    tgt.w[k] = max(tgt.w.get(k, 0), v)


NV = 400
VL = 80
V_NORMF = 320
V_SUBLN = 336
V_CONV = 338
NMASK = 20
GROUPS = [[0, 1, 2, 3], [4, 5, 6, 7]]
LAM_INIT = 0.8 - 0.6 * float(np.exp(-0.3 * 1))

W_SHAPES = {}
for _i in range(4):
    W_SHAPES["w1i%d" % _i] = (D, 2 * DFF)
    W_SHAPES["w1o%d" % _i] = (DFF, D)
    W_SHAPES["w2i%d" % _i] = (D, 2 * DFF)
    W_SHAPES["w2o%d" % _i] = (DFF, D)
    W_SHAPES["wpg%d" % _i] = (D, D)
    W_SHAPES["wpp%d" % _i] = (PLE, D)
for _j in range(2):
    W_SHAPES["aqkv%d" % _j] = (D, 5120)
    W_SHAPES["ao%d" % _j] = (D, D)
W_SHAPES.update(bqkv=(D, 3 * D), bo=(D, D), cin=(D, 3 * D), cout=(D, D))


def build_net(stop=None, dbg=None):
    P = Prog()
    xT = P.dram_in("xT", [D, TOK])
    pT = [P.dram_in("pT%d" % i, [PLE, TOK]) for i in range(4)]
    vecs = P.dram_in("vecs", [128, NV])
    masks = P.dram_in("masks", [128, NMASK * 128])
    ident = P.dram_in("ident", [128, 128])
    sel = P.dram_in("sel", [128, 16])
    lamT = P.dram_in("lamT", [128, 4])
    class _LazyW(dict):
        def __missing__(self, k):
            self[k] = P.dram_in(k, list(W_SHAPES[k]))
            return self[k]
    W = _LazyW()
    outT = P.dram_out("outT", [D, TOK])
    h = P.dram_tmp("h", [D, TOK])
    q = P.dram_tmp("q", [D, TOK], BF16)
    klA = P.dram_tmp("klA", [1536, TOK], BF16)
    vlA = P.dram_tmp("vlA", [TOK, 1536], BF16)
    kaA = P.dram_tmp("kaA", [4 * 1536, TOK], BF16)
    vaA = P.dram_tmp("vaA", [4 * TOK, 1536], BF16)
    klB = P.dram_tmp("klB", [D, TOK], BF16)
    vlB = P.dram_tmp("vlB", [TOK, D], BF16)
    kaB = P.dram_tmp("kaB", [4 * D, TOK], BF16)
    vaB = P.dram_tmp("vaB", [4 * TOK, D], BF16)
    zT = P.dram_tmp("zT", [D, TOK])
    bgT = P.dram_tmp("bgT", [D, TOK])
    zhl = P.dram_tmp("zhl", [D, 8])
    zha = P.dram_tmp("zha", [4 * D, 8])
    attn = P.dram_tmp("attn", [D, TOK], BF16)

    R = Rows(P, vecs, NV)
    R.init_attn(masks, NMASK, sel, ident)
    R.lam_setup(lamT, LAM_INIT)

    xb = [P.vbuf("x") for _ in range(4)]
    hb = [P.vbuf("h") for _ in range(4)]
    ob = [P.vbuf("out")]
    P.out_bufs += ob
    qb, klb, vlb, kab, vab, atb = (P.vbuf(n) for n in ("q", "kl", "vl", "ka", "va", "attn"))
    zb, zhlb, zhab = P.vbuf("z"), P.vbuf("zhl"), P.vbuf("zha")
    ccb = P.vbuf("cc")

    def cc(in_ap, out_ap, in_bufs, out_bufs):
        E = P.engs["gpsimd"]
        if ccb.sem is None:
            ccb.sem = P.new_sem("cc")
        waits = P._collect(E, in_bufs, out_bufs)
        for s_, v_ in waits.items():
            E.waited[s_] = v_
        ccb.cnt += 1
        E.ops.append((sorted(waits.items()), lambda e, i=in_ap, o=out_ap: e.collective_compute(
            "AllGather", ALU.bypass, replica_groups=GROUPS, ins=[i], outs=[o]), (ccb.sem, 1)))
        P._mark((ccb.sem, ccb.cnt), in_bufs, out_bufs)

    def gather(loc, allt, rows, lb, ab):
        for i in range(rows // 256):
            cc(loc[i * 256:(i + 1) * 256, :], allt[i * 1024:(i + 1) * 1024, :], [lb], [ab])

    state = {"hsrc": xT, "hsb": xb}

    def hs():
        return state["hsrc"], state["hsb"]

    def wrote_h():
        state["hsrc"], state["hsb"] = h, hb

    def chain(fns):
        for f in fns:
            for half in range(2):
                f(half)
            if getattr(f, "writes_h", False):
                wrote_h()

    def mk(fn, writes_h=False):
        fn.writes_h = writes_h
        return fn

    def s_ffn(i, which):
        g = i * VL + (0 if which == 1 else 32)
        wi, wo = W["w%di%d" % (which, i)], W["w%do%d" % (which, i)]

        def f(half):
            src, sb_ = hs()
            R.norm(src, sb_, half, g)
            R.ffn(src, sb_, h, hb, half, wi, wo)
        return mk(f, True)

    def s_ple(i):
        def f(half):
            src, sb_ = hs()
            R.norm(src, sb_, half, i * VL + 48)
            R.ple(src, sb_, h, hb, half, W["wpg%d" % i], i * VL + 64, pT[i], W["wpp%d" % i])
        return mk(f, True)

    def s_oproj(w):
        def f(half):
            src, sb_ = hs()
            R.load_xn(attn, [atb], half)
            R.oproj(src, sb_, h, hb, half, w)
        return mk(f, True)

    def s_projA(i, j):
        w = W["aqkv%d" % j]

        def f(half):
            src, sb_ = hs()
            R.norm(src, sb_, half, i * VL + 16)
            qv = q.rearrange("(c p) n -> c p n", p=128)
            R.proj_fm(half, w, 0, 16, lambda c, jj: qv[c, :, half * HALF + jj * TT:half * HALF + (jj + 1) * TT], [qb])
            kv = klA.rearrange("(c p) n -> c p n", p=128)
            for g in range(3):
                R.proj_fm(half, w, D + g * 1024, 4,
                          lambda c, jj, g=g: kv[g * 4 + c, :, half * HALF + jj * TT:half * HALF + (jj + 1) * TT], [klb])
                for grp in range(2):
                    R.proj_tm(half, w, D + g * 1024 + 512 + grp * 256, 256,
                              lambda tb, g=g, grp=grp: vlA[half * HALF + tb * 128:half * HALF + (tb + 1) * 128,
                                                           g * 512 + grp * 256:g * 512 + (grp + 1) * 256], [vlb])
        return mk(f)

    def s_projB(i):
        w = W["bqkv"]

        def f(half):
            src, sb_ = hs()
            R.norm(src, sb_, half, i * VL + 16)
            qv = q.rearrange("(c p) n -> c p n", p=128)
            R.proj_fm(half, w, 0, 16, lambda c, jj: qv[c, :, half * HALF + jj * TT:half * HALF + (jj + 1) * TT], [qb])
            kv = klB.rearrange("(c p) n -> c p n", p=128)
            R.proj_fm(half, w, D, 16, lambda c, jj: kv[c, :, half * HALF + jj * TT:half * HALF + (jj + 1) * TT], [klb])
            for hh in range(8):
                R.proj_tm(half, w, 2 * D + hh * 256, 256,
                          lambda tb, hh=hh: vlB[half * HALF + tb * 128:half * HALF + (tb + 1) * 128, hh * 256:(hh + 1) * 256], [vlb])
        return mk(f)

    def q_stage(i, w):
        def run():
            qv = q.rearrange("(c p) n -> c p n", p=128)
            for half in (1, 0):
                src, sb_ = hs()
                if half == 0:
                    R.norm(src, sb_, half, i * VL + 16)
                R.proj_fm(half, w, 0, 16, lambda c, jj, half=half: qv[c, :, half * HALF + jj * TT:half * HALF + (jj + 1) * TT], [qb])
        return run

    def s_projC(i):
        def f(half):
            src, sb_ = hs()
            R.norm(src, sb_, half, i * VL + 16)
            R.proj_conv_in(half, W["cin"], zT, bgT, [zb])
        return mk(f)

    def s_final():
        def f(half):
            src, sb_ = hs()
            R.final_norm(src, sb_, outT, ob, half, V_NORMF)
        return mk(f)

    def halo():
        P.dma("sync", zhl[:, 0:1], zT[:, 2 * 512 + 511:2 * 512 + 512], zhlb, reads=[zb], writes=[zhlb], allow_slow_non_contiguous=True)
        P.dma("sync", zhl[:, 1:2], zT[:, 3 * 512 + 511:3 * 512 + 512], zhlb, reads=[zb], writes=[zhlb], allow_slow_non_contiguous=True)
        cc(zhl, zha, [zhlb], [zhab])

    def gA():
        gather(klA, kaA, 1536, klb, kab)
        gather(vlA, vaA, TOK, vlb, vab)

    def gB():
        gather(klB, kaB, D, klb, kab)
        gather(vlB, vaB, TOK, vlb, vab)

    mixA = lambda: R.attn_A(q, klA, vlA, kaA, vaA, [qb, klb, vlb], [kab, vab], attn, [atb])
    mixB = lambda: R.attn_B(q, klB, vlB, kaB, vaB, [qb, klb, vlb], [kab, vab], attn, [atb], V_SUBLN, LAM_INIT)
    mixC = lambda: R.conv_C(zT, bgT, zha, [zb, zhab], attn, [atb], V_CONV)
    C1 = lambda th: (lambda: chain([th()]))
    stages = [
        C1(lambda: s_ffn(0, 1)), C1(lambda: s_projA(0, 0)), gA, mixA, C1(lambda: s_oproj(W["ao0"])), C1(lambda: s_ffn(0, 2)), C1(lambda: s_ple(0)),
        C1(lambda: s_ffn(1, 1)), C1(lambda: s_projB(1)), gB, mixB, C1(lambda: s_oproj(W["bo"])), C1(lambda: s_ffn(1, 2)), C1(lambda: s_ple(1)),
        C1(lambda: s_ffn(2, 1)), C1(lambda: s_projC(2)), halo, mixC, C1(lambda: s_oproj(W["cout"])), C1(lambda: s_ffn(2, 2)), C1(lambda: s_ple(2)),
        C1(lambda: s_ffn(3, 1)), C1(lambda: s_projA(3, 1)), gA, mixA, C1(lambda: s_oproj(W["ao1"])), C1(lambda: s_ffn(3, 2)), C1(lambda: s_ple(3)),
        C1(lambda: s_final()),
    ]
    n = len(stages) if stop is None else stop
    for st in stages[:n]:
        st()
    if stop is not None:
        if dbg is None:
            P.dma("sync", outT, h, ob[0], reads=hb, writes=ob)
        else:
            src = {"attn": attn, "q": q, "klA": klA, "vlA": vlA, "klB": klB, "vlB": vlB}[dbg]
            dbo = P.dram_out("dbg", list(src.shape), BF16)
            P.dma("sync", dbo, src, ob[0], reads=[atb, qb, klb, vlb], writes=ob)
            P.dma("sync", outT, h, ob[0], reads=hb, writes=ob)
    nc = P.emit()
    nc._in_names = list(P.in_names)
    return nc


def _r4perm():
    n = np.arange(TOK)
    return 4 * (n % 512) + n // 512


def _masks():
    jk = np.arange(128)[:, None]
    jq = np.arange(128)[None, :]
    vis = []
    vis.append(jk <= jq)
    vis.append(jk >= jq)
    comb = ((jq - jk) % 4) == 0
    vis.append((jk <= jq) & comb)
    vis.append(comb | (jk < -1))
    vis.append((jk >= jq) & comb)
    for dr in range(-3, 4):
        for djb in range(2):
            dt = 4 * (128 * djb + jq - jk) + dr
            vis.append((dt >= 0) & (dt <= 128))
    vis.append(jk < jq)
    m = np.concatenate([np.where(v, 0.0, NEG) for v in vis], axis=1).astype(np.float32)
    assert m.shape == (128, NMASK * 128)
    return m


def _col16(v):
    return np.ascontiguousarray(np.asarray(v, np.float32).reshape(-1, 128).T)


def make_inputs(inputs):
    perm = _r4perm()
    x = np.asarray(inputs["x"], np.float32)
    p = np.asarray(inputs["p"], np.float32)
    vecs = np.zeros((128, NV), np.float32)
    for i in range(4):
        b = i * VL
        vecs[:, b:b + 16] = _col16(inputs["norm_ffn1"][i])
        vecs[:, b + 16:b + 32] = _col16(inputs["norm_mix"][i])
        vecs[:, b + 32:b + 48] = _col16(inputs["norm_ffn2"][i])
        vecs[:, b + 48:b + 64] = _col16(inputs["norm_ple"][i])
        vecs[:, b + 64:b + 80] = _col16(inputs["b_ple_gate"][i])
    vecs[:, V_NORMF:V_NORMF + 16] = _col16(inputs["norm_f"])
    vecs[:, V_SUBLN:V_SUBLN + 2] = _col16(inputs["b_subln"][0])
    for k in range(3):
        vecs[:, V_CONV + 16 * k:V_CONV + 16 * (k + 1)] = _col16(inputs["c_conv_w"][0][k])
    shared = {"vecs": vecs, "masks": _masks(), "ident": np.eye(128, dtype=np.float32),
              "lamT": np.ascontiguousarray(np.asarray(inputs["b_lambda"][0], np.float32).T)}
    for i in range(4):
        shared["w1i%d" % i] = np.asarray(inputs["w_ffn1_in"][i], np.float32)
        shared["w1o%d" % i] = np.asarray(inputs["w_ffn1_out"][i], np.float32)
        shared["w2i%d" % i] = np.asarray(inputs["w_ffn2_in"][i], np.float32)
        shared["w2o%d" % i] = np.asarray(inputs["w_ffn2_out"][i], np.float32)
        shared["wpg%d" % i] = np.asarray(inputs["w_ple_gate"][i], np.float32)
        shared["wpp%d" % i] = np.asarray(inputs["w_ple_proj"][i], np.float32)
    for j in range(2):
        shared["aqkv%d" % j] = np.asarray(inputs["a_w_qkv"][j], np.float32)
        shared["ao%d" % j] = np.asarray(inputs["a_w_o"][j], np.float32)
    shared["bqkv"] = np.asarray(inputs["b_w_qkv"][0], np.float32)
    shared["bo"] = np.asarray(inputs["b_w_o"][0], np.float32)
    shared["cin"] = np.asarray(inputs["c_w_in"][0], np.float32)
    shared["cout"] = np.asarray(inputs["c_w_out"][0], np.float32)
    maps = []
    for core in range(NCORES):
        b, c = divmod(core, 4)
        sl = slice(c * TOK, (c + 1) * TOK)
        m = dict(shared)
        m["xT"] = np.ascontiguousarray(x[b, sl][perm].T)
        for i in range(4):
            m["pT%d" % i] = np.ascontiguousarray(p[i, b, sl][perm].T)
        s = np.zeros((128, 16), np.float32)
        for k in range(3):
            s[:, k] = 0.0 if k == c - 1 else NEG
            s[:, 7 + k] = 1.0 if k == c - 1 else 0.0
        for k in range(4):
            s[:, 3 + k] = 0.0 if k < c else NEG
        m["sel"] = s
        maps.append(m)
    return maps


def assemble(results, key="outT"):
    perm = _r4perm()
    out = np.empty((2, 4 * TOK, D), np.float32)
    for core in range(NCORES):
        b, c = divmod(core, 4)
        o = np.asarray(results[core][key]).astype(np.float32).T
        blk = np.empty_like(o)
        blk[perm] = o
        out[b, c * TOK:(c + 1) * TOK] = blk
    return out


_NC_CACHE = {}


def kernel(**inputs):
    if "net" not in _NC_CACHE:
        _NC_CACHE["net"] = build_net()
    nc = _NC_CACHE["net"]
    maps = [{k: m[k] for k in nc._in_names} for m in make_inputs(inputs)]
    res = run_bass_kernel_spmd(nc, maps, core_ids=list(range(NCORES)))
    return assemble(res.results)
```

```python
import contextlib
import numpy as np
import ml_dtypes
import concourse.bass as bass
import concourse.mybir as mybir
from concourse.bass_utils import run_bass_kernel_spmd

F32 = mybir.dt.float32
BF16 = mybir.dt.bfloat16
AF = mybir.ActivationFunctionType
ALU = mybir.AluOpType

D = 2048
KC = 16
TOK = 2048
HALF = 1024
TT = 512
DFF = 5632
FS = 44
PLE = 256
NCORES = 8
NEG = -30000.0
RMS_EPS = 1e-6
SUBLN_EPS = 1e-5


class Buf:
    __slots__ = ("name", "w", "r", "sem", "cnt", "t")

    def __init__(self, name, t=None):
        self.name = name
        self.w = {}
        self.r = {}
        self.sem = None
        self.cnt = 0
        self.t = t


class Eng:
    def __init__(self, name):
        self.name = name
        self.ops = []
        self.sem = None
        self.cnt = 0
        self.waited = {}


class Prog:
    def __init__(self):
        self.nc = bass.Bass("TRN2", target_bir_lowering=False)
        self.es = contextlib.ExitStack()
        self.engs = {n: Eng(n) for n in ("tensor", "vector", "scalar", "gpsimd", "sync")}
        self.semh = {}
        self.nsem = 0
        for n in ("tensor", "vector", "scalar"):
            self.engs[n].sem = self.new_sem("e_" + n)
        self.out_bufs = []
        self.in_names = []
        self.nbuf = 0

    def new_sem(self, name=None):
        self.nsem += 1
        h = self.es.enter_context(self.nc.semaphore(name or ("s%d" % self.nsem)))
        self.semh[self.nsem] = h
        return self.nsem

    def sb(self, name, shape, dtype):
        t = self.es.enter_context(self.nc.sbuf_tensor("sb_" + name, list(shape), dtype))
        return Buf(name, t)

    def ps(self, name):
        t = self.es.enter_context(self.nc.psum_tensor("ps_" + name, [128, 512], F32))
        return Buf(name, t)

    def dram_in(self, name, shape, dtype=F32):
        self.in_names.append(name)
        return self.nc.dram_tensor(name, list(shape), dtype, kind="ExternalInput").ap()

    def dram_out(self, name, shape, dtype=F32):
        return self.nc.dram_tensor(name, list(shape), dtype, kind="ExternalOutput").ap()

    def dram_tmp(self, name, shape, dtype=F32):
        return self.nc.dram_tensor(name, list(shape), dtype).ap()

    def vbuf(self, name):
        self.nbuf += 1
        return Buf("%s_%d" % (name, self.nbuf))

    def _collect(self, E, reads, writes, extra=()):
        waits = {}

        def need(sem, val):
            if E.waited.get(sem, 0) >= val:
                return
            if waits.get(sem, 0) < val:
                waits[sem] = val

        for b in reads:
            for sem, val in b.w.items():
                need(sem, val)
        for b in writes:
            for sem, val in b.w.items():
                need(sem, val)
            for sem, val in b.r.items():
                need(sem, val)
        for sem, val in extra:
            need(sem, val)
        return waits

    @staticmethod
    def _mark(ev, reads, writes):
        sem, val = ev
        for b in reads:
            if b.r.get(sem, 0) < val:
                b.r[sem] = val
        for b in writes:
            if b.w.get(sem, 0) < val:
                b.w[sem] = val

    def op(self, eng, fn, reads=(), writes=(), inc=True):
        E = self.engs[eng]
        waits = self._collect(E, reads, writes)
        if eng == "tensor":
            waits.pop(E.sem, None)
        for sem, val in waits.items():
            E.waited[sem] = val
        if inc:
            E.cnt += 1
            ev = (E.sem, E.cnt)
        else:
            ev = (E.sem, E.cnt + 1)
        E.ops.append((sorted(waits.items()), fn, (E.sem, 1) if inc else None))
        self._mark(ev, reads, writes)

    def dma(self, eng, out, in_, owner, reads=(), writes=(), **kw):
        E = self.engs[eng]
        if owner.sem is None:
            owner.sem = self.new_sem("d_" + owner.name)
        waits = self._collect(E, reads, writes, extra=[(owner.sem, owner.cnt)] if owner.cnt else [])
        for sem, val in waits.items():
            E.waited[sem] = val
        owner.cnt += 16
        ev = (owner.sem, owner.cnt)
        E.ops.append((sorted(waits.items()), lambda e, o=out, i=in_: e.dma_start(out=o, in_=i, **kw), (owner.sem, 16)))
        self._mark(ev, reads, writes)

    def barrier_wait(self, eng, bufs):
        E = self.engs[eng]
        waits = self._collect(E, [], bufs)
        for sem, val in waits.items():
            E.waited[sem] = val
        if waits:
            E.ops.append((sorted(waits.items()), None, None))

    def check(self):
        sems = {}
        pos = {n: 0 for n in self.engs}
        progress = True
        while progress:
            progress = False
            for n, E in self.engs.items():
                while pos[n] < len(E.ops):
                    waits, fn, inc = E.ops[pos[n]]
                    if any(sems.get(s, 0) < v for s, v in waits):
                        break
                    if inc is not None:
                        sems[inc[0]] = sems.get(inc[0], 0) + inc[1]
                    pos[n] += 1
                    progress = True
        stuck = {n: (pos[n], len(E.ops)) for n, E in self.engs.items() if pos[n] < len(E.ops)}
        if stuck:
            msg = []
            for n in stuck:
                waits, fn, inc = self.engs[n].ops[pos[n]]
                msg.append("%s@%d waits %s have %s" % (n, pos[n], waits, [(s, sems.get(s, 0)) for s, _ in waits]))
            raise RuntimeError("DEADLOCK: " + "; ".join(msg))
        return {n: len(E.ops) for n, E in self.engs.items()}

    def emit(self):
        self.barrier_wait("sync", self.out_bufs)
        self.check()
        nc = self.nc
        semh = self.semh

        def replay(e, E):
            for waits, fn, inc in E.ops:
                for sem, val in waits:
                    e.wait_ge(semh[sem], val)
                if fn is None:
                    continue
                inst = fn(e)
                if inc is not None:
                    inst.then_inc(semh[inc[0]], inc[1])

        with nc.Block() as block:
            @block.tensor
            def _(e):
                replay(e, self.engs["tensor"])

            @block.vector
            def _(e):
                replay(e, self.engs["vector"])

            @block.scalar
            def _(e):
                replay(e, self.engs["scalar"])

            @block.gpsimd
            def _(e):
                replay(e, self.engs["gpsimd"])

            @block.sync
            def _(e):
                replay(e, self.engs["sync"])
        self.es.close()
        return nc


class Rows:
    def __init__(self, P, vecs_ap, nvec):
        self.P = P
        self.act = P.sb("act", [128, FS * HALF], BF16)
        self.xn = P.sb("xn", [128, KC * HALF], BF16)
        self.win = [P.sb("win%d" % i, [128, KC * 128], BF16) for i in range(4)]
        self.wout = [P.sb("wout%d" % i, [128, FS * 128], BF16) for i in range(2)]
        self.ht = [P.sb("ht%d" % i, [128, TT], F32) for i in range(6)]
        self.tmp = [P.sb("tmp%d" % i, [128, TT], F32) for i in range(3)]
        self.sq = [P.sb("sq%d" % i, [128, TT], BF16) for i in range(2)]
        self.rstd = [P.sb("rstd%d" % i, [128, TT], F32) for i in range(2)]
        self.epsc = P.sb("epsc", [128, 2], F32)
        self.rtmp = P.sb("rtmp", [128, TT], F32)
        self.ob = [P.sb("ob%d" % i, [128, TT], BF16) for i in range(4)]
        self.ones = P.sb("ones", [128, 128], BF16)
        self.vecs = P.sb("vecs", [128, nvec], F32)
        self.pt = P.sb("pt", [128, 2 * HALF], BF16)
        self.wp = [P.sb("wp%d" % i, [128, 2 * 128], BF16) for i in range(2)]
        self.banks = [P.ps("bank%d" % i) for i in range(8)]
        self.cnt = {}
        P.op("vector", lambda e: e.memset(self.ones.t[:], 1.0), writes=[self.ones])
        P.op("vector", lambda e: e.memset(self.epsc.t[:, 0:1], RMS_EPS), writes=[self.epsc])
        P.dma("sync", self.vecs.t[:], vecs_ap, self.vecs, writes=[self.vecs])

    def rr(self, key, lst):
        i = self.cnt.get(key, 0)
        self.cnt[key] = i + 1
        return lst[i % len(lst)]

    def xs(self, k, j):
        return self.xn.t[:, k * HALF + j * TT:k * HALF + (j + 1) * TT]

    def stats(self, hsrc, hsrc_bufs, n0, j, eps, dim=D):
        P = self.P
        hs = hsrc.rearrange("(k p) n -> k p n", p=128)
        bank = self.banks[6 + j]
        for k in range(KC):
            ht = self.rr("ht", self.ht)
            P.dma("gpsimd", ht.t[:], hs[k, :, n0:n0 + TT], ht, reads=[hsrc_bufs[n0 // TT]], writes=[ht])
            sq = self.rr("sq", self.sq)
            P.op("scalar", lambda e, s=sq, t=ht: e.activation(out=s.t[:], in_=t.t[:], func=AF.Square),
                 reads=[ht], writes=[sq])
            P.op("tensor", lambda e, s=sq, k=k: e.matmul(bank.t[:], self.ones.t[:], s.t[:], start=(k == 0), stop=(k == KC - 1)),
                 reads=[sq, self.ones], writes=[bank], inc=True)
        P.op("scalar", lambda e: e.activation(out=self.rtmp.t[:], in_=bank.t[:], func=AF.Sqrt, bias=self.epsc.t[:, 0:1], scale=1.0 / dim),
             reads=[bank, self.epsc], writes=[self.rtmp])
        P.op("vector", lambda e: e.reciprocal(out=self.rstd[j].t[:], in_=self.rtmp.t[:]), reads=[self.rtmp], writes=[self.rstd[j]])

    def norm(self, hsrc, hsrc_bufs, half, gcol, eps=RMS_EPS):
        P = self.P
        hs = hsrc.rearrange("(k p) n -> k p n", p=128)
        for j in range(2):
            n0 = half * HALF + j * TT
            bank = self.banks[6 + j]
            for k in range(KC):
                ht = self.rr("ht", self.ht)
                P.dma("gpsimd", ht.t[:], hs[k, :, n0:n0 + TT], ht, reads=[hsrc_bufs[n0 // TT]], writes=[ht])
                sq = self.rr("sq", self.sq)
                P.op("scalar", lambda e, s=sq, t=ht: e.activation(out=s.t[:], in_=t.t[:], func=AF.Square),
                     reads=[ht], writes=[sq])
                P.op("tensor", lambda e, s=sq, k=k, bank=bank: e.matmul(bank.t[:], self.ones.t[:], s.t[:], start=(k == 0), stop=(k == KC - 1)),
                     reads=[sq, self.ones], writes=[bank], inc=True)
                P.op("vector", lambda e, k=k, j=j, t=ht: e.tensor_scalar(
                    out=self.xs(k, j), in0=t.t[:], scalar1=self.vecs.t[:, gcol + k:gcol + k + 1], scalar2=None, op0=ALU.mult),
                    reads=[ht, self.vecs], writes=[self.xn])
            P.op("scalar", lambda e, bank=bank: e.activation(out=self.rtmp.t[:], in_=bank.t[:], func=AF.Sqrt, bias=self.epsc.t[:, 0:1], scale=1.0 / D),
                 reads=[bank, self.epsc], writes=[self.rtmp])
            P.op("vector", lambda e, j=j: e.reciprocal(out=self.rstd[j].t[:], in_=self.rtmp.t[:]), reads=[self.rtmp], writes=[self.rstd[j]])
        for j in range(2):
            for k in range(KC):
                P.op("vector", lambda e, k=k, j=j: e.tensor_tensor(out=self.xs(k, j), in0=self.xs(k, j), in1=self.rstd[j].t[:], op=ALU.mult),
                     reads=[self.rstd[j], self.xn], writes=[self.xn])

    def load_w(self, slot, w_ap, c0, ncol, kchunks, r0=0):
        src = w_ap[r0:r0 + kchunks * 128, :].rearrange("(k p) c -> p k c", p=128)[:, :, c0:c0 + ncol]
        dst = slot.t[:, 0:kchunks * ncol].rearrange("p (k c) -> p k c", k=kchunks)
        self.P.dma("gpsimd", dst, src, slot, writes=[slot])

    def mm_group(self, bank, lhs_fn, rhs_fn, nk, reads, ncol=TT):
        for k in range(nk):
            self.P.op("tensor", lambda e, k=k: e.matmul(bank.t[:, 0:ncol], lhs_fn(k), rhs_fn(k), start=(k == 0), stop=(k == nk - 1)),
                      reads=reads, writes=[bank], inc=(k == nk - 1))

    def add_store(self, bank, scale, hsrc, hsrc_bufs, hdst, hdst_bufs, oc, n0, mul_by=None):
        P = self.P
        hs = hsrc.rearrange("(k p) n -> k p n", p=128)
        hd = hdst.rearrange("(k p) n -> k p n", p=128)
        ht = self.rr("ht", self.ht)
        P.dma("gpsimd", ht.t[:], hs[oc, :, n0:n0 + TT], ht, reads=[hsrc_bufs[n0 // TT]], writes=[ht])
        if mul_by is None:
            P.op("vector", lambda e, t=ht, b=bank: e.scalar_tensor_tensor(
                out=t.t[:], in0=b.t[:], scalar=scale, in1=t.t[:], op0=ALU.mult, op1=ALU.add), reads=[bank, ht], writes=[ht])
        else:
            P.op("vector", lambda e, m=mul_by, b=bank: e.tensor_tensor(out=m.t[:], in0=m.t[:], in1=b.t[:], op=ALU.mult),
                 reads=[bank, mul_by], writes=[mul_by])
            P.op("vector", lambda e, t=ht, m=mul_by: e.tensor_tensor(out=t.t[:], in0=t.t[:], in1=m.t[:], op=ALU.add),
                 reads=[mul_by, ht], writes=[ht])
        P.dma("sync", hd[oc, :, n0:n0 + TT], ht.t[:], ht, reads=[ht], writes=[hdst_bufs[n0 // TT]])

    def ffn(self, hsrc, hsrc_bufs, hdst, hdst_bufs, half, w_in, w_out):
        P = self.P
        for s in range(FS):
            wg = self.rr("win", self.win)
            wu = self.rr("win", self.win)
            self.load_w(wg, w_in, s * 128, 128, KC)
            self.load_w(wu, w_in, DFF + s * 128, 128, KC)
            for j in range(2):
                ba = self.rr("bA", self.banks[0:2])
                bb = self.rr("bB", self.banks[2:4])
                self.mm_group(ba, lambda k, w=wg: w.t[:, k * 128:(k + 1) * 128], lambda k, j=j: self.xs(k, j), KC, [wg, self.xn])
                self.mm_group(bb, lambda k, w=wu: w.t[:, k * 128:(k + 1) * 128], lambda k, j=j: self.xs(k, j), KC, [wu, self.xn])
                tmp = self.rr("tmp", self.tmp)
                P.op("scalar", lambda e, t=tmp, b=ba: e.activation(out=t.t[:], in_=b.t[:], func=AF.Silu), reads=[ba], writes=[tmp])
                P.op("vector", lambda e, t=tmp, b=bb, s=s, j=j: e.tensor_tensor(
                    out=self.act.t[:, s * HALF + j * TT:s * HALF + (j + 1) * TT], in0=t.t[:], in1=b.t[:], op=ALU.mult),
                    reads=[tmp, bb], writes=[self.act])
        for oc in range(KC):
            wo = self.rr("wout", self.wout)
            self.load_w(wo, w_out, oc * 128, 128, FS)
            for j in range(2):
                n0 = half * HALF + j * TT
                bo = self.rr("bO", self.banks[4:6])
                self.mm_group(bo, lambda k, w=wo: w.t[:, k * 128:(k + 1) * 128],
                              lambda k, j=j: self.act.t[:, k * HALF + j * TT:k * HALF + (j + 1) * TT], FS, [wo, self.act])
                self.add_store(bo, 0.5, hsrc, hsrc_bufs, hdst, hdst_bufs, oc, n0)

    def ple(self, hsrc, hsrc_bufs, hdst, hdst_bufs, half, w_gate, bcol, pT, w_proj):
        P = self.P
        ptv = pT.rearrange("(k p) n -> p k n", p=128)[:, :, half * HALF:(half + 1) * HALF]
        P.dma("gpsimd", self.pt.t[:].rearrange("p (k n) -> p k n", k=2), ptv, self.pt, writes=[self.pt])
        for oc in range(KC):
            wg = self.rr("win", self.win)
            self.load_w(wg, w_gate, oc * 128, 128, KC)
            wp = self.rr("wp", self.wp)
            self.load_w(wp, w_proj, oc * 128, 128, 2)
            for j in range(2):
                n0 = half * HALF + j * TT
                ba = self.rr("bA", self.banks[0:2])
                bb = self.rr("bB", self.banks[2:4])
                self.mm_group(ba, lambda k, w=wg: w.t[:, k * 128:(k + 1) * 128], lambda k, j=j: self.xs(k, j), KC, [wg, self.xn])
                self.mm_group(bb, lambda k, w=wp: w.t[:, k * 128:(k + 1) * 128],
                              lambda k, j=j: self.pt.t[:, k * HALF + j * TT:k * HALF + (j + 1) * TT], 2, [wp, self.pt])
                tmp = self.rr("tmp", self.tmp)
                P.op("scalar", lambda e, t=tmp, b=ba, oc=oc: e.activation(
                    out=t.t[:], in_=b.t[:], func=AF.Sigmoid, bias=self.vecs.t[:, bcol + oc:bcol + oc + 1], scale=1.0),
                    reads=[ba, self.vecs], writes=[tmp])
                self.add_store(bb, 1.0, hsrc, hsrc_bufs, hdst, hdst_bufs, oc, n0, mul_by=tmp)

    def proj_fm(self, half, w_ap, c0, nchunks, dst_fn, dst_bufs, evac="copy"):
        P = self.P
        for c in range(nchunks):
            w = self.rr("win", self.win)
            self.load_w(w, w_ap, c0 + c * 128, 128, KC)
            for j in range(2):
                b = self.rr("bA", self.banks[0:4])
                self.mm_group(b, lambda k, w=w: w.t[:, k * 128:(k + 1) * 128], lambda k, j=j: self.xs(k, j), KC, [w, self.xn])
                ob = self.rr("ob", self.ob)
                eng = self.rr("evac", ["scalar", "vector"])
                if eng == "scalar":
                    P.op("scalar", lambda e, o=ob, b=b: e.activation(out=o.t[:], in_=b.t[:], func=AF.Copy), reads=[b], writes=[ob])
                else:
                    P.op("vector", lambda e, o=ob, b=b: e.tensor_copy(out=o.t[:], in_=b.t[:]), reads=[b], writes=[ob])
                P.dma("sync", dst_fn(c, j), ob.t[:], ob, reads=[ob], writes=dst_bufs)

    def oproj(self, hsrc, hsrc_bufs, hdst, hdst_bufs, half, w_o):
        for oc in range(KC):
            w = self.rr("win", self.win)
            self.load_w(w, w_o, oc * 128, 128, KC)
            for j in range(2):
                n0 = half * HALF + j * TT
                b = self.rr("bO", self.banks[4:6])
                self.mm_group(b, lambda k, w=w: w.t[:, k * 128:(k + 1) * 128], lambda k, j=j: self.xs(k, j), KC, [w, self.xn])
                self.add_store(b, 1.0, hsrc, hsrc_bufs, hdst, hdst_bufs, oc, n0)

    def final_norm(self, hsrc, hsrc_bufs, out_ap, out_bufs, half, gcol):
        P = self.P
        hs = hsrc.rearrange("(k p) n -> k p n", p=128)
        od = out_ap.rearrange("(k p) n -> k p n", p=128)
        for j in range(2):
            self.stats(hsrc, hsrc_bufs, half * HALF + j * TT, j, RMS_EPS)
        for j in range(2):
            n0 = half * HALF + j * TT
            for k in range(KC):
                ht = self.rr("ht", self.ht)
                P.dma("gpsimd", ht.t[:], hs[k, :, n0:n0 + TT], ht, reads=[hsrc_bufs[n0 // TT]], writes=[ht])
                P.op("vector", lambda e, k=k, t=ht, j=j: e.scalar_tensor_tensor(
                    out=t.t[:], in0=t.t[:], scalar=self.vecs.t[:, gcol + k:gcol + k + 1], in1=self.rstd[j].t[:],
                    op0=ALU.mult, op1=ALU.mult), reads=[ht, self.vecs, self.rstd[j]], writes=[ht])
                P.dma("sync", od[k, :, n0:n0 + TT], ht.t[:], ht, reads=[ht], writes=out_bufs)

    def evac_bf16(self, bank, ncol=TT):
        P = self.P
        ob = self.rr("ob", self.ob)
        eng = self.rr("evac", ["scalar", "vector"])
        if eng == "scalar":
            P.op("scalar", lambda e, o=ob, b=bank: e.activation(out=o.t[:, 0:ncol], in_=b.t[:, 0:ncol], func=AF.Copy), reads=[bank], writes=[ob])
        else:
            P.op("vector", lambda e, o=ob, b=bank: e.tensor_copy(out=o.t[:, 0:ncol], in_=b.t[:, 0:ncol]), reads=[bank], writes=[ob])
        return ob

    def proj_tm(self, half, w_ap, c0, ncol, dst_fn, dst_bufs):
        P = self.P
        w = self.rr("wout", self.wout)
        self.load_w(w, w_ap, c0, ncol, KC)
        for tb in range(HALF // 128):
            b = self.rr("bA", self.banks[0:4])
            self.mm_group(b, lambda k, tb=tb: self.xn.t[:, k * HALF + tb * 128:k * HALF + (tb + 1) * 128],
                          lambda k, w=w: w.t[:, k * ncol:(k + 1) * ncol], KC, [w, self.xn], ncol=ncol)
            ob = self.evac_bf16(b, ncol)
            P.dma("sync", dst_fn(tb), ob.t[:, 0:ncol], ob, reads=[ob], writes=dst_bufs)

    def load_xn(self, src, src_bufs, half):
        v = src.rearrange("(k p) n -> p k n", p=128)[:, :, half * HALF:(half + 1) * HALF]
        self.P.dma("sync", self.xn.t[:].rearrange("p (k n) -> p k n", k=KC), v, self.xn, reads=src_bufs, writes=[self.xn])

    def proj_conv_in(self, half, w_ap, zT, bgT, dst_bufs):
        P = self.P
        zd = zT.rearrange("(k p) n -> k p n", p=128)
        bd = bgT.rearrange("(k p) n -> k p n", p=128)
        for c in range(KC):
            wb = self.rr("win", self.win)
            wc = self.rr("win", self.win)
            wu = self.rr("win", self.win)
            self.load_w(wb, w_ap, c * 128, 128, KC)
            self.load_w(wc, w_ap, D + c * 128, 128, KC)
            self.load_w(wu, w_ap, 2 * D + c * 128, 128, KC)
            for j in range(2):
                n0 = half * HALF + j * TT
                b0 = self.rr("bA", self.banks[0:2])
                b1 = self.rr("bB", self.banks[2:4])
                b2 = self.rr("bO", self.banks[4:6])
                for (b, w) in ((b0, wb), (b1, wc), (b2, wu)):
                    self.mm_group(b, lambda k, w=w: w.t[:, k * 128:(k + 1) * 128], lambda k, j=j: self.xs(k, j), KC, [w, self.xn])
                t0 = self.rr("ht", self.ht)
                P.op("scalar", lambda e, t=t0, b=b0: e.activation(out=t.t[:], in_=b.t[:], func=AF.Copy), reads=[b0], writes=[t0])
                P.dma("sync", bd[c, :, n0:n0 + TT], t0.t[:], t0, reads=[t0], writes=dst_bufs)
                t1 = self.rr("ht", self.ht)
                P.op("scalar", lambda e, t=t1, b=b1: e.activation(out=t.t[:], in_=b.t[:], func=AF.Copy), reads=[b1], writes=[t1])
                P.op("vector", lambda e, t=t1, b=b2: e.tensor_tensor(out=t.t[:], in0=t.t[:], in1=b.t[:], op=ALU.mult), reads=[b2, t1], writes=[t1])
                P.dma("sync", zd[c, :, n0:n0 + TT], t1.t[:], t1, reads=[t1], writes=dst_bufs)

    def init_attn(self, masks_ap, nmask, sel_ap, ident_ap):
        P = self.P
        self.masks = P.sb("masks", [128, nmask * 128], BF16)
        self.ident = P.sb("ident", [128, 128], BF16)
        self.sel = P.sb("sel", [128, 16], F32)
        P.dma("gpsimd", self.masks.t[:], masks_ap, self.masks, writes=[self.masks])
        P.dma("gpsimd", self.ident.t[:], ident_ap, self.ident, writes=[self.ident])
        P.dma("sync", self.sel.t[:], sel_ap, self.sel, writes=[self.sel])
        self.NPT = 6
        self.LA = 3
        self.pts = [P.vbuf("pt") for _ in range(self.NPT)]
        self.laccs = [P.sb("lacc%d" % i, [128, TT], F32) for i in range(2)]
        self.PT0 = FS * HALF - self.NPT * TT

    def pt_ap(self, i):
        return self.act.t[:, self.PT0 + i * TT:self.PT0 + (i + 1) * TT]

    def mask_ap(self, mi):
        return self.masks.t[:, mi * 128:(mi + 1) * 128]

    def attn_run(self, blocks, q_ap, qbuf, o_banks, l_bank, scale):
        P = self.P
        n = len(blocks)
        st = [None] * n

        def stage_s(i):
            b = blocks[i]
            c0 = b.get("c0", 0)
            ms = b.get("masks", ())
            sb = self.rr("bS", [self.banks[0], self.banks[1], self.banks[2], self.banks[7]])
            P.op("tensor", lambda e, b=b, sb=sb, c0=c0, ms=ms: e.matmul(sb.t[:, c0:TT], b["K"], q_ap[:, c0:TT], start=True, stop=(len(ms) == 0)),
                 reads=[qbuf] + b["bufs"], writes=[sb], inc=(len(ms) == 0))
            for mi, (map_, mc0, w) in enumerate(ms):
                last = mi == len(ms) - 1
                P.op("tensor", lambda e, sb=sb, map_=map_, mc0=mc0, w=w, last=last: e.matmul(
                    sb.t[:, mc0:mc0 + w], self.ident.t[:], map_, start=False, stop=last),
                    reads=[self.ident, self.masks], writes=[sb], inc=last)
            pi = self.cnt.get("pti", 0)
            self.cnt["pti"] = pi + 1
            pt = self.pts[pi % self.NPT]
            pap = self.pt_ap(pi % self.NPT)
            bias = b.get("bias")
            if bias is None:
                bias = self.sel.t[:, 10:11]
            P.op("scalar", lambda e, sb=sb, pap=pap, c0=c0, bias=bias: e.activation(
                out=pap[:, c0:TT], in_=sb.t[:, c0:TT], func=AF.Exp, bias=bias, scale=scale),
                reads=[sb, self.sel], writes=[pt])
            st[i] = (pt, pap, c0)

        lacc = self.rr("lacc", self.laccs)

        def stage_pv(i):
            pt, pap, c0 = st[i]
            b = blocks[i]
            first, last = (i == 0), (i == n - 1)
            for oi, ob in enumerate(o_banks):
                P.op("tensor", lambda e, ob=ob, v=b["V"][oi], pap=pap, c0=c0, first=first, last=last: e.matmul(
                    ob.t[:, c0:TT], v, pap[:, c0:TT], start=first, stop=last),
                    reads=[pt] + b["bufs"], writes=[ob], inc=(oi == len(o_banks) - 1))
            if first:
                P.op("vector", lambda e, pap=pap: e.tensor_copy(out=lacc.t[:], in_=pap), reads=[pt], writes=[lacc])
            elif last:
                pass
            else:
                P.op("vector", lambda e, pap=pap, c0=c0: e.tensor_tensor(out=lacc.t[:, c0:TT], in0=lacc.t[:, c0:TT], in1=pap[:, c0:TT], op=ALU.add),
                     reads=[pt, lacc], writes=[lacc])

        LA = self.LA
        for i in range(min(LA, n)):
            stage_s(i)
        for i in range(n):
            stage_pv(i)
            if i + LA < n:
                stage_s(i + LA)
        if n > 1:
            ptl, papl, c0l = st[n - 1]
            assert c0l == 0
            P.op("tensor", lambda e: e.matmul(l_bank.t[:], self.onesf.t[:], lacc.t[:], start=True, stop=False),
                 reads=[lacc, self.onesf], writes=[l_bank], inc=False)
            P.op("tensor", lambda e: e.matmul(l_bank.t[:], self.ones.t[:], papl, start=False, stop=True),
                 reads=[ptl, self.ones], writes=[l_bank])
        else:
            P.op("tensor", lambda e: e.matmul(l_bank.t[:], self.onesf.t[:], lacc.t[:], start=True, stop=True),
                 reads=[lacc, self.onesf], writes=[l_bank])

    def attn_A(self, qA, kloc, vloc, kall, vall, own_bufs, src_bufs, attn, attn_bufs):
        P = self.P
        scale = 128.0 ** -0.5
        A = self.act.t
        QO, KO, VO, PO = 0, 8192, 14336, 20480
        qb, kb, vb = P.vbuf("qA"), P.vbuf("kA"), P.vbuf("vA")
        kpb = [P.vbuf("kpA") for _ in range(3)]
        vpb = [P.vbuf("vpA") for _ in range(3)]
        allb = [qb, kb, vb] + kpb + vpb + self.pts
        for b in allb:
            b.r.update(self.act.r); b.w.update(self.act.w)
        osets = [(self.banks[3], self.banks[4]), (self.banks[5], self.banks[6])]
        attn_v = attn.rearrange("(h d) n -> d h n", d=128)
        for kvh in range(4):
            qsrc = qA.rearrange("(h d) (nb q) -> d nb h q", d=128, q=128)[:, :, kvh * 4:(kvh + 1) * 4, :]
            qdst = A[:, QO:QO + 8192].rearrange("p (nb h q) -> p nb h q", nb=16, h=4)
            for hh in range(4):
                P.dma("sync", qdst[:, :, hh, :], qsrc[:, :, hh, :], qb, reads=own_bufs, writes=[qb])
            ksrc = kloc.rearrange("(g v d) n -> d g v n", g=3, v=4)[:, :, kvh, :]
            P.dma("sync", A[:, KO:KO + 6144].rearrange("p (g n) -> p g n", g=3), ksrc, kb, reads=own_bufs, writes=[kb])
            vsrc = vloc.rearrange("(b p) (g v d) -> p g b v d", p=128, g=3, v=4)[:, :, :, kvh, :]
            vdst = A[:, VO:VO + 6144].rearrange("p (g b d) -> p g b d", g=3, b=16)
            for g in range(3):
                P.dma("sync", vdst[:, g], vsrc[:, g], vb, reads=own_bufs, writes=[vb])
            for s in range(3):
                kbase = PO + s * 6144
                vbase = kbase + 3072
                vv = vall.rearrange("(r ih s jl p) (g v d) -> s p g v r ih jl d", r=4, ih=2, s=4, jl=2, p=128, g=3, v=4)[s]
                for g in range(3):
                    r0 = (g * 4 + kvh) * 128
                    base = ((r0 // 256) * 4 + s) * 256 + (r0 % 256)
                    ka = kall[base:base + 128, :].rearrange("d (r j) -> d r j", r=4)
                    if g < 2:
                        P.dma("sync", A[:, kbase + g * 512:kbase + (g + 1) * 512].rearrange("p (r j) -> p r j", r=4),
                              ka[:, :, 384:512], kpb[s], reads=src_bufs, writes=[kpb[s]])
                        P.dma("sync", A[:, vbase + g * 512:vbase + (g + 1) * 512].rearrange("p (r d) -> p r d", r=4),
                              vv[:, g, kvh, :, 1, 1, :], vpb[s], reads=src_bufs, writes=[vpb[s]])
                    else:
                        P.dma("sync", A[:, kbase + 1024:kbase + 3072].rearrange("p (r j) -> p r j", r=4),
                              ka, kpb[s], reads=src_bufs, writes=[kpb[s]])
                        for r in range(4):
                            for ih in range(2):
                                o_ = vbase + 1024 + r * 512 + ih * 256
                                P.dma("sync", A[:, o_:o_ + 256].rearrange("p (jl d) -> p jl d", jl=2),
                                      vv[:, g, kvh, r, ih, :, :], vpb[s], reads=src_bufs, writes=[vpb[s]])
            for r4 in range(4):
                for jb in range(4):
                    blocks = []

                    def add(g, rk, jbk, mi):
                        m4 = [(self.mask_ap(mi), hh * 128, 128) for hh in range(4)]
                        if jbk >= 0:
                            blk = rk * 4 + jbk
                            blocks.append(dict(K=A[:, KO + g * 2048 + blk * 128:KO + g * 2048 + (blk + 1) * 128],
                                               V=[A[:, VO + (g * 16 + blk) * 128:VO + (g * 16 + blk + 1) * 128]],
                                               bufs=[kb, vb], masks=m4))
                        else:
                            jp = jbk + 4
                            for s in range(3):
                                kbase = PO + s * 6144
                                vbase = kbase + 3072
                                if g < 2:
                                    o = g * 512 + rk * 128
                                else:
                                    o = 1024 + (rk * 4 + jp) * 128
                                blocks.append(dict(K=A[:, kbase + o:kbase + o + 128], V=[A[:, vbase + o:vbase + o + 128]],
                                                   bufs=[kpb[s], vpb[s]], masks=m4, bias=self.sel.t[:, s:s + 1]))

                    add(1, r4, jb, 0)
                    add(1, r4, jb - 1, 1)
                    for dj in range(5):
                        add(2, r4, jb - dj, (2, 3, 3, 3, 4)[dj])
                    for rk in range(4):
                        for dj in range(2):
                            add(0, rk, jb - dj, 5 + (r4 - rk + 3) * 2 + dj)
                    nb = r4 * 4 + jb
                    ob, lb = self.rr("oset", osets)
                    self.attn_run(blocks, A[:, QO + nb * 512:QO + (nb + 1) * 512], qb, [ob], lb, scale)
                    rl = self.rr("tmp", self.tmp)
                    P.op("vector", lambda e, rl=rl, lb=lb: e.reciprocal(out=rl.t[:], in_=lb.t[:]), reads=[lb], writes=[rl])
                    o = self.rr("ob", self.ob)
                    P.op("vector", lambda e, o=o, ob=ob, rl=rl: e.tensor_tensor(out=o.t[:], in0=ob.t[:], in1=rl.t[:], op=ALU.mult),
                         reads=[ob, rl], writes=[o])
                    P.dma("sync", attn_v[:, kvh * 4:(kvh + 1) * 4, nb * 128:(nb + 1) * 128],
                          o.t[:].rearrange("p (h q) -> p h q", h=4), o, reads=[o], writes=attn_bufs)
        for b in allb:
            for k, v in b.r.items():
                self.act.r[k] = max(self.act.r.get(k, 0), v)
            for k, v in b.w.items():
                self.act.w[k] = max(self.act.w.get(k, 0), v)

    def lam_setup(self, lam_ap, lam_init):
        P = self.P
        self.lamt = P.sb("lamt", [128, 8], F32)
        self.onesf = P.sb("onesf", [128, 128], F32)
        P.op("vector", lambda e: e.memset(self.onesf.t[:], 1.0), writes=[self.onesf])
        P.dma("sync", self.lamt.t[:, 0:4], lam_ap, self.lamt, writes=[self.lamt])
        L = self.lamt
        P.op("vector", lambda e: e.tensor_tensor(out=L.t[:, 4:5], in0=L.t[:, 0:1], in1=L.t[:, 1:2], op=ALU.mult), reads=[L], writes=[L])
        P.op("vector", lambda e: e.tensor_tensor(out=L.t[:, 5:6], in0=L.t[:, 2:3], in1=L.t[:, 3:4], op=ALU.mult), reads=[L], writes=[L])
        bank = self.banks[7]
        P.op("tensor", lambda e: e.matmul(bank.t[:, 0:2], self.onesf.t[:], L.t[:, 4:6], start=True, stop=True),
             reads=[L, self.onesf], writes=[bank])
        P.op("scalar", lambda e: e.activation(out=L.t[:, 6:8], in_=bank.t[:, 0:2], func=AF.Exp), reads=[bank], writes=[L])
        P.op("vector", lambda e: e.scalar_tensor_tensor(out=L.t[:, 4:5], in0=L.t[:, 7:8], scalar=-lam_init, in1=L.t[:, 6:7],
                                                        op0=ALU.add, op1=ALU.subtract), reads=[L], writes=[L])

    def attn_B(self, qB, kloc, vloc, kall, vall, own_bufs, src_bufs, attn, attn_bufs, gcol, lam_init):
        P = self.P
        scale = 128.0 ** -0.5
        A = self.act.t
        X = self.xn.t
        XF = self.xn.t.bitcast(F32)
        QO, KA, KO, VA = 0, 4096, 20480, 24576
        qb, kob, vob = P.vbuf("qB"), P.vbuf("koB"), P.vbuf("voB")
        kab = [[P.vbuf("kaB") for _ in range(3)] for _ in range(2)]
        vab = [P.vbuf("vaB") for _ in range(3)]
        ocb = [P.vbuf("oc0"), P.vbuf("oc1")]
        dfb = P.vbuf("diff")
        allb = [qb, kob, vob, dfb] + kab[0] + kab[1] + vab + ocb + self.pts
        for b in allb:
            b.r.update(self.act.r); b.w.update(self.act.w)
            b.r.update(self.xn.r); b.w.update(self.xn.w)
        oc_ap = [XF[:, 2048:3072], XF[:, 3072:4096]]
        df_ap = XF[:, 4096:5120]
        o_banks = [self.banks[3], self.banks[4]]
        l_bank = self.banks[5]
        sbank = self.banks[6]
        one_m = 1.0 - lam_init
        P.op("vector", lambda e: e.memset(self.epsc.t[:, 1:2], SUBLN_EPS / (one_m * one_m)), reads=[], writes=[self.epsc])
        attn_v = attn.rearrange("(h c d) n -> h c d n", c=2, d=128)
        for h in range(8):
            qs = qB.rearrange("(h c d) n -> h d c n", c=2, d=128)[h]
            P.dma("sync", A[:, QO:QO + 4096].rearrange("p (c n) -> p c n", c=2), qs, qb, reads=own_bufs, writes=[qb])
            va = vall.rearrange("(i s wl p) (h e) -> h s p i wl e", s=4, wl=2, p=128, e=256)[h]
            for s in range(3):
                for c in range(2):
                    ka = kall.rearrange("(h s c d) n -> h c s d n", s=4, c=2, d=128)[h, c, s]
                    P.dma("sync", A[:, KA + c * 8192 + s * 2048:KA + c * 8192 + (s + 1) * 2048], ka, kab[c][s], reads=src_bufs, writes=[kab[c][s]])
                vd = A[:, VA + s * 4096:VA + (s + 1) * 4096].rearrange("p (i wl e) -> p i wl e", wl=2, e=256)
                for wl in range(2):
                    P.dma("sync", vd[:, :, wl, :], va[s][:, :, wl, :], vab[s], reads=src_bufs, writes=[vab[s]])
            ks = kloc.rearrange("(h c d) n -> h d c n", c=2, d=128)[h]
            P.dma("sync", A[:, KO:KO + 4096].rearrange("p (c n) -> p c n", c=2), ks, kob, reads=own_bufs, writes=[kob])
            vo = vloc.rearrange("(b p) (h e) -> h p b e", p=128, e=256)[h]
            P.dma("sync", X[:, 0:4096].rearrange("p (b e) -> p b e", e=256), vo, vob, reads=own_bufs, writes=[vob])
            for r4 in range(4):
                for c in range(2):
                    blocks = []
                    for s in range(3):
                        for blk in range(16):
                            ko = KA + c * 8192 + s * 2048 + blk * 128
                            vo_ = VA + (s * 16 + blk) * 256
                            blocks.append(dict(K=A[:, ko:ko + 128], V=[A[:, vo_:vo_ + 128], A[:, vo_ + 128:vo_ + 256]],
                                               bufs=[kab[c][s], vab[s]], bias=self.sel.t[:, 3 + s:4 + s]))
                    for jbk in (3, 2, 1, 0):
                        for rk in range(4):
                            blk = rk * 4 + jbk
                            ko = KO + c * 2048 + blk * 128
                            vo_ = blk * 256
                            mi = 0 if rk <= r4 else 19
                            blocks.append(dict(K=A[:, ko:ko + 128], V=[X[:, vo_:vo_ + 128], X[:, vo_ + 128:vo_ + 256]],
                                               bufs=[kob, vob], masks=[(self.mask_ap(mi), jbk * 128, 128)], c0=jbk * 128))
                    q_ap = A[:, QO + c * 2048 + r4 * 512:QO + c * 2048 + (r4 + 1) * 512]
                    self.attn_run(blocks, q_ap, qb, o_banks, l_bank, scale)
                    for e2 in range(2):
                        P.op("vector", lambda e, c=c, e2=e2: e.tensor_copy(out=oc_ap[c][:, e2 * 512:(e2 + 1) * 512], in_=o_banks[e2].t[:]),
                             reads=[o_banks[e2]], writes=[ocb[c]])
                    rl = self.rr("tmp", self.tmp)
                    P.op("vector", lambda e, rl=rl: e.reciprocal(out=rl.t[:], in_=l_bank.t[:]), reads=[l_bank], writes=[rl])
                    for e2 in range(2):
                        P.op("vector", lambda e, c=c, e2=e2, rl=rl: e.tensor_tensor(
                            out=oc_ap[c][:, e2 * 512:(e2 + 1) * 512], in0=oc_ap[c][:, e2 * 512:(e2 + 1) * 512], in1=rl.t[:], op=ALU.mult),
                            reads=[ocb[c], rl], writes=[ocb[c]])
                P.op("vector", lambda e: e.scalar_tensor_tensor(out=df_ap, in0=oc_ap[1], scalar=self.lamt.t[:, 4:5], in1=oc_ap[0],
                                                                op0=ALU.mult, op1=ALU.add), reads=ocb + [self.lamt], writes=[dfb])
                for e2 in range(2):
                    sq = self.rr("sq", self.sq)
                    P.op("scalar", lambda e, sq=sq, e2=e2: e.activation(out=sq.t[:], in_=df_ap[:, e2 * 512:(e2 + 1) * 512], func=AF.Square),
                         reads=[dfb], writes=[sq])
                    P.op("tensor", lambda e, sq=sq, e2=e2: e.matmul(sbank.t[:], self.ones.t[:], sq.t[:], start=(e2 == 0), stop=(e2 == 1)),
                         reads=[sq, self.ones], writes=[sbank], inc=True)
                P.op("scalar", lambda e: e.activation(out=self.rtmp.t[:], in_=sbank.t[:], func=AF.Sqrt, bias=self.epsc.t[:, 1:2],
                                                      scale=1.0 / (256.0 * one_m * one_m)), reads=[sbank, self.epsc], writes=[self.rtmp])
                P.op("vector", lambda e: e.reciprocal(out=self.rstd[0].t[:], in_=self.rtmp.t[:]), reads=[self.rtmp], writes=[self.rstd[0]])
                for e2 in range(2):
                    o = self.rr("ob", self.ob)
                    P.op("vector", lambda e, o=o, e2=e2: e.scalar_tensor_tensor(
                        out=o.t[:], in0=df_ap[:, e2 * 512:(e2 + 1) * 512], scalar=self.vecs.t[:, gcol + e2:gcol + e2 + 1],
                        in1=self.rstd[0].t[:], op0=ALU.mult, op1=ALU.mult), reads=[dfb, self.vecs, self.rstd[0]], writes=[o])
                    P.dma("sync", attn_v[h, e2, :, r4 * 512:(r4 + 1) * 512], o.t[:], o, reads=[o], writes=attn_bufs)
        for b in allb:
            for tgt in (self.act, self.xn):
                for k, v in b.r.items():
                    tgt.r[k] = max(tgt.r.get(k, 0), v)
                for k, v in b.w.items():
                    tgt.w[k] = max(tgt.w.get(k, 0), v)

    def conv_C(self, zT, bgT, zh_all, src_bufs, attn, attn_bufs, wcol):
        P = self.P
        AF32 = self.act.t.bitcast(F32)
        X = self.xn.t
        sets = []
        for i in range(2):
            base = i * 8192
            sets.append(dict(z=AF32[:, base:base + 2048], bg=AF32[:, base + 2048:base + 4096], acc=AF32[:, base + 4096:base + 6144],
                             hin=AF32[:, base + 6144:base + 6144 + 24], hal=AF32[:, base + 6200:base + 6202],
                             out=X[:, i * 2048:(i + 1) * 2048],
                             zb=P.vbuf("cz"), bb=P.vbuf("cbg"), ab=P.vbuf("cacc"), hb=P.vbuf("chin"), ob=P.vbuf("cout")))
        allb = [s[k] for s in sets for k in ("zb", "bb", "ab", "hb", "ob")]
        for b in allb:
            b.r.update(self.act.r); b.w.update(self.act.w)
            b.r.update(self.xn.r); b.w.update(self.xn.w)
        zd = zT.rearrange("(k p) n -> k p n", p=128)
        bd = bgT.rearrange("(k p) n -> k p n", p=128)
        ad = attn.rearrange("(k p) n -> k p n", p=128)
        zh = zh_all.rearrange("(s k p) e -> k p s e", s=4, p=128)
        for c in range(KC):
            S = sets[c % 2]
            P.dma("gpsimd", S["z"], zd[c], S["zb"], reads=src_bufs, writes=[S["zb"]])
            P.dma("gpsimd", S["bg"], bd[c], S["bb"], reads=src_bufs, writes=[S["bb"]])
            P.dma("sync", S["hin"].rearrange("p (s e) -> p s e", s=3), zh[c][:, 0:3, :], S["hb"], reads=src_bufs, writes=[S["hb"]])
            hin, hal = S["hin"], S["hal"]
            P.op("vector", lambda e, hin=hin, hal=hal: e.tensor_scalar(out=hal, in0=hin[:, 0:2], scalar1=self.sel.t[:, 7:8], scalar2=None, op0=ALU.mult),
                 reads=[S["hb"], self.sel], writes=[S["hb"]])
            for s in (1, 2):
                P.op("vector", lambda e, hin=hin, hal=hal, s=s: e.scalar_tensor_tensor(
                    out=hal, in0=hin[:, 8 * s:8 * s + 2], scalar=self.sel.t[:, 7 + s:8 + s], in1=hal, op0=ALU.mult, op1=ALU.add),
                    reads=[S["hb"], self.sel], writes=[S["hb"]])
            w0 = self.vecs.t[:, wcol + c:wcol + c + 1]
            w1 = self.vecs.t[:, wcol + KC + c:wcol + KC + c + 1]
            w2 = self.vecs.t[:, wcol + 2 * KC + c:wcol + 2 * KC + c + 1]
            z, acc = S["z"], S["acc"]
            rd = [S["zb"], S["hb"], self.vecs]
            P.op("vector", lambda e, z=z, acc=acc, w2=w2: e.tensor_scalar(out=acc, in0=z, scalar1=w2, scalar2=None, op0=ALU.mult),
                 reads=rd, writes=[S["ab"]])

            def fma(dst, src, w):
                P.op("vector", lambda e, dst=dst, src=src, w=w: e.scalar_tensor_tensor(out=dst, in0=src, scalar=w, in1=dst, op0=ALU.mult, op1=ALU.add),
                     reads=rd + [S["ab"]], writes=[S["ab"]])

            for r4 in range(4):
                a = acc[:, r4 * 512:(r4 + 1) * 512]
                if r4 >= 1:
                    fma(a, z[:, (r4 - 1) * 512:r4 * 512], w1)
                else:
                    fma(a[:, 1:512], z[:, 3 * 512:3 * 512 + 511], w1)
                    fma(a[:, 0:1], S["hal"][:, 1:2], w1)
                if r4 >= 2:
                    fma(a, z[:, (r4 - 2) * 512:(r4 - 1) * 512], w0)
                else:
                    fma(a[:, 1:512], z[:, (r4 + 2) * 512:(r4 + 2) * 512 + 511], w0)
                    fma(a[:, 0:1], S["hal"][:, r4:r4 + 1], w0)
            P.op("vector", lambda e, S=S: e.tensor_tensor(out=S["out"], in0=S["acc"], in1=S["bg"], op=ALU.mult),
                 reads=[S["ab"], S["bb"]], writes=[S["ob"]])
            P.dma("sync", ad[c], S["out"], S["ob"], reads=[S["ob"]], writes=attn_bufs)
        for b in allb:
            for tgt in (self.act, self.xn):
                for k, v in b.r.items():
                    tgt.r[k] = max(tgt.r.get(k, 0), v)
                for k, v in b.w.items():
                    tgt.w[k] = max(tgt.w.get(k, 0), v)


NV = 400
VL = 80
V_NORMF = 320
V_SUBLN = 336
V_CONV = 338
NMASK = 20
GROUPS = [[0, 1, 2, 3], [4, 5, 6, 7]]
LAM_INIT = 0.8 - 0.6 * float(np.exp(-0.3 * 1))

W_SHAPES = {}
for _i in range(4):
    W_SHAPES["w1i%d" % _i] = (D, 2 * DFF)
    W_SHAPES["w1o%d" % _i] = (DFF, D)
    W_SHAPES["w2i%d" % _i] = (D, 2 * DFF)
    W_SHAPES["w2o%d" % _i] = (DFF, D)
    W_SHAPES["wpg%d" % _i] = (D, D)
    W_SHAPES["wpp%d" % _i] = (PLE, D)
for _j in range(2):
    W_SHAPES["aqkv%d" % _j] = (D, 5120)
    W_SHAPES["ao%d" % _j] = (D, D)
W_SHAPES.update(bqkv=(D, 3 * D), bo=(D, D), cin=(D, 3 * D), cout=(D, D))


def build_net(stop=None, dbg=None):
    P = Prog()
    xT = P.dram_in("xT", [D, TOK])
    pT = [P.dram_in("pT%d" % i, [PLE, TOK]) for i in range(4)]
    vecs = P.dram_in("vecs", [128, NV])
    masks = P.dram_in("masks", [128, NMASK * 128])
    ident = P.dram_in("ident", [128, 128])
    sel = P.dram_in("sel", [128, 16])
    lamT = P.dram_in("lamT", [128, 4])
    class _LazyW(dict):
        def __missing__(self, k):
            self[k] = P.dram_in(k, list(W_SHAPES[k]))
            return self[k]
    W = _LazyW()
    outT = P.dram_out("outT", [D, TOK])
    h = P.dram_tmp("h", [D, TOK])
    q = P.dram_tmp("q", [D, TOK], BF16)
    klA = P.dram_tmp("klA", [1536, TOK], BF16)
    vlA = P.dram_tmp("vlA", [TOK, 1536], BF16)
    kaA = P.dram_tmp("kaA", [4 * 1536, TOK], BF16)
    vaA = P.dram_tmp("vaA", [4 * TOK, 1536], BF16)
    klB = P.dram_tmp("klB", [D, TOK], BF16)
    vlB = P.dram_tmp("vlB", [TOK, D], BF16)
    kaB = P.dram_tmp("kaB", [4 * D, TOK], BF16)
    vaB = P.dram_tmp("vaB", [4 * TOK, D], BF16)
    zT = P.dram_tmp("zT", [D, TOK])
    bgT = P.dram_tmp("bgT", [D, TOK])
    zhl = P.dram_tmp("zhl", [D, 8])
    zha = P.dram_tmp("zha", [4 * D, 8])
    attn = P.dram_tmp("attn", [D, TOK], BF16)

    R = Rows(P, vecs, NV)
    R.init_attn(masks, NMASK, sel, ident)
    R.lam_setup(lamT, LAM_INIT)

    xb = [P.vbuf("x") for _ in range(4)]
    hb = [P.vbuf("h") for _ in range(4)]
    ob = [P.vbuf("out")]
    P.out_bufs += ob
    qb, klb, vlb, kab, vab, atb = (P.vbuf(n) for n in ("q", "kl", "vl", "ka", "va", "attn"))
    zb, zhlb, zhab = P.vbuf("z"), P.vbuf("zhl"), P.vbuf("zha")
    ccb = P.vbuf("cc")

    def cc(in_ap, out_ap, in_bufs, out_bufs):
        E = P.engs["gpsimd"]
        if ccb.sem is None:
            ccb.sem = P.new_sem("cc")
        waits = P._collect(E, in_bufs, out_bufs)
        for s_, v_ in waits.items():
            E.waited[s_] = v_
        ccb.cnt += 1
        E.ops.append((sorted(waits.items()), lambda e, i=in_ap, o=out_ap: e.collective_compute(
            "AllGather", ALU.bypass, replica_groups=GROUPS, ins=[i], outs=[o]), (ccb.sem, 1)))
        P._mark((ccb.sem, ccb.cnt), in_bufs, out_bufs)

    def gather(loc, allt, rows, lb, ab):
        for i in range(rows // 256):
            cc(loc[i * 256:(i + 1) * 256, :], allt[i * 1024:(i + 1) * 1024, :], [lb], [ab])

    state = {"hsrc": xT, "hsb": xb}

    def hs():
        return state["hsrc"], state["hsb"]

    def wrote_h():
        state["hsrc"], state["hsb"] = h, hb

    def chain(fns):
        for f in fns:
            for half in range(2):
                f(half)
            if getattr(f, "writes_h", False):
                wrote_h()

    def mk(fn, writes_h=False):
        fn.writes_h = writes_h
        return fn

    def s_ffn(i, which):
        g = i * VL + (0 if which == 1 else 32)
        wi, wo = W["w%di%d" % (which, i)], W["w%do%d" % (which, i)]

        def f(half):
            src, sb_ = hs()
            R.norm(src, sb_, half, g)
            R.ffn(src, sb_, h, hb, half, wi, wo)
        return mk(f, True)

    def s_ple(i):
        def f(half):
            src, sb_ = hs()
            R.norm(src, sb_, half, i * VL + 48)
            R.ple(src, sb_, h, hb, half, W["wpg%d" % i], i * VL + 64, pT[i], W["wpp%d" % i])
        return mk(f, True)

    def s_oproj(w):
        def f(half):
            src, sb_ = hs()
            R.load_xn(attn, [atb], half)
            R.oproj(src, sb_, h, hb, half, w)
        return mk(f, True)

    def s_projA(i, j):
        w = W["aqkv%d" % j]

        def f(half):
            src, sb_ = hs()
            R.norm(src, sb_, half, i * VL + 16)
            qv = q.rearrange("(c p) n -> c p n", p=128)
            R.proj_fm(half, w, 0, 16, lambda c, jj: qv[c, :, half * HALF + jj * TT:half * HALF + (jj + 1) * TT], [qb])
            kv = klA.rearrange("(c p) n -> c p n", p=128)
            for g in range(3):
                R.proj_fm(half, w, D + g * 1024, 4,
                          lambda c, jj, g=g: kv[g * 4 + c, :, half * HALF + jj * TT:half * HALF + (jj + 1) * TT], [klb])
                for grp in range(2):
                    R.proj_tm(half, w, D + g * 1024 + 512 + grp * 256, 256,
                              lambda tb, g=g, grp=grp: vlA[half * HALF + tb * 128:half * HALF + (tb + 1) * 128,
                                                           g * 512 + grp * 256:g * 512 + (grp + 1) * 256], [vlb])
        return mk(f)

    def s_projB(i):
        w = W["bqkv"]

        def f(half):
            src, sb_ = hs()
            R.norm(src, sb_, half, i * VL + 16)
            qv = q.rearrange("(c p) n -> c p n", p=128)
            R.proj_fm(half, w, 0, 16, lambda c, jj: qv[c, :, half * HALF + jj * TT:half * HALF + (jj + 1) * TT], [qb])
            kv = klB.rearrange("(c p) n -> c p n", p=128)
            R.proj_fm(half, w, D, 16, lambda c, jj: kv[c, :, half * HALF + jj * TT:half * HALF + (jj + 1) * TT], [klb])
            for hh in range(8):
                R.proj_tm(half, w, 2 * D + hh * 256, 256,
                          lambda tb, hh=hh: vlB[half * HALF + tb * 128:half * HALF + (tb + 1) * 128, hh * 256:(hh + 1) * 256], [vlb])
        return mk(f)

    def q_stage(i, w):
        def run():
            qv = q.rearrange("(c p) n -> c p n", p=128)
            for half in (1, 0):
                src, sb_ = hs()
                if half == 0:
                    R.norm(src, sb_, half, i * VL + 16)
                R.proj_fm(half, w, 0, 16, lambda c, jj, half=half: qv[c, :, half * HALF + jj * TT:half * HALF + (jj + 1) * TT], [qb])
        return run

    def s_projC(i):
        def f(half):
            src, sb_ = hs()
            R.norm(src, sb_, half, i * VL + 16)
            R.proj_conv_in(half, W["cin"], zT, bgT, [zb])
        return mk(f)

    def s_final():
        def f(half):
            src, sb_ = hs()
            R.final_norm(src, sb_, outT, ob, half, V_NORMF)
        return mk(f)

    def halo():
        P.dma("sync", zhl[:, 0:1], zT[:, 2 * 512 + 511:2 * 512 + 512], zhlb, reads=[zb], writes=[zhlb], allow_slow_non_contiguous=True)
        P.dma("sync", zhl[:, 1:2], zT[:, 3 * 512 + 511:3 * 512 + 512], zhlb, reads=[zb], writes=[zhlb], allow_slow_non_contiguous=True)
        cc(zhl, zha, [zhlb], [zhab])

    def gA():
        gather(klA, kaA, 1536, klb, kab)
        gather(vlA, vaA, TOK, vlb, vab)

    def gB():
        gather(klB, kaB, D, klb, kab)
        gather(vlB, vaB, TOK, vlb, vab)

    mixA = lambda: R.attn_A(q, klA, vlA, kaA, vaA, [qb, klb, vlb], [kab, vab], attn, [atb])
    mixB = lambda: R.attn_B(q, klB, vlB, kaB, vaB, [qb, klb, vlb], [kab, vab], attn, [atb], V_SUBLN, LAM_INIT)
    mixC = lambda: R.conv_C(zT, bgT, zha, [zb, zhab], attn, [atb], V_CONV)
    C1 = lambda th: (lambda: chain([th()]))
    stages = [
        C1(lambda: s_ffn(0, 1)), C1(lambda: s_projA(0, 0)), gA, mixA, C1(lambda: s_oproj(W["ao0"])), C1(lambda: s_ffn(0, 2)), C1(lambda: s_ple(0)),
        C1(lambda: s_ffn(1, 1)), C1(lambda: s_projB(1)), gB, mixB, C1(lambda: s_oproj(W["bo"])), C1(lambda: s_ffn(1, 2)), C1(lambda: s_ple(1)),
        C1(lambda: s_ffn(2, 1)), C1(lambda: s_projC(2)), halo, mixC, C1(lambda: s_oproj(W["cout"])), C1(lambda: s_ffn(2, 2)), C1(lambda: s_ple(2)),
        C1(lambda: s_ffn(3, 1)), C1(lambda: s_projA(3, 1)), gA, mixA, C1(lambda: s_oproj(W["ao1"])), C1(lambda: s_ffn(3, 2)), C1(lambda: s_ple(3)),
        C1(lambda: s_final()),
    ]
    n = len(stages) if stop is None else stop
    for st in stages[:n]:
        st()
    if stop is not None:
        if dbg is None:
            P.dma("sync", outT, h, ob[0], reads=hb, writes=ob)
        else:
            src = {"attn": attn, "q": q, "klA": klA, "vlA": vlA, "klB": klB, "vlB": vlB}[dbg]
            dbo = P.dram_out("dbg", list(src.shape), BF16)
            P.dma("sync", dbo, src, ob[0], reads=[atb, qb, klb, vlb], writes=ob)
            P.dma("sync", outT, h, ob[0], reads=hb, writes=ob)
    nc = P.emit()
    nc._in_names = list(P.in_names)
    return nc


def _r4perm():
    n = np.arange(TOK)
    return 4 * (n % 512) + n // 512


def _masks():
    jk = np.arange(128)[:, None]
    jq = np.arange(128)[None, :]
    vis = []
    vis.append(jk <= jq)
    vis.append(jk >= jq)
    comb = ((jq - jk) % 4) == 0
    vis.append((jk <= jq) & comb)
    vis.append(comb | (jk < -1))
    vis.append((jk >= jq) & comb)
    for dr in range(-3, 4):
        for djb in range(2):
            dt = 4 * (128 * djb + jq - jk) + dr
            vis.append((dt >= 0) & (dt <= 128))
    vis.append(jk < jq)
    m = np.concatenate([np.where(v, 0.0, NEG) for v in vis], axis=1).astype(np.float32)
    assert m.shape == (128, NMASK * 128)
    return m


def _col16(v):
    return np.ascontiguousarray(np.asarray(v, np.float32).reshape(-1, 128).T)


def make_inputs(inputs):
    perm = _r4perm()
    x = np.asarray(inputs["x"], np.float32)
    p = np.asarray(inputs["p"], np.float32)
    vecs = np.zeros((128, NV), np.float32)
    for i in range(4):
        b = i * VL
        vecs[:, b:b + 16] = _col16(inputs["norm_ffn1"][i])
        vecs[:, b + 16:b + 32] = _col16(inputs["norm_mix"][i])
        vecs[:, b + 32:b + 48] = _col16(inputs["norm_ffn2"][i])
        vecs[:, b + 48:b + 64] = _col16(inputs["norm_ple"][i])
        vecs[:, b + 64:b + 80] = _col16(inputs["b_ple_gate"][i])
    vecs[:, V_NORMF:V_NORMF + 16] = _col16(inputs["norm_f"])
    vecs[:, V_SUBLN:V_SUBLN + 2] = _col16(inputs["b_subln"][0])
    for k in range(3):
        vecs[:, V_CONV + 16 * k:V_CONV + 16 * (k + 1)] = _col16(inputs["c_conv_w"][0][k])
    shared = {"vecs": vecs, "masks": _masks(), "ident": np.eye(128, dtype=np.float32),
              "lamT": np.ascontiguousarray(np.asarray(inputs["b_lambda"][0], np.float32).T)}
    for i in range(4):
        shared["w1i%d" % i] = np.asarray(inputs["w_ffn1_in"][i], np.float32)
        shared["w1o%d" % i] = np.asarray(inputs["w_ffn1_out"][i], np.float32)
        shared["w2i%d" % i] = np.asarray(inputs["w_ffn2_in"][i], np.float32)
        shared["w2o%d" % i] = np.asarray(inputs["w_ffn2_out"][i], np.float32)
        shared["wpg%d" % i] = np.asarray(inputs["w_ple_gate"][i], np.float32)
        shared["wpp%d" % i] = np.asarray(inputs["w_ple_proj"][i], np.float32)
    for j in range(2):
        shared["aqkv%d" % j] = np.asarray(inputs["a_w_qkv"][j], np.float32)
        shared["ao%d" % j] = np.asarray(inputs["a_w_o"][j], np.float32)
    shared["bqkv"] = np.asarray(inputs["b_w_qkv"][0], np.float32)
    shared["bo"] = np.asarray(inputs["b_w_o"][0], np.float32)
    shared["cin"] = np.asarray(inputs["c_w_in"][0], np.float32)
    shared["cout"] = np.asarray(inputs["c_w_out"][0], np.float32)
    maps = []
    for core in range(NCORES):
        b, c = divmod(core, 4)
        sl = slice(c * TOK, (c + 1) * TOK)
        m = dict(shared)
        m["xT"] = np.ascontiguousarray(x[b, sl][perm].T)
        for i in range(4):
            m["pT%d" % i] = np.ascontiguousarray(p[i, b, sl][perm].T)
        s = np.zeros((128, 16), np.float32)
        for k in range(3):
            s[:, k] = 0.0 if k == c - 1 else NEG
            s[:, 7 + k] = 1.0 if k == c - 1 else 0.0
        for k in range(4):
            s[:, 3 + k] = 0.0 if k < c else NEG
        m["sel"] = s
        maps.append(m)
    return maps


def assemble(results, key="outT"):
    perm = _r4perm()
    out = np.empty((2, 4 * TOK, D), np.float32)
    for core in range(NCORES):
        b, c = divmod(core, 4)
        o = np.asarray(results[core][key]).astype(np.float32).T
        blk = np.empty_like(o)
        blk[perm] = o
        out[b, c * TOK:(c + 1) * TOK] = blk
    return out


_NC_CACHE = {}


def kernel(**inputs):
    if "net" not in _NC_CACHE:
        _NC_CACHE["net"] = build_net()
    nc = _NC_CACHE["net"]
    maps = [{k: m[k] for k in nc._in_names} for m in make_inputs(inputs)]
    res = run_bass_kernel_spmd(nc, maps, core_ids=list(range(NCORES)))
    return assemble(res.results)
```

```python
import contextlib
import numpy as np
import ml_dtypes
import concourse.bass as bass
import concourse.mybir as mybir
from concourse.bass_utils import run_bass_kernel_spmd

F32 = mybir.dt.float32
BF16 = mybir.dt.bfloat16
AF = mybir.ActivationFunctionType
ALU = mybir.AluOpType

D = 2048
KC = 16
TOK = 2048
HALF = 1024
TT = 512
DFF = 5632
FS = 44
PLE = 256
NCORES = 8
NEG = -30000.0
RMS_EPS = 1e-6
SUBLN_EPS = 1e-5


class Buf:
    __slots__ = ("name", "w", "r", "sem", "cnt", "t")

    def __init__(self, name, t=None):
        self.name = name
        self.w = {}
        self.r = {}
        self.sem = None
        self.cnt = 0
        self.t = t


class Eng:
    def __init__(self, name):
        self.name = name
        self.ops = []
        self.sem = None
        self.cnt = 0
        self.waited = {}


class Prog:
    def __init__(self):
        self.nc = bass.Bass("TRN2", target_bir_lowering=False)
        self.es = contextlib.ExitStack()
        self.engs = {n: Eng(n) for n in ("tensor", "vector", "scalar", "gpsimd", "sync")}
        self.semh = {}
        self.nsem = 0
        for n in ("tensor", "vector", "scalar"):
            self.engs[n].sem = self.new_sem("e_" + n)
        self.out_bufs = []
        self.in_names = []
        self.nbuf = 0

    def new_sem(self, name=None):
        self.nsem += 1
        h = self.es.enter_context(self.nc.semaphore(name or ("s%d" % self.nsem)))
        self.semh[self.nsem] = h
        return self.nsem

    def sb(self, name, shape, dtype):
        t = self.es.enter_context(self.nc.sbuf_tensor("sb_" + name, list(shape), dtype))
        return Buf(name, t)

    def ps(self, name):
        t = self.es.enter_context(self.nc.psum_tensor("ps_" + name, [128, 512], F32))
        return Buf(name, t)

    def dram_in(self, name, shape, dtype=F32):
        self.in_names.append(name)
        return self.nc.dram_tensor(name, list(shape), dtype, kind="ExternalInput").ap()

    def dram_out(self, name, shape, dtype=F32):
        return self.nc.dram_tensor(name, list(shape), dtype, kind="ExternalOutput").ap()

    def dram_tmp(self, name, shape, dtype=F32):
        return self.nc.dram_tensor(name, list(shape), dtype).ap()

    def vbuf(self, name):
        self.nbuf += 1
        return Buf("%s_%d" % (name, self.nbuf))

    def _collect(self, E, reads, writes, extra=()):
        waits = {}

        def need(sem, val):
            if E.waited.get(sem, 0) >= val:
                return
            if waits.get(sem, 0) < val:
                waits[sem] = val

        for b in reads:
            for sem, val in b.w.items():
                need(sem, val)
        for b in writes:
            for sem, val in b.w.items():
                need(sem, val)
            for sem, val in b.r.items():
                need(sem, val)
        for sem, val in extra:
            need(sem, val)
        return waits

    @staticmethod
    def _mark(ev, reads, writes):
        sem, val = ev
        for b in reads:
            if b.r.get(sem, 0) < val:
                b.r[sem] = val
        for b in writes:
            if b.w.get(sem, 0) < val:
                b.w[sem] = val

    def op(self, eng, fn, reads=(), writes=(), inc=True):
        E = self.engs[eng]
        waits = self._collect(E, reads, writes)
        if eng == "tensor":
            waits.pop(E.sem, None)
        for sem, val in waits.items():
            E.waited[sem] = val
        if inc:
            E.cnt += 1
            ev = (E.sem, E.cnt)
        else:
            ev = (E.sem, E.cnt + 1)
        E.ops.append((sorted(waits.items()), fn, (E.sem, 1) if inc else None))
        self._mark(ev, reads, writes)

    def dma(self, eng, out, in_, owner, reads=(), writes=(), **kw):
        E = self.engs[eng]
        if owner.sem is None:
            owner.sem = self.new_sem("d_" + owner.name)
        waits = self._collect(E, reads, writes, extra=[(owner.sem, owner.cnt)] if owner.cnt else [])
        for sem, val in waits.items():
            E.waited[sem] = val
        owner.cnt += 16
        ev = (owner.sem, owner.cnt)
        E.ops.append((sorted(waits.items()), lambda e, o=out, i=in_: e.dma_start(out=o, in_=i, **kw), (owner.sem, 16)))
        self._mark(ev, reads, writes)

    def barrier_wait(self, eng, bufs):
        E = self.engs[eng]
        waits = self._collect(E, [], bufs)
        for sem, val in waits.items():
            E.waited[sem] = val
        if waits:
            E.ops.append((sorted(waits.items()), None, None))

    def check(self):
        sems = {}
        pos = {n: 0 for n in self.engs}
        progress = True
        while progress:
            progress = False
            for n, E in self.engs.items():
                while pos[n] < len(E.ops):
                    waits, fn, inc = E.ops[pos[n]]
                    if any(sems.get(s, 0) < v for s, v in waits):
                        break
                    if inc is not None:
                        sems[inc[0]] = sems.get(inc[0], 0) + inc[1]
                    pos[n] += 1
                    progress = True
        stuck = {n: (pos[n], len(E.ops)) for n, E in self.engs.items() if pos[n] < len(E.ops)}
        if stuck:
            msg = []
            for n in stuck:
                waits, fn, inc = self.engs[n].ops[pos[n]]
                msg.append("%s@%d waits %s have %s" % (n, pos[n], waits, [(s, sems.get(s, 0)) for s, _ in waits]))
            raise RuntimeError("DEADLOCK: " + "; ".join(msg))
        return {n: len(E.ops) for n, E in self.engs.items()}

    def emit(self):
        self.barrier_wait("sync", self.out_bufs)
        self.check()
        nc = self.nc
        semh = self.semh

        def replay(e, E):
            for waits, fn, inc in E.ops:
                for sem, val in waits:
                    e.wait_ge(semh[sem], val)
                if fn is None:
                    continue
                inst = fn(e)
                if inc is not None:
                    inst.then_inc(semh[inc[0]], inc[1])

        with nc.Block() as block:
            @block.tensor
            def _(e):
                replay(e, self.engs["tensor"])

            @block.vector
            def _(e):
                replay(e, self.engs["vector"])

            @block.scalar
            def _(e):
                replay(e, self.engs["scalar"])

            @block.gpsimd
            def _(e):
                replay(e, self.engs["gpsimd"])

            @block.sync
            def _(e):
                replay(e, self.engs["sync"])
        self.es.close()
        return nc


class Rows:
    def __init__(self, P, vecs_ap, nvec):
        self.P = P
        self.act = P.sb("act", [128, FS * HALF], BF16)
        self.xn = P.sb("xn", [128, KC * HALF], BF16)
        self.xj = [P.vbuf("xj0"), P.vbuf("xj1")]
        self.win = [P.sb("win%d" % i, [128, KC * 128], BF16) for i in range(4)]
        self.wout = [P.sb("wout%d" % i, [128, FS * 128], BF16) for i in range(2)]
        self.ht = [P.sb("ht%d" % i, [128, TT], F32) for i in range(6)]
        self.tmp = [P.sb("tmp%d" % i, [128, TT], F32) for i in range(3)]
        self.sq = [P.sb("sq%d" % i, [128, TT], BF16) for i in range(2)]
        self.rstd = [P.sb("rstd%d" % i, [128, TT], F32) for i in range(2)]
        self.epsc = P.sb("epsc", [128, 2], F32)
        self.rtmp = P.sb("rtmp", [128, TT], F32)
        self.ob = [P.sb("ob%d" % i, [128, TT], BF16) for i in range(4)]
        self.ones = P.sb("ones", [128, 128], BF16)
        self.vecs = P.sb("vecs", [128, nvec], F32)
        self.pt = P.sb("pt", [128, 2 * HALF], BF16)
        self.wp = [P.sb("wp%d" % i, [128, 2 * 128], BF16) for i in range(2)]
        self.banks = [P.ps("bank%d" % i) for i in range(8)]
        self.cnt = {}
        P.op("vector", lambda e: e.memset(self.ones.t[:], 1.0), writes=[self.ones])
        P.op("vector", lambda e: e.memset(self.epsc.t[:, 0:1], RMS_EPS), writes=[self.epsc])
        P.dma("sync", self.vecs.t[:], vecs_ap, self.vecs, writes=[self.vecs])

    def rr(self, key, lst):
        i = self.cnt.get(key, 0)
        self.cnt[key] = i + 1
        return lst[i % len(lst)]

    def xs(self, k, j):
        return self.xn.t[:, k * HALF + j * TT:k * HALF + (j + 1) * TT]

    def stats(self, hsrc, hsrc_bufs, n0, j, eps, dim=D):
        P = self.P
        hs = hsrc.rearrange("(k p) n -> k p n", p=128)
        bank = self.banks[6 + j]
        for k in range(KC):
            ht = self.rr("ht", self.ht)
            P.dma("gpsimd", ht.t[:], hs[k, :, n0:n0 + TT], ht, reads=[hsrc_bufs[n0 // TT]], writes=[ht])
            sq = self.rr("sq", self.sq)
            P.op("scalar", lambda e, s=sq, t=ht: e.activation(out=s.t[:], in_=t.t[:], func=AF.Square),
                 reads=[ht], writes=[sq])
            P.op("tensor", lambda e, s=sq, k=k: e.matmul(bank.t[:], self.ones.t[:], s.t[:], start=(k == 0), stop=(k == KC - 1)),
                 reads=[sq, self.ones], writes=[bank], inc=True)
        P.op("scalar", lambda e: e.activation(out=self.rtmp.t[:], in_=bank.t[:], func=AF.Sqrt, bias=self.epsc.t[:, 0:1], scale=1.0 / dim),
             reads=[bank, self.epsc], writes=[self.rtmp])
        P.op("vector", lambda e: e.reciprocal(out=self.rstd[j].t[:], in_=self.rtmp.t[:]), reads=[self.rtmp], writes=[self.rstd[j]])

    def norm(self, hsrc, hsrc_bufs, half, gcol, eps=RMS_EPS):
        P = self.P
        hs = hsrc.rearrange("(k p) n -> k p n", p=128)
        for j in range(2):
            n0 = half * HALF + j * TT
            bank = self.banks[6 + j]
            for k in range(KC):
                ht = self.rr("ht", self.ht)
                P.dma("gpsimd", ht.t[:], hs[k, :, n0:n0 + TT], ht, reads=[hsrc_bufs[n0 // TT]], writes=[ht])
                sq = self.rr("sq", self.sq)
                P.op("scalar", lambda e, s=sq, t=ht: e.activation(out=s.t[:], in_=t.t[:], func=AF.Square),
                     reads=[ht], writes=[sq])
                P.op("tensor", lambda e, s=sq, k=k, bank=bank: e.matmul(bank.t[:], self.ones.t[:], s.t[:], start=(k == 0), stop=(k == KC - 1)),
                     reads=[sq, self.ones], writes=[bank], inc=True)
                P.op("vector", lambda e, k=k, j=j, t=ht: e.tensor_scalar(
                    out=self.xs(k, j), in0=t.t[:], scalar1=self.vecs.t[:, gcol + k:gcol + k + 1], scalar2=None, op0=ALU.mult),
                    reads=[ht, self.vecs], writes=[self.xn, self.xj[j]])
            P.op("scalar", lambda e, bank=bank: e.activation(out=self.rtmp.t[:], in_=bank.t[:], func=AF.Sqrt, bias=self.epsc.t[:, 0:1], scale=1.0 / D),
                 reads=[bank, self.epsc], writes=[self.rtmp])
            P.op("vector", lambda e, j=j: e.reciprocal(out=self.rstd[j].t[:], in_=self.rtmp.t[:]), reads=[self.rtmp], writes=[self.rstd[j]])
        for j in range(2):
            for k in range(KC):
                P.op("vector", lambda e, k=k, j=j: e.tensor_tensor(out=self.xs(k, j), in0=self.xs(k, j), in1=self.rstd[j].t[:], op=ALU.mult),
                     reads=[self.rstd[j], self.xj[j]], writes=[self.xj[j]])

    def load_w(self, slot, w_ap, c0, ncol, kchunks, r0=0):
        src = w_ap[r0:r0 + kchunks * 128, :].rearrange("(k p) c -> p k c", p=128)[:, :, c0:c0 + ncol]
        dst = slot.t[:, 0:kchunks * ncol].rearrange("p (k c) -> p k c", k=kchunks)
        self.P.dma("gpsimd", dst, src, slot, writes=[slot])

    def mm_group(self, bank, lhs_fn, rhs_fn, nk, reads, ncol=TT):
        for k in range(nk):
            self.P.op("tensor", lambda e, k=k: e.matmul(bank.t[:, 0:ncol], lhs_fn(k), rhs_fn(k), start=(k == 0), stop=(k == nk - 1)),
                      reads=reads, writes=[bank], inc=(k == nk - 1))

    def add_store(self, bank, scale, hsrc, hsrc_bufs, hdst, hdst_bufs, oc, n0, mul_by=None):
        P = self.P
        hs = hsrc.rearrange("(k p) n -> k p n", p=128)
        hd = hdst.rearrange("(k p) n -> k p n", p=128)
        ht = self.rr("ht", self.ht)
        P.dma("gpsimd", ht.t[:], hs[oc, :, n0:n0 + TT], ht, reads=[hsrc_bufs[n0 // TT]], writes=[ht])
        if mul_by is None:
            P.op("vector", lambda e, t=ht, b=bank: e.scalar_tensor_tensor(
                out=t.t[:], in0=b.t[:], scalar=scale, in1=t.t[:], op0=ALU.mult, op1=ALU.add), reads=[bank, ht], writes=[ht])
        else:
            P.op("vector", lambda e, m=mul_by, b=bank: e.tensor_tensor(out=m.t[:], in0=m.t[:], in1=b.t[:], op=ALU.mult),
                 reads=[bank, mul_by], writes=[mul_by])
            P.op("vector", lambda e, t=ht, m=mul_by: e.tensor_tensor(out=t.t[:], in0=t.t[:], in1=m.t[:], op=ALU.add),
                 reads=[mul_by, ht], writes=[ht])
        P.dma("sync", hd[oc, :, n0:n0 + TT], ht.t[:], ht, reads=[ht], writes=[hdst_bufs[n0 // TT]])

    def ffn(self, hsrc, hsrc_bufs, hdst, hdst_bufs, half, w_in, w_out):
        P = self.P
        for s in range(FS):
            wg = self.rr("win", self.win)
            wu = self.rr("win", self.win)
            self.load_w(wg, w_in, s * 128, 128, KC)
            self.load_w(wu, w_in, DFF + s * 128, 128, KC)
            for j in range(2):
                ba = self.rr("bA", self.banks[0:2])
                bb = self.rr("bB", self.banks[2:4])
                self.mm_group(ba, lambda k, w=wg: w.t[:, k * 128:(k + 1) * 128], lambda k, j=j: self.xs(k, j), KC, [wg, self.xn, self.xj[j]])
                self.mm_group(bb, lambda k, w=wu: w.t[:, k * 128:(k + 1) * 128], lambda k, j=j: self.xs(k, j), KC, [wu, self.xn, self.xj[j]])
                tmp = self.rr("tmp", self.tmp)
                P.op("scalar", lambda e, t=tmp, b=ba: e.activation(out=t.t[:], in_=b.t[:], func=AF.Silu), reads=[ba], writes=[tmp])
                P.op("vector", lambda e, t=tmp, b=bb, s=s, j=j: e.tensor_tensor(
                    out=self.act.t[:, s * HALF + j * TT:s * HALF + (j + 1) * TT], in0=t.t[:], in1=b.t[:], op=ALU.mult),
                    reads=[tmp, bb], writes=[self.act])
        for oc in range(KC):
            wo = self.rr("wout", self.wout)
            self.load_w(wo, w_out, oc * 128, 128, FS)
            for j in range(2):
                n0 = half * HALF + j * TT
                bo = self.rr("bO", self.banks[4:6])
                self.mm_group(bo, lambda k, w=wo: w.t[:, k * 128:(k + 1) * 128],
                              lambda k, j=j: self.act.t[:, k * HALF + j * TT:k * HALF + (j + 1) * TT], FS, [wo, self.act])
                self.add_store(bo, 0.5, hsrc, hsrc_bufs, hdst, hdst_bufs, oc, n0)

    def ple(self, hsrc, hsrc_bufs, hdst, hdst_bufs, half, w_gate, bcol, pT, w_proj):
        P = self.P
        ptv = pT.rearrange("(k p) n -> p k n", p=128)[:, :, half * HALF:(half + 1) * HALF]
        P.dma("gpsimd", self.pt.t[:].rearrange("p (k n) -> p k n", k=2), ptv, self.pt, writes=[self.pt])
        for oc in range(KC):
            wg = self.rr("win", self.win)
            self.load_w(wg, w_gate, oc * 128, 128, KC)
            wp = self.rr("wp", self.wp)
            self.load_w(wp, w_proj, oc * 128, 128, 2)
            for j in range(2):
                n0 = half * HALF + j * TT
                ba = self.rr("bA", self.banks[0:2])
                bb = self.rr("bB", self.banks[2:4])
                self.mm_group(ba, lambda k, w=wg: w.t[:, k * 128:(k + 1) * 128], lambda k, j=j: self.xs(k, j), KC, [wg, self.xn, self.xj[j]])
                self.mm_group(bb, lambda k, w=wp: w.t[:, k * 128:(k + 1) * 128],
                              lambda k, j=j: self.pt.t[:, k * HALF + j * TT:k * HALF + (j + 1) * TT], 2, [wp, self.pt])
                tmp = self.rr("tmp", self.tmp)
                P.op("scalar", lambda e, t=tmp, b=ba, oc=oc: e.activation(
                    out=t.t[:], in_=b.t[:], func=AF.Sigmoid, bias=self.vecs.t[:, bcol + oc:bcol + oc + 1], scale=1.0),
                    reads=[ba, self.vecs], writes=[tmp])
                self.add_store(bb, 1.0, hsrc, hsrc_bufs, hdst, hdst_bufs, oc, n0, mul_by=tmp)

    def proj_fm(self, half, w_ap, c0, nchunks, dst_fn, dst_bufs, evac="copy"):
        P = self.P
        for c in range(nchunks):
            w = self.rr("win", self.win)
            self.load_w(w, w_ap, c0 + c * 128, 128, KC)
            for j in range(2):
                b = self.rr("bA", self.banks[0:4])
                self.mm_group(b, lambda k, w=w: w.t[:, k * 128:(k + 1) * 128], lambda k, j=j: self.xs(k, j), KC, [w, self.xn, self.xj[j]])
                ob = self.rr("ob", self.ob)
                eng = self.rr("evac", ["scalar", "vector"])
                if eng == "scalar":
                    P.op("scalar", lambda e, o=ob, b=b: e.activation(out=o.t[:], in_=b.t[:], func=AF.Copy), reads=[b], writes=[ob])
                else:
                    P.op("vector", lambda e, o=ob, b=b: e.tensor_copy(out=o.t[:], in_=b.t[:]), reads=[b], writes=[ob])
                P.dma("sync", dst_fn(c, j), ob.t[:], ob, reads=[ob], writes=dst_bufs)

    def oproj(self, hsrc, hsrc_bufs, hdst, hdst_bufs, half, w_o):
        for oc in range(KC):
            w = self.rr("win", self.win)
            self.load_w(w, w_o, oc * 128, 128, KC)
            for j in range(2):
                n0 = half * HALF + j * TT
                b = self.rr("bO", self.banks[4:6])
                self.mm_group(b, lambda k, w=w: w.t[:, k * 128:(k + 1) * 128], lambda k, j=j: self.xs(k, j), KC, [w, self.xn, self.xj[j]])
                self.add_store(b, 1.0, hsrc, hsrc_bufs, hdst, hdst_bufs, oc, n0)

    def final_norm(self, hsrc, hsrc_bufs, out_ap, out_bufs, half, gcol):
        P = self.P
        hs = hsrc.rearrange("(k p) n -> k p n", p=128)
        od = out_ap.rearrange("(k p) n -> k p n", p=128)
        for j in range(2):
            self.stats(hsrc, hsrc_bufs, half * HALF + j * TT, j, RMS_EPS)
        for j in range(2):
            n0 = half * HALF + j * TT
            for k in range(KC):
                ht = self.rr("ht", self.ht)
                P.dma("gpsimd", ht.t[:], hs[k, :, n0:n0 + TT], ht, reads=[hsrc_bufs[n0 // TT]], writes=[ht])
                P.op("vector", lambda e, k=k, t=ht, j=j: e.scalar_tensor_tensor(
                    out=t.t[:], in0=t.t[:], scalar=self.vecs.t[:, gcol + k:gcol + k + 1], in1=self.rstd[j].t[:],
                    op0=ALU.mult, op1=ALU.mult), reads=[ht, self.vecs, self.rstd[j]], writes=[ht])
                P.dma("sync", od[k, :, n0:n0 + TT], ht.t[:], ht, reads=[ht], writes=out_bufs)

    def evac_bf16(self, bank, ncol=TT):
        P = self.P
        ob = self.rr("ob", self.ob)
        eng = self.rr("evac", ["scalar", "vector"])
        if eng == "scalar":
            P.op("scalar", lambda e, o=ob, b=bank: e.activation(out=o.t[:, 0:ncol], in_=b.t[:, 0:ncol], func=AF.Copy), reads=[bank], writes=[ob])
        else:
            P.op("vector", lambda e, o=ob, b=bank: e.tensor_copy(out=o.t[:, 0:ncol], in_=b.t[:, 0:ncol]), reads=[bank], writes=[ob])
        return ob

    def proj_tm(self, half, w_ap, c0, ncol, dst_fn, dst_bufs):
        P = self.P
        w = self.rr("wout", self.wout)
        self.load_w(w, w_ap, c0, ncol, KC)
        for tb in range(HALF // 128):
            b = self.rr("bA", self.banks[0:4])
            self.mm_group(b, lambda k, tb=tb: self.xn.t[:, k * HALF + tb * 128:k * HALF + (tb + 1) * 128],
                          lambda k, w=w: w.t[:, k * ncol:(k + 1) * ncol], KC, [w, self.xn] + self.xj, ncol=ncol)
            ob = self.evac_bf16(b, ncol)
            P.dma("sync", dst_fn(tb), ob.t[:, 0:ncol], ob, reads=[ob], writes=dst_bufs)

    def load_xn(self, src, src_bufs, half):
        v = src.rearrange("(k p) n -> p k n", p=128)[:, :, half * HALF:(half + 1) * HALF]
        self.P.dma("sync", self.xn.t[:].rearrange("p (k n) -> p k n", k=KC), v, self.xn, reads=src_bufs, writes=[self.xn])

    def proj_conv_in(self, half, w_ap, zT, bgT, dst_bufs):
        P = self.P
        zd = zT.rearrange("(k p) n -> k p n", p=128)
        bd = bgT.rearrange("(k p) n -> k p n", p=128)
        for c in range(KC):
            wb = self.rr("win", self.win)
            wc = self.rr("win", self.win)
            wu = self.rr("win", self.win)
            self.load_w(wb, w_ap, c * 128, 128, KC)
            self.load_w(wc, w_ap, D + c * 128, 128, KC)
            self.load_w(wu, w_ap, 2 * D + c * 128, 128, KC)
            for j in range(2):
                n0 = half * HALF + j * TT
                b0 = self.rr("bA", self.banks[0:2])
                b1 = self.rr("bB", self.banks[2:4])
                b2 = self.rr("bO", self.banks[4:6])
                for (b, w) in ((b0, wb), (b1, wc), (b2, wu)):
                    self.mm_group(b, lambda k, w=w: w.t[:, k * 128:(k + 1) * 128], lambda k, j=j: self.xs(k, j), KC, [w, self.xn, self.xj[j]])
                t0 = self.rr("ht", self.ht)
                P.op("scalar", lambda e, t=t0, b=b0: e.activation(out=t.t[:], in_=b.t[:], func=AF.Copy), reads=[b0], writes=[t0])
                P.dma("sync", bd[c, :, n0:n0 + TT], t0.t[:], t0, reads=[t0], writes=dst_bufs)
                t1 = self.rr("ht", self.ht)
                P.op("scalar", lambda e, t=t1, b=b1: e.activation(out=t.t[:], in_=b.t[:], func=AF.Copy), reads=[b1], writes=[t1])
                P.op("vector", lambda e, t=t1, b=b2: e.tensor_tensor(out=t.t[:], in0=t.t[:], in1=b.t[:], op=ALU.mult), reads=[b2, t1], writes=[t1])
                P.dma("sync", zd[c, :, n0:n0 + TT], t1.t[:], t1, reads=[t1], writes=dst_bufs)

    def init_attn(self, masks_ap, nmask, sel_ap, ident_ap):
        P = self.P
        self.masks = P.sb("masks", [128, nmask * 128], BF16)
        self.ident = P.sb("ident", [128, 128], BF16)
        self.sel = P.sb("sel", [128, 16], F32)
        P.dma("gpsimd", self.masks.t[:], masks_ap, self.masks, writes=[self.masks])
        P.dma("gpsimd", self.ident.t[:], ident_ap, self.ident, writes=[self.ident])
        P.dma("sync", self.sel.t[:], sel_ap, self.sel, writes=[self.sel])
        self.NPT = 6
        self.LA = 3
        self.pts = [P.vbuf("pt") for _ in range(self.NPT)]
        self.laccs = [P.sb("lacc%d" % i, [128, TT], F32) for i in range(2)]
        self.PT0 = FS * HALF - self.NPT * TT

    def pt_ap(self, i):
        return self.act.t[:, self.PT0 + i * TT:self.PT0 + (i + 1) * TT]

    def mask_ap(self, mi):
        return self.masks.t[:, mi * 128:(mi + 1) * 128]

    def attn_run(self, blocks, q_ap, qbuf, o_banks, l_bank, scale):
        P = self.P
        n = len(blocks)
        st = [None] * n

        def stage_s(i):
            b = blocks[i]
            c0 = b.get("c0", 0)
            ms = b.get("masks", ())
            sb = self.rr("bS", [self.banks[0], self.banks[1], self.banks[2], self.banks[7]])
            P.op("tensor", lambda e, b=b, sb=sb, c0=c0, ms=ms: e.matmul(sb.t[:, c0:TT], b["K"], q_ap[:, c0:TT], start=True, stop=(len(ms) == 0)),
                 reads=[qbuf] + b["bufs"], writes=[sb], inc=(len(ms) == 0))
            for mi, (map_, mc0, w) in enumerate(ms):
                last = mi == len(ms) - 1
                P.op("tensor", lambda e, sb=sb, map_=map_, mc0=mc0, w=w, last=last: e.matmul(
                    sb.t[:, mc0:mc0 + w], self.ident.t[:], map_, start=False, stop=last),
                    reads=[self.ident, self.masks], writes=[sb], inc=last)
            pi = self.cnt.get("pti", 0)
            self.cnt["pti"] = pi + 1
            pt = self.pts[pi % self.NPT]
            pap = self.pt_ap(pi % self.NPT)
            bias = b.get("bias")
            if bias is None:
                bias = self.sel.t[:, 10:11]
            P.op("scalar", lambda e, sb=sb, pap=pap, c0=c0, bias=bias: e.activation(
                out=pap[:, c0:TT], in_=sb.t[:, c0:TT], func=AF.Exp, bias=bias, scale=scale),
                reads=[sb, self.sel], writes=[pt])
            st[i] = (pt, pap, c0)

        lacc = self.rr("lacc", self.laccs)

        def stage_pv(i):
            pt, pap, c0 = st[i]
            b = blocks[i]
            first, last = (i == 0), (i == n - 1)
            for oi, ob in enumerate(o_banks):
                P.op("tensor", lambda e, ob=ob, v=b["V"][oi], pap=pap, c0=c0, first=first, last=last: e.matmul(
                    ob.t[:, c0:TT], v, pap[:, c0:TT], start=first, stop=last),
                    reads=[pt] + b["bufs"], writes=[ob], inc=(oi == len(o_banks) - 1))
            if first:
                P.op("vector", lambda e, pap=pap: e.tensor_copy(out=lacc.t[:], in_=pap), reads=[pt], writes=[lacc])
            elif last:
                pass
            else:
                P.op("vector", lambda e, pap=pap, c0=c0: e.tensor_tensor(out=lacc.t[:, c0:TT], in0=lacc.t[:, c0:TT], in1=pap[:, c0:TT], op=ALU.add),
                     reads=[pt, lacc], writes=[lacc])

        LA = self.LA
        for i in range(min(LA, n)):
            stage_s(i)
        for i in range(n):
            stage_pv(i)
            if i + LA < n:
                stage_s(i + LA)
        if n > 1:
            ptl, papl, c0l = st[n - 1]
            assert c0l == 0
            P.op("tensor", lambda e: e.matmul(l_bank.t[:], self.onesf.t[:], lacc.t[:], start=True, stop=False),
                 reads=[lacc, self.onesf], writes=[l_bank], inc=False)
            P.op("tensor", lambda e: e.matmul(l_bank.t[:], self.ones.t[:], papl, start=False, stop=True),
                 reads=[ptl, self.ones], writes=[l_bank])
        else:
            P.op("tensor", lambda e: e.matmul(l_bank.t[:], self.onesf.t[:], lacc.t[:], start=True, stop=True),
                 reads=[lacc, self.onesf], writes=[l_bank])

    def attn_A(self, qA, kloc, vloc, kall, vall, own_bufs, src_bufs, attn, attn_bufs):
        P = self.P
        scale = 128.0 ** -0.5
        A = self.act.t
        QO, KO, VO, PO = 0, 8192, 14336, 20480
        qb, kb, vb = P.vbuf("qA"), P.vbuf("kA"), P.vbuf("vA")
        kpb = [P.vbuf("kpA") for _ in range(3)]
        vpb = [P.vbuf("vpA") for _ in range(3)]
        allb = [qb, kb, vb] + kpb + vpb + self.pts
        for b in allb:
            b.r.update(self.act.r); b.w.update(self.act.w)
        osets = [(self.banks[3], self.banks[4]), (self.banks[5], self.banks[6])]
        attn_v = attn.rearrange("(h d) n -> d h n", d=128)
        for kvh in range(4):
            qsrc = qA.rearrange("(h d) (nb q) -> d nb h q", d=128, q=128)[:, :, kvh * 4:(kvh + 1) * 4, :]
            qdst = A[:, QO:QO + 8192].rearrange("p (nb h q) -> p nb h q", nb=16, h=4)
            for hh in range(4):
                P.dma("sync", qdst[:, :, hh, :], qsrc[:, :, hh, :], qb, reads=own_bufs, writes=[qb])
            ksrc = kloc.rearrange("(g v d) n -> d g v n", g=3, v=4)[:, :, kvh, :]
            P.dma("sync", A[:, KO:KO + 6144].rearrange("p (g n) -> p g n", g=3), ksrc, kb, reads=own_bufs, writes=[kb])
            vsrc = vloc.rearrange("(b p) (g v d) -> p g b v d", p=128, g=3, v=4)[:, :, :, kvh, :]
            vdst = A[:, VO:VO + 6144].rearrange("p (g b d) -> p g b d", g=3, b=16)
            for g in range(3):
                P.dma("sync", vdst[:, g], vsrc[:, g], vb, reads=own_bufs, writes=[vb])
            for s in range(3):
                kbase = PO + s * 6144
                vbase = kbase + 3072
                vv = vall.rearrange("(r ih s jl p) (g v d) -> s p g v r ih jl d", r=4, ih=2, s=4, jl=2, p=128, g=3, v=4)[s]
                for g in range(3):
                    r0 = (g * 4 + kvh) * 128
                    base = ((r0 // 256) * 4 + s) * 256 + (r0 % 256)
                    ka = kall[base:base + 128, :].rearrange("d (r j) -> d r j", r=4)
                    if g < 2:
                        P.dma("sync", A[:, kbase + g * 512:kbase + (g + 1) * 512].rearrange("p (r j) -> p r j", r=4),
                              ka[:, :, 384:512], kpb[s], reads=src_bufs, writes=[kpb[s]])
                        P.dma("sync", A[:, vbase + g * 512:vbase + (g + 1) * 512].rearrange("p (r d) -> p r d", r=4),
                              vv[:, g, kvh, :, 1, 1, :], vpb[s], reads=src_bufs, writes=[vpb[s]])
                    else:
                        P.dma("sync", A[:, kbase + 1024:kbase + 3072].rearrange("p (r j) -> p r j", r=4),
                              ka, kpb[s], reads=src_bufs, writes=[kpb[s]])
                        for r in range(4):
                            for ih in range(2):
                                o_ = vbase + 1024 + r * 512 + ih * 256
                                P.dma("sync", A[:, o_:o_ + 256].rearrange("p (jl d) -> p jl d", jl=2),
                                      vv[:, g, kvh, r, ih, :, :], vpb[s], reads=src_bufs, writes=[vpb[s]])
            for r4 in range(4):
                for jb in range(4):
                    blocks = []

                    def add(g, rk, jbk, mi):
                        m4 = [(self.mask_ap(mi), hh * 128, 128) for hh in range(4)]
                        if jbk >= 0:
                            blk = rk * 4 + jbk
                            blocks.append(dict(K=A[:, KO + g * 2048 + blk * 128:KO + g * 2048 + (blk + 1) * 128],
                                               V=[A[:, VO + (g * 16 + blk) * 128:VO + (g * 16 + blk + 1) * 128]],
                                               bufs=[kb, vb], masks=m4))
                        else:
                            jp = jbk + 4
                            for s in range(3):
                                kbase = PO + s * 6144
                                vbase = kbase + 3072
                                if g < 2:
                                    o = g * 512 + rk * 128
                                else:
                                    o = 1024 + (rk * 4 + jp) * 128
                                blocks.append(dict(K=A[:, kbase + o:kbase + o + 128], V=[A[:, vbase + o:vbase + o + 128]],
                                                   bufs=[kpb[s], vpb[s]], masks=m4, bias=self.sel.t[:, s:s + 1]))

                    add(1, r4, jb, 0)
                    add(1, r4, jb - 1, 1)
                    for dj in range(5):
                        add(2, r4, jb - dj, (2, 3, 3, 3, 4)[dj])
                    for rk in range(4):
                        for dj in range(2):
                            add(0, rk, jb - dj, 5 + (r4 - rk + 3) * 2 + dj)
                    nb = r4 * 4 + jb
                    ob, lb = self.rr("oset", osets)
                    self.attn_run(blocks, A[:, QO + nb * 512:QO + (nb + 1) * 512], qb, [ob], lb, scale)
                    rl = self.rr("tmp", self.tmp)
                    P.op("vector", lambda e, rl=rl, lb=lb: e.reciprocal(out=rl.t[:], in_=lb.t[:]), reads=[lb], writes=[rl])
                    o = self.rr("ob", self.ob)
                    P.op("vector", lambda e, o=o, ob=ob, rl=rl: e.tensor_tensor(out=o.t[:], in0=ob.t[:], in1=rl.t[:], op=ALU.mult),
                         reads=[ob, rl], writes=[o])
                    P.dma("sync", attn_v[:, kvh * 4:(kvh + 1) * 4, nb * 128:(nb + 1) * 128],
                          o.t[:].rearrange("p (h q) -> p h q", h=4), o, reads=[o], writes=attn_bufs)
        for b in allb:
            for k, v in b.r.items():
                self.act.r[k] = max(self.act.r.get(k, 0), v)
            for k, v in b.w.items():
                self.act.w[k] = max(self.act.w.get(k, 0), v)

    def lam_setup(self, lam_ap, lam_init):
        P = self.P
        self.lamt = P.sb("lamt", [128, 8], F32)
        self.onesf = P.sb("onesf", [128, 128], F32)
        P.op("vector", lambda e: e.memset(self.onesf.t[:], 1.0), writes=[self.onesf])
        P.dma("sync", self.lamt.t[:, 0:4], lam_ap, self.lamt, writes=[self.lamt])
        L = self.lamt
        P.op("vector", lambda e: e.tensor_tensor(out=L.t[:, 4:5], in0=L.t[:, 0:1], in1=L.t[:, 1:2], op=ALU.mult), reads=[L], writes=[L])
        P.op("vector", lambda e: e.tensor_tensor(out=L.t[:, 5:6], in0=L.t[:, 2:3], in1=L.t[:, 3:4], op=ALU.mult), reads=[L], writes=[L])
        bank = self.banks[7]
        P.op("tensor", lambda e: e.matmul(bank.t[:, 0:2], self.onesf.t[:], L.t[:, 4:6], start=True, stop=True),
             reads=[L, self.onesf], writes=[bank])
        P.op("scalar", lambda e: e.activation(out=L.t[:, 6:8], in_=bank.t[:, 0:2], func=AF.Exp), reads=[bank], writes=[L])
        P.op("vector", lambda e: e.scalar_tensor_tensor(out=L.t[:, 4:5], in0=L.t[:, 7:8], scalar=-lam_init, in1=L.t[:, 6:7],
                                                        op0=ALU.add, op1=ALU.subtract), reads=[L], writes=[L])

    def attn_B(self, qB, kloc, vloc, kall, vall, own_bufs, src_bufs, attn, attn_bufs, gcol, lam_init):
        P = self.P
        scale = 128.0 ** -0.5
        A = self.act.t
        X = self.xn.t
        XF = self.xn.t.bitcast(F32)
        QO, KA, KO, VA = 0, 4096, 20480, 24576
        qb, kob, vob = P.vbuf("qB"), P.vbuf("koB"), P.vbuf("voB")
        kab = [[P.vbuf("kaB") for _ in range(3)] for _ in range(2)]
        vab = [P.vbuf("vaB") for _ in range(3)]
        ocb = [P.vbuf("oc0"), P.vbuf("oc1")]
        dfb = P.vbuf("diff")
        allb = [qb, kob, vob, dfb] + kab[0] + kab[1] + vab + ocb + self.pts
        for b in allb:
            b.r.update(self.act.r); b.w.update(self.act.w)
            b.r.update(self.xn.r); b.w.update(self.xn.w)
        oc_ap = [XF[:, 2048:3072], XF[:, 3072:4096]]
        df_ap = XF[:, 4096:5120]
        o_banks = [self.banks[3], self.banks[4]]
        l_bank = self.banks[5]
        sbank = self.banks[6]
        one_m = 1.0 - lam_init
        P.op("vector", lambda e: e.memset(self.epsc.t[:, 1:2], SUBLN_EPS / (one_m * one_m)), reads=[], writes=[self.epsc])
        attn_v = attn.rearrange("(h c d) n -> h c d n", c=2, d=128)
        for h in range(8):
            qs = qB.rearrange("(h c d) n -> h d c n", c=2, d=128)[h]
            P.dma("sync", A[:, QO:QO + 4096].rearrange("p (c n) -> p c n", c=2), qs, qb, reads=own_bufs, writes=[qb])
            va = vall.rearrange("(i s wl p) (h e) -> h s p i wl e", s=4, wl=2, p=128, e=256)[h]
            for s in range(3):
                for c in range(2):
                    ka = kall.rearrange("(h s c d) n -> h c s d n", s=4, c=2, d=128)[h, c, s]
                    P.dma("sync", A[:, KA + c * 8192 + s * 2048:KA + c * 8192 + (s + 1) * 2048], ka, kab[c][s], reads=src_bufs, writes=[kab[c][s]])
                vd = A[:, VA + s * 4096:VA + (s + 1) * 4096].rearrange("p (i wl e) -> p i wl e", wl=2, e=256)
                for wl in range(2):
                    P.dma("sync", vd[:, :, wl, :], va[s][:, :, wl, :], vab[s], reads=src_bufs, writes=[vab[s]])
            ks = kloc.rearrange("(h c d) n -> h d c n", c=2, d=128)[h]
            P.dma("sync", A[:, KO:KO + 4096].rearrange("p (c n) -> p c n", c=2), ks, kob, reads=own_bufs, writes=[kob])
            vo = vloc.rearrange("(b p) (h e) -> h p b e", p=128, e=256)[h]
            P.dma("sync", X[:, 0:4096].rearrange("p (b e) -> p b e", e=256), vo, vob, reads=own_bufs, writes=[vob])
            for r4 in range(4):
                for c in range(2):
                    blocks = []
                    for s in range(3):
                        for blk in range(16):
                            ko = KA + c * 8192 + s * 2048 + blk * 128
                            vo_ = VA + (s * 16 + blk) * 256
                            blocks.append(dict(K=A[:, ko:ko + 128], V=[A[:, vo_:vo_ + 128], A[:, vo_ + 128:vo_ + 256]],
                                               bufs=[kab[c][s], vab[s]], bias=self.sel.t[:, 3 + s:4 + s]))
                    for jbk in (3, 2, 1, 0):
                        for rk in range(4):
                            blk = rk * 4 + jbk
                            ko = KO + c * 2048 + blk * 128
                            vo_ = blk * 256
                            mi = 0 if rk <= r4 else 19
                            blocks.append(dict(K=A[:, ko:ko + 128], V=[X[:, vo_:vo_ + 128], X[:, vo_ + 128:vo_ + 256]],
                                               bufs=[kob, vob], masks=[(self.mask_ap(mi), jbk * 128, 128)], c0=jbk * 128))
                    q_ap = A[:, QO + c * 2048 + r4 * 512:QO + c * 2048 + (r4 + 1) * 512]
                    self.attn_run(blocks, q_ap, qb, o_banks, l_bank, scale)
                    for e2 in range(2):
                        P.op("vector", lambda e, c=c, e2=e2: e.tensor_copy(out=oc_ap[c][:, e2 * 512:(e2 + 1) * 512], in_=o_banks[e2].t[:]),
                             reads=[o_banks[e2]], writes=[ocb[c]])
                    rl = self.rr("tmp", self.tmp)
                    P.op("vector", lambda e, rl=rl: e.reciprocal(out=rl.t[:], in_=l_bank.t[:]), reads=[l_bank], writes=[rl])
                    for e2 in range(2):
                        P.op("vector", lambda e, c=c, e2=e2, rl=rl: e.tensor_tensor(
                            out=oc_ap[c][:, e2 * 512:(e2 + 1) * 512], in0=oc_ap[c][:, e2 * 512:(e2 + 1) * 512], in1=rl.t[:], op=ALU.mult),
                            reads=[ocb[c], rl], writes=[ocb[c]])
                P.op("vector", lambda e: e.scalar_tensor_tensor(out=df_ap, in0=oc_ap[1], scalar=self.lamt.t[:, 4:5], in1=oc_ap[0],
                                                                op0=ALU.mult, op1=ALU.add), reads=ocb + [self.lamt], writes=[dfb])
                for e2 in range(2):
                    sq = self.rr("sq", self.sq)
                    P.op("scalar", lambda e, sq=sq, e2=e2: e.activation(out=sq.t[:], in_=df_ap[:, e2 * 512:(e2 + 1) * 512], func=AF.Square),
                         reads=[dfb], writes=[sq])
                    P.op("tensor", lambda e, sq=sq, e2=e2: e.matmul(sbank.t[:], self.ones.t[:], sq.t[:], start=(e2 == 0), stop=(e2 == 1)),
                         reads=[sq, self.ones], writes=[sbank], inc=True)
                P.op("scalar", lambda e: e.activation(out=self.rtmp.t[:], in_=sbank.t[:], func=AF.Sqrt, bias=self.epsc.t[:, 1:2],
                                                      scale=1.0 / (256.0 * one_m * one_m)), reads=[sbank, self.epsc], writes=[self.rtmp])
                P.op("vector", lambda e: e.reciprocal(out=self.rstd[0].t[:], in_=self.rtmp.t[:]), reads=[self.rtmp], writes=[self.rstd[0]])
                for e2 in range(2):
                    o = self.rr("ob", self.ob)
                    P.op("vector", lambda e, o=o, e2=e2: e.scalar_tensor_tensor(
                        out=o.t[:], in0=df_ap[:, e2 * 512:(e2 + 1) * 512], scalar=self.vecs.t[:, gcol + e2:gcol + e2 + 1],
                        in1=self.rstd[0].t[:], op0=ALU.mult, op1=ALU.mult), reads=[dfb, self.vecs, self.rstd[0]], writes=[o])
                    P.dma("sync", attn_v[h, e2, :, r4 * 512:(r4 + 1) * 512], o.t[:], o, reads=[o], writes=attn_bufs)
        for b in allb:
            for tgt in (self.act, self.xn):
                for k, v in b.r.items():
                    tgt.r[k] = max(tgt.r.get(k, 0), v)
                for k, v in b.w.items():
                    tgt.w[k] = max(tgt.w.get(k, 0), v)

    def conv_C(self, zT, bgT, zh_all, src_bufs, attn, attn_bufs, wcol):
        P = self.P
        AF32 = self.act.t.bitcast(F32)
        X = self.xn.t
        sets = []
        for i in range(2):
            base = i * 8192
            sets.append(dict(z=AF32[:, base:base + 2048], bg=AF32[:, base + 2048:base + 4096], acc=AF32[:, base + 4096:base + 6144],
                             hin=AF32[:, base + 6144:base + 6144 + 24], hal=AF32[:, base + 6200:base + 6202],
                             out=X[:, i * 2048:(i + 1) * 2048],
                             zb=P.vbuf("cz"), bb=P.vbuf("cbg"), ab=P.vbuf("cacc"), hb=P.vbuf("chin"), ob=P.vbuf("cout")))
        allb = [s[k] for s in sets for k in ("zb", "bb", "ab", "hb", "ob")]
        for b in allb:
            b.r.update(self.act.r); b.w.update(self.act.w)
            b.r.update(self.xn.r); b.w.update(self.xn.w)
        zd = zT.rearrange("(k p) n -> k p n", p=128)
        bd = bgT.rearrange("(k p) n -> k p n", p=128)
        ad = attn.rearrange("(k p) n -> k p n", p=128)
        zh = zh_all.rearrange("(s k p) e -> k p s e", s=4, p=128)
        for c in range(KC):
            S = sets[c % 2]
            P.dma("gpsimd", S["z"], zd[c], S["zb"], reads=src_bufs, writes=[S["zb"]])
            P.dma("gpsimd", S["bg"], bd[c], S["bb"], reads=src_bufs, writes=[S["bb"]])
            P.dma("sync", S["hin"].rearrange("p (s e) -> p s e", s=3), zh[c][:, 0:3, :], S["hb"], reads=src_bufs, writes=[S["hb"]])
            hin, hal = S["hin"], S["hal"]
            P.op("vector", lambda e, hin=hin, hal=hal: e.tensor_scalar(out=hal, in0=hin[:, 0:2], scalar1=self.sel.t[:, 7:8], scalar2=None, op0=ALU.mult),
                 reads=[S["hb"], self.sel], writes=[S["hb"]])
            for s in (1, 2):
                P.op("vector", lambda e, hin=hin, hal=hal, s=s: e.scalar_tensor_tensor(
                    out=hal, in0=hin[:, 8 * s:8 * s + 2], scalar=self.sel.t[:, 7 + s:8 + s], in1=hal, op0=ALU.mult, op1=ALU.add),
                    reads=[S["hb"], self.sel], writes=[S["hb"]])
            w0 = self.vecs.t[:, wcol + c:wcol + c + 1]
            w1 = self.vecs.t[:, wcol + KC + c:wcol + KC + c + 1]
            w2 = self.vecs.t[:, wcol + 2 * KC + c:wcol + 2 * KC + c + 1]
            z, acc = S["z"], S["acc"]
            rd = [S["zb"], S["hb"], self.vecs]
            P.op("vector", lambda e, z=z, acc=acc, w2=w2: e.tensor_scalar(out=acc, in0=z, scalar1=w2, scalar2=None, op0=ALU.mult),
                 reads=rd, writes=[S["ab"]])

            def fma(dst, src, w):
                P.op("vector", lambda e, dst=dst, src=src, w=w: e.scalar_tensor_tensor(out=dst, in0=src, scalar=w, in1=dst, op0=ALU.mult, op1=ALU.add),
                     reads=rd + [S["ab"]], writes=[S["ab"]])

            for r4 in range(4):
                a = acc[:, r4 * 512:(r4 + 1) * 512]
                if r4 >= 1:
                    fma(a, z[:, (r4 - 1) * 512:r4 * 512], w1)
                else:
                    fma(a[:, 1:512], z[:, 3 * 512:3 * 512 + 511], w1)
                    fma(a[:, 0:1], S["hal"][:, 1:2], w1)
                if r4 >= 2:
                    fma(a, z[:, (r4 - 2) * 512:(r4 - 1) * 512], w0)
                else:
                    fma(a[:, 1:512], z[:, (r4 + 2) * 512:(r4 + 2) * 512 + 511], w0)
                    fma(a[:, 0:1], S["hal"][:, r4:r4 + 1], w0)
            P.op("vector", lambda e, S=S: e.tensor_tensor(out=S["out"], in0=S["acc"], in1=S["bg"], op=ALU.mult),
                 reads=[S["ab"], S["bb"]], writes=[S["ob"]])
            P.dma("sync", ad[c], S["out"], S["ob"], reads=[S["ob"]], writes=attn_bufs)
        for b in allb:
            for tgt in (self.act, self.xn):
                for k, v in b.r.items():
                    tgt.r[k] = max(tgt.r.get(k, 0), v)
                for k, v in b.w.items():
                    tgt.w[k] = max(tgt.w.get(k, 0), v)


NV = 400
VL = 80
V_NORMF = 320
V_SUBLN = 336
V_CONV = 338
NMASK = 20
GROUPS = [[0, 1, 2, 3], [4, 5, 6, 7]]
LAM_INIT = 0.8 - 0.6 * float(np.exp(-0.3 * 1))

W_SHAPES = {}
for _i in range(4):
    W_SHAPES["w1i%d" % _i] = (D, 2 * DFF)
    W_SHAPES["w1o%d" % _i] = (DFF, D)
    W_SHAPES["w2i%d" % _i] = (D, 2 * DFF)
    W_SHAPES["w2o%d" % _i] = (DFF, D)
    W_SHAPES["wpg%d" % _i] = (D, D)
    W_SHAPES["wpp%d" % _i] = (PLE, D)
for _j in range(2):
    W_SHAPES["aqkv%d" % _j] = (D, 5120)
    W_SHAPES["ao%d" % _j] = (D, D)
W_SHAPES.update(bqkv=(D, 3 * D), bo=(D, D), cin=(D, 3 * D), cout=(D, D))


def build_net(stop=None, dbg=None):
    P = Prog()
    xT = P.dram_in("xT", [D, TOK])
    pT = [P.dram_in("pT%d" % i, [PLE, TOK]) for i in range(4)]
    vecs = P.dram_in("vecs", [128, NV])
    masks = P.dram_in("masks", [128, NMASK * 128])
    ident = P.dram_in("ident", [128, 128])
    sel = P.dram_in("sel", [128, 16])
    lamT = P.dram_in("lamT", [128, 4])
    class _LazyW(dict):
        def __missing__(self, k):
            self[k] = P.dram_in(k, list(W_SHAPES[k]))
            return self[k]
    W = _LazyW()
    outT = P.dram_out("outT", [D, TOK])
    h = P.dram_tmp("h", [D, TOK])
    q = P.dram_tmp("q", [D, TOK], BF16)
    klA = P.dram_tmp("klA", [1536, TOK], BF16)
    vlA = P.dram_tmp("vlA", [TOK, 1536], BF16)
    kaA = P.dram_tmp("kaA", [4 * 1536, TOK], BF16)
    vaA = P.dram_tmp("vaA", [4 * TOK, 1536], BF16)
    klB = P.dram_tmp("klB", [D, TOK], BF16)
    vlB = P.dram_tmp("vlB", [TOK, D], BF16)
    kaB = P.dram_tmp("kaB", [4 * D, TOK], BF16)
    vaB = P.dram_tmp("vaB", [4 * TOK, D], BF16)
    zT = P.dram_tmp("zT", [D, TOK])
    bgT = P.dram_tmp("bgT", [D, TOK])
    zhl = P.dram_tmp("zhl", [D, 8])
    zha = P.dram_tmp("zha", [4 * D, 8])
    attn = P.dram_tmp("attn", [D, TOK], BF16)

    R = Rows(P, vecs, NV)
    R.init_attn(masks, NMASK, sel, ident)
    R.lam_setup(lamT, LAM_INIT)

    xb = [P.vbuf("x") for _ in range(4)]
    hb = [P.vbuf("h") for _ in range(4)]
    ob = [P.vbuf("out")]
    P.out_bufs += ob
    qb, klb, vlb, kab, vab, atb = (P.vbuf(n) for n in ("q", "kl", "vl", "ka", "va", "attn"))
    zb, zhlb, zhab = P.vbuf("z"), P.vbuf("zhl"), P.vbuf("zha")
    ccb = P.vbuf("cc")

    def cc(in_ap, out_ap, in_bufs, out_bufs):
        E = P.engs["gpsimd"]
        if ccb.sem is None:
            ccb.sem = P.new_sem("cc")
        waits = P._collect(E, in_bufs, out_bufs)
        for s_, v_ in waits.items():
            E.waited[s_] = v_
        ccb.cnt += 1
        E.ops.append((sorted(waits.items()), lambda e, i=in_ap, o=out_ap: e.collective_compute(
            "AllGather", ALU.bypass, replica_groups=GROUPS, ins=[i], outs=[o]), (ccb.sem, 1)))
        P._mark((ccb.sem, ccb.cnt), in_bufs, out_bufs)

    def gather(loc, allt, rows, lb, ab):
        for i in range(rows // 256):
            cc(loc[i * 256:(i + 1) * 256, :], allt[i * 1024:(i + 1) * 1024, :], [lb], [ab])

    state = {"hsrc": xT, "hsb": xb}

    def hs():
        return state["hsrc"], state["hsb"]

    def wrote_h():
        state["hsrc"], state["hsb"] = h, hb

    def chain(fns):
        for f in fns:
            for half in range(2):
                f(half)
            if getattr(f, "writes_h", False):
                wrote_h()

    def mk(fn, writes_h=False):
        fn.writes_h = writes_h
        return fn

    def s_ffn(i, which):
        g = i * VL + (0 if which == 1 else 32)
        wi, wo = W["w%di%d" % (which, i)], W["w%do%d" % (which, i)]

        def f(half):
            src, sb_ = hs()
            R.norm(src, sb_, half, g)
            R.ffn(src, sb_, h, hb, half, wi, wo)
        return mk(f, True)

    def s_ple(i):
        def f(half):
            src, sb_ = hs()
            R.norm(src, sb_, half, i * VL + 48)
            R.ple(src, sb_, h, hb, half, W["wpg%d" % i], i * VL + 64, pT[i], W["wpp%d" % i])
        return mk(f, True)

    def s_oproj(w):
        def f(half):
            src, sb_ = hs()
            R.load_xn(attn, [atb], half)
            R.oproj(src, sb_, h, hb, half, w)
        return mk(f, True)

    def s_projA(i, j):
        w = W["aqkv%d" % j]

        def f(half):
            src, sb_ = hs()
            R.norm(src, sb_, half, i * VL + 16)
            qv = q.rearrange("(c p) n -> c p n", p=128)
            R.proj_fm(half, w, 0, 16, lambda c, jj: qv[c, :, half * HALF + jj * TT:half * HALF + (jj + 1) * TT], [qb])
            kv = klA.rearrange("(c p) n -> c p n", p=128)
            for g in range(3):
                R.proj_fm(half, w, D + g * 1024, 4,
                          lambda c, jj, g=g: kv[g * 4 + c, :, half * HALF + jj * TT:half * HALF + (jj + 1) * TT], [klb])
                for grp in range(2):
                    R.proj_tm(half, w, D + g * 1024 + 512 + grp * 256, 256,
                              lambda tb, g=g, grp=grp: vlA[half * HALF + tb * 128:half * HALF + (tb + 1) * 128,
                                                           g * 512 + grp * 256:g * 512 + (grp + 1) * 256], [vlb])
        return mk(f)

    def s_projB(i):
        w = W["bqkv"]

        def f(half):
            src, sb_ = hs()
            R.norm(src, sb_, half, i * VL + 16)
            qv = q.rearrange("(c p) n -> c p n", p=128)
            R.proj_fm(half, w, 0, 16, lambda c, jj: qv[c, :, half * HALF + jj * TT:half * HALF + (jj + 1) * TT], [qb])
            kv = klB.rearrange("(c p) n -> c p n", p=128)
            R.proj_fm(half, w, D, 16, lambda c, jj: kv[c, :, half * HALF + jj * TT:half * HALF + (jj + 1) * TT], [klb])
            for hh in range(8):
                R.proj_tm(half, w, 2 * D + hh * 256, 256,
                          lambda tb, hh=hh: vlB[half * HALF + tb * 128:half * HALF + (tb + 1) * 128, hh * 256:(hh + 1) * 256], [vlb])
        return mk(f)

    def q_stage(i, w):
        def run():
            qv = q.rearrange("(c p) n -> c p n", p=128)
            for half in (1, 0):
                src, sb_ = hs()
                if half == 0:
                    R.norm(src, sb_, half, i * VL + 16)
                R.proj_fm(half, w, 0, 16, lambda c, jj, half=half: qv[c, :, half * HALF + jj * TT:half * HALF + (jj + 1) * TT], [qb])
        return run

    def s_projC(i):
        def f(half):
            src, sb_ = hs()
            R.norm(src, sb_, half, i * VL + 16)
            R.proj_conv_in(half, W["cin"], zT, bgT, [zb])
        return mk(f)

    def s_final():
        def f(half):
            src, sb_ = hs()
            R.final_norm(src, sb_, outT, ob, half, V_NORMF)
        return mk(f)

    def halo():
        P.dma("sync", zhl[:, 0:1], zT[:, 2 * 512 + 511:2 * 512 + 512], zhlb, reads=[zb], writes=[zhlb], allow_slow_non_contiguous=True)
        P.dma("sync", zhl[:, 1:2], zT[:, 3 * 512 + 511:3 * 512 + 512], zhlb, reads=[zb], writes=[zhlb], allow_slow_non_contiguous=True)
        cc(zhl, zha, [zhlb], [zhab])

    def gA():
        gather(klA, kaA, 1536, klb, kab)
        gather(vlA, vaA, TOK, vlb, vab)

    def gB():
        gather(klB, kaB, D, klb, kab)
        gather(vlB, vaB, TOK, vlb, vab)

    mixA = lambda: R.attn_A(q, klA, vlA, kaA, vaA, [qb, klb, vlb], [kab, vab], attn, [atb])
    mixB = lambda: R.attn_B(q, klB, vlB, kaB, vaB, [qb, klb, vlb], [kab, vab], attn, [atb], V_SUBLN, LAM_INIT)
    mixC = lambda: R.conv_C(zT, bgT, zha, [zb, zhab], attn, [atb], V_CONV)
    C1 = lambda th: (lambda: chain([th()]))
    stages = [
        C1(lambda: s_ffn(0, 1)), C1(lambda: s_projA(0, 0)), gA, mixA, C1(lambda: s_oproj(W["ao0"])), C1(lambda: s_ffn(0, 2)), C1(lambda: s_ple(0)),
        C1(lambda: s_ffn(1, 1)), C1(lambda: s_projB(1)), gB, mixB, C1(lambda: s_oproj(W["bo"])), C1(lambda: s_ffn(1, 2)), C1(lambda: s_ple(1)),
        C1(lambda: s_ffn(2, 1)), C1(lambda: s_projC(2)), halo, mixC, C1(lambda: s_oproj(W["cout"])), C1(lambda: s_ffn(2, 2)), C1(lambda: s_ple(2)),
        C1(lambda: s_ffn(3, 1)), C1(lambda: s_projA(3, 1)), gA, mixA, C1(lambda: s_oproj(W["ao1"])), C1(lambda: s_ffn(3, 2)), C1(lambda: s_ple(3)),
        C1(lambda: s_final()),
    ]
    n = len(stages) if stop is None else stop
    for st in stages[:n]:
        st()
    if stop is not None:
        if dbg is None:
            P.dma("sync", outT, h, ob[0], reads=hb, writes=ob)
        else:
            src = {"attn": attn, "q": q, "klA": klA, "vlA": vlA, "klB": klB, "vlB": vlB}[dbg]
            dbo = P.dram_out("dbg", list(src.shape), BF16)
            P.dma("sync", dbo, src, ob[0], reads=[atb, qb, klb, vlb], writes=ob)
            P.dma("sync", outT, h, ob[0], reads=hb, writes=ob)
    nc = P.emit()
    nc._in_names = list(P.in_names)
    return nc


def _r4perm():
    n = np.arange(TOK)
    return 4 * (n % 512) + n // 512


def _masks():
    jk = np.arange(128)[:, None]
    jq = np.arange(128)[None, :]
    vis = []
    vis.append(jk <= jq)
    vis.append(jk >= jq)
    comb = ((jq - jk) % 4) == 0
    vis.append((jk <= jq) & comb)
    vis.append(comb | (jk < -1))
    vis.append((jk >= jq) & comb)
    for dr in range(-3, 4):
        for djb in range(2):
            dt = 4 * (128 * djb + jq - jk) + dr
            vis.append((dt >= 0) & (dt <= 128))
    vis.append(jk < jq)
    m = np.concatenate([np.where(v, 0.0, NEG) for v in vis], axis=1).astype(np.float32)
    assert m.shape == (128, NMASK * 128)
    return m


def _col16(v):
    return np.ascontiguousarray(np.asarray(v, np.float32).reshape(-1, 128).T)


def make_inputs(inputs):
    perm = _r4perm()
    x = np.asarray(inputs["x"], np.float32)
    p = np.asarray(inputs["p"], np.float32)
    vecs = np.zeros((128, NV), np.float32)
    for i in range(4):
        b = i * VL
        vecs[:, b:b + 16] = _col16(inputs["norm_ffn1"][i])
        vecs[:, b + 16:b + 32] = _col16(inputs["norm_mix"][i])
        vecs[:, b + 32:b + 48] = _col16(inputs["norm_ffn2"][i])
        vecs[:, b + 48:b + 64] = _col16(inputs["norm_ple"][i])
        vecs[:, b + 64:b + 80] = _col16(inputs["b_ple_gate"][i])
    vecs[:, V_NORMF:V_NORMF + 16] = _col16(inputs["norm_f"])
    vecs[:, V_SUBLN:V_SUBLN + 2] = _col16(inputs["b_subln"][0])
    for k in range(3):
        vecs[:, V_CONV + 16 * k:V_CONV + 16 * (k + 1)] = _col16(inputs["c_conv_w"][0][k])
    shared = {"vecs": vecs, "masks": _masks(), "ident": np.eye(128, dtype=np.float32),
              "lamT": np.ascontiguousarray(np.asarray(inputs["b_lambda"][0], np.float32).T)}
    for i in range(4):
        shared["w1i%d" % i] = np.asarray(inputs["w_ffn1_in"][i], np.float32)
        shared["w1o%d" % i] = np.asarray(inputs["w_ffn1_out"][i], np.float32)
        shared["w2i%d" % i] = np.asarray(inputs["w_ffn2_in"][i], np.float32)
        shared["w2o%d" % i] = np.asarray(inputs["w_ffn2_out"][i], np.float32)
        shared["wpg%d" % i] = np.asarray(inputs["w_ple_gate"][i], np.float32)
        shared["wpp%d" % i] = np.asarray(inputs["w_ple_proj"][i], np.float32)
    for j in range(2):
        shared["aqkv%d" % j] = np.asarray(inputs["a_w_qkv"][j], np.float32)
        shared["ao%d" % j] = np.asarray(inputs["a_w_o"][j], np.float32)
    shared["bqkv"] = np.asarray(inputs["b_w_qkv"][0], np.float32)
    shared["bo"] = np.asarray(inputs["b_w_o"][0], np.float32)
    shared["cin"] = np.asarray(inputs["c_w_in"][0], np.float32)
    shared["cout"] = np.asarray(inputs["c_w_out"][0], np.float32)
    maps = []
    for core in range(NCORES):
        b, c = divmod(core, 4)
        sl = slice(c * TOK, (c + 1) * TOK)
        m = dict(shared)
        m["xT"] = np.ascontiguousarray(x[b, sl][perm].T)
        for i in range(4):
            m["pT%d" % i] = np.ascontiguousarray(p[i, b, sl][perm].T)
        s = np.zeros((128, 16), np.float32)
        for k in range(3):
            s[:, k] = 0.0 if k == c - 1 else NEG
            s[:, 7 + k] = 1.0 if k == c - 1 else 0.0
        for k in range(4):
            s[:, 3 + k] = 0.0 if k < c else NEG
        m["sel"] = s
        maps.append(m)
    return maps


def assemble(results, key="outT"):
    perm = _r4perm()
    out = np.empty((2, 4 * TOK, D), np.float32)
    for core in range(NCORES):
        b, c = divmod(core, 4)
        o = np.asarray(results[core][key]).astype(np.float32).T
        blk = np.empty_like(o)
        blk[perm] = o
        out[b, c * TOK:(c + 1) * TOK] = blk
    return out


_NC_CACHE = {}


def kernel(**inputs):
    if "net" not in _NC_CACHE:
        _NC_CACHE["net"] = build_net()
    nc = _NC_CACHE["net"]
    maps = [{k: m[k] for k in nc._in_names} for m in make_inputs(inputs)]
    res = run_bass_kernel_spmd(nc, maps, core_ids=list(range(NCORES)))
    return assemble(res.results)
```
